# Optimizing a Trainium2 kernel written in Bass

```python
import functools
import jax
import jax.numpy as jnp
from jax import lax
import numpy as np


D_MODEL = 1024
BATCH = 8
SEQ = 4096
DEPTH = 2

GRID_W = 64
CTX_LEN = 256
CHUNK = 64
MIX_W = D_MODEL
N_BRANCH = 3
RET_HEADS = 4
RET_DK = 256
RET_DV = MIX_W // RET_HEADS
MLSTM_HEADS = 4
MLSTM_DK = 128
MLSTM_DV = MIX_W // MLSTM_HEADS
GDN_HEADS = 8
GDN_DK = 128
GDN_DV = MIX_W // GDN_HEADS
CONV_K = 5
D_FF = 4 * D_MODEL
ROPE_BASE = 10000.0
LN_EPS = 1e-5
RMS_EPS = 1e-6
DN_ALPHA = (2 * DEPTH) ** 0.25
DN_BETA = (8 * DEPTH) ** -0.25

RET_WIDTHS = (RET_HEADS * RET_DK, RET_HEADS * RET_DK, MIX_W, MIX_W)
MLSTM_WIDTHS = (MLSTM_HEADS * MLSTM_DK, MLSTM_HEADS * MLSTM_DK, MIX_W, MIX_W, 2 * MLSTM_HEADS, 2 * MLSTM_HEADS)
GDN_QKV_W = 2 * GDN_HEADS * GDN_DK + MIX_W
GDN_WIDTHS = (GDN_QKV_W, MIX_W, 2 * GDN_HEADS, 2 * GDN_HEADS)
GATE_W = N_BRANCH * D_MODEL
GROUP_WIDTHS = (sum(RET_WIDTHS), sum(MLSTM_WIDTHS), sum(GDN_WIDTHS), GATE_W)
IN_DIM = sum(GROUP_WIDTHS)

kernel_name = 'hybrid_ret_mlstm_gdn_prefix_block'


def _split_cols(t, widths):
    idx = []
    acc = 0
    for w in widths[:-1]:
        acc += w
        idx.append(acc)
    return jnp.split(t, idx, axis=-1)


def _layernorm(t):
    tf = t.astype(jnp.float32)
    mu = jnp.mean(tf, -1, keepdims=True)
    var = jnp.mean(jnp.square(tf - mu), -1, keepdims=True)
    return ((tf - mu) * lax.rsqrt(var + LN_EPS)).astype(t.dtype)


def _rmsnorm(t):
    return t * lax.rsqrt(jnp.mean(jnp.square(t), -1, keepdims=True) + RMS_EPS)


def _l2norm(t):
    return t * lax.rsqrt(jnp.sum(jnp.square(t), -1, keepdims=True) + RMS_EPS)


def _modulate(t, shift, scale):
    return _layernorm(t) * (1.0 + scale) + shift


def _post_norm(resid, update, g, b):
    return _layernorm(DN_ALPHA * resid + update) * g + b


def _heads(t, n_heads):
    B, L, _ = t.shape
    return t.reshape(B, L, n_heads, -1).transpose(0, 2, 1, 3)


def _merge_heads(t):
    B, H, L, d = t.shape
    return t.transpose(0, 2, 1, 3).reshape(B, L, H * d)


def _dir_heads(t, n_heads):
    B, L, _ = t.shape
    return t.reshape(B, L, 2, n_heads).transpose(2, 0, 3, 1)


def _reverse_segments(t, n_ctx):
    return jnp.concatenate([jnp.flip(t[:, :, :n_ctx], 2), jnp.flip(t[:, :, n_ctx:], 2)], 2)


def _chunks(t):
    B, H, L = t.shape[:3]
    return t.reshape(B, H, L // CHUNK, CHUNK, *t.shape[3:])


def _grid_angles(n_ctx, n_rows):
    n_freq = RET_DK // 4
    freqs = ROPE_BASE ** (-jnp.arange(n_freq, dtype=jnp.float32) / n_freq)
    row = jnp.broadcast_to(jnp.arange(n_rows, dtype=jnp.float32)[:, None], (n_rows, GRID_W)).reshape(-1)
    col = jnp.broadcast_to(jnp.arange(GRID_W, dtype=jnp.float32)[None, :], (n_rows, GRID_W)).reshape(-1)
    lat = jnp.stack([row[:, None] * freqs, col[:, None] * freqs], 1)
    return jnp.concatenate([jnp.zeros((n_ctx, 2, n_freq), jnp.float32), lat], 0)


def _rope_2d(t, ang):
    B, H, L, dk = t.shape
    tt = t.reshape(B, H, L, 2, 2, dk // 4)
    a, b = tt[..., 0, :], tt[..., 1, :]
    cos, sin = jnp.cos(ang), jnp.sin(ang)
    return jnp.stack([a * cos - b * sin, b * cos + a * sin], axis=-2).reshape(B, H, L, dk)


def _short_conv(t, w):
    return lax.conv_general_dilated(
        t, w[:, None, :].astype(t.dtype), window_strides=(1,),
        padding=((CONV_K // 2, CONV_K // 2),), dimension_numbers=('NWC', 'WIO', 'NWC'),
        feature_group_count=t.shape[-1])


def _retention_scan(q, k, v, log_gamma):
    B, H, L, dk = q.shape
    dv = v.shape[-1]
    qc, kc, vc = _chunks(q), _chunks(k), _chunks(v)
    pos = jnp.arange(CHUNK, dtype=jnp.float32)
    lg = log_gamma[:, None]
    diff = pos[:, None] - pos[None, :]
    decay = jnp.where(diff >= 0, jnp.exp(jnp.maximum(diff, 0.0) * lg[:, :, None]), 0.0)
    scores = jnp.einsum('bhnid,bhnjd->bhnij', qc, kc) * decay[None, :, None]
    o_intra = jnp.einsum('bhnij,bhnje->bhnie', scores, vc)
    q_dec = qc * jnp.exp((pos + 1.0) * lg)[None, :, None, :, None]
    k_dec = kc * jnp.exp((CHUNK - 1.0 - pos) * lg)[None, :, None, :, None]
    chunk_decay = jnp.exp(CHUNK * lg[:, 0])[None, :, None, None]

    def step(S, xs):
        qd, kd, vv = xs
        o = jnp.einsum('bhcd,bhde->bhce', qd, S)
        S = S * chunk_decay + jnp.einsum('bhcd,bhce->bhde', kd, vv)
        return S, o

    S0 = jnp.zeros((B, H, dk, dv), jnp.float32)
    _, o_inter = lax.scan(step, S0, tuple(jnp.moveaxis(t, 2, 0) for t in (q_dec, k_dec, vc)))
    return (o_intra + jnp.moveaxis(o_inter, 0, 2)).reshape(B, H, L, dv)


def _mlstm_scan(q, k, v, i_pre, log_f):
    B, H, L, dk = q.shape
    dv = v.shape[-1]
    qc, kc, vc = _chunks(q), _chunks(k), _chunks(v)
    ic = _chunks(i_pre)
    b = jnp.cumsum(_chunks(log_f), axis=-1)
    causal = jnp.tril(jnp.ones((CHUNK, CHUNK), bool))
    d_log = jnp.where(causal, b[..., :, None] - b[..., None, :] + ic[..., None, :], -jnp.inf)
    s_log = b[..., -1:] - b + ic

    def step(carry, xs):
        C_s, n_s, m = carry
        qq, kk, vv, bb, dl, sl = xs
        inter = bb + m[..., None]
        m_i = jnp.maximum(inter, jnp.max(dl, -1))
        w_inter = jnp.exp(inter - m_i)
        w = jnp.exp(dl - m_i[..., None]) * jnp.einsum('bhid,bhjd->bhij', qq, kk)
        num = w_inter[..., None] * jnp.einsum('bhid,bhde->bhie', qq, C_s) + jnp.einsum('bhij,bhje->bhie', w, vv)
        den = w_inter * jnp.einsum('bhid,bhd->bhi', qq, n_s) + jnp.sum(w, -1)
        h = num / jnp.maximum(jnp.abs(den), jnp.exp(-m_i))[..., None]
        m_new = jnp.maximum(bb[..., -1] + m, jnp.max(sl, -1))
        w_old = jnp.exp(bb[..., -1] + m - m_new)
        kw = kk * jnp.exp(sl - m_new[..., None])[..., None]
        C_s = w_old[..., None, None] * C_s + jnp.einsum('bhcd,bhce->bhde', kw, vv)
        n_s = w_old[..., None] * n_s + jnp.sum(kw, 2)
        return (C_s, n_s, m_new), h

    carry0 = (jnp.zeros((B, H, dk, dv), jnp.float32), jnp.zeros((B, H, dk), jnp.float32),
              jnp.zeros((B, H), jnp.float32))
    xs = tuple(jnp.moveaxis(t, 2, 0) for t in (qc, kc, vc, b, d_log, s_log))
    _, h = lax.scan(step, carry0, xs)
    return jnp.moveaxis(h, 0, 2).reshape(B, H, L, dv)


def _gdn_scan(q, k, v, beta, g):
    B, H, L, dk = q.shape
    dv = v.shape[-1]
    qc, kc, vc, bc = _chunks(q), _chunks(k), _chunks(v), _chunks(beta)
    gc = jnp.cumsum(_chunks(g), axis=-1)
    idx = jnp.arange(CHUNK)
    lower_incl = idx[:, None] >= idx[None, :]
    strict = idx[:, None] > idx[None, :]
    gdiff = gc[..., :, None] - gc[..., None, :]
    decay = jnp.where(lower_incl, jnp.exp(jnp.where(lower_incl, gdiff, 0.0)), 0.0)
    kb = kc * bc[..., None]
    a_mat = jnp.where(strict, jnp.einsum('bhnid,bhnjd->bhnij', kb, kc) * decay, 0.0)
    m_mat = a_mat + jnp.eye(CHUNK, dtype=a_mat.dtype)
    rhs = jnp.concatenate([vc * bc[..., None], kb * jnp.exp(gc)[..., None]], -1)
    sol = jax.lax.linalg.triangular_solve(m_mat, rhs, left_side=True, lower=True, unit_diagonal=True)
    u, w = sol[..., :dv], sol[..., dv:]
    attn = jnp.where(lower_incl, jnp.einsum('bhnid,bhnjd->bhnij', qc, kc) * decay, 0.0)
    q_dec = qc * jnp.exp(gc)[..., None]
    g_last = gc[..., -1]
    k_dec = kc * jnp.exp(g_last[..., None] - gc)[..., None]

    def step(S, xs):
        qd, kd, uu, ww, at, gl = xs
        v_new = uu - jnp.einsum('bhcd,bhde->bhce', ww, S)
        o = jnp.einsum('bhcd,bhde->bhce', qd, S) + jnp.einsum('bhij,bhje->bhie', at, v_new)
        S = S * jnp.exp(gl)[..., None, None] + jnp.einsum('bhcd,bhce->bhde', kd, v_new)
        return S, o

    S0 = jnp.zeros((B, H, dk, dv), jnp.float32)
    xs = tuple(jnp.moveaxis(t, 2, 0) for t in (q_dec, k_dec, u, w, attn, g_last))
    _, o = lax.scan(step, S0, xs)
    return jnp.moveaxis(o, 0, 2).reshape(B, H, L, dv)


def _token_mixers(h, n_ctx, ang, w_in, conv_w, ret_decay, mlstm_i_bias, mlstm_f_bias, gdn_a_log,
                  gdn_dt_bias, mlstm_norm_w, gdn_norm_w, w_branch, w_out, keep_ctx):
    f32 = jnp.float32
    dt = h.dtype
    rev = functools.partial(_reverse_segments, n_ctx=n_ctx)
    w_ret, w_ml, w_gd, w_gate = _split_cols(w_in, GROUP_WIDTHS)

    rq, rk, rv, rg = _split_cols(h @ w_ret, RET_WIDTHS)
    rq = _rope_2d(_heads(rq.astype(f32), RET_HEADS), ang)
    rk = _rope_2d(_heads(rk.astype(f32), RET_HEADS), ang) * RET_DK ** -0.5
    rv = _heads(rv.astype(f32), RET_HEADS)
    log_gamma = -jnp.exp(ret_decay.astype(f32))
    o_r = (_retention_scan(rq, rk, rv, log_gamma[0])
           + rev(_retention_scan(rev(rq), rev(rk), rev(rv), log_gamma[1])))
    y_ret = (_merge_heads(_layernorm(o_r)) * jax.nn.silu(rg.astype(f32))).astype(dt)

    mq, mk, mv, mo, mi, mf = _split_cols(h @ w_ml, MLSTM_WIDTHS)
    mq = _heads(mq.astype(f32), MLSTM_HEADS) * MLSTM_DK ** -0.5
    mk = _heads(mk.astype(f32), MLSTM_HEADS)
    mv = _heads(mv.astype(f32), MLSTM_HEADS)
    i_pre = _dir_heads(mi.astype(f32), MLSTM_HEADS) + mlstm_i_bias.astype(f32)[:, None, :, None]
    log_f = jax.nn.log_sigmoid(_dir_heads(mf.astype(f32), MLSTM_HEADS) + mlstm_f_bias.astype(f32)[:, None, :, None])
    o_m = (_mlstm_scan(mq, mk, mv, i_pre[0], log_f[0])
           + rev(_mlstm_scan(rev(mq), rev(mk), rev(mv), rev(i_pre[1]), rev(log_f[1]))))
    y_ml = (jax.nn.sigmoid(mo.astype(f32)) * _merge_heads(_layernorm(o_m)) * mlstm_norm_w.astype(f32)).astype(dt)

    gqkv, gg, gb, ga = _split_cols(h @ w_gd, GDN_WIDTHS)
    gqkv = jnp.concatenate([_short_conv(gqkv[:, :n_ctx], conv_w), _short_conv(gqkv[:, n_ctx:], conv_w)], 1)
    gq, gk, gv = _split_cols(jax.nn.silu(gqkv.astype(f32)), (GDN_HEADS * GDN_DK, GDN_HEADS * GDN_DK, MIX_W))
    gq = _l2norm(_heads(gq, GDN_HEADS)) * GDN_DK ** -0.5
    gk = _l2norm(_heads(gk, GDN_HEADS))
    gv = _heads(gv, GDN_HEADS)
    beta = jax.nn.sigmoid(_dir_heads(gb.astype(f32), GDN_HEADS))
    g = -jnp.exp(gdn_a_log.astype(f32))[:, None, :, None] * jax.nn.softplus(
        _dir_heads(ga.astype(f32), GDN_HEADS) + gdn_dt_bias.astype(f32)[:, None, :, None])
    o_g = (_gdn_scan(gq, gk, gv, beta[0], g[0])
           + rev(_gdn_scan(rev(gq), rev(gk), rev(gv), rev(beta[1]), rev(g[1]))))
    y_gd = (_merge_heads(_rmsnorm(o_g) * gdn_norm_w.astype(f32)) * jax.nn.silu(gg.astype(f32))).astype(dt)

    if not keep_ctx:
        h, y_ret, y_ml, y_gd = h[:, n_ctx:], y_ret[:, n_ctx:], y_ml[:, n_ctx:], y_gd[:, n_ctx:]
    gates = jax.nn.sigmoid((h @ w_gate).astype(f32)).astype(dt)
    g_ret, g_ml, g_gd = _split_cols(gates, (D_MODEL, D_MODEL, D_MODEL))
    merged = g_ret * (y_ret @ w_branch[0]) + g_ml * (y_ml @ w_branch[1]) + g_gd * (y_gd @ w_branch[2])
    return merged @ w_out


def _mlp(h, w1, w2):
    return jnp.square(jax.nn.relu(h @ w1)) @ w2


def setup_inputs(seed: int = 0) -> dict:
    key = jax.random.key(seed)
    ks = jax.random.split(key, 24)
    f32 = jnp.float32

    def nrm(k, shape, scale):
        return jax.random.normal(k, shape, f32) * scale

    ret_h = jnp.arange(RET_HEADS, dtype=f32)
    ret_base = jnp.log(-jnp.log1p(-(2.0 ** (-5.0 - ret_h))))
    dt_init = jnp.exp(jax.random.uniform(ks[12], (DEPTH, 2, GDN_HEADS), f32, np.log(1e-3), np.log(1e-1)))
    return {
        'x': nrm(ks[0], (BATCH, SEQ, D_MODEL), 1.0),
        'c': nrm(ks[1], (BATCH, D_MODEL), 1.0),
        'ctx': nrm(ks[2], (BATCH, CTX_LEN, D_MODEL), 1.0),
        'c_ctx': nrm(ks[3], (D_MODEL,), 1.0),
        'w_ada': nrm(ks[4], (DEPTH, D_MODEL, 6 * D_MODEL), D_MODEL ** -0.5),
        'b_ada': nrm(ks[5], (DEPTH, 6 * D_MODEL), 0.02),
        'w_in': nrm(ks[6], (DEPTH, D_MODEL, IN_DIM), D_MODEL ** -0.5),
        'conv_w': nrm(ks[7], (DEPTH, CONV_K, GDN_QKV_W), CONV_K ** -0.5),
        'ret_decay': ret_base + nrm(ks[8], (DEPTH, 2, RET_HEADS), 0.05),
        'mlstm_i_bias': nrm(ks[9], (DEPTH, 2, MLSTM_HEADS), 0.1),
        'mlstm_f_bias': jnp.linspace(3.0, 6.0, MLSTM_HEADS, dtype=f32) + nrm(ks[10], (DEPTH, 2, MLSTM_HEADS), 0.1),
        'gdn_a_log': jnp.log(jax.random.uniform(ks[11], (DEPTH, 2, GDN_HEADS), f32, 1.0, 16.0)),
        'gdn_dt_bias': dt_init + jnp.log(-jnp.expm1(-dt_init)),
        'mlstm_norm_w': 1.0 + nrm(ks[13], (DEPTH, MIX_W), 0.02),
        'gdn_norm_w': 1.0 + nrm(ks[14], (DEPTH, GDN_DV), 0.02),
        'w_branch': nrm(ks[15], (DEPTH, N_BRANCH, MIX_W, D_MODEL), MIX_W ** -0.5),
        'w_out': nrm(ks[16], (DEPTH, D_MODEL, D_MODEL), D_MODEL ** -0.5 * DN_BETA),
        'ln1_g': 1.0 + nrm(ks[17], (DEPTH, D_MODEL), 0.02),
        'ln1_b': nrm(ks[18], (DEPTH, D_MODEL), 0.02),
        'w_mlp1': nrm(ks[19], (DEPTH, D_MODEL, D_FF), D_MODEL ** -0.5),
        'w_mlp2': nrm(ks[20], (DEPTH, D_FF, D_MODEL), D_FF ** -0.5 * DN_BETA),
        'ln2_g': 1.0 + nrm(ks[21], (DEPTH, D_MODEL), 0.02),
        'ln2_b': nrm(ks[22], (DEPTH, D_MODEL), 0.02),
    }


def reference(x, c, ctx, c_ctx, w_ada, b_ada, w_in, conv_w, ret_decay, mlstm_i_bias, mlstm_f_bias,
              gdn_a_log, gdn_dt_bias, mlstm_norm_w, gdn_norm_w, w_branch, w_out, ln1_g, ln1_b,
              w_mlp1, w_mlp2, ln2_g, ln2_b):
    n_ctx = ctx.shape[1]
    n_lat = x.shape[1]
    n_rows = n_lat // GRID_W
    ang = _grid_angles(n_ctx, n_rows)
    z = ctx
    sc = jax.nn.silu(c)
    scc = jax.nn.silu(c_ctx)
    for l in range(DEPTH):
        keep_ctx = l < DEPTH - 1
        mx = _split_cols(sc @ w_ada[l] + b_ada[l], (D_MODEL,) * 6)
        mz = _split_cols(scc @ w_ada[l] + b_ada[l], (D_MODEL,) * 6)
        hx = _modulate(x, mx[0][:, None], mx[1][:, None])
        hz = _modulate(z, mz[0], mz[1])
        y = _token_mixers(jnp.concatenate([hz, hx], 1), n_ctx, ang, w_in[l], conv_w[l], ret_decay[l],
                          mlstm_i_bias[l], mlstm_f_bias[l], gdn_a_log[l], gdn_dt_bias[l],
                          mlstm_norm_w[l], gdn_norm_w[l], w_branch[l], w_out[l], keep_ctx)
        if keep_ctx:
            z = _post_norm(z, mz[2] * y[:, :n_ctx], ln1_g[l], ln1_b[l])
            y = y[:, n_ctx:]
        x = _post_norm(x, mx[2][:, None] * y, ln1_g[l], ln1_b[l])
        x = _post_norm(x, mx[5][:, None] * _mlp(_modulate(x, mx[3][:, None], mx[4][:, None]), w_mlp1[l], w_mlp2[l]),
                       ln2_g[l], ln2_b[l])
        if keep_ctx:
            z = _post_norm(z, mz[5] * _mlp(_modulate(z, mz[3], mz[4]), w_mlp1[l], w_mlp2[l]), ln2_g[l], ln2_b[l])
    return x
```

```python
import contextlib
import math
import numpy as np
import ml_dtypes
import concourse.bass as bass
import concourse.mybir as mybir
from concourse.bass_utils import run_bass_kernel_spmd

F32 = mybir.dt.float32
BF = mybir.dt.bfloat16
AF = mybir.ActivationFunctionType
ALU = mybir.AluOpType
AX = mybir.AxisListType

PE, ACT, DVE, POOL, SP = "pe", "act", "dve", "pool", "sp"
COMPUTE = (PE, ACT, DVE, POOL)
EPOCH = 12000
NDMASEM = 12

NCTX = 256
NLAT = 4096
T = NCTX + NLAT
NT = T // 128
D = 1024
DEPTH = 2
IN_DIM = 14384
DFF = 4096
LN_EPS = 1e-5
RMS_EPS = 1e-6
DN_ALPHA = (2 * DEPTH) ** 0.25
NEG = -30000.0


class Sched:
    def __init__(self, nc):
        self.nc = nc
        self.q = {e: [] for e in (PE, ACT, DVE, POOL, SP)}
        self.cnt = {e: 0 for e in COMPUTE}
        self.dcnt = {POOL: 0, SP: 0}
        self.lastw = {}
        self.readers = {}
        self.waited = {}
        self.waited_dma = {e: set() for e in self.q}

    def _deps(self, reads, writes):
        deps = []
        for r in reads:
            t = self.lastw.get(r)
            if t is not None:
                deps.append(t)
        for w in writes:
            t = self.lastw.get(w)
            if t is not None:
                deps.append(t)
            deps.extend(self.readers.get(w, ()))
        return deps

    def _emit_waits(self, eng, deps):
        best = {}
        dmas = []
        for t in deps:
            if t[0] == "dma":
                if t not in self.waited_dma[eng]:
                    self.waited_dma[eng].add(t)
                    dmas.append(t)
            else:
                p, n = t
                if p == eng and (eng == PE or n <= self.cnt[eng] - 3):
                    continue
                if n > best.get(p, 0):
                    best[p] = n
        for p, n in best.items():
            if self.waited.get((eng, p), 0) >= n:
                continue
            self.waited[(eng, p)] = n
            self.q[eng].append(("wait", p, n))
        for t in dmas:
            self.q[eng].append(("waitdma", t[1], t[2]))

    def _record(self, tok, reads, writes):
        for r in reads:
            lst = self.readers.setdefault(r, [])
            if tok[0] == "dma":
                lst[:] = [x for x in lst if not (x[0] == "dma" and x[1] == tok[1] and x[2] <= tok[2] - NDMASEM)]
            else:
                lst[:] = [x for x in lst if x[0] != tok[0]]
            lst.append(tok)
        for w in writes:
            self.lastw[w] = tok
            self.readers[w] = []

    def op(self, eng, fn, reads=(), writes=()):
        self._emit_waits(eng, self._deps(reads, writes))
        self.cnt[eng] += 1
        tok = (eng, self.cnt[eng])
        self.q[eng].append(("op", fn, self.cnt[eng]))
        self._record(tok, reads, writes)
        return tok

    def dma(self, eng, out, in_, reads=(), writes=()):
        deps = self._deps(reads, writes)
        k = self.dcnt[eng]
        self.dcnt[eng] += 1
        if k >= NDMASEM:
            deps.append(("dma", eng, k - NDMASEM))
        self._emit_waits(eng, deps)
        tok = ("dma", eng, k)
        self.q[eng].append(("dma", out, in_, k))
        self._record(tok, reads, writes)
        return tok

    def barrier(self):
        for e in self.q:
            deps = [(p, self.cnt[p]) for p in COMPUTE if self.cnt[p] > 0 and p != e]
            for q_ in self.dcnt:
                lo = max(0, self.dcnt[q_] - NDMASEM)
                deps += [("dma", q_, k) for k in range(lo, self.dcnt[q_])]
            self._emit_waits(e, deps)
        self.lastw = {}
        self.readers = {}

    def emit(self):
        nc = self.nc
        with contextlib.ExitStack() as st:
            sems = {}
            for e in COMPUTE:
                n = max(1, (self.cnt[e] + EPOCH - 1) // EPOCH)
                sems[e] = [st.enter_context(nc.semaphore(f"c_{e}_{i}")) for i in range(n)]
            dsems = {e: [st.enter_context(nc.semaphore(f"d_{e}_{i}")) for i in range(NDMASEM)] for e in self.dcnt}
            block = st.enter_context(nc.Block())

            def run(name):
                def body(eng):
                    for item in self.q[name]:
                        k = item[0]
                        if k == "op":
                            _, fn, n = item
                            fn(eng).then_inc(sems[name][(n - 1) // EPOCH], 1)
                        elif k == "wait":
                            _, p, n = item
                            ep = (n - 1) // EPOCH
                            eng.wait_ge(sems[p][ep], n - ep * EPOCH)
                        elif k == "waitdma":
                            _, q_, kk = item
                            eng.wait_ge(dsems[q_][kk % NDMASEM], 16 * (kk // NDMASEM + 1))
                        else:
                            _, out, in_, kk = item
                            eng.dma_start(out=out, in_=in_).then_inc(dsems[name][kk % NDMASEM], 16)
                return body

            block.tensor(run(PE))
            block.scalar(run(ACT))
            block.vector(run(DVE))
            block.gpsimd(run(POOL))
            block.sync(run(SP))


class Arena:
    def __init__(self, nc, limit):
        self.nc, self.off, self.limit, self.n = nc, 16640, 16640 + limit, 0

    def alloc(self, name, shape, dtype):
        nb = int(np.prod(shape[1:])) * (2 if dtype == BF else 4)
        nb = (nb + 31) // 32 * 32
        assert self.off + nb <= self.limit, (name, self.off, nb, self.limit)
        self.n += 1
        t = self.nc.alloc_sbuf_tensor_at(f"{name}_{self.n}", list(shape), dtype, offset=self.off)
        self.off += nb
        return t


CST = {}


def _build_consts():
    cols = []

    def add(name, arr):
        arr = np.asarray(arr, np.float32)
        if arr.ndim == 1:
            arr = arr[:, None]
        CST[name] = (sum(a.shape[1] for a in cols), arr.shape[1])
        cols.append(arr)

    p = np.arange(128)
    t_, i_ = p[:, None], p[None, :]
    add("ident", np.eye(128))
    add("ones", np.ones((128, 128)))
    add("triF", t_ <= i_)
    add("triB", t_ >= i_)
    add("sF", t_ > i_)
    add("sB", t_ < i_)
    add("negF", np.where(i_ < t_, NEG, 0.0))
    add("negB", np.where(i_ > t_, NEG, 0.0))
    add("m01F", i_ >= t_)
    add("m01B", i_ <= t_)
    add("s01F", i_ > t_)
    add("s01B", i_ < t_)
    add("diffF", np.maximum(i_ - t_, 0))
    add("diffB", np.maximum(t_ - i_, 0))
    add("posv", np.stack([p + 1, 127 - p, 128 - p, p, np.full(128, 128)], 1))
    return np.concatenate(cols, 1)


def _gmasks():
    p = np.arange(128)
    j, i = p[:, None], p[None, :]
    ms = []
    for l in range(7):
        B = 2 ** (l + 1)
        ms.append((((j // B) == (i // B)) & ((j % B) < B // 2) & ((i % B) >= B // 2)).astype(np.float32))
    return np.stack(ms + [m.T for m in ms], 1)


CONSTS = _build_consts()
NCST = CONSTS.shape[1]


def _rope_tables():
    n_freq = 64
    freqs = (10000.0 ** (-np.arange(n_freq, dtype=np.float32) / n_freq)).astype(np.float32)
    row = np.repeat(np.arange(64, dtype=np.float32), 64)
    col = np.tile(np.arange(64, dtype=np.float32), 64)
    lat = np.stack([row[:, None] * freqs, col[:, None] * freqs], 1).astype(np.float32)
    ang = np.concatenate([np.zeros((NCTX, 2, n_freq), np.float32), lat], 0)
    cos, sin = np.cos(ang), np.sin(ang)
    tq = np.stack([np.stack([cos, cos], 1), np.stack([sin, sin], 1)], 1).reshape(T, 512)
    return tq.astype(np.float32), (tq / 16.0).astype(np.float32)


def _groups():
    g = []
    c = 0
    for name, w in (("RQ", 1024), ("RK", 1024), ("RV", 1024), ("RG", 1024), ("MQ", 512), ("MK", 512),
                    ("MV", 1024), ("MO", 1024), ("MIF", 16), ("GQKV", 3072), ("GG", 1024), ("GBA", 32),
                    ("MG", 3072)):
        g.append((name, c, w))
        c += w
    assert c == IN_DIM
    return g


GROUPS = _groups()


class Builder:
    def __init__(self, nc, debug=(), layers=DEPTH, upto="all", ext_in=()):
        self.nc = nc
        self.debug = set(debug)
        self.ext_in = set(ext_in)
        self.S = Sched(nc)
        self.layers = layers
        self.upto = upto
        self.A = Arena(nc, 206 * 1024)
        self.uid = 0
        self._dram()
        self._psum()

    def _din(self, name, shape, dt=F32):
        return self.nc.dram_tensor(name, list(shape), dt, kind="ExternalInput").ap()

    def _dscr(self, name, shape, dt):
        kind = "ExternalOutput" if name in self.debug else ("ExternalInput" if name in self.ext_in else "Internal")
        return self.nc.dram_tensor(name, list(shape), dt, kind=kind).ap()

    def _dram(self):
        i = self._din
        self.xz = i("xz", [T, D])
        self.cc = i("cc", [128, 8, 2])
        self.w_ada = i("w_ada", [DEPTH, D, 6 * D])
        self.b_ada = i("b_ada", [DEPTH, 128, 48])
        self.w_in = i("w_in", [DEPTH, D, IN_DIM])
        self.conv_w = i("conv_w", [DEPTH, 128, 24, 5])
        self.smallp = i("smallp", [DEPTH, 64])
        self.mnorm_w = i("mlstm_norm_w", [DEPTH, D])
        self.gnorm_w = i("gdn_norm_w", [DEPTH, 128])
        self.w_branch = i("w_branch", [DEPTH, 3, D, D])
        self.w_out = i("w_out", [DEPTH, D, D])
        self.ln1_g = i("ln1_g", [DEPTH, D]); self.ln1_b = i("ln1_b", [DEPTH, D])
        self.ln2_g = i("ln2_g", [DEPTH, D]); self.ln2_b = i("ln2_b", [DEPTH, D])
        self.w_mlp1 = i("w_mlp1", [DEPTH, D, DFF]); self.w_mlp2 = i("w_mlp2", [DEPTH, DFF, D])
        self.cst_d = i("consts", [128, NCST])
        self.ropeq_d = i("ropeq", [T, 512]); self.ropek_d = i("ropek", [T, 512])
        self.gmask_d = i("gmask", [128, 14, 128])
        self.out = self.nc.dram_tensor("out", [NLAT, D], F32, kind="ExternalOutput").ap()
        s = self._dscr
        self.scr = {}
        for name, w, dt in (("RQ", 1024, BF), ("RK", 1024, BF), ("RV", 1024, BF), ("RG", 1024, BF),
                            ("MQ", 512, BF), ("MK", 512, BF), ("MV", 1024, BF), ("MO", 1024, BF),
                            ("MIF", 16, F32), ("GQ", 1024, BF), ("GK", 1024, BF), ("GV", 1024, BF),
                            ("GG", 1024, BF), ("GBA", 32, F32), ("MG", 3072, BF),
                            ("OB_R", 1024, F32), ("OB_M", 1024, F32), ("OB_G", 1024, F32),
                            ("Y_R", 1024, BF), ("Y_M", 1024, BF), ("Y_G", 1024, BF),
                            ("X1", 1024, F32), ("X2", 1024, F32)):
            self.scr[name] = s(name, [T, w], dt)

    def _psum(self):
        self.ps = [self.nc.alloc_psum_tensor(f"ps{i}", [128, 512], F32) for i in range(8)]

    def cst(self, name, rows=128):
        o, w = CST[name]
        return self.cstt[0:rows, o:o + w]

    def hk(self, t):
        return [("hT", t, c) for c in range(8)]

    def key(self, base):
        self.uid += 1
        return (base, self.uid)

    def mm(self, out, lhsT, rhs, start, stop, reads, writes):
        self.S.op(PE, lambda e: e.matmul(out, lhsT, rhs, start=start, stop=stop), reads, writes)

    def tr(self, out, in_, ident, reads, writes):
        self.S.op(PE, lambda e: e.transpose(out, in_, ident), reads, writes)

    def act(self, out, in_, func, reads, writes, bias=0.0, scale=1.0, eng=ACT):
        self.S.op(ACT, lambda e: e.activation(out, in_, func, bias=bias, scale=scale), reads, writes)

    def tt(self, eng, out, a, b, op, reads, writes):
        self.S.op(eng, lambda e: e.tensor_tensor(out, a, b, op), reads, writes)

    def ts(self, eng, out, a, s1, s2, op0, op1, reads, writes):
        if s2 is None:
            self.S.op(eng, lambda e: e.tensor_scalar(out, a, s1, None, op0), reads, writes)
        else:
            self.S.op(eng, lambda e: e.tensor_scalar(out, a, s1, s2, op0, op1), reads, writes)

    def stt(self, out, a, s, b, op0, op1, reads, writes):
        self.S.op(DVE, lambda e: e.scalar_tensor_tensor(out, a, s, b, op0, op1), reads, writes)

    def cp(self, eng, out, in_, reads, writes):
        if eng == ACT:
            self.S.op(ACT, lambda e: e.copy(out, in_), reads, writes)
        else:
            self.S.op(eng, lambda e: e.tensor_copy(out, in_), reads, writes)

    def dma(self, out, in_, reads, writes, q=SP):
        return self.S.dma(q, out, in_, reads, writes)

    def dbg(self, name, ap, shape, reads):
        if name in self.debug:
            d = self.nc.dram_tensor(name, list(shape), F32, kind="ExternalOutput").ap()
            self.dma(d, ap, reads, [("dbgout", name)])

    def setup(self):
        A = self.A
        self.cstt = A.alloc("cst", [128, NCST], F32)
        self.dma(self.cstt[:, :], self.cst_d, [], ["cst"])
        self.identb = A.alloc("identb", [128, 128], BF)
        self.cp(DVE, self.identb[:, :], self.cst("ident"), ["cst"], ["identb"])
        self.onesb = A.alloc("onesb", [128, 128], BF)
        self.cp(DVE, self.onesb[:, :], self.cst("ones"), ["cst"], ["onesb"])
        self.cct = A.alloc("cct", [128, 8, 2], F32)
        self.dma(self.cct[:, :, :], self.cc, [], ["cct"])
        self.scs = A.alloc("scs", [128, 8, 2], F32)
        self.act(self.scs[:, :, :], self.cct[:, :, :], AF.Silu, ["cct"], ["scs"])
        self.modc = A.alloc("modc", [128, 48, 2], F32)
        self.ops = A.alloc("ops", [128, 2, 8, 2], F32)
        self.gbc = A.alloc("gbc", [128, 4, 1024], F32)
        self.smp = A.alloc("smp", [128, 64], F32)
        self.persist_end = A.off

    def phase_ada(self, l):
        A, S = self.A, self.S
        A.off = self.persist_end
        badat = A.alloc("badat", [128, 48], F32)
        self.dma(badat[:, :], self.b_ada[l], [], ["badat"])
        self.dma(self.smp[:, :], self.smallp[l:l + 1, :].partition_broadcast(128), [], ["smp"])
        wsl = [A.alloc("wada", [128, 8, 768], F32) for _ in range(2)]
        wv = self.w_ada[l].rearrange("(kc p) c -> p kc c", p=128)
        psA = self.ps[0]
        psAk = ("ps", 0)
        for s in range(8):
            wt = wsl[s % 2]
            wk = ("wada", s % 2)
            self.dma(wt[:, :, :], wv[:, :, s * 768:(s + 1) * 768], [], [wk])
            for jj in range(6):
                j = s * 6 + jj
                for kc in range(8):
                    self.mm(psA[:, 2 * j:2 * j + 2], wt[:, kc, jj * 128:(jj + 1) * 128], self.scs[:, kc, :],
                            kc == 0, kc == 7, [wk, "scs"], [psAk])
        self.tt(DVE, self.modc[:, :, :], psA[:, 0:96].rearrange("p (j v) -> p j v", v=2),
                badat[:, :].unsqueeze(2).broadcast_to([128, 48, 2]), ALU.add, [psAk, "badat"], ["modc"])
        for sub in range(2):
            j0 = 8 + 24 * sub
            self.ts(DVE, self.ops[:, sub, :, :], self.modc[:, j0:j0 + 8, :], 1.0, None, ALU.add, None, ["modc"], ["ops"])
        tmp = [A.alloc("gtmp", [128, 128], F32) for _ in range(2)]
        n = 0
        for sub in range(2):
            for v in range(2):
                gi = sub * 2 + v
                pb = 1 + (gi % 2) * 2
                pst = self.ps[pb: pb + 2]
                for c in range(8):
                    tk = ("gtmp", n % 2)
                    col = self.modc[:, 16 + 24 * sub + c, v:v + 1]
                    self.ts(DVE, tmp[n % 2][:, :], self.cst("ones"), col, None, ALU.mult, None, ["cst", "modc"], [tk])
                    self.mm(pst[c // 4][:, (c % 4) * 128:(c % 4 + 1) * 128], tmp[n % 2][:, :], self.cst("ident"),
                            True, True, [tk, "cst"], [("ps", pb + c // 4)])
                    n += 1
                for hh in range(2):
                    self.cp(ACT, self.gbc[:, gi, hh * 512:(hh + 1) * 512], pst[hh][:, :], [("ps", pb + hh)], [("gbc", gi, hh)])
        S.barrier()
        self.dbg("dbg_modc", self.modc[:, :, :].rearrange("p j v -> p (j v)"), [128, 96], [])
        self.dbg("dbg_gbc", self.gbc[:, :, :].rearrange("p g d -> p (g d)"), [128, 4096], [])

    def ln_stats(self, xt, xk, st, mv, rs, uk):
        S = self.S
        for hh in range(2):
            S.op(DVE, lambda e, hh=hh: e.bn_stats(st[:, hh, :], xt[:, hh * 512:(hh + 1) * 512]), [xk], [("st", uk)])
        S.op(DVE, lambda e: e.bn_aggr(mv[:, :], st[:, :, :].rearrange("p a b -> p (a b)")), [("st", uk)], [("mv", uk)])
        self.act(rs[:, 2:3], mv[:, 1:2], AF.Ln, [("mv", uk)], [("rs2", uk)], bias=LN_EPS)
        self.act(rs[:, 0:1], rs[:, 2:3], AF.Exp, [("rs2", uk)], [("rs0", uk)], scale=-0.5)
        self.ts(DVE, rs[:, 1:2], mv[:, 0:1], rs[:, 0:1], -1.0, ALU.mult, ALU.mult, [("mv", uk), ("rs0", uk)], [("rs1", uk)])

    def phase_ln1(self, l, src, hT):
        A, S = self.A, self.S
        xt = [A.alloc("xt", [128, 1024], F32) for _ in range(2)]
        xn = [A.alloc("xn", [128, 1024], F32) for _ in range(2)]
        st = [A.alloc("st", [128, 2, 6], F32) for _ in range(2)]
        mv = [A.alloc("mv", [128, 2], F32) for _ in range(2)]
        rs = [A.alloc("rs", [128, 4], F32) for _ in range(2)]
        import os
        STEPS = int(os.environ.get("LN1_STEPS", "9"))
        evm = os.environ.get("EVM", "split")
        EV = (lambda c: True) if evm == "dve" else ((lambda c: False) if evm == "act" else (lambda c: c // 4 == 0))
        for t in range(int(os.environ.get("LN1_NT", NT))):
            b = t % 2
            v = 1 if t < 2 else 0
            self.dma(xt[b][:, :], src[t * 128:(t + 1) * 128, :], [], [("xt", b)])
            if STEPS < 2:
                continue
            self.ln_stats(xt[b], ("xt", b), st[b], mv[b], rs[b], b)
            if STEPS < 4:
                continue
            self.act(xn[b][:, :], xt[b][:, :], AF.Identity, [("xt", b), ("rs0", b), ("rs1", b)], [("xn", b)],
                     bias=rs[b][:, 1:2], scale=rs[b][:, 0:1])
            if STEPS < 5:
                continue
            for c in range(8):
                pt = self.ps[(t % 2) * 2 + c // 4]
                pk = ("ps", (t % 2) * 2 + c // 4)
                self.tr(pt[:, (c % 4) * 128:(c % 4 + 1) * 128], xn[b][:, c * 128:(c + 1) * 128], self.cst("ident"),
                        [("xn", b), "cst"], [pk])
            if STEPS < 6:
                continue
            for c in range(8):
                pt = self.ps[(t % 2) * 2 + c // 4]
                pk = ("ps", (t % 2) * 2 + c // 4)
                o = hT[:, c, t * 128:(t + 1) * 128]
                i_ = pt[:, (c % 4) * 128:(c % 4 + 1) * 128]
                sc_ = self.ops[:, 0, c, v:v + 1]
                sh_ = self.modc[:, c, v:v + 1]
                lock = ["evlock"] if os.environ.get("EVLOCK") else []
                if EV(c):
                    self.ts(DVE, o, i_, sc_, sh_, ALU.mult, ALU.add, [pk, "modc", "ops"], [("hT", t, c)] + lock)
                else:
                    self.act(o, i_, AF.Identity, [pk, "modc", "ops"], [("hT", t, c)] + lock, bias=sh_, scale=sc_)

    def phase_inproj(self, l, hT):
        A, S = self.A, self.S
        wg = [A.alloc("wg", [128, 8, 512], BF) for _ in range(2)]
        stg = [A.alloc("stg", [128, 512], BF) for _ in range(4)]
        stf = [A.alloc("stf", [128, 32], F32) for _ in range(2)]
        rp = [A.alloc("rp", [128, 512], F32) for _ in range(2)]
        t12 = [A.alloc("t12", [128, 2, 256], F32) for _ in range(2)]
        xc = A.alloc("xc", [128, 4, T + 8], BF)
        cwt = A.alloc("cwt", [128, 24, 5], F32)
        dg = A.alloc("dg", [128, 20, 128], BF)
        sl = [A.alloc("sl", [128, 512], F32) for _ in range(2)]
        sq = [A.alloc("sq", [128, 512], F32) for _ in range(2)]
        ssn = [A.alloc("ssn", [128, 12], F32) for _ in range(2)]
        self.dma(cwt[:, :, :], self.conv_w[l], [], ["cwt"])
        S.op(POOL, lambda e: e.memset(xc[:, :, :], 0.0), [], [("xc", i) for i in range(4)])
        wv = self.w_in[l].rearrange("(kc p) c -> p kc c", p=128)
        gi = 0
        si = 0
        for name, c0, width in GROUPS:
            if getattr(self, "only_groups", None) and name not in self.only_groups:
                continue
            nsub = max(1, width // 512)
            w_ = min(width, 512)
            for sub in range(nsub):
                cs = c0 + sub * 512
                wb = wg[gi % 2]
                wk = ("wg", gi % 2)
                gi += 1
                self.dma(wb[:, :, 0:w_], wv[:, :, cs:cs + w_], [], [wk], q=POOL)
                if name == "GQKV":
                    self._gdn_group(l, sub, wb, wk, hT, xc, cwt, dg, sl, sq, ssn, stg)
                    continue
                for t in range(NT):
                    pt = self.ps[4 + t % 4]
                    pk = ("ps", 4 + t % 4)
                    for kc in range(8):
                        self.mm(pt[:, 0:w_], hT[:, kc, t * 128:(t + 1) * 128], wb[:, kc, 0:w_], kc == 0, kc == 7,
                                self.hk(t) + [wk], [pk])
                    rows = slice(t * 128, (t + 1) * 128)
                    if name in ("MIF", "GBA"):
                        sb = stf[si % 2]; sk = ("stf", si % 2); si += 1
                        self.cp(DVE, sb[:, 0:w_], pt[:, 0:w_], [pk], [sk])
                        self.dma(self.scr[name][rows, :], sb[:, 0:w_], [sk], [(name, t)])
                        continue
                    sb = stg[si % 4]; sk = ("stg", si % 4); si += 1
                    dst = self.scr[name][rows, sub * 512:(sub + 1) * 512]
                    if name in ("RQ", "RK"):
                        b = t % 2
                        tab = self.ropeq_d if name == "RQ" else self.ropek_d
                        self.dma(rp[b][:, :], tab[rows, :], [], [("rp", b)])
                        psv = pt[:, :].rearrange("p (g ab f) -> p g ab f", g=4, ab=2)
                        cosv = rp[b][:, 0:256].rearrange("p (g f) -> p g f", g=4)
                        sinv = rp[b][:, 256:512].rearrange("p (g f) -> p g f", g=4)
                        sbv = sb[:, :].rearrange("p (g ab f) -> p g ab f", g=4, ab=2)
                        t1 = t12[b][:, 0, :].rearrange("p (g f) -> p g f", g=4)
                        t2 = t12[b][:, 1, :].rearrange("p (g f) -> p g f", g=4)
                        tk = ("t12", b)
                        a_, b_ = psv[:, :, 0, :], psv[:, :, 1, :]
                        self.tt(DVE, t1, a_, cosv, ALU.mult, [pk, ("rp", b)], [(tk, 0)])
                        self.tt(DVE, t2, b_, sinv, ALU.mult, [pk, ("rp", b)], [(tk, 1)])
                        self.tt(POOL, sbv[:, :, 0, :], t1, t2, ALU.subtract, [(tk, 0), (tk, 1)], [sk])
                        self.tt(DVE, t1, b_, cosv, ALU.mult, [pk, ("rp", b)], [(tk, 0)])
                        self.tt(DVE, t2, a_, sinv, ALU.mult, [pk, ("rp", b)], [(tk, 1)])
                        self.tt(POOL, sbv[:, :, 1, :], t1, t2, ALU.add, [(tk, 0), (tk, 1)], [sk])
                        self.dma(dst, sb[:, :], [sk], [(name, t, sub)])
                        continue
                    if name in ("RG", "GG"):
                        self.act(sb[:, :], pt[:, :], AF.Silu, [pk], [sk])
                    elif name in ("MO", "MG"):
                        self.act(sb[:, :], pt[:, :], AF.Sigmoid, [pk], [sk])
                    elif name == "MQ":
                        self.act(sb[:, :], pt[:, :], AF.Copy, [pk], [sk], scale=128 ** -0.5)
                    elif t % 2 == 0:
                        self.cp(ACT, sb[:, :], pt[:, :], [pk], [sk])
                    else:
                        self.cp(DVE, sb[:, :], pt[:, :], [pk], [sk])
                    self.dma(dst, sb[:, :], [sk], [(name, t, sub)])

    def _gdn_group(self, l, sub, wb, wk, hT, xc, cwt, dg, sl, sq, ssn, stg):
        S = self.S
        for ct in range(4):
            for tap in range(5):
                self.ts(POOL, dg[:, ct * 5 + tap, :], self.cst("ident"), cwt[:, sub * 4 + ct, tap:tap + 1], None,
                        ALU.mult, None, ["cst", "cwt"], [("dg", ct)])
        blocks = [(0, 256)] + [(256 + 512 * i, 512) for i in range(8)]
        n = 0
        for (t0, tw) in blocks:
            col0 = 2 + t0 if t0 < NCTX else 6 + t0
            for ct in range(4):
                pt = self.ps[n % 4]
                pk = ("ps", n % 4)
                for kc in range(8):
                    self.mm(pt[:, 0:tw], wb[:, kc, ct * 128:(ct + 1) * 128], hT[:, kc, t0:t0 + tw], kc == 0, kc == 7,
                            [wk] + sum([self.hk(t0 // 128 + i) for i in range(tw // 128)], []), [pk])
                self.cp(ACT if n % 2 == 0 else DVE, xc[:, ct, col0:col0 + tw], pt[:, 0:tw], [pk], [("xc", ct)])
                n += 1
        which = "GQ" if sub < 2 else ("GK" if sub < 4 else "GV")
        for t in range(NT):
            base = (2 + t * 128 if t < 2 else 6 + t * 128) - 2
            pt = self.ps[4 + t % 4]
            pk = ("ps", 4 + t % 4)
            for ct in range(4):
                for tap in range(5):
                    self.mm(pt[:, ct * 128:(ct + 1) * 128], xc[:, ct, base + tap:base + tap + 128], dg[:, ct * 5 + tap, :],
                            tap == 0, tap == 4, [("xc", ct), ("dg", ct)], [pk])
            b = t % 2
            sb = stg[t % 4]; sk = ("stg", t % 4)
            rows = slice(t * 128, (t + 1) * 128)
            dst = self.scr[which][rows, (sub % 2) * 512:(sub % 2 + 1) * 512]
            if which == "GV":
                self.act(sb[:, :], pt[:, :], AF.Silu, [pk], [sk])
            else:
                self.act(sl[b][:, :], pt[:, :], AF.Silu, [pk], [("sl", b)])
                self.tt(POOL, sq[b][:, :], sl[b][:, :], sl[b][:, :], ALU.mult, [("sl", b)], [("sq", b)])
                S.op(DVE, lambda e, b=b: e.tensor_reduce(ssn[b][:, 0:4], sq[b][:, :].rearrange("p (h f) -> p h f", h=4), AX.X, ALU.add),
                     [("sq", b)], [("ss", b)])
                self.act(ssn[b][:, 4:8], ssn[b][:, 0:4], AF.Ln, [("ss", b)], [("ssl", b)], bias=RMS_EPS)
                qs = math.log(128 ** -0.5) if which == "GQ" else 0.0
                self.act(ssn[b][:, 8:12], ssn[b][:, 4:8], AF.Exp, [("ssl", b)], [("ssr", b)], scale=-0.5, bias=qs)
                self.tt(DVE, sb[:, :].rearrange("p (h f) -> p h f", h=4), sl[b][:, :].rearrange("p (h f) -> p h f", h=4),
                        ssn[b][:, 8:12].unsqueeze(2).broadcast_to([128, 4, 128]), ALU.mult, [("sl", b), ("ssr", b)], [sk])
            self.dma(dst, sb[:, :], [sk], [(which, t, sub)])

    MIX = {"R": dict(H=4, nkt=2, dv=256, dvp=256, q="RQ", k="RK", v="RV", gate="RG", ob="OB_R", y="Y_R"),
           "M": dict(H=4, nkt=1, dv=256, dvp=257, q="MQ", k="MK", v="MV", gate="MO", ob="OB_M", y="Y_M"),
           "G": dict(H=8, nkt=1, dv=128, dvp=128, q="GQ", k="GK", v="GV", gate="GG", ob="OB_G", y="Y_G")}

    def scan_setup(self, l):
        A = self.A
        sp_ = self.smp
        self.prm = A.alloc("prm", [128, 64], F32)
        prm = self.prm
        self.act(prm[:, 0:8], sp_[:, 0:8], AF.Exp, ["smp"], ["prm_r0"])
        self.ts(DVE, prm[:, 0:8], prm[:, 0:8], -1.0, None, ALU.mult, None, ["prm_r0"], ["prm_r"])
        self.act(prm[:, 8:24], sp_[:, 24:40], AF.Exp, ["smp"], ["prm_g0"])
        self.ts(DVE, prm[:, 8:24], prm[:, 8:24], -1.0, None, ALU.mult, None, ["prm_g0"], ["prm_g"])
        self.retDT = A.alloc("retDT", [128, 8, 128], F32)
        self.retc = A.alloc("retc", [128, 8, 3], F32)
        o, _ = CST["posv"]
        for d_ in range(2):
            sfx = "F" if d_ == 0 else "B"
            for h in range(4):
                i = d_ * 4 + h
                lgc = prm[:, i:i + 1]
                self.act(self.retDT[:, i, :], self.cst("diff" + sfx), AF.Exp, ["cst", "prm_r"], [("retDT0", i)], scale=lgc)
                self.tt(DVE, self.retDT[:, i, :], self.retDT[:, i, :], self.cst("m01" + sfx), ALU.mult, [("retDT0", i), "cst"], [("retDT", i)])
                cols = (o + 0, o + 1) if d_ == 0 else (o + 2, o + 3)
                for j, cc in enumerate(cols + (o + 4,)):
                    self.act(self.retc[:, i, j:j + 1], self.cstt[:, cc:cc + 1], AF.Exp, ["cst", "prm_r"], [("retc", i, j)], scale=lgc)

    def phase_scan(self, l, mx):
        A, S = self.A, self.S
        cfg = self.MIX[mx]
        H, nkt, dv, dvp = cfg["H"], cfg["nkt"], cfg["dv"], cfg["dvp"]
        dk = 128 * nkt
        QW = H * dk
        base = A.off
        qt = [A.alloc("qt", [128, QW], BF) for _ in range(2)]
        kt = [A.alloc("kt", [128, QW], BF) for _ in range(2)]
        vt = [A.alloc("vt", [128, H, dvp], BF) for _ in range(2)]
        ks = A.alloc("ks", [128, QW], BF)
        QT = A.alloc("QT", [128, H * nkt, 128], BF)
        KT = A.alloc("KT", [128, H * nkt, 128], BF)
        Sf = A.alloc("Sf", [128, H * nkt, dvp], F32)
        Sb = A.alloc("Sb", [128, H * nkt, dvp], BF)
        pT = [A.alloc("pT", [128, 128], BF) for _ in range(2)]
        acc = [A.alloc("acc", [128, 1024], F32) for _ in range(2)]
        obt = [A.alloc("obt", [128, 1024], F32) for _ in range(2)]
        gt = [A.alloc("gt", [128, 1024], BF) for _ in range(2)]
        yt = [A.alloc("yt", [128, 1024], BF) for _ in range(2)]
        t1 = A.alloc("pp1", [128, 1024], F32)
        sc = [A.alloc("scl", [128, 96], F32) for _ in range(2)]
        sraw = [A.alloc("sraw", [128, 32], F32) for _ in range(2)]
        pst = A.alloc("pst", [128, 8, 6], F32)
        pmv = A.alloc("pmv", [128, 8, 2], F32)
        prs = A.alloc("prs", [128, 16], F32)
        if mx != "R":
            Lm = [A.alloc("Lm", [128, 128], F32) for _ in range(2)]
            dtb = A.alloc("dtb", [128, H, 128], F32)
        if mx == "M":
            tmpo = [A.alloc("tmpo", [128, 257], F32) for _ in range(2)]
            dn = [A.alloc("dn", [128, 2], F32) for _ in range(2)]
            nwb = A.alloc("nwb", [128, 1024], F32)
            self.dma(nwb[:, :], self.mnorm_w[l:l + 1, :].partition_broadcast(128), [], ["nwb"])
            for b in range(2):
                S.op(POOL, lambda e, b=b: e.memset(vt[b][:, :, 256:257], 1.0), [], [("vt", b)])
        if mx == "G":
            dts = A.alloc("dts", [128, 8, 128], F32)
            gm = A.alloc("gm", [128, 14, 128], F32)
            self.dma(gm[:, :, :], self.gmask_d, [], ["gm"])
            M0 = A.alloc("M0", [128, 8, 128], F32)
            MTt = A.alloc("MTt", [128, 8, 128], F32)
            MTm = A.alloc("MTm", [128, 6, 8, 128], F32)
            Um = [A.alloc("Um", [128, 8, 128], F32) for _ in range(2)]
            Vm = A.alloc("Vm", [128, 8, 128], F32)
            Pm = A.alloc("Pm", [128, 8, 128], F32)
            self.gdn_Wb = A.alloc("Wb", [128, 8, 128], BF)
            self.ginv = (gm, M0, MTt, MTm, Um, Vm, Pm)
            r0 = [A.alloc("r0", [128, 128], BF) for _ in range(2)]
            vn = A.alloc("vn", [128, 8, 128], BF)
            nwb = A.alloc("nwb", [128, 128], F32)
            self.dma(nwb[:, :], self.gnorm_w[l:l + 1, :].partition_broadcast(128), [], ["nwb"])
        ps = self.ps
        PK = lambda i: ("ps", i)
        identb = self.identb
        for d_ in (1, 0):
            sfx = "F" if d_ == 0 else "B"
            order = [0, 1] + list(range(2, NT)) if d_ == 0 else [1, 0] + list(range(NT - 1, 1, -1))
            S.op(POOL, lambda e: e.memset(Sf[:, :, :], 0.0), [], [("Sf", i) for i in range(H)])
            S.op(POOL, lambda e: e.memset(Sb[:, :, :], 0.0), [], [("Sb", i) for i in range(H)])
            for n, c in enumerate(order):
                b = n % 2
                rows = slice(c * 128, (c + 1) * 128)
                self.dma(qt[b][:, :], self.scr[cfg["q"]][rows, :], [(cfg["q"], c)], [("qt", b)])
                self.dma(kt[b][:, :], self.scr[cfg["k"]][rows, :], [(cfg["k"], c)], [("kt", b)])
                self.dma(vt[b][:, :, 0:dv], self.scr[cfg["v"]][rows, :].rearrange("p (h e) -> p h e", h=H), [(cfg["v"], c)], [("vt", b)])
                if mx == "M":
                    self.dma(sraw[b][:, 0:16], self.scr["MIF"][rows, :], [], [("sraw", b)])
                if mx == "G":
                    self.dma(sraw[b][:, 0:32], self.scr["GBA"][rows, :], [], [("sraw", b)])
                if d_ == 0:
                    self.dma(obt[b][:, :], self.scr[cfg["ob"]][rows, :], [(cfg["ob"], c)], [("obt", b)])
                    self.dma(gt[b][:, :], self.scr[cfg["gate"]][rows, :], [], [("gt", b)])
                for src_, dst_, bank, nm in ((qt[b], QT, 0, "QT"), (kt[b], KT, 1, "KT")):
                    pv = ps[bank][:, :].bitcast(BF)
                    for j in range(H * nkt):
                        self.tr(pv[:, j * 128:(j + 1) * 128], src_[:, j * 128:(j + 1) * 128], identb[:, :],
                                [(nm.lower(), b), "identb"], [PK(bank)])
                    self.cp(ACT if bank == 0 else DVE, dst_[:, :, :].rearrange("p a t -> p (a t)"), pv[:, 0:H * nkt * 128],
                            [PK(bank)], [nm])
                s_ = sc[b]
                sk = ("scl", b)
                if mx == "R":
                    a_col = lambda h: self.retc[:, d_ * 4 + h, 0:1]
                    cd_col = lambda h: self.retc[:, d_ * 4 + h, 2:3]
                    s_bc = self.retc[:, d_ * 4:d_ * 4 + 4, 1:2].broadcast_to([128, 4, 256])
                    s_reads = [("retc", d_ * 4 + h, j) for h in range(4) for j in range(3)]
                    DT = lambda h: self.retDT[:, d_ * 4 + h, :]
                    dt_reads = lambda h: [("retDT", d_ * 4 + h)]
                else:
                    nh = H
                    raw = sraw[b]
                    if mx == "M":
                        self.tt(DVE, s_[:, 0:4], raw[:, d_ * 4:d_ * 4 + 4], self.smp[:, 8 + d_ * 4:12 + d_ * 4], ALU.add, [("sraw", b), "smp"], [(sk, "ig")])
                        self.tt(DVE, s_[:, 4:8], raw[:, 8 + d_ * 4:12 + d_ * 4], self.smp[:, 16 + d_ * 4:20 + d_ * 4], ALU.add, [("sraw", b), "smp"], [(sk, "x")])
                        self.act(s_[:, 8:12], s_[:, 4:8], AF.Exp, [(sk, "x")], [(sk, "e")], scale=-1.0)
                        self.act(s_[:, 12:16], s_[:, 8:12], AF.Ln, [(sk, "e")], [(sk, "l")], bias=1.0)
                        self.ts(DVE, s_[:, 16:20], s_[:, 12:16], -1.0, None, ALU.mult, None, [(sk, "l")], [(sk, "g")])
                        gcol = lambda h: s_[:, 16 + h:17 + h]
                        gall = s_[:, 16:20]
                    else:
                        self.act(s_[:, 0:8], raw[:, d_ * 8:d_ * 8 + 8], AF.Sigmoid, [("sraw", b)], [(sk, "beta")])
                        self.ts(DVE, s_[:, 88:96], s_[:, 0:8], -1.0, None, ALU.mult, None, [(sk, "beta")], [(sk, "nbeta")])
                        self.tt(DVE, s_[:, 8:16], raw[:, 16 + d_ * 8:24 + d_ * 8], self.smp[:, 40 + d_ * 8:48 + d_ * 8], ALU.add, [("sraw", b), "smp"], [(sk, "x")])
                        self.act(s_[:, 16:24], s_[:, 8:16], AF.Exp, [(sk, "x")], [(sk, "e")])
                        self.act(s_[:, 24:32], s_[:, 16:24], AF.Ln, [(sk, "e")], [(sk, "l")], bias=1.0)
                        self.tt(DVE, s_[:, 32:40], s_[:, 24:32], self.prm[:, 8 + d_ * 8:16 + d_ * 8], ALU.mult, [(sk, "l"), "prm_g"], [(sk, "g")])
                        gcol = lambda h: s_[:, 32 + h:33 + h]
                        gall = s_[:, 32:32 + 8]
                    self.mm(ps[6][:, 0:nh], self.cst("tri" + sfx), gall, True, True, ["cst", (sk, "g")], [PK(6)])
                    self.mm(ps[6][:, nh:2 * nh], self.cst("ones"), gall, True, True, ["cst", (sk, "g")], [PK(6)])
                    self.cp(ACT, s_[:, 40:40 + 2 * nh], ps[6][:, 0:2 * nh], [PK(6)], [(sk, "B")])
                    Bc, Ba = s_[:, 40:40 + nh], s_[:, 40 + nh:40 + 2 * nh]
                    self.act(s_[:, 56:56 + nh], Bc, AF.Exp, [(sk, "B")], [(sk, "a")])
                    self.act(s_[:, 64:64 + nh], Ba, AF.Exp, [(sk, "B")], [(sk, "cd")])
                    self.tt(DVE, s_[:, 72:72 + nh], Ba, Bc, ALU.subtract, [(sk, "B")], [(sk, "s0")])
                    if mx == "M":
                        self.tt(DVE, s_[:, 72:72 + nh], s_[:, 72:72 + nh], s_[:, 0:4], ALU.add, [(sk, "s0"), (sk, "ig")], [(sk, "s1")])
                    else:
                        self.ts(DVE, s_[:, 80:88], s_[:, 56:64], -1.0, None, ALU.mult, None, [(sk, "a")], [(sk, "na")])
                    self.act(s_[:, 72:72 + nh], s_[:, 72:72 + nh], AF.Exp, [(sk, "s0"), (sk, "s1")], [(sk, "s")])
                    a_col = lambda h: s_[:, 56 + h:57 + h]
                    cd_col = lambda h: s_[:, 64 + h:65 + h]
                    s_bc = s_[:, 72:72 + nh].unsqueeze(2).broadcast_to([128, nh, 128])
                    s_reads = [(sk, "s")]
                    for h in range(H):
                        Lb = Lm[h % 2]
                        self.ts(DVE, Lb[:, :], self.cst("s" + sfx), gcol(h), None, ALU.mult, None, ["cst", (sk, "g")], [("Lm", h % 2)])
                        self.mm(ps[7][:, 0:128], Lb[:, :], self.cst("tri" + sfx), True, False, [("Lm", h % 2), "cst"], [PK(7)])
                        self.mm(ps[7][:, 0:128], self.cst("ident"), self.cst("neg" + sfx), False, True, ["cst"], [PK(7)])
                        if mx == "M":
                            self.act(dtb[:, h, :], ps[7][:, 0:128], AF.Exp, [PK(7), (sk, "ig")], [("dtb", h)], bias=s_[:, h:h + 1])
                        else:
                            self.act(dtb[:, h, :], ps[7][:, 0:128], AF.Exp, [PK(7)], [("dtb", h)])
                            self.tt(POOL, dts[:, h, :], dtb[:, h, :], self.cst("s01" + sfx), ALU.mult, [("dtb", h), "cst"], [("dts", h)])
                    DT = lambda h: dtb[:, h, :]
                    dt_reads = lambda h: [("dtb", h)]
                self.tt(POOL, ks[:, :].rearrange("p (h e) -> p h e", h=H), kt[b][:, :].rearrange("p (h e) -> p h e", h=H), s_bc,
                        ALU.mult, [("kt", b)] + s_reads, ["ks"])
                if mx == "G":
                    self._gdn_inverse(KT, s_, sk, dts, d_)
                    Wt = self.gdn_W
                A_ = acc[b]
                ak = ("acc", b)
                for h in range(H):
                    pb = pT[h % 2]
                    pk_ = ("pT", h % 2)
                    for kk in range(nkt):
                        self.mm(ps[2][:, 0:128], KT[:, h * nkt + kk, :], QT[:, h * nkt + kk, :], kk == 0, kk == nkt - 1, ["KT", "QT"], [PK(2)])
                    self.tt(DVE, pb[:, :], ps[2][:, 0:128], DT(h), ALU.mult, [PK(2)] + dt_reads(h), [pk_])
                    vh = vt[b][:, h, :]
                    if mx == "G":
                        self.mm(ps[3][:, 0:128], KT[:, h, :], Sb[:, h, :], True, True, ["KT", ("Sb", h)], [PK(3)])
                        rb = r0[h % 2]
                        self.stt(rb[:, :], ps[3][:, 0:128], s_[:, 80 + h:81 + h], vh, ALU.mult, ALU.add, [PK(3), (sk, "na"), ("vt", b)], [("r0", h % 2)])
                        self.mm(ps[3][:, 128:256], Wt[:, h, :], rb[:, :], True, True, [("Wb", h // 4), ("r0", h % 2)], [PK(3)])
                        self.act(vn[:, h, :], ps[3][:, 128:256], AF.Copy, [PK(3), (sk, "beta")], [("vn", h)], scale=s_[:, h:h + 1])
                        vh = vn[:, h, :]
                        vreads = [("vn", h)]
                    else:
                        vreads = [("vt", b)]
                    self.mm(ps[4][:, 0:dvp], pb[:, :], vh, True, True, [pk_] + vreads, [PK(4)])
                    for kk in range(nkt):
                        self.mm(ps[5][:, 0:dvp], QT[:, h * nkt + kk, :], Sb[:, h * nkt + kk, :], kk == 0, kk == nkt - 1, ["QT", ("Sb", h)], [PK(5)])
                    if mx == "M":
                        to = tmpo[h % 2]
                        tk = ("tmpo", h % 2)
                        self.act(to[:, :], ps[5][:, 0:257], AF.Copy, [PK(5), (sk, "a")], [tk], scale=a_col(h))
                        self.tt(DVE, to[:, :], to[:, :], ps[4][:, 0:257], ALU.add, [tk, PK(4)], [tk])
                        self.act(dn[h % 2][:, 0:1], to[:, 256:257], AF.Abs, [tk], [("dn0", h % 2)])
                        self.ts(DVE, dn[h % 2][:, 0:1], dn[h % 2][:, 0:1], 1.0, None, ALU.max, None, [("dn0", h % 2)], [("dn", h % 2)])
                        S.op(DVE, lambda e, h=h: e.reciprocal(dn[h % 2][:, 1:2], dn[h % 2][:, 0:1]), [("dn", h % 2)], [("dn1", h % 2)])
                        self.ts(DVE, A_[:, h * 256:(h + 1) * 256], to[:, 0:256], dn[h % 2][:, 1:2], None, ALU.mult, None, [tk, ("dn1", h % 2)], [(ak, h)])
                    else:
                        o_ = A_[:, h * dv:(h + 1) * dv]
                        self.act(o_, ps[5][:, 0:dv], AF.Copy, [PK(5)] + ([(sk, "a")] if mx == "G" else [("retc", d_ * 4 + h, 0)]), [(ak, h)], scale=a_col(h))
                        self.tt(DVE, o_, o_, ps[4][:, 0:dv], ALU.add, [(ak, h), PK(4)], [(ak, h)])
                    for kk in range(nkt):
                        i = h * nkt + kk
                        self.mm(ps[6][:, 0:dvp], ks[:, i * 128:(i + 1) * 128], vh, True, True, ["ks"] + vreads, [PK(6)])
                        self.stt(Sf[:, i, :], Sf[:, i, :], cd_col(h), ps[6][:, 0:dvp], ALU.mult, ALU.add,
                                 [("Sf", h), PK(6)] + ([(sk, "cd")] if mx != "R" else [("retc", d_ * 4 + h, 2)]), [("Sf", h)])
                        self.cp(ACT, Sb[:, i, :], Sf[:, i, :], [("Sf", h)], [("Sb", h)])
                akeys = [(ak, h) for h in range(H)]
                if d_ == 1:
                    self.dma(self.scr[cfg["ob"]][rows, :], A_[:, :], akeys, [(cfg["ob"], c)])
                    continue
                self.tt(POOL, A_[:, :], A_[:, :], obt[b][:, :], ALU.add, akeys + [("obt", b)], akeys)
                A3 = A_[:, :].rearrange("p (h e) -> p h e", h=H)
                Y = yt[b]
                if "dbg_acc" in self.debug:
                    if not hasattr(self, "dbg_acc_d"):
                        self.dbg_acc_d = self.nc.dram_tensor("dbg_acc", [T, 1024], F32, kind="ExternalOutput").ap()
                    self.dma(self.dbg_acc_d[rows, :], A_[:, :], akeys, [("dbgacc", c)])
                if mx == "G":
                    self.tt(POOL, t1[:, :], A_[:, :], A_[:, :], ALU.mult, akeys, ["pp1"])
                    S.op(DVE, lambda e: e.tensor_reduce(prs[:, 0:8], t1[:, :].rearrange("p (h e) -> p h e", h=8), AX.X, ALU.add), ["pp1"], ["prs0"])
                    self.act(prs[:, 8:16], prs[:, 0:8], AF.Ln, ["prs0"], ["prs1"], bias=RMS_EPS, scale=1.0 / 128)
                    self.act(prs[:, 0:8], prs[:, 8:16], AF.Exp, ["prs1"], ["prs2"], scale=-0.5)
                    self.tt(DVE, t1[:, :].rearrange("p (h e) -> p h e", h=8), A3, prs[:, 0:8].unsqueeze(2).broadcast_to([128, 8, 128]), ALU.mult, akeys + ["prs2"], ["pp1"])
                    self.tt(POOL, t1[:, :].rearrange("p (h e) -> p h e", h=8), t1[:, :].rearrange("p (h e) -> p h e", h=8),
                            nwb[:, :].unsqueeze(1).broadcast_to([128, 8, 128]), ALU.mult, ["pp1", "nwb"], ["pp1"])
                    self.tt(DVE, Y[:, :], t1[:, :], gt[b][:, :], ALU.mult, ["pp1", ("gt", b)], [("yt", b)])
                else:
                    for h in range(4):
                        S.op(DVE, lambda e, h=h, A_=A_: e.bn_stats(pst[:, h, :], A_[:, h * 256:(h + 1) * 256]), akeys, [("pst", h)])
                        S.op(DVE, lambda e, h=h: e.bn_aggr(pmv[:, h, :], pst[:, h, :]), [("pst", h)], ["pmv"])
                    self.act(prs[:, 0:4], pmv[:, 0:4, 1:2].rearrange("p h o -> p (h o)"), AF.Ln, ["pmv"], ["prs0"], bias=LN_EPS)
                    self.act(prs[:, 4:8], prs[:, 0:4], AF.Exp, ["prs0"], ["prs1"], scale=-0.5)
                    t3 = t1[:, :].rearrange("p (h e) -> p h e", h=4)
                    self.tt(DVE, t3, A3, pmv[:, 0:4, 0:1].broadcast_to([128, 4, 256]), ALU.subtract, akeys + ["pmv"], ["pp1"])
                    self.tt(POOL, t3, t3, prs[:, 4:8].unsqueeze(2).broadcast_to([128, 4, 256]), ALU.mult, ["pp1", "prs1"], ["pp1"])
                    if "dbg_t1" in self.debug:
                        if not hasattr(self, "dbg_t1_d"):
                            self.dbg_t1_d = self.nc.dram_tensor("dbg_t1", [T, 1024], F32, kind="ExternalOutput").ap()
                            self.dbg_pmv_d = self.nc.dram_tensor("dbg_pmv", [T, 16], F32, kind="ExternalOutput").ap()
                            self.dbg_prs_d = self.nc.dram_tensor("dbg_prs", [T, 16], F32, kind="ExternalOutput").ap()
                        self.dma(self.dbg_t1_d[rows, :], t1[:, :], ["pp1"], [("dbgt1", c)])
                        self.dma(self.dbg_pmv_d[rows, 0:8], pmv[:, 0:4, :].rearrange("p a b -> p (a b)"), ["pmv"], [("dbgpmv", c)])
                        self.dma(self.dbg_prs_d[rows, 0:8], prs[:, 0:8], ["prs1", "prs0"], [("dbgprs", c)])
                    if mx == "M":
                        self.tt(POOL, t1[:, :], t1[:, :], nwb[:, :], ALU.mult, ["pp1", "nwb"], ["pp1"])
                    self.tt(DVE, Y[:, :], t1[:, :], gt[b][:, :], ALU.mult, ["pp1", ("gt", b)], [("yt", b)])
                self.dma(self.scr[cfg["y"]][rows, :], Y[:, :], [("yt", b)], [(cfg["y"], c)])
            S.barrier()
        A.off = base

    def _gdn_inverse(self, KT, s_, sk, dts, d_):
        S, ps = self.S, self.ps
        PK = lambda i: ("ps", i)
        gm, M0, MTt, MTm, Um, Vm, Pm = self.ginv
        fo = 0 if d_ == 0 else 7
        to = 7 if d_ == 0 else 0
        ident = self.cst("ident")
        for hh in range(2):
            bank = 2 + hh
            for q in range(4):
                h = hh * 4 + q
                self.mm(ps[bank][:, q * 128:(q + 1) * 128], KT[:, h, :], KT[:, h, :], True, True, ["KT"], [PK(bank)])
            for q in range(4):
                h = hh * 4 + q
                self.stt(M0[:, h, :], ps[bank][:, q * 128:(q + 1) * 128], s_[:, 88 + h:89 + h], dts[:, h, :], ALU.mult, ALU.mult,
                         [PK(bank), (sk, "nbeta"), ("dts", h)], [("M0", hh)])
            bank = 4 + hh
            for q in range(4):
                h = hh * 4 + q
                self.tr(ps[bank][:, q * 128:(q + 1) * 128], M0[:, h, :], ident, [("M0", hh), "cst"], [PK(bank)])
            self.cp(ACT, MTt[:, hh * 4:(hh + 1) * 4, :].rearrange("p a t -> p (a t)"), ps[bank][:, :], [PK(bank)], [("MTt", hh)])
        for lev in range(1, 7):
            self.tt(POOL, MTm[:, lev - 1, :, :], MTt[:, :, :], gm[:, to + lev:to + lev + 1, :].broadcast_to([128, 8, 128]), ALU.mult,
                    [("MTt", 0), ("MTt", 1), "gm"], [("MTm", lev)])
        U = Um[0]
        self.tt(DVE, U[:, :, :], M0[:, :, :], gm[:, fo:fo + 1, :].broadcast_to([128, 8, 128]), ALU.mult, [("M0", 0), ("M0", 1), "gm"], [("Um", 0, 0), ("Um", 0, 1)])
        self.tt(DVE, U[:, :, :], U[:, :, :], ident.unsqueeze(1).broadcast_to([128, 8, 128]), ALU.add, [("Um", 0, 0), ("Um", 0, 1), "cst"], [("Um", 0, 0), ("Um", 0, 1)])
        cur = 0
        for lev in range(1, 7):
            nxt = 1 - cur
            Uc, Un = Um[cur], Um[nxt]
            for hh in range(2):
                hs = slice(hh * 4, (hh + 1) * 4)
                uk = ("Um", cur, hh)
                for q in range(4):
                    h = hh * 4 + q
                    self.tr(ps[2 + hh][:, q * 128:(q + 1) * 128], Uc[:, h, :], ident, [uk, "cst"], [PK(2 + hh)])
                self.cp(ACT, Vm[:, hs, :].rearrange("p a t -> p (a t)"), ps[2 + hh][:, :], [PK(2 + hh)], [("Vm", hh)])
                for q in range(4):
                    h = hh * 4 + q
                    self.mm(ps[4 + hh][:, q * 128:(q + 1) * 128], MTm[:, lev - 1, h, :], Uc[:, h, :], True, True, [("MTm", lev), uk], [PK(4 + hh)])
                self.cp(DVE, Pm[:, hs, :].rearrange("p a t -> p (a t)"), ps[4 + hh][:, :], [PK(4 + hh)], [("Pm", hh)])
                for q in range(4):
                    h = hh * 4 + q
                    self.mm(ps[6 + hh][:, q * 128:(q + 1) * 128], Vm[:, h, :], Pm[:, h, :], True, True, [("Vm", hh), ("Pm", hh)], [PK(6 + hh)])
                self.tt(DVE, Un[:, hs, :].rearrange("p a t -> p (a t)"), Uc[:, hs, :].rearrange("p a t -> p (a t)"), ps[6 + hh][:, :], ALU.add,
                        [uk, PK(6 + hh)], [("Um", nxt, hh)])
            cur = nxt
        for hh in range(2):
            hs = slice(hh * 4, (hh + 1) * 4)
            self.cp(ACT if hh == 0 else DVE, self.gdn_Wb[:, hs, :], Um[cur][:, hs, :], [("Um", cur, hh)], [("Wb", hh)])
        self.gdn_W = self.gdn_Wb

    def postnorm(self, pbanks, xt, xk, gi, li, ub, st, mv, rs, slot, dst, dstkey):
        uk = ("ub", slot)
        for hh in range(2):
            cs = slice(hh * 512, (hh + 1) * 512)
            self.tt(DVE, ub[:, cs], self.ps[pbanks[hh]][:, :], self.gbc[:, gi, cs], ALU.mult, [("ps", pbanks[hh]), ("gbc", gi, hh)], [(uk, hh)])
            self.stt(ub[:, cs], xt[:, cs], DN_ALPHA, ub[:, cs], ALU.mult, ALU.add, [xk, (uk, hh)], [(uk, hh)])
        S = self.S
        for hh in range(2):
            S.op(DVE, lambda e, hh=hh: e.bn_stats(st[:, hh, :], ub[:, hh * 512:(hh + 1) * 512]), [(uk, hh)], [("pst", slot)])
        S.op(DVE, lambda e: e.bn_aggr(mv[:, :], st[:, :, :].rearrange("p a b -> p (a b)")), [("pst", slot)], [("pmv", slot)])
        self.act(rs[:, 2:3], mv[:, 1:2], AF.Ln, [("pmv", slot)], [("prs2", slot)], bias=LN_EPS)
        self.act(rs[:, 0:1], rs[:, 2:3], AF.Exp, [("prs2", slot)], [("prs0", slot)], scale=-0.5)
        self.ts(DVE, rs[:, 1:2], mv[:, 0:1], rs[:, 0:1], -1.0, ALU.mult, ALU.mult, [("pmv", slot), ("prs0", slot)], [("prs1", slot)])
        self.act(ub[:, :], ub[:, :], AF.Identity, [(uk, 0), (uk, 1), ("prs0", slot), ("prs1", slot)], [(uk, 0), (uk, 1)],
                 bias=rs[:, 1:2], scale=rs[:, 0:1])
        self.tt(POOL, ub[:, :], ub[:, :], self.lnl[:, 0, :], ALU.mult, [(uk, 0), (uk, 1), ("lnl", 0)], [(uk, 0), (uk, 1)])
        self.tt(DVE, ub[:, :], ub[:, :], self.lnl[:, 1, :], ALU.add, [(uk, 0), (uk, 1), ("lnl", 1)], [(uk, 0), (uk, 1)])
        return self.dma(dst, ub[:, :], [(uk, 0), (uk, 1)], [dstkey])

    def phase_merge(self, l):
        A, S, ps = self.A, self.S, self.ps
        base = A.off
        last = (l == DEPTH - 1)
        self.lnl = A.alloc("lnl", [128, 2, 1024], F32)
        for j, src_ in enumerate((self.ln1_g, self.ln1_b)):
            self.dma(self.lnl[:, j, :], src_[l:l + 1, :].partition_broadcast(128), [], [("lnl", j)])
        wbr = A.alloc("wbr", [128, 3, 8, 1024], BF)
        wo = A.alloc("wo", [128, 8, 1024], BF)
        for br in range(3):
            for hh in range(2):
                self.dma(wbr[:, br, :, hh * 512:(hh + 1) * 512],
                         self.w_branch[l, br].rearrange("(kc p) c -> p kc c", p=128)[:, :, hh * 512:(hh + 1) * 512], [], [("wbr", br)], q=POOL)
        for hh in range(2):
            self.dma(wo[:, :, hh * 512:(hh + 1) * 512], self.w_out[l].rearrange("(kc p) c -> p kc c", p=128)[:, :, hh * 512:(hh + 1) * 512], [], ["wo"], q=POOL)
        yin = [[A.alloc("yin", [128, 1024], BF) for _ in range(3)] for _ in range(2)]
        mg = [A.alloc("mg", [128, 3072], BF) for _ in range(2)]
        xt = [A.alloc("xt", [128, 1024], F32) for _ in range(2)]
        yT = [A.alloc("yT", [128, 8, 128], BF) for _ in range(3)]
        mrg = A.alloc("mrg", [128, 1024], F32)
        mt2 = A.alloc("mt2", [128, 512], F32)
        mrb = A.alloc("mrb", [128, 1024], BF)
        mT = A.alloc("mT", [128, 8, 128], BF)
        ub = [A.alloc("ub", [128, 1024], F32) for _ in range(2)]
        st = [A.alloc("st", [128, 2, 6], F32) for _ in range(2)]
        mv = [A.alloc("mv", [128, 2], F32) for _ in range(2)]
        rs = [A.alloc("rs", [128, 4], F32) for _ in range(2)]
        src = self.xz if l == 0 else self.scr["X2"]
        names = ("Y_R", "Y_M", "Y_G")
        n = 0
        for t in range(2 if last else 0, NT):
            b = n % 2
            n += 1
            rows = slice(t * 128, (t + 1) * 128)
            v = 1 if t < 2 else 0
            for br in range(3):
                self.dma(yin[b][br][:, :], self.scr[names[br]][rows, :], [], [("yin", b, br)])
            self.dma(mg[b][:, :], self.scr["MG"][rows, :], [], [("mg", b)])
            self.dma(xt[b][:, :], src[rows, :], [], [("xt", b)])
            for br in range(3):
                bank = br % 2
                pv = ps[bank][:, :].bitcast(BF)
                for c in range(8):
                    self.tr(pv[:, c * 128:(c + 1) * 128], yin[b][br][:, c * 128:(c + 1) * 128], self.identb[:, :],
                            [("yin", b, br), "identb"], [("ps", bank)])
                self.cp(ACT if br != 1 else DVE, yT[br][:, :, :].rearrange("p a t -> p (a t)"), pv[:, :], [("ps", bank)], [("yT", br)])
            for hh in range(2):
                cs = slice(hh * 512, (hh + 1) * 512)
                for br in range(3):
                    for kc in range(8):
                        self.mm(ps[2 + br][:, :], yT[br][:, kc, :], wbr[:, br, kc, cs], kc == 0, kc == 7, [("yT", br), ("wbr", br)], [("ps", 2 + br)])
                self.tt(DVE, mrg[:, cs], ps[2][:, :], mg[b][:, hh * 512:(hh + 1) * 512], ALU.mult, [("ps", 2), ("mg", b)], [("mrg", hh)])
                self.tt(DVE, mt2[:, :], ps[3][:, :], mg[b][:, 1024 + hh * 512:1024 + (hh + 1) * 512], ALU.mult, [("ps", 3), ("mg", b)], ["mt2"])
                self.tt(POOL, mrg[:, cs], mrg[:, cs], mt2[:, :], ALU.add, [("mrg", hh), "mt2"], [("mrg", hh)])
                self.tt(DVE, mt2[:, :], ps[4][:, :], mg[b][:, 2048 + hh * 512:2048 + (hh + 1) * 512], ALU.mult, [("ps", 4), ("mg", b)], ["mt2"])
                self.tt(POOL, mrb[:, cs], mrg[:, cs], mt2[:, :], ALU.add, [("mrg", hh), "mt2"], [("mrb", hh)])
            pv = ps[5][:, :].bitcast(BF)
            for c in range(8):
                self.tr(pv[:, c * 128:(c + 1) * 128], mrb[:, c * 128:(c + 1) * 128], self.identb[:, :], [("mrb", c // 4), "identb"], [("ps", 5)])
            self.cp(ACT, mT[:, :, :].rearrange("p a t -> p (a t)"), pv[:, :], [("ps", 5)], ["mT"])
            for hh in range(2):
                for kc in range(8):
                    self.mm(ps[6 + hh][:, :], mT[:, kc, :], wo[:, kc, hh * 512:(hh + 1) * 512], kc == 0, kc == 7, ["mT", "wo"], [("ps", 6 + hh)])
            self.postnorm((6, 7), xt[b], ("xt", b), 0 + v, 0, ub[b], st[b], mv[b], rs[b], b, self.scr["X1"][rows, :], ("X1", t))
        S.barrier()
        A.off = base

    def phase_mlp(self, l):
        A, S, ps = self.A, self.S, self.ps
        base = A.off
        last = (l == DEPTH - 1)
        self.lnl = A.alloc("lnl", [128, 2, 1024], F32)
        for j, src_ in enumerate((self.ln2_g, self.ln2_b)):
            self.dma(self.lnl[:, j, :], src_[l:l + 1, :].partition_broadcast(128), [], [("lnl", j)])
        w1 = A.alloc("w1", [128, 8, DFF], BF)
        w2 = A.alloc("w2", [128, 32, D], BF)
        w1v = self.w_mlp1[l].rearrange("(kc p) c -> p kc c", p=128)
        w2v = self.w_mlp2[l].rearrange("(fc p) c -> p fc c", p=128)
        for i in range(8):
            self.dma(w1[:, :, i * 512:(i + 1) * 512], w1v[:, :, i * 512:(i + 1) * 512], [], [("w1", i)], q=POOL)
        for i in range(8):
            self.dma(w2[:, i * 4:(i + 1) * 4, :], w2v[:, i * 4:(i + 1) * 4, :], [], [("w2", i)], q=POOL)
        xt = [A.alloc("xt", [128, 1024], F32) for _ in range(3)]
        xn = [A.alloc("xn", [128, 1024], F32) for _ in range(1)] * 2
        h2T = A.alloc("h2T", [128, 8, 256], BF)
        hid = A.alloc("hid", [128, 32, 256], BF)
        rl = [A.alloc("rl", [128, 256], F32) for _ in range(2)]
        ub = [A.alloc("ub", [128, 1024], F32) for _ in range(1)] * 2
        st = [A.alloc("st", [128, 2, 6], F32) for _ in range(4)]
        mv = [A.alloc("mv", [128, 2], F32) for _ in range(4)]
        rs = [A.alloc("rs", [128, 4], F32) for _ in range(4)]
        tiles = list(range(2 if last else 0, NT))
        blocks = [tiles[i:i + 2] for i in range(0, len(tiles), 2)]
        xi = 0
        un = 0
        outs = []
        for blk in blocks:
            nb = len(blk)
            xts = []
            for j, t in enumerate(blk):
                b5 = xi % 3
                xi += 1
                b = 0
                v = 1 if t < 2 else 0
                rows = slice(t * 128, (t + 1) * 128)
                X = xt[b5]
                xk = ("xt", b5)
                xts.append((X, xk))
                self.dma(X[:, :], self.scr["X1"][rows, :], [], [xk])
                self.ln_stats(X, xk, st[b], mv[b], rs[b], ("m", b))
                self.act(xn[b][:, :], X[:, :], AF.Identity, [xk, ("rs0", ("m", b)), ("rs1", ("m", b))], [("xn", b)],
                         bias=rs[b][:, 1:2], scale=rs[b][:, 0:1])
                for c in range(8):
                    self.tr(ps[c // 4][:, (c % 4) * 128:(c % 4 + 1) * 128], xn[b][:, c * 128:(c + 1) * 128], self.cst("ident"),
                            [("xn", b), "cst"], [("ps", c // 4)])
                for c in range(8):
                    o = h2T[:, c, j * 128:(j + 1) * 128]
                    i_ = ps[c // 4][:, (c % 4) * 128:(c % 4 + 1) * 128]
                    sc_ = self.ops[:, 1, c, v:v + 1]
                    sh_ = self.modc[:, 24 + c, v:v + 1]
                    if c // 4 == 0:
                        self.ts(DVE, o, i_, sc_, sh_, ALU.mult, ALU.add, [("ps", 0), "modc", "ops"], [("h2T", j, c)])
                    else:
                        self.act(o, i_, AF.Identity, [("ps", 1), "modc", "ops"], [("h2T", j, c)], bias=sh_, scale=sc_)
            hkeys = [("h2T", j, c) for j in range(nb) for c in range(8)]
            N = nb * 128
            for f in range(32):
                bank = 2 + f % 4
                for kc in range(8):
                    self.mm(ps[bank][:, 0:N], w1[:, kc, f * 128:(f + 1) * 128], h2T[:, kc, 0:N], kc == 0, kc == 7,
                            hkeys + [("w1", f // 4)], [("ps", bank)])
                r_ = rl[f % 2]
                self.act(r_[:, 0:N], ps[bank][:, 0:N], AF.Relu, [("ps", bank)], [("rl", f % 2)])
                self.tt(POOL if f % 2 == 0 else DVE, hid[:, f, 0:N], r_[:, 0:N], r_[:, 0:N], ALU.mult, [("rl", f % 2)], [("hid", f)])
            for j, t in enumerate(blk):
                v = 1 if t < 2 else 0
                rows = slice(t * 128, (t + 1) * 128)
                for f in range(32):
                    for hh in range(2):
                        self.mm(ps[6 + hh][:, :], hid[:, f, j * 128:(j + 1) * 128], w2[:, f, hh * 512:(hh + 1) * 512], f == 0, f == 31,
                                [("hid", f), ("w2", f // 4)], [("ps", 6 + hh)])
                if last:
                    dst = self.out[t * 128 - NCTX:(t + 1) * 128 - NCTX, :]
                else:
                    dst = self.scr["X2"][rows, :]
                u = 0
                un += 1
                X, xk = xts[j]
                tok = self.postnorm((6, 7), X, xk, 2 + v, 2, ub[u], st[2 + u], mv[2 + u], rs[2 + u], ("p", u), dst, ("X2", t))
                outs.append(tok)
        S.barrier()
        A.off = base
        return outs

    def build(self):
        self.setup()
        for l in range(self.layers):
            self.phase_ada(l)
            if self.upto == "ada":
                break
            A = self.A
            A.off = self.persist_end
            hT = A.alloc("hT", [128, 8, T], BF)
            self.hT = hT
            src = self.xz if l == 0 else self.scr["X2"]
            if not getattr(self, "skip_inproj", False):
                self.phase_ln1(l, src, hT)
            if "dbg_hT" in self.debug:
                self.S.barrier()
                hb = A.alloc("hdbg", [128, 1024], F32)
                self.cp(DVE, hb[:, :].rearrange("p (c t) -> p c t", c=8), hT[:, :, 0:128], [], ["hdbg"])
                self.dbg("dbg_hT", hb[:, :], [128, 1024], ["hdbg"])
            if self.upto == "ln1":
                break
            if not getattr(self, "skip_inproj", False):
                self.phase_inproj(l, hT)
            self.S.barrier()
            if self.upto == "inproj":
                break
            A.off = self.persist_end
            self.scan_setup(l)
            for mx in getattr(self, "mixers", "RMG"):
                self.phase_scan(l, mx)
            if self.upto == "scan":
                break
            self.S.barrier()
            A.off = self.persist_end
            if not getattr(self, "skip_merge", False):
                self.phase_merge(l)
            if self.upto == "merge":
                break
            self.phase_mlp(l)
        self.S.barrier()
        self.S.emit()


def prep_inputs(inputs, b):
    f = lambda a: np.ascontiguousarray(np.asarray(a, np.float32))
    m = {}
    m["xz"] = f(np.concatenate([inputs["ctx"][b], inputs["x"][b]], 0))
    cc = np.stack([np.asarray(inputs["c"][b]), np.asarray(inputs["c_ctx"])], -1)
    m["cc"] = f(cc.reshape(8, 128, 2).transpose(1, 0, 2))
    m["w_ada"] = f(inputs["w_ada"])
    m["b_ada"] = f(np.asarray(inputs["b_ada"]).reshape(DEPTH, 48, 128).transpose(0, 2, 1))
    m["w_in"] = f(inputs["w_in"])
    m["conv_w"] = f(np.asarray(inputs["conv_w"]).reshape(DEPTH, 5, 24, 128).transpose(0, 3, 2, 1))
    m["smallp"] = f(np.concatenate([np.asarray(inputs[k]).reshape(DEPTH, -1) for k in
                                    ("ret_decay", "mlstm_i_bias", "mlstm_f_bias", "gdn_a_log", "gdn_dt_bias")], 1)[:, :56])
    m["smallp"] = f(np.pad(m["smallp"], ((0, 0), (0, 8))))
    for k in ("mlstm_norm_w", "gdn_norm_w", "w_branch", "w_out", "ln1_g", "ln1_b", "ln2_g", "ln2_b", "w_mlp1", "w_mlp2"):
        m[k] = f(inputs[k])
    m["consts"] = f(CONSTS)
    rq, rk = _rope_tables()
    m["ropeq"], m["ropek"] = f(rq), f(rk)
    m["gmask"] = f(_gmasks())
    return m


def kernel(**inputs):
    nc = bass.Bass("TRN2", target_bir_lowering=False)
    Builder(nc).build()
    in_maps = [prep_inputs(inputs, b) for b in range(8)]
    res = run_bass_kernel_spmd(nc, in_maps, core_ids=list(range(8)))
    return np.stack([np.asarray(r["out"], np.float32) for r in res.results], 0)
```

```python
import contextlib
import math
import numpy as np
import ml_dtypes
import concourse.bass as bass
import concourse.mybir as mybir
from concourse.bass_utils import run_bass_kernel_spmd

F32 = mybir.dt.float32
BF = mybir.dt.bfloat16
AF = mybir.ActivationFunctionType
ALU = mybir.AluOpType
AX = mybir.AxisListType

PE, ACT, DVE, POOL, SP = "pe", "act", "dve", "pool", "sp"
COMPUTE = (PE, ACT, DVE, POOL)
EPOCH = 12000
NDMASEM = 12

NCTX = 256
NLAT = 4096
T = NCTX + NLAT
NT = T // 128
D = 1024
DEPTH = 2
IN_DIM = 14384
DFF = 4096
LN_EPS = 1e-5
RMS_EPS = 1e-6
DN_ALPHA = (2 * DEPTH) ** 0.25
NEG = -30000.0


class Sched:
    def __init__(self, nc):
        self.nc = nc
        self.q = {e: [] for e in (PE, ACT, DVE, POOL, SP)}
        self.cnt = {e: 0 for e in COMPUTE}
        self.dcnt = {POOL: 0, SP: 0}
        self.lastw = {}
        self.readers = {}
        self.waited = {}
        self.waited_dma = {e: set() for e in self.q}

    def _deps(self, reads, writes):
        deps = []
        for r in reads:
            t = self.lastw.get(r)
            if t is not None:
                deps.append(t)
        for w in writes:
            t = self.lastw.get(w)
            if t is not None:
                deps.append(t)
            deps.extend(self.readers.get(w, ()))
        return deps

    def _emit_waits(self, eng, deps):
        best = {}
        dmas = []
        for t in deps:
            if t[0] == "dma":
                if t not in self.waited_dma[eng]:
                    self.waited_dma[eng].add(t)
                    dmas.append(t)
            else:
                p, n = t
                if p == eng and (eng == PE or n <= self.cnt[eng] - 3):
                    continue
                if n > best.get(p, 0):
                    best[p] = n
        for p, n in best.items():
            if self.waited.get((eng, p), 0) >= n:
                continue
            self.waited[(eng, p)] = n
            self.q[eng].append(("wait", p, n))
        for t in dmas:
            self.q[eng].append(("waitdma", t[1], t[2]))

    def _record(self, tok, reads, writes):
        for r in reads:
            lst = self.readers.setdefault(r, [])
            if tok[0] == "dma":
                lst[:] = [x for x in lst if not (x[0] == "dma" and x[1] == tok[1] and x[2] <= tok[2] - NDMASEM)]
            else:
                lst[:] = [x for x in lst if x[0] != tok[0]]
            lst.append(tok)
        for w in writes:
            self.lastw[w] = tok
            self.readers[w] = []

    def op(self, eng, fn, reads=(), writes=()):
        self._emit_waits(eng, self._deps(reads, writes))
        self.cnt[eng] += 1
        tok = (eng, self.cnt[eng])
        self.q[eng].append(("op", fn, self.cnt[eng]))
        self._record(tok, reads, writes)
        return tok

    def dma(self, eng, out, in_, reads=(), writes=()):
        deps = self._deps(reads, writes)
        k = self.dcnt[eng]
        self.dcnt[eng] += 1
        if k >= NDMASEM:
            deps.append(("dma", eng, k - NDMASEM))
        self._emit_waits(eng, deps)
        tok = ("dma", eng, k)
        self.q[eng].append(("dma", out, in_, k))
        self._record(tok, reads, writes)
        return tok

    def barrier(self):
        for e in self.q:
            deps = [(p, self.cnt[p]) for p in COMPUTE if self.cnt[p] > 0 and p != e]
            for q_ in self.dcnt:
                lo = max(0, self.dcnt[q_] - NDMASEM)
                deps += [("dma", q_, k) for k in range(lo, self.dcnt[q_])]
            self._emit_waits(e, deps)
        self.lastw = {}
        self.readers = {}

    def emit(self):
        nc = self.nc
        with contextlib.ExitStack() as st:
            sems = {}
            for e in COMPUTE:
                n = max(1, (self.cnt[e] + EPOCH - 1) // EPOCH)
                sems[e] = [st.enter_context(nc.semaphore(f"c_{e}_{i}")) for i in range(n)]
            dsems = {e: [st.enter_context(nc.semaphore(f"d_{e}_{i}")) for i in range(NDMASEM)] for e in self.dcnt}
            block = st.enter_context(nc.Block())

            def run(name):
                def body(eng):
                    for item in self.q[name]:
                        k = item[0]
                        if k == "op":
                            _, fn, n = item
                            fn(eng).then_inc(sems[name][(n - 1) // EPOCH], 1)
                        elif k == "wait":
                            _, p, n = item
                            ep = (n - 1) // EPOCH
                            eng.wait_ge(sems[p][ep], n - ep * EPOCH)
                        elif k == "waitdma":
                            _, q_, kk = item
                            eng.wait_ge(dsems[q_][kk % NDMASEM], 16 * (kk // NDMASEM + 1))
                        else:
                            _, out, in_, kk = item
                            eng.dma_start(out=out, in_=in_).then_inc(dsems[name][kk % NDMASEM], 16)
                return body

            block.tensor(run(PE))
            block.scalar(run(ACT))
            block.vector(run(DVE))
            block.gpsimd(run(POOL))
            block.sync(run(SP))


class Arena:
    def __init__(self, nc, limit):
        self.nc, self.off, self.limit, self.n = nc, 16640, 16640 + limit, 0

    def alloc(self, name, shape, dtype):
        nb = int(np.prod(shape[1:])) * (2 if dtype == BF else 4)
        nb = (nb + 31) // 32 * 32
        assert self.off + nb <= self.limit, (name, self.off, nb, self.limit)
        self.n += 1
        t = self.nc.alloc_sbuf_tensor_at(f"{name}_{self.n}", list(shape), dtype, offset=self.off)
        self.off += nb
        return t


CST = {}


def _build_consts():
    cols = []

    def add(name, arr):
        arr = np.asarray(arr, np.float32)
        if arr.ndim == 1:
            arr = arr[:, None]
        CST[name] = (sum(a.shape[1] for a in cols), arr.shape[1])
        cols.append(arr)

    p = np.arange(128)
    t_, i_ = p[:, None], p[None, :]
    add("ident", np.eye(128))
    add("ones", np.ones((128, 128)))
    add("triF", t_ <= i_)
    add("triB", t_ >= i_)
    add("sF", t_ > i_)
    add("sB", t_ < i_)
    add("negF", np.where(i_ < t_, NEG, 0.0))
    add("negB", np.where(i_ > t_, NEG, 0.0))
    add("m01F", i_ >= t_)
    add("m01B", i_ <= t_)
    add("s01F", i_ > t_)
    add("s01B", i_ < t_)
    add("diffF", np.maximum(i_ - t_, 0))
    add("diffB", np.maximum(t_ - i_, 0))
    add("posv", np.stack([p + 1, 127 - p, 128 - p, p, np.full(128, 128)], 1))
    return np.concatenate(cols, 1)


def _gmasks():
    p = np.arange(128)
    j, i = p[:, None], p[None, :]
    ms = []
    for l in range(7):
        B = 2 ** (l + 1)
        ms.append((((j // B) == (i // B)) & ((j % B) < B // 2) & ((i % B) >= B // 2)).astype(np.float32))
    return np.stack(ms + [m.T for m in ms], 1)


CONSTS = _build_consts()
NCST = CONSTS.shape[1]


def _rope_tables():
    n_freq = 64
    freqs = (10000.0 ** (-np.arange(n_freq, dtype=np.float32) / n_freq)).astype(np.float32)
    row = np.repeat(np.arange(64, dtype=np.float32), 64)
    col = np.tile(np.arange(64, dtype=np.float32), 64)
    lat = np.stack([row[:, None] * freqs, col[:, None] * freqs], 1).astype(np.float32)
    ang = np.concatenate([np.zeros((NCTX, 2, n_freq), np.float32), lat], 0)
    cos, sin = np.cos(ang), np.sin(ang)
    tq = np.stack([np.stack([cos, cos], 1), np.stack([sin, sin], 1)], 1).reshape(T, 512)
    return tq.astype(np.float32), (tq / 16.0).astype(np.float32)


def _groups():
    g = []
    c = 0
    for name, w in (("RQ", 1024), ("RK", 1024), ("RV", 1024), ("RG", 1024), ("MQ", 512), ("MK", 512),
                    ("MV", 1024), ("MO", 1024), ("MIF", 16), ("GQKV", 3072), ("GG", 1024), ("GBA", 32),
                    ("MG", 3072)):
        g.append((name, c, w))
        c += w
    assert c == IN_DIM
    return g


GROUPS = _groups()


class Builder:
    def __init__(self, nc, debug=(), layers=DEPTH, upto="all", ext_in=()):
        self.nc = nc
        self.debug = set(debug)
        self.ext_in = set(ext_in)
        self.S = Sched(nc)
        self.layers = layers
        self.upto = upto
        self.A = Arena(nc, 206 * 1024)
        self.uid = 0
        self._dram()
        self._psum()

    def _din(self, name, shape, dt=F32):
        return self.nc.dram_tensor(name, list(shape), dt, kind="ExternalInput").ap()

    def _dscr(self, name, shape, dt):
        kind = "ExternalOutput" if name in self.debug else ("ExternalInput" if name in self.ext_in else "Internal")
        return self.nc.dram_tensor(name, list(shape), dt, kind=kind).ap()

    def _dram(self):
        i = self._din
        self.xz = i("xz", [T, D])
        self.cc = i("cc", [128, 8, 2])
        self.w_ada = i("w_ada", [DEPTH, D, 6 * D])
        self.b_ada = i("b_ada", [DEPTH, 128, 48])
        self.w_in = i("w_in", [DEPTH, D, IN_DIM])
        self.conv_w = i("conv_w", [DEPTH, 128, 24, 5])
        self.smallp = i("smallp", [DEPTH, 64])
        self.mnorm_w = i("mlstm_norm_w", [DEPTH, D])
        self.gnorm_w = i("gdn_norm_w", [DEPTH, 128])
        self.w_branch = i("w_branch", [DEPTH, 3, D, D])
        self.w_out = i("w_out", [DEPTH, D, D])
        self.ln1_g = i("ln1_g", [DEPTH, D]); self.ln1_b = i("ln1_b", [DEPTH, D])
        self.ln2_g = i("ln2_g", [DEPTH, D]); self.ln2_b = i("ln2_b", [DEPTH, D])
        self.w_mlp1 = i("w_mlp1", [DEPTH, D, DFF]); self.w_mlp2 = i("w_mlp2", [DEPTH, DFF, D])
        self.cst_d = i("consts", [128, NCST])
        self.ropeq_d = i("ropeq", [T, 512]); self.ropek_d = i("ropek", [T, 512])
        self.gmask_d = i("gmask", [128, 14, 128])
        self.out = self.nc.dram_tensor("out", [NLAT, D], F32, kind="ExternalOutput").ap()
        s = self._dscr
        self.scr = {}
        for name, w, dt in (("RQ", 1024, BF), ("RK", 1024, BF), ("RV", 1024, BF), ("RG", 1024, BF),
                            ("MQ", 512, BF), ("MK", 512, BF), ("MV", 1024, BF), ("MO", 1024, BF),
                            ("MIF", 16, F32), ("GQ", 1024, BF), ("GK", 1024, BF), ("GV", 1024, BF),
                            ("GG", 1024, BF), ("GBA", 32, F32), ("MG", 3072, BF),
                            ("OB_R", 1024, F32), ("OB_M", 1024, F32), ("OB_G", 1024, F32),
                            ("Y_R", 1024, BF), ("Y_M", 1024, BF), ("Y_G", 1024, BF),
                            ("X1", 1024, F32), ("X2", 1024, F32)):
            self.scr[name] = s(name, [T, w], dt)

    def _psum(self):
        self.ps = [self.nc.alloc_psum_tensor(f"ps{i}", [128, 512], F32) for i in range(8)]
        self._ring = 0

    def ring(self):
        self._ring = (self._ring + 1) % 6
        return 2 + self._ring

    def cst(self, name, rows=128):
        o, w = CST[name]
        return self.cstt[0:rows, o:o + w]

    def hk(self, t):
        return [("hT", t, c) for c in range(8)]

    def key(self, base):
        self.uid += 1
        return (base, self.uid)

    def mm(self, out, lhsT, rhs, start, stop, reads, writes):
        self.S.op(PE, lambda e: e.matmul(out, lhsT, rhs, start=start, stop=stop), reads, writes)

    def tr(self, out, in_, ident, reads, writes):
        self.S.op(PE, lambda e: e.transpose(out, in_, ident), reads, writes)

    def act(self, out, in_, func, reads, writes, bias=0.0, scale=1.0, eng=ACT):
        self.S.op(ACT, lambda e: e.activation(out, in_, func, bias=bias, scale=scale), reads, writes)

    def tt(self, eng, out, a, b, op, reads, writes):
        self.S.op(eng, lambda e: e.tensor_tensor(out, a, b, op), reads, writes)

    def ts(self, eng, out, a, s1, s2, op0, op1, reads, writes):
        if s2 is None:
            self.S.op(eng, lambda e: e.tensor_scalar(out, a, s1, None, op0), reads, writes)
        else:
            self.S.op(eng, lambda e: e.tensor_scalar(out, a, s1, s2, op0, op1), reads, writes)

    def stt(self, out, a, s, b, op0, op1, reads, writes):
        self.S.op(DVE, lambda e: e.scalar_tensor_tensor(out, a, s, b, op0, op1), reads, writes)

    def cp(self, eng, out, in_, reads, writes):
        if eng == ACT:
            self.S.op(ACT, lambda e: e.copy(out, in_), reads, writes)
        else:
            self.S.op(eng, lambda e: e.tensor_copy(out, in_), reads, writes)

    def dma(self, out, in_, reads, writes, q=SP):
        return self.S.dma(q, out, in_, reads, writes)

    def dbg(self, name, ap, shape, reads):
        if name in self.debug:
            d = self.nc.dram_tensor(name, list(shape), F32, kind="ExternalOutput").ap()
            self.dma(d, ap, reads, [("dbgout", name)])

    def setup(self):
        A = self.A
        self.cstt = A.alloc("cst", [128, NCST], F32)
        self.dma(self.cstt[:, :], self.cst_d, [], ["cst"])
        self.identb = A.alloc("identb", [128, 128], BF)
        self.cp(DVE, self.identb[:, :], self.cst("ident"), ["cst"], ["identb"])
        self.onesb = A.alloc("onesb", [128, 128], BF)
        self.cp(DVE, self.onesb[:, :], self.cst("ones"), ["cst"], ["onesb"])
        self.cct = A.alloc("cct", [128, 8, 2], F32)
        self.dma(self.cct[:, :, :], self.cc, [], ["cct"])
        self.scs = A.alloc("scs", [128, 8, 2], F32)
        self.act(self.scs[:, :, :], self.cct[:, :, :], AF.Silu, ["cct"], ["scs"])
        self.modc = A.alloc("modc", [128, 48, 2], F32)
        self.ops = A.alloc("ops", [128, 2, 8, 2], F32)
        self.gbc = A.alloc("gbc", [128, 4, 1024], F32)
        self.smp = A.alloc("smp", [128, 64], F32)
        self.persist_end = A.off

    def phase_ada(self, l):
        A, S = self.A, self.S
        A.off = self.persist_end
        badat = A.alloc("badat", [128, 48], F32)
        self.dma(badat[:, :], self.b_ada[l], [], ["badat"])
        self.dma(self.smp[:, :], self.smallp[l:l + 1, :].partition_broadcast(128), [], ["smp"])
        wsl = [A.alloc("wada", [128, 8, 768], F32) for _ in range(2)]
        wv = self.w_ada[l].rearrange("(kc p) c -> p kc c", p=128)
        psA = self.ps[0]
        psAk = ("ps", 0)
        for s in range(8):
            wt = wsl[s % 2]
            wk = ("wada", s % 2)
            self.dma(wt[:, :, :], wv[:, :, s * 768:(s + 1) * 768], [], [wk])
            for jj in range(6):
                j = s * 6 + jj
                for kc in range(8):
                    self.mm(psA[:, 2 * j:2 * j + 2], wt[:, kc, jj * 128:(jj + 1) * 128], self.scs[:, kc, :],
                            kc == 0, kc == 7, [wk, "scs"], [psAk])
        self.tt(DVE, self.modc[:, :, :], psA[:, 0:96].rearrange("p (j v) -> p j v", v=2),
                badat[:, :].unsqueeze(2).broadcast_to([128, 48, 2]), ALU.add, [psAk, "badat"], ["modc"])
        for sub in range(2):
            j0 = 8 + 24 * sub
            self.ts(DVE, self.ops[:, sub, :, :], self.modc[:, j0:j0 + 8, :], 1.0, None, ALU.add, None, ["modc"], ["ops"])
        tmp = [A.alloc("gtmp", [128, 128], F32) for _ in range(2)]
        n = 0
        for sub in range(2):
            for v in range(2):
                gi = sub * 2 + v
                pb = 1 + (gi % 2) * 2
                pst = self.ps[pb: pb + 2]
                for c in range(8):
                    tk = ("gtmp", n % 2)
                    col = self.modc[:, 16 + 24 * sub + c, v:v + 1]
                    self.ts(DVE, tmp[n % 2][:, :], self.cst("ones"), col, None, ALU.mult, None, ["cst", "modc"], [tk])
                    self.mm(pst[c // 4][:, (c % 4) * 128:(c % 4 + 1) * 128], tmp[n % 2][:, :], self.cst("ident"),
                            True, True, [tk, "cst"], [("ps", pb + c // 4)])
                    n += 1
                for hh in range(2):
                    self.cp(ACT, self.gbc[:, gi, hh * 512:(hh + 1) * 512], pst[hh][:, :], [("ps", pb + hh)], [("gbc", gi, hh)])
        S.barrier()
        self.dbg("dbg_modc", self.modc[:, :, :].rearrange("p j v -> p (j v)"), [128, 96], [])
        self.dbg("dbg_gbc", self.gbc[:, :, :].rearrange("p g d -> p (g d)"), [128, 4096], [])

    def ln_stats(self, xt, xk, st, mv, rs, uk):
        S = self.S
        for hh in range(2):
            S.op(DVE, lambda e, hh=hh: e.bn_stats(st[:, hh, :], xt[:, hh * 512:(hh + 1) * 512]), [xk], [("st", uk)])
        S.op(DVE, lambda e: e.bn_aggr(mv[:, :], st[:, :, :].rearrange("p a b -> p (a b)")), [("st", uk)], [("mv", uk)])
        self.act(rs[:, 2:3], mv[:, 1:2], AF.Ln, [("mv", uk)], [("rs2", uk)], bias=LN_EPS)
        self.act(rs[:, 0:1], rs[:, 2:3], AF.Exp, [("rs2", uk)], [("rs0", uk)], scale=-0.5)
        self.ts(DVE, rs[:, 1:2], mv[:, 0:1], rs[:, 0:1], -1.0, ALU.mult, ALU.mult, [("mv", uk), ("rs0", uk)], [("rs1", uk)])

    def phase_ln1(self, l, src, hT):
        A, S = self.A, self.S
        xt = [A.alloc("xt", [128, 1024], F32) for _ in range(2)]
        xn = [A.alloc("xn", [128, 1024], F32) for _ in range(2)]
        st = [A.alloc("st", [128, 2, 6], F32) for _ in range(2)]
        mv = [A.alloc("mv", [128, 2], F32) for _ in range(2)]
        rs = [A.alloc("rs", [128, 4], F32) for _ in range(2)]
        import os
        STEPS = int(os.environ.get("LN1_STEPS", "9"))
        evm = os.environ.get("EVM", "split")
        EV = (lambda c: True) if evm == "dve" else ((lambda c: False) if evm == "act" else (lambda c: c // 4 == 0))
        for t in range(int(os.environ.get("LN1_NT", NT))):
            b = t % 2
            v = 1 if t < 2 else 0
            self.dma(xt[b][:, :], src[t * 128:(t + 1) * 128, :], [], [("xt", b)])
            if STEPS < 2:
                continue
            self.ln_stats(xt[b], ("xt", b), st[b], mv[b], rs[b], b)
            if STEPS < 4:
                continue
            self.act(xn[b][:, :], xt[b][:, :], AF.Identity, [("xt", b), ("rs0", b), ("rs1", b)], [("xn", b)],
                     bias=rs[b][:, 1:2], scale=rs[b][:, 0:1])
            if STEPS < 5:
                continue
            for c in range(8):
                pt = self.ps[(t % 2) * 2 + c // 4]
                pk = ("ps", (t % 2) * 2 + c // 4)
                self.tr(pt[:, (c % 4) * 128:(c % 4 + 1) * 128], xn[b][:, c * 128:(c + 1) * 128], self.cst("ident"),
                        [("xn", b), "cst"], [pk])
            if STEPS < 6:
                continue
            for c in range(8):
                pt = self.ps[(t % 2) * 2 + c // 4]
                pk = ("ps", (t % 2) * 2 + c // 4)
                o = hT[:, c, t * 128:(t + 1) * 128]
                i_ = pt[:, (c % 4) * 128:(c % 4 + 1) * 128]
                sc_ = self.ops[:, 0, c, v:v + 1]
                sh_ = self.modc[:, c, v:v + 1]
                lock = ["evlock"] if os.environ.get("EVLOCK") else []
                if EV(c):
                    self.ts(DVE, o, i_, sc_, sh_, ALU.mult, ALU.add, [pk, "modc", "ops"], [("hT", t, c)] + lock)
                else:
                    self.act(o, i_, AF.Identity, [pk, "modc", "ops"], [("hT", t, c)] + lock, bias=sh_, scale=sc_)

    def phase_inproj(self, l, hT):
        A, S = self.A, self.S
        wg = [A.alloc("wg", [128, 8, 512], BF) for _ in range(2)]
        stg = [A.alloc("stg", [128, 512], BF) for _ in range(4)]
        stf = [A.alloc("stf", [128, 32], F32) for _ in range(2)]
        rp = [A.alloc("rp", [128, 512], F32) for _ in range(2)]
        t12 = [A.alloc("t12", [128, 2, 256], F32) for _ in range(2)]
        xc = A.alloc("xc", [128, 4, T + 8], BF)
        cwt = A.alloc("cwt", [128, 24, 5], F32)
        dg = A.alloc("dg", [128, 20, 128], BF)
        sl = [A.alloc("sl", [128, 512], F32) for _ in range(2)]
        sq = [A.alloc("sq", [128, 512], F32) for _ in range(2)]
        ssn = [A.alloc("ssn", [128, 12], F32) for _ in range(2)]
        self.dma(cwt[:, :, :], self.conv_w[l], [], ["cwt"])
        S.op(POOL, lambda e: e.memset(xc[:, :, :], 0.0), [], [("xc", i) for i in range(4)])
        wv = self.w_in[l].rearrange("(kc p) c -> p kc c", p=128)
        gi = 0
        si = 0
        for name, c0, width in GROUPS:
            if getattr(self, "only_groups", None) and name not in self.only_groups:
                continue
            nsub = max(1, width // 512)
            w_ = min(width, 512)
            for sub in range(nsub):
                cs = c0 + sub * 512
                wb = wg[gi % 2]
                wk = ("wg", gi % 2)
                gi += 1
                self.dma(wb[:, :, 0:w_], wv[:, :, cs:cs + w_], [], [wk], q=POOL)
                if name == "GQKV":
                    self._gdn_group(l, sub, wb, wk, hT, xc, cwt, dg, sl, sq, ssn, stg)
                    continue
                for t in range(NT):
                    pt = self.ps[4 + t % 4]
                    pk = ("ps", 4 + t % 4)
                    for kc in range(8):
                        self.mm(pt[:, 0:w_], hT[:, kc, t * 128:(t + 1) * 128], wb[:, kc, 0:w_], kc == 0, kc == 7,
                                self.hk(t) + [wk], [pk])
                    rows = slice(t * 128, (t + 1) * 128)
                    if name in ("MIF", "GBA"):
                        sb = stf[si % 2]; sk = ("stf", si % 2); si += 1
                        self.cp(DVE, sb[:, 0:w_], pt[:, 0:w_], [pk], [sk])
                        self.dma(self.scr[name][rows, :], sb[:, 0:w_], [sk], [(name, t)])
                        continue
                    sb = stg[si % 4]; sk = ("stg", si % 4); si += 1
                    dst = self.scr[name][rows, sub * 512:(sub + 1) * 512]
                    if name in ("RQ", "RK"):
                        b = t % 2
                        tab = self.ropeq_d if name == "RQ" else self.ropek_d
                        self.dma(rp[b][:, :], tab[rows, :], [], [("rp", b)])
                        psv = pt[:, :].rearrange("p (g ab f) -> p g ab f", g=4, ab=2)
                        cosv = rp[b][:, 0:256].rearrange("p (g f) -> p g f", g=4)
                        sinv = rp[b][:, 256:512].rearrange("p (g f) -> p g f", g=4)
                        sbv = sb[:, :].rearrange("p (g ab f) -> p g ab f", g=4, ab=2)
                        t1 = t12[b][:, 0, :].rearrange("p (g f) -> p g f", g=4)
                        t2 = t12[b][:, 1, :].rearrange("p (g f) -> p g f", g=4)
                        tk = ("t12", b)
                        a_, b_ = psv[:, :, 0, :], psv[:, :, 1, :]
                        self.tt(DVE, t1, a_, cosv, ALU.mult, [pk, ("rp", b)], [(tk, 0)])
                        self.tt(DVE, t2, b_, sinv, ALU.mult, [pk, ("rp", b)], [(tk, 1)])
                        self.tt(POOL, sbv[:, :, 0, :], t1, t2, ALU.subtract, [(tk, 0), (tk, 1)], [sk])
                        self.tt(DVE, t1, b_, cosv, ALU.mult, [pk, ("rp", b)], [(tk, 0)])
                        self.tt(DVE, t2, a_, sinv, ALU.mult, [pk, ("rp", b)], [(tk, 1)])
                        self.tt(POOL, sbv[:, :, 1, :], t1, t2, ALU.add, [(tk, 0), (tk, 1)], [sk])
                        self.dma(dst, sb[:, :], [sk], [(name, t, sub)])
                        continue
                    if name in ("RG", "GG"):
                        self.act(sb[:, :], pt[:, :], AF.Silu, [pk], [sk])
                    elif name in ("MO", "MG"):
                        self.act(sb[:, :], pt[:, :], AF.Sigmoid, [pk], [sk])
                    elif name == "MQ":
                        self.act(sb[:, :], pt[:, :], AF.Copy, [pk], [sk], scale=128 ** -0.5)
                    elif t % 2 == 0:
                        self.cp(ACT, sb[:, :], pt[:, :], [pk], [sk])
                    else:
                        self.cp(DVE, sb[:, :], pt[:, :], [pk], [sk])
                    self.dma(dst, sb[:, :], [sk], [(name, t, sub)])

    def _gdn_group(self, l, sub, wb, wk, hT, xc, cwt, dg, sl, sq, ssn, stg):
        S = self.S
        for ct in range(4):
            for tap in range(5):
                self.ts(POOL, dg[:, ct * 5 + tap, :], self.cst("ident"), cwt[:, sub * 4 + ct, tap:tap + 1], None,
                        ALU.mult, None, ["cst", "cwt"], [("dg", ct)])
        blocks = [(0, 256)] + [(256 + 512 * i, 512) for i in range(8)]
        n = 0
        for (t0, tw) in blocks:
            col0 = 2 + t0 if t0 < NCTX else 6 + t0
            for ct in range(4):
                pt = self.ps[n % 4]
                pk = ("ps", n % 4)
                for kc in range(8):
                    self.mm(pt[:, 0:tw], wb[:, kc, ct * 128:(ct + 1) * 128], hT[:, kc, t0:t0 + tw], kc == 0, kc == 7,
                            [wk] + sum([self.hk(t0 // 128 + i) for i in range(tw // 128)], []), [pk])
                self.cp(ACT if n % 2 == 0 else DVE, xc[:, ct, col0:col0 + tw], pt[:, 0:tw], [pk], [("xc", ct)])
                n += 1
        which = "GQ" if sub < 2 else ("GK" if sub < 4 else "GV")
        for t in range(NT):
            base = (2 + t * 128 if t < 2 else 6 + t * 128) - 2
            pt = self.ps[4 + t % 4]
            pk = ("ps", 4 + t % 4)
            for ct in range(4):
                for tap in range(5):
                    self.mm(pt[:, ct * 128:(ct + 1) * 128], xc[:, ct, base + tap:base + tap + 128], dg[:, ct * 5 + tap, :],
                            tap == 0, tap == 4, [("xc", ct), ("dg", ct)], [pk])
            b = t % 2
            sb = stg[t % 4]; sk = ("stg", t % 4)
            rows = slice(t * 128, (t + 1) * 128)
            dst = self.scr[which][rows, (sub % 2) * 512:(sub % 2 + 1) * 512]
            if which == "GV":
                self.act(sb[:, :], pt[:, :], AF.Silu, [pk], [sk])
            else:
                self.act(sl[b][:, :], pt[:, :], AF.Silu, [pk], [("sl", b)])
                self.tt(POOL, sq[b][:, :], sl[b][:, :], sl[b][:, :], ALU.mult, [("sl", b)], [("sq", b)])
                S.op(DVE, lambda e, b=b: e.tensor_reduce(ssn[b][:, 0:4], sq[b][:, :].rearrange("p (h f) -> p h f", h=4), AX.X, ALU.add),
                     [("sq", b)], [("ss", b)])
                self.act(ssn[b][:, 4:8], ssn[b][:, 0:4], AF.Ln, [("ss", b)], [("ssl", b)], bias=RMS_EPS)
                qs = math.log(128 ** -0.5) if which == "GQ" else 0.0
                self.act(ssn[b][:, 8:12], ssn[b][:, 4:8], AF.Exp, [("ssl", b)], [("ssr", b)], scale=-0.5, bias=qs)
                self.tt(DVE, sb[:, :].rearrange("p (h f) -> p h f", h=4), sl[b][:, :].rearrange("p (h f) -> p h f", h=4),
                        ssn[b][:, 8:12].unsqueeze(2).broadcast_to([128, 4, 128]), ALU.mult, [("sl", b), ("ssr", b)], [sk])
            self.dma(dst, sb[:, :], [sk], [(which, t, sub)])

    MIX = {"R": dict(H=4, nkt=2, dv=256, dvp=256, q="RQ", k="RK", v="RV", gate="RG", ob="OB_R", y="Y_R"),
           "M": dict(H=4, nkt=1, dv=256, dvp=257, q="MQ", k="MK", v="MV", gate="MO", ob="OB_M", y="Y_M"),
           "G": dict(H=8, nkt=1, dv=128, dvp=128, q="GQ", k="GK", v="GV", gate="GG", ob="OB_G", y="Y_G")}

    def scan_setup(self, l):
        A = self.A
        sp_ = self.smp
        self.prm = A.alloc("prm", [128, 64], F32)
        prm = self.prm
        self.act(prm[:, 0:8], sp_[:, 0:8], AF.Exp, ["smp"], ["prm_r0"])
        self.ts(DVE, prm[:, 0:8], prm[:, 0:8], -1.0, None, ALU.mult, None, ["prm_r0"], ["prm_r"])
        self.act(prm[:, 8:24], sp_[:, 24:40], AF.Exp, ["smp"], ["prm_g0"])
        self.ts(DVE, prm[:, 8:24], prm[:, 8:24], -1.0, None, ALU.mult, None, ["prm_g0"], ["prm_g"])
        self.retDT = A.alloc("retDT", [128, 8, 128], F32)
        self.retc = A.alloc("retc", [128, 8, 3], F32)
        o, _ = CST["posv"]
        for d_ in range(2):
            sfx = "F" if d_ == 0 else "B"
            for h in range(4):
                i = d_ * 4 + h
                lgc = prm[:, i:i + 1]
                self.act(self.retDT[:, i, :], self.cst("diff" + sfx), AF.Exp, ["cst", "prm_r"], [("retDT0", i)], scale=lgc)
                self.tt(DVE, self.retDT[:, i, :], self.retDT[:, i, :], self.cst("m01" + sfx), ALU.mult, [("retDT0", i), "cst"], [("retDT", i)])
                cols = (o + 0, o + 1) if d_ == 0 else (o + 2, o + 3)
                for j, cc in enumerate(cols + (o + 4,)):
                    self.act(self.retc[:, i, j:j + 1], self.cstt[:, cc:cc + 1], AF.Exp, ["cst", "prm_r"], [("retc", i, j)], scale=lgc)

    def phase_scan(self, l, mx):
        A, S = self.A, self.S
        cfg = self.MIX[mx]
        H, nkt, dv, dvp = cfg["H"], cfg["nkt"], cfg["dv"], cfg["dvp"]
        dk = 128 * nkt
        QW = H * dk
        base = A.off
        qt = [A.alloc("qt", [128, QW], BF) for _ in range(2)]
        kt = [A.alloc("kt", [128, QW], BF) for _ in range(2)]
        vt = [A.alloc("vt", [128, H, dvp], BF) for _ in range(2)]
        ks = A.alloc("ks", [128, QW], BF)
        QT = A.alloc("QT", [128, H * nkt, 128], BF)
        KT = A.alloc("KT", [128, H * nkt, 128], BF)
        Sf = A.alloc("Sf", [128, H * nkt, dvp], F32)
        Sb = A.alloc("Sb", [128, H * nkt, dvp], BF)
        pTa = A.alloc("pTa", [128, H, 128], BF)
        acc = [A.alloc("acc", [128, 1024], F32) for _ in range(2)]
        obt = [A.alloc("obt", [128, 1024], F32) for _ in range(2)]
        gt = [A.alloc("gt", [128, 1024], BF) for _ in range(2)]
        yt = [A.alloc("yt", [128, 1024], BF) for _ in range(2)]
        t1 = A.alloc("pp1", [128, 1024], F32)
        sc = [A.alloc("scl", [128, 96], F32) for _ in range(2)]
        sraw = [A.alloc("sraw", [128, 32], F32) for _ in range(2)]
        pst = A.alloc("pst", [128, 8, 6], F32)
        pmv = A.alloc("pmv", [128, 8, 2], F32)
        prs = A.alloc("prs", [128, 16], F32)
        if mx != "R":
            Lm = A.alloc("Lm", [128, H, 128], F32)
            dtb = A.alloc("dtb", [128, H, 128], F32)
        if mx == "M":
            dn = A.alloc("dn", [128, 12], F32)
            nwb = A.alloc("nwb", [128, 1024], F32)
            self.dma(nwb[:, :], self.mnorm_w[l:l + 1, :].partition_broadcast(128), [], ["nwb"])
            for b in range(2):
                S.op(POOL, lambda e, b=b: e.memset(vt[b][:, :, 256:257], 1.0), [], [("vt", b)])
        if mx == "G":
            dts = A.alloc("dts", [128, 8, 128], F32)
            gm = A.alloc("gm", [128, 14, 128], F32)
            self.dma(gm[:, :, :], self.gmask_d, [], ["gm"])
            M0 = A.alloc("M0", [128, 8, 128], F32)
            MTt = A.alloc("MTt", [128, 8, 128], F32)
            MTm = A.alloc("MTm", [128, 6, 8, 128], F32)
            Um = [A.alloc("Um", [128, 8, 128], F32) for _ in range(2)]
            Vm = A.alloc("Vm", [128, 8, 128], F32)
            Pm = A.alloc("Pm", [128, 8, 128], F32)
            self.gdn_Wb = A.alloc("Wb", [128, 8, 128], BF)
            self.ginv = (gm, M0, MTt, MTm, Um, Vm, Pm)
            r0 = A.alloc("r0", [128, 8, 128], BF)
            vn = A.alloc("vn", [128, 8, 128], BF)
            nwb = A.alloc("nwb", [128, 128], F32)
            self.dma(nwb[:, :], self.gnorm_w[l:l + 1, :].partition_broadcast(128), [], ["nwb"])
        ps = self.ps
        PK = lambda i: ("ps", i)
        identb = self.identb
        for d_ in (1, 0):
            sfx = "F" if d_ == 0 else "B"
            order = [0, 1] + list(range(2, NT)) if d_ == 0 else [1, 0] + list(range(NT - 1, 1, -1))
            S.op(POOL, lambda e: e.memset(Sf[:, :, :], 0.0), [], [("Sf", i) for i in range(H)])
            S.op(POOL, lambda e: e.memset(Sb[:, :, :], 0.0), [], ["Sb"])
            for n, c in enumerate(order):
                b = n % 2
                rows = slice(c * 128, (c + 1) * 128)
                self.dma(qt[b][:, :], self.scr[cfg["q"]][rows, :], [(cfg["q"], c)], [("qt", b)])
                self.dma(kt[b][:, :], self.scr[cfg["k"]][rows, :], [(cfg["k"], c)], [("kt", b)])
                self.dma(vt[b][:, :, 0:dv], self.scr[cfg["v"]][rows, :].rearrange("p (h e) -> p h e", h=H), [(cfg["v"], c)], [("vt", b)])
                if mx == "M":
                    self.dma(sraw[b][:, 0:16], self.scr["MIF"][rows, :], [], [("sraw", b)])
                if mx == "G":
                    self.dma(sraw[b][:, 0:32], self.scr["GBA"][rows, :], [], [("sraw", b)])
                if d_ == 0:
                    self.dma(obt[b][:, :], self.scr[cfg["ob"]][rows, :], [(cfg["ob"], c)], [("obt", b)])
                    self.dma(gt[b][:, :], self.scr[cfg["gate"]][rows, :], [], [("gt", b)])
                for src_, dst_, bank, nm in ((qt[b], QT, 0, "QT"), (kt[b], KT, 1, "KT")):
                    pv = ps[bank][:, :].bitcast(BF)
                    for j in range(H * nkt):
                        self.tr(pv[:, j * 128:(j + 1) * 128], src_[:, j * 128:(j + 1) * 128], identb[:, :],
                                [(nm.lower(), b), "identb"], [PK(bank)])
                    self.cp(ACT if bank == 0 else DVE, dst_[:, :, :].rearrange("p a t -> p (a t)"), pv[:, 0:H * nkt * 128],
                            [PK(bank)], [nm])
                s_ = sc[b]
                sk = ("scl", b)
                HB = 4
                nbk = H // HB
                if mx == "R":
                    a_bc = lambda hs_, w: self.retc[:, d_ * 4 + hs_.start:d_ * 4 + hs_.stop, 0:1].broadcast_to([128, hs_.stop - hs_.start, w])
                    a_reads = [("retc", d_ * 4 + h, 0) for h in range(4)]
                    cd_col = lambda h: self.retc[:, d_ * 4 + h, 2:3]
                    cd_reads = [("retc", d_ * 4 + h, 2) for h in range(4)]
                    s_bc = self.retc[:, d_ * 4:d_ * 4 + 4, 1:2].broadcast_to([128, 4, 256])
                    s_reads = [("retc", d_ * 4 + h, 1) for h in range(4)]
                    DTall = lambda hs_: self.retDT[:, d_ * 4 + hs_.start:d_ * 4 + hs_.stop, :]
                    dt_reads = [("retDT", d_ * 4 + h) for h in range(4)]
                else:
                    nh = H
                    raw = sraw[b]
                    if mx == "M":
                        self.tt(DVE, s_[:, 0:4], raw[:, d_ * 4:d_ * 4 + 4], self.smp[:, 8 + d_ * 4:12 + d_ * 4], ALU.add, [("sraw", b), "smp"], [(sk, "ig")])
                        self.tt(DVE, s_[:, 4:8], raw[:, 8 + d_ * 4:12 + d_ * 4], self.smp[:, 16 + d_ * 4:20 + d_ * 4], ALU.add, [("sraw", b), "smp"], [(sk, "x")])
                        self.act(s_[:, 8:12], s_[:, 4:8], AF.Exp, [(sk, "x")], [(sk, "e")], scale=-1.0)
                        self.act(s_[:, 12:16], s_[:, 8:12], AF.Ln, [(sk, "e")], [(sk, "l")], bias=1.0)
                        self.ts(DVE, s_[:, 16:20], s_[:, 12:16], -1.0, None, ALU.mult, None, [(sk, "l")], [(sk, "g")])
                        self.act(s_[:, 20:24], s_[:, 0:4], AF.Exp, [(sk, "ig")], [(sk, "eig")])
                        gall = s_[:, 16:20]
                    else:
                        self.act(s_[:, 0:8], raw[:, d_ * 8:d_ * 8 + 8], AF.Sigmoid, [("sraw", b)], [(sk, "beta")])
                        self.ts(DVE, s_[:, 88:96], s_[:, 0:8], -1.0, None, ALU.mult, None, [(sk, "beta")], [(sk, "nbeta")])
                        self.tt(DVE, s_[:, 8:16], raw[:, 16 + d_ * 8:24 + d_ * 8], self.smp[:, 40 + d_ * 8:48 + d_ * 8], ALU.add, [("sraw", b), "smp"], [(sk, "x")])
                        self.act(s_[:, 16:24], s_[:, 8:16], AF.Exp, [(sk, "x")], [(sk, "e")])
                        self.act(s_[:, 24:32], s_[:, 16:24], AF.Ln, [(sk, "e")], [(sk, "l")], bias=1.0)
                        self.tt(DVE, s_[:, 32:40], s_[:, 24:32], self.prm[:, 8 + d_ * 8:16 + d_ * 8], ALU.mult, [(sk, "l"), "prm_g"], [(sk, "g")])
                        gall = s_[:, 32:40]
                    self.mm(ps[6][:, 0:nh], self.cst("tri" + sfx), gall, True, True, ["cst", (sk, "g")], [PK(6)])
                    self.mm(ps[6][:, nh:2 * nh], self.cst("ones"), gall, True, True, ["cst", (sk, "g")], [PK(6)])
                    self.cp(ACT, s_[:, 40:40 + 2 * nh], ps[6][:, 0:2 * nh], [PK(6)], [(sk, "B")])
                    Bc, Ba = s_[:, 40:40 + nh], s_[:, 40 + nh:40 + 2 * nh]
                    self.act(s_[:, 56:56 + nh], Bc, AF.Exp, [(sk, "B")], [(sk, "a")])
                    self.act(s_[:, 64:64 + nh], Ba, AF.Exp, [(sk, "B")], [(sk, "cd")])
                    self.tt(DVE, s_[:, 72:72 + nh], Ba, Bc, ALU.subtract, [(sk, "B")], [(sk, "s0")])
                    if mx == "M":
                        self.tt(DVE, s_[:, 72:72 + nh], s_[:, 72:72 + nh], s_[:, 0:4], ALU.add, [(sk, "s0"), (sk, "ig")], [(sk, "s1")])
                    else:
                        self.ts(DVE, s_[:, 80:88], s_[:, 56:64], -1.0, None, ALU.mult, None, [(sk, "a")], [(sk, "na")])
                    self.act(s_[:, 72:72 + nh], s_[:, 72:72 + nh], AF.Exp, [(sk, "s0"), (sk, "s1")], [(sk, "s")])
                    a_bc = lambda hs_, w: s_[:, 56 + hs_.start:56 + hs_.stop].unsqueeze(2).broadcast_to([128, hs_.stop - hs_.start, w])
                    a_reads = [(sk, "a")]
                    cd_col = lambda h: s_[:, 64 + h:65 + h]
                    cd_reads = [(sk, "cd")]
                    s_bc = s_[:, 72:72 + nh].unsqueeze(2).broadcast_to([128, nh, 128])
                    s_reads = [(sk, "s")]
                    g0 = 16 if mx == "M" else 32
                    for bk in range(nbk):
                        hs = slice(bk * HB, (bk + 1) * HB)
                        self.tt(DVE, Lm[:, hs, :], self.cst("s" + sfx).unsqueeze(1).broadcast_to([128, HB, 128]),
                                s_[:, g0 + hs.start:g0 + hs.stop].unsqueeze(2).broadcast_to([128, HB, 128]), ALU.mult, ["cst", (sk, "g")], [("Lm", bk)])
                        for q in range(HB):
                            h = bk * HB + q
                            self.mm(ps[7 - bk][:, q * 128:(q + 1) * 128], Lm[:, h, :], self.cst("tri" + sfx), True, False, [("Lm", bk), "cst"], [PK(7 - bk)])
                            self.mm(ps[7 - bk][:, q * 128:(q + 1) * 128], self.cst("ident"), self.cst("neg" + sfx), False, True, ["cst"], [PK(7 - bk)])
                        self.act(dtb[:, hs, :].rearrange("p a t -> p (a t)"), ps[7 - bk][:, :], AF.Exp, [PK(7 - bk)], [("dtb", bk)])
                        if mx == "M":
                            self.tt(POOL, dtb[:, hs, :], dtb[:, hs, :], s_[:, 20:24].unsqueeze(2).broadcast_to([128, 4, 128]), ALU.mult,
                                    [("dtb", bk), (sk, "eig")], [("dtb", bk)])
                        else:
                            self.tt(POOL, dts[:, hs, :], dtb[:, hs, :], self.cst("s01" + sfx).unsqueeze(1).broadcast_to([128, HB, 128]), ALU.mult,
                                    [("dtb", bk), "cst"], [("dts", h) for h in range(hs.start, hs.stop)])
                    DTall = lambda hs_: dtb[:, hs_, :]
                    dt_reads = [("dtb", bk) for bk in range(nbk)]
                self.tt(POOL, ks[:, :].rearrange("p (h e) -> p h e", h=H), kt[b][:, :].rearrange("p (h e) -> p h e", h=H), s_bc,
                        ALU.mult, [("kt", b)] + s_reads, ["ks"])
                if mx == "G":
                    self._gdn_inverse(KT, s_, sk, dts, d_)
                    Wt = self.gdn_W
                A_ = acc[b]
                ak = ("acc", b)
                ring = self.ring
                for bk in range(nbk):
                    hs = slice(bk * HB, (bk + 1) * HB)
                    bn = ring()
                    for q in range(HB):
                        h = bk * HB + q
                        for kk in range(nkt):
                            self.mm(ps[bn][:, q * 128:(q + 1) * 128], KT[:, h * nkt + kk, :], QT[:, h * nkt + kk, :], kk == 0, kk == nkt - 1, ["KT", "QT"], [PK(bn)])
                    self.tt(DVE, pTa[:, hs, :].rearrange("p a t -> p (a t)"), ps[bn][:, :], DTall(hs).rearrange("p a t -> p (a t)"), ALU.mult,
                            [PK(bn)] + dt_reads, [("pTa", bk)])
                vsrc = lambda h: vt[b][:, h, 0:dv]
                vreads = [("vt", b)]
                if mx == "G":
                    for bk in range(nbk):
                        hs = slice(bk * HB, (bk + 1) * HB)
                        bn = ring()
                        for q in range(HB):
                            h = bk * HB + q
                            self.mm(ps[bn][:, q * 128:(q + 1) * 128], KT[:, h, :], Sb[:, h, :], True, True, ["KT", "Sb"], [PK(bn)])
                        for q in range(HB):
                            h = bk * HB + q
                            self.stt(r0[:, h, :], ps[bn][:, q * 128:(q + 1) * 128], s_[:, 80 + h:81 + h], vt[b][:, h, :], ALU.mult, ALU.add,
                                     [PK(bn), (sk, "na"), ("vt", b)], [("r0", bk)])
                        bn2 = ring()
                        for q in range(HB):
                            h = bk * HB + q
                            self.mm(ps[bn2][:, q * 128:(q + 1) * 128], Wt[:, h, :], r0[:, h, :], True, True, [("Wb", bk), ("r0", bk)], [PK(bn2)])
                        self.tt(DVE, vn[:, hs, :], ps[bn2][:, :].rearrange("p (a t) -> p a t", a=HB),
                                s_[:, hs.start:hs.stop].unsqueeze(2).broadcast_to([128, HB, 128]), ALU.mult, [PK(bn2), (sk, "beta")], [("vn", bk)])
                    vsrc = lambda h: vn[:, h, :]
                    vreads = [("vn", bk) for bk in range(nbk)]
                hpb = 512 // dv
                for g_ in range(H // hpb):
                    hs = slice(g_ * hpb, (g_ + 1) * hpb)
                    by, bz = ring(), ring()
                    for q in range(hpb):
                        h = g_ * hpb + q
                        self.mm(ps[by][:, q * dv:(q + 1) * dv], pTa[:, h, :], vsrc(h), True, True, [("pTa", h // HB)] + vreads, [PK(by)])
                    for q in range(hpb):
                        h = g_ * hpb + q
                        for kk in range(nkt):
                            self.mm(ps[bz][:, q * dv:(q + 1) * dv], QT[:, h * nkt + kk, :], Sb[:, h * nkt + kk, 0:dv], kk == 0, kk == nkt - 1, ["QT", "Sb"], [PK(bz)])
                    o3 = A_[:, hs.start * dv:hs.stop * dv].rearrange("p (a e) -> p a e", a=hpb)
                    self.tt(DVE, o3, ps[bz][:, :].rearrange("p (a e) -> p a e", a=hpb), a_bc(hs, dv), ALU.mult, [PK(bz)] + a_reads, [(ak, g_)])
                    self.tt(DVE, o3, o3, ps[by][:, :].rearrange("p (a e) -> p a e", a=hpb), ALU.add, [(ak, g_), PK(by)], [(ak, g_)])
                nacc = H // hpb
                if mx == "M":
                    bd = ring()
                    for h in range(4):
                        self.mm(ps[bd][:, h:h + 1], pTa[:, h, :], self.onesb[:, 0:1], True, True, [("pTa", 0), "onesb"], [PK(bd)])
                    for h in range(4):
                        self.mm(ps[bd][:, 4 + h:5 + h], QT[:, h, :], Sb[:, h, 256:257], True, True, ["QT", "Sb"], [PK(bd)])
                    self.tt(DVE, dn[:, 0:4], ps[bd][:, 4:8], s_[:, 56:60], ALU.mult, [PK(bd), (sk, "a")], ["dn0"])
                    self.tt(DVE, dn[:, 0:4], dn[:, 0:4], ps[bd][:, 0:4], ALU.add, ["dn0", PK(bd)], ["dn0"])
                    self.act(dn[:, 4:8], dn[:, 0:4], AF.Abs, ["dn0"], ["dn1"])
                    self.ts(DVE, dn[:, 4:8], dn[:, 4:8], 1.0, None, ALU.max, None, ["dn1"], ["dn2"])
                    S.op(DVE, lambda e: e.reciprocal(dn[:, 8:12], dn[:, 4:8]), ["dn2"], ["dn3"])
                    A3_ = A_[:, :].rearrange("p (h e) -> p h e", h=4)
                    self.tt(POOL, A3_, A3_, dn[:, 8:12].unsqueeze(2).broadcast_to([128, 4, 256]), ALU.mult, [(ak, 0), (ak, 1), "dn3"], [(ak, 0), (ak, 1)])
                if mx == "G":
                    for bk in range(nbk):
                        hs = slice(bk * HB, (bk + 1) * HB)
                        bu = ring()
                        for q in range(HB):
                            h = bk * HB + q
                            self.mm(ps[bu][:, q * 128:(q + 1) * 128], ks[:, h * 128:(h + 1) * 128], vn[:, h, :], True, True, ["ks", ("vn", bk)], [PK(bu)])
                        self.tt(POOL, Sf[:, hs, :], Sf[:, hs, :], s_[:, 64 + hs.start:64 + hs.stop].unsqueeze(2).broadcast_to([128, HB, 128]), ALU.mult,
                                [("Sf", bk), (sk, "cd")], [("Sf", bk)])
                        self.tt(DVE, Sf[:, hs, :], Sf[:, hs, :], ps[bu][:, :].rearrange("p (a t) -> p a t", a=HB), ALU.add, [("Sf", bk), PK(bu)], [("Sf", bk)])
                    self.cp(ACT, Sb[:, :, :], Sf[:, :, :], [("Sf", bk) for bk in range(nbk)], ["Sb"])
                else:
                    for h in range(H):
                        bu = ring()
                        for kk in range(nkt):
                            i = h * nkt + kk
                            self.mm(ps[bu][:, kk * 256:kk * 256 + dvp], ks[:, i * 128:(i + 1) * 128], vt[b][:, h, :], True, True, ["ks", ("vt", b)], [PK(bu)])
                        if nkt == 2:
                            self.stt(Sf[:, h * 2:h * 2 + 2, :].rearrange("p a e -> p (a e)"), Sf[:, h * 2:h * 2 + 2, :].rearrange("p a e -> p (a e)"),
                                     cd_col(h), ps[bu][:, :], ALU.mult, ALU.add, [("Sf", h), PK(bu)] + cd_reads, [("Sf", h)])
                        else:
                            self.stt(Sf[:, h, :], Sf[:, h, :], cd_col(h), ps[bu][:, 0:dvp], ALU.mult, ALU.add, [("Sf", h), PK(bu)] + cd_reads, [("Sf", h)])
                    self.cp(ACT, Sb[:, :, :], Sf[:, :, :], [("Sf", h) for h in range(H)], ["Sb"])
                H_keys = nacc
                akeys = [(ak, g_) for g_ in range(nacc)]
                if d_ == 1:
                    self.dma(self.scr[cfg["ob"]][rows, :], A_[:, :], akeys, [(cfg["ob"], c)])
                    continue
                self.tt(POOL, A_[:, :], A_[:, :], obt[b][:, :], ALU.add, akeys + [("obt", b)], akeys)
                A3 = A_[:, :].rearrange("p (h e) -> p h e", h=H)
                Y = yt[b]
                if "dbg_acc" in self.debug:
                    if not hasattr(self, "dbg_acc_d"):
                        self.dbg_acc_d = self.nc.dram_tensor("dbg_acc", [T, 1024], F32, kind="ExternalOutput").ap()
                    self.dma(self.dbg_acc_d[rows, :], A_[:, :], akeys, [("dbgacc", c)])
                if mx == "G":
                    self.tt(POOL, t1[:, :], A_[:, :], A_[:, :], ALU.mult, akeys, ["pp1"])
                    S.op(DVE, lambda e: e.tensor_reduce(prs[:, 0:8], t1[:, :].rearrange("p (h e) -> p h e", h=8), AX.X, ALU.add), ["pp1"], ["prs0"])
                    self.act(prs[:, 8:16], prs[:, 0:8], AF.Ln, ["prs0"], ["prs1"], bias=RMS_EPS, scale=1.0 / 128)
                    self.act(prs[:, 0:8], prs[:, 8:16], AF.Exp, ["prs1"], ["prs2"], scale=-0.5)
                    self.tt(DVE, t1[:, :].rearrange("p (h e) -> p h e", h=8), A3, prs[:, 0:8].unsqueeze(2).broadcast_to([128, 8, 128]), ALU.mult, akeys + ["prs2"], ["pp1"])
                    self.tt(POOL, t1[:, :].rearrange("p (h e) -> p h e", h=8), t1[:, :].rearrange("p (h e) -> p h e", h=8),
                            nwb[:, :].unsqueeze(1).broadcast_to([128, 8, 128]), ALU.mult, ["pp1", "nwb"], ["pp1"])
                    self.tt(DVE, Y[:, :], t1[:, :], gt[b][:, :], ALU.mult, ["pp1", ("gt", b)], [("yt", b)])
                else:
                    for h in range(4):
                        S.op(DVE, lambda e, h=h, A_=A_: e.bn_stats(pst[:, h, :], A_[:, h * 256:(h + 1) * 256]), akeys, [("pst", h)])
                        S.op(DVE, lambda e, h=h: e.bn_aggr(pmv[:, h, :], pst[:, h, :]), [("pst", h)], ["pmv"])
                    self.act(prs[:, 0:4], pmv[:, 0:4, 1:2].rearrange("p h o -> p (h o)"), AF.Ln, ["pmv"], ["prs0"], bias=LN_EPS)
                    self.act(prs[:, 4:8], prs[:, 0:4], AF.Exp, ["prs0"], ["prs1"], scale=-0.5)
                    t3 = t1[:, :].rearrange("p (h e) -> p h e", h=4)
                    self.tt(DVE, t3, A3, pmv[:, 0:4, 0:1].broadcast_to([128, 4, 256]), ALU.subtract, akeys + ["pmv"], ["pp1"])
                    self.tt(POOL, t3, t3, prs[:, 4:8].unsqueeze(2).broadcast_to([128, 4, 256]), ALU.mult, ["pp1", "prs1"], ["pp1"])
                    if "dbg_t1" in self.debug:
                        if not hasattr(self, "dbg_t1_d"):
                            self.dbg_t1_d = self.nc.dram_tensor("dbg_t1", [T, 1024], F32, kind="ExternalOutput").ap()
                            self.dbg_pmv_d = self.nc.dram_tensor("dbg_pmv", [T, 16], F32, kind="ExternalOutput").ap()
                            self.dbg_prs_d = self.nc.dram_tensor("dbg_prs", [T, 16], F32, kind="ExternalOutput").ap()
                        self.dma(self.dbg_t1_d[rows, :], t1[:, :], ["pp1"], [("dbgt1", c)])
                        self.dma(self.dbg_pmv_d[rows, 0:8], pmv[:, 0:4, :].rearrange("p a b -> p (a b)"), ["pmv"], [("dbgpmv", c)])
                        self.dma(self.dbg_prs_d[rows, 0:8], prs[:, 0:8], ["prs1", "prs0"], [("dbgprs", c)])
                    if mx == "M":
                        self.tt(POOL, t1[:, :], t1[:, :], nwb[:, :], ALU.mult, ["pp1", "nwb"], ["pp1"])
                    self.tt(DVE, Y[:, :], t1[:, :], gt[b][:, :], ALU.mult, ["pp1", ("gt", b)], [("yt", b)])
                self.dma(self.scr[cfg["y"]][rows, :], Y[:, :], [("yt", b)], [(cfg["y"], c)])
            S.barrier()
        A.off = base

    def _gdn_inverse(self, KT, s_, sk, dts, d_):
        S, ps = self.S, self.ps
        PK = lambda i: ("ps", i)
        gm, M0, MTt, MTm, Um, Vm, Pm = self.ginv
        fo = 0 if d_ == 0 else 7
        to = 7 if d_ == 0 else 0
        ident = self.cst("ident")
        for hh in range(2):
            bank = 2 + hh
            for q in range(4):
                h = hh * 4 + q
                self.mm(ps[bank][:, q * 128:(q + 1) * 128], KT[:, h, :], KT[:, h, :], True, True, ["KT"], [PK(bank)])
            for q in range(4):
                h = hh * 4 + q
                self.stt(M0[:, h, :], ps[bank][:, q * 128:(q + 1) * 128], s_[:, 88 + h:89 + h], dts[:, h, :], ALU.mult, ALU.mult,
                         [PK(bank), (sk, "nbeta"), ("dts", h)], [("M0", hh)])
            bank = 4 + hh
            for q in range(4):
                h = hh * 4 + q
                self.tr(ps[bank][:, q * 128:(q + 1) * 128], M0[:, h, :], ident, [("M0", hh), "cst"], [PK(bank)])
            self.cp(ACT, MTt[:, hh * 4:(hh + 1) * 4, :].rearrange("p a t -> p (a t)"), ps[bank][:, :], [PK(bank)], [("MTt", hh)])
        for lev in range(1, 7):
            self.tt(POOL, MTm[:, lev - 1, :, :], MTt[:, :, :], gm[:, to + lev:to + lev + 1, :].broadcast_to([128, 8, 128]), ALU.mult,
                    [("MTt", 0), ("MTt", 1), "gm"], [("MTm", lev)])
        U = Um[0]
        self.tt(DVE, U[:, :, :], M0[:, :, :], gm[:, fo:fo + 1, :].broadcast_to([128, 8, 128]), ALU.mult, [("M0", 0), ("M0", 1), "gm"], [("Um", 0, 0), ("Um", 0, 1)])
        self.tt(DVE, U[:, :, :], U[:, :, :], ident.unsqueeze(1).broadcast_to([128, 8, 128]), ALU.add, [("Um", 0, 0), ("Um", 0, 1), "cst"], [("Um", 0, 0), ("Um", 0, 1)])
        cur = 0
        for lev in range(1, 7):
            nxt = 1 - cur
            Uc, Un = Um[cur], Um[nxt]
            for hh in range(2):
                hs = slice(hh * 4, (hh + 1) * 4)
                uk = ("Um", cur, hh)
                for q in range(4):
                    h = hh * 4 + q
                    self.tr(ps[2 + hh][:, q * 128:(q + 1) * 128], Uc[:, h, :], ident, [uk, "cst"], [PK(2 + hh)])
                self.cp(ACT, Vm[:, hs, :].rearrange("p a t -> p (a t)"), ps[2 + hh][:, :], [PK(2 + hh)], [("Vm", hh)])
                for q in range(4):
                    h = hh * 4 + q
                    self.mm(ps[4 + hh][:, q * 128:(q + 1) * 128], MTm[:, lev - 1, h, :], Uc[:, h, :], True, True, [("MTm", lev), uk], [PK(4 + hh)])
                self.cp(DVE, Pm[:, hs, :].rearrange("p a t -> p (a t)"), ps[4 + hh][:, :], [PK(4 + hh)], [("Pm", hh)])
                for q in range(4):
                    h = hh * 4 + q
                    self.mm(ps[6 + hh][:, q * 128:(q + 1) * 128], Vm[:, h, :], Pm[:, h, :], True, True, [("Vm", hh), ("Pm", hh)], [PK(6 + hh)])
                self.tt(DVE, Un[:, hs, :].rearrange("p a t -> p (a t)"), Uc[:, hs, :].rearrange("p a t -> p (a t)"), ps[6 + hh][:, :], ALU.add,
                        [uk, PK(6 + hh)], [("Um", nxt, hh)])
            cur = nxt
        for hh in range(2):
            hs = slice(hh * 4, (hh + 1) * 4)
            self.cp(ACT if hh == 0 else DVE, self.gdn_Wb[:, hs, :], Um[cur][:, hs, :], [("Um", cur, hh)], [("Wb", hh)])
        self.gdn_W = self.gdn_Wb

    def postnorm(self, pbanks, xt, xk, gi, li, ub, st, mv, rs, slot, dst, dstkey):
        uk = ("ub", slot)
        for hh in range(2):
            cs = slice(hh * 512, (hh + 1) * 512)
            self.tt(DVE, ub[:, cs], self.ps[pbanks[hh]][:, :], self.gbc[:, gi, cs], ALU.mult, [("ps", pbanks[hh]), ("gbc", gi, hh)], [(uk, hh)])
            self.stt(ub[:, cs], xt[:, cs], DN_ALPHA, ub[:, cs], ALU.mult, ALU.add, [xk, (uk, hh)], [(uk, hh)])
        S = self.S
        for hh in range(2):
            S.op(DVE, lambda e, hh=hh: e.bn_stats(st[:, hh, :], ub[:, hh * 512:(hh + 1) * 512]), [(uk, hh)], [("pst", slot)])
        S.op(DVE, lambda e: e.bn_aggr(mv[:, :], st[:, :, :].rearrange("p a b -> p (a b)")), [("pst", slot)], [("pmv", slot)])
        self.act(rs[:, 2:3], mv[:, 1:2], AF.Ln, [("pmv", slot)], [("prs2", slot)], bias=LN_EPS)
        self.act(rs[:, 0:1], rs[:, 2:3], AF.Exp, [("prs2", slot)], [("prs0", slot)], scale=-0.5)
        self.ts(DVE, rs[:, 1:2], mv[:, 0:1], rs[:, 0:1], -1.0, ALU.mult, ALU.mult, [("pmv", slot), ("prs0", slot)], [("prs1", slot)])
        self.act(ub[:, :], ub[:, :], AF.Identity, [(uk, 0), (uk, 1), ("prs0", slot), ("prs1", slot)], [(uk, 0), (uk, 1)],
                 bias=rs[:, 1:2], scale=rs[:, 0:1])
        self.tt(POOL, ub[:, :], ub[:, :], self.lnl[:, 0, :], ALU.mult, [(uk, 0), (uk, 1), ("lnl", 0)], [(uk, 0), (uk, 1)])
        self.tt(DVE, ub[:, :], ub[:, :], self.lnl[:, 1, :], ALU.add, [(uk, 0), (uk, 1), ("lnl", 1)], [(uk, 0), (uk, 1)])
        return self.dma(dst, ub[:, :], [(uk, 0), (uk, 1)], [dstkey])

    def phase_merge(self, l):
        A, S, ps = self.A, self.S, self.ps
        base = A.off
        last = (l == DEPTH - 1)
        self.lnl = A.alloc("lnl", [128, 2, 1024], F32)
        for j, src_ in enumerate((self.ln1_g, self.ln1_b)):
            self.dma(self.lnl[:, j, :], src_[l:l + 1, :].partition_broadcast(128), [], [("lnl", j)])
        wbr = A.alloc("wbr", [128, 3, 8, 1024], BF)
        wo = A.alloc("wo", [128, 8, 1024], BF)
        for br in range(3):
            for hh in range(2):
                self.dma(wbr[:, br, :, hh * 512:(hh + 1) * 512],
                         self.w_branch[l, br].rearrange("(kc p) c -> p kc c", p=128)[:, :, hh * 512:(hh + 1) * 512], [], [("wbr", br)], q=POOL)
        for hh in range(2):
            self.dma(wo[:, :, hh * 512:(hh + 1) * 512], self.w_out[l].rearrange("(kc p) c -> p kc c", p=128)[:, :, hh * 512:(hh + 1) * 512], [], ["wo"], q=POOL)
        yin = [[A.alloc("yin", [128, 1024], BF) for _ in range(3)] for _ in range(2)]
        mg = [A.alloc("mg", [128, 3072], BF) for _ in range(2)]
        xt = [A.alloc("xt", [128, 1024], F32) for _ in range(2)]
        yT = [A.alloc("yT", [128, 8, 128], BF) for _ in range(3)]
        mrg = A.alloc("mrg", [128, 1024], F32)
        mt2 = A.alloc("mt2", [128, 512], F32)
        mrb = A.alloc("mrb", [128, 1024], BF)
        mT = A.alloc("mT", [128, 8, 128], BF)
        ub = [A.alloc("ub", [128, 1024], F32) for _ in range(2)]
        st = [A.alloc("st", [128, 2, 6], F32) for _ in range(2)]
        mv = [A.alloc("mv", [128, 2], F32) for _ in range(2)]
        rs = [A.alloc("rs", [128, 4], F32) for _ in range(2)]
        src = self.xz if l == 0 else self.scr["X2"]
        names = ("Y_R", "Y_M", "Y_G")
        n = 0
        for t in range(2 if last else 0, NT):
            b = n % 2
            n += 1
            rows = slice(t * 128, (t + 1) * 128)
            v = 1 if t < 2 else 0
            for br in range(3):
                self.dma(yin[b][br][:, :], self.scr[names[br]][rows, :], [], [("yin", b, br)])
            self.dma(mg[b][:, :], self.scr["MG"][rows, :], [], [("mg", b)])
            self.dma(xt[b][:, :], src[rows, :], [], [("xt", b)])
            for br in range(3):
                bank = br % 2
                pv = ps[bank][:, :].bitcast(BF)
                for c in range(8):
                    self.tr(pv[:, c * 128:(c + 1) * 128], yin[b][br][:, c * 128:(c + 1) * 128], self.identb[:, :],
                            [("yin", b, br), "identb"], [("ps", bank)])
                self.cp(ACT if br != 1 else DVE, yT[br][:, :, :].rearrange("p a t -> p (a t)"), pv[:, :], [("ps", bank)], [("yT", br)])
            for hh in range(2):
                cs = slice(hh * 512, (hh + 1) * 512)
                for br in range(3):
                    for kc in range(8):
                        self.mm(ps[2 + br][:, :], yT[br][:, kc, :], wbr[:, br, kc, cs], kc == 0, kc == 7, [("yT", br), ("wbr", br)], [("ps", 2 + br)])
                self.tt(DVE, mrg[:, cs], ps[2][:, :], mg[b][:, hh * 512:(hh + 1) * 512], ALU.mult, [("ps", 2), ("mg", b)], [("mrg", hh)])
                self.tt(DVE, mt2[:, :], ps[3][:, :], mg[b][:, 1024 + hh * 512:1024 + (hh + 1) * 512], ALU.mult, [("ps", 3), ("mg", b)], ["mt2"])
                self.tt(POOL, mrg[:, cs], mrg[:, cs], mt2[:, :], ALU.add, [("mrg", hh), "mt2"], [("mrg", hh)])
                self.tt(DVE, mt2[:, :], ps[4][:, :], mg[b][:, 2048 + hh * 512:2048 + (hh + 1) * 512], ALU.mult, [("ps", 4), ("mg", b)], ["mt2"])
                self.tt(POOL, mrb[:, cs], mrg[:, cs], mt2[:, :], ALU.add, [("mrg", hh), "mt2"], [("mrb", hh)])
            pv = ps[5][:, :].bitcast(BF)
            for c in range(8):
                self.tr(pv[:, c * 128:(c + 1) * 128], mrb[:, c * 128:(c + 1) * 128], self.identb[:, :], [("mrb", c // 4), "identb"], [("ps", 5)])
            self.cp(ACT, mT[:, :, :].rearrange("p a t -> p (a t)"), pv[:, :], [("ps", 5)], ["mT"])
            for hh in range(2):
                for kc in range(8):
                    self.mm(ps[6 + hh][:, :], mT[:, kc, :], wo[:, kc, hh * 512:(hh + 1) * 512], kc == 0, kc == 7, ["mT", "wo"], [("ps", 6 + hh)])
            self.postnorm((6, 7), xt[b], ("xt", b), 0 + v, 0, ub[b], st[b], mv[b], rs[b], b, self.scr["X1"][rows, :], ("X1", t))
        S.barrier()
        A.off = base

    def phase_mlp(self, l):
        A, S, ps = self.A, self.S, self.ps
        base = A.off
        last = (l == DEPTH - 1)
        self.lnl = A.alloc("lnl", [128, 2, 1024], F32)
        for j, src_ in enumerate((self.ln2_g, self.ln2_b)):
            self.dma(self.lnl[:, j, :], src_[l:l + 1, :].partition_broadcast(128), [], [("lnl", j)])
        w1 = A.alloc("w1", [128, 8, DFF], BF)
        w2 = A.alloc("w2", [128, 32, D], BF)
        w1v = self.w_mlp1[l].rearrange("(kc p) c -> p kc c", p=128)
        w2v = self.w_mlp2[l].rearrange("(fc p) c -> p fc c", p=128)
        for i in range(8):
            self.dma(w1[:, :, i * 512:(i + 1) * 512], w1v[:, :, i * 512:(i + 1) * 512], [], [("w1", i)], q=POOL)
        for i in range(8):
            self.dma(w2[:, i * 4:(i + 1) * 4, :], w2v[:, i * 4:(i + 1) * 4, :], [], [("w2", i)], q=POOL)
        xt = [A.alloc("xt", [128, 1024], F32) for _ in range(3)]
        xn = [A.alloc("xn", [128, 1024], F32) for _ in range(1)] * 2
        h2T = A.alloc("h2T", [128, 8, 256], BF)
        hid = A.alloc("hid", [128, 32, 256], BF)
        rl = [A.alloc("rl", [128, 256], F32) for _ in range(2)]
        ub = [A.alloc("ub", [128, 1024], F32) for _ in range(1)] * 2
        st = [A.alloc("st", [128, 2, 6], F32) for _ in range(4)]
        mv = [A.alloc("mv", [128, 2], F32) for _ in range(4)]
        rs = [A.alloc("rs", [128, 4], F32) for _ in range(4)]
        tiles = list(range(2 if last else 0, NT))
        blocks = [tiles[i:i + 2] for i in range(0, len(tiles), 2)]
        xi = 0
        un = 0
        outs = []
        for blk in blocks:
            nb = len(blk)
            xts = []
            for j, t in enumerate(blk):
                b5 = xi % 3
                xi += 1
                b = 0
                v = 1 if t < 2 else 0
                rows = slice(t * 128, (t + 1) * 128)
                X = xt[b5]
                xk = ("xt", b5)
                xts.append((X, xk))
                self.dma(X[:, :], self.scr["X1"][rows, :], [], [xk])
                self.ln_stats(X, xk, st[b], mv[b], rs[b], ("m", b))
                self.act(xn[b][:, :], X[:, :], AF.Identity, [xk, ("rs0", ("m", b)), ("rs1", ("m", b))], [("xn", b)],
                         bias=rs[b][:, 1:2], scale=rs[b][:, 0:1])
                for c in range(8):
                    self.tr(ps[c // 4][:, (c % 4) * 128:(c % 4 + 1) * 128], xn[b][:, c * 128:(c + 1) * 128], self.cst("ident"),
                            [("xn", b), "cst"], [("ps", c // 4)])
                for c in range(8):
                    o = h2T[:, c, j * 128:(j + 1) * 128]
                    i_ = ps[c // 4][:, (c % 4) * 128:(c % 4 + 1) * 128]
                    sc_ = self.ops[:, 1, c, v:v + 1]
                    sh_ = self.modc[:, 24 + c, v:v + 1]
                    if c // 4 == 0:
                        self.ts(DVE, o, i_, sc_, sh_, ALU.mult, ALU.add, [("ps", 0), "modc", "ops"], [("h2T", j, c)])
                    else:
                        self.act(o, i_, AF.Identity, [("ps", 1), "modc", "ops"], [("h2T", j, c)], bias=sh_, scale=sc_)
            hkeys = [("h2T", j, c) for j in range(nb) for c in range(8)]
            N = nb * 128
            for f in range(32):
                bank = 2 + f % 4
                for kc in range(8):
                    self.mm(ps[bank][:, 0:N], w1[:, kc, f * 128:(f + 1) * 128], h2T[:, kc, 0:N], kc == 0, kc == 7,
                            hkeys + [("w1", f // 4)], [("ps", bank)])
                r_ = rl[f % 2]
                self.act(r_[:, 0:N], ps[bank][:, 0:N], AF.Relu, [("ps", bank)], [("rl", f % 2)])
                self.tt(POOL if f % 2 == 0 else DVE, hid[:, f, 0:N], r_[:, 0:N], r_[:, 0:N], ALU.mult, [("rl", f % 2)], [("hid", f)])
            for j, t in enumerate(blk):
                v = 1 if t < 2 else 0
                rows = slice(t * 128, (t + 1) * 128)
                for f in range(32):
                    for hh in range(2):
                        self.mm(ps[6 + hh][:, :], hid[:, f, j * 128:(j + 1) * 128], w2[:, f, hh * 512:(hh + 1) * 512], f == 0, f == 31,
                                [("hid", f), ("w2", f // 4)], [("ps", 6 + hh)])
                if last:
                    dst = self.out[t * 128 - NCTX:(t + 1) * 128 - NCTX, :]
                else:
                    dst = self.scr["X2"][rows, :]
                u = 0
                un += 1
                X, xk = xts[j]
                tok = self.postnorm((6, 7), X, xk, 2 + v, 2, ub[u], st[2 + u], mv[2 + u], rs[2 + u], ("p", u), dst, ("X2", t))
                outs.append(tok)
        S.barrier()
        A.off = base
        return outs

    def build(self):
        self.setup()
        for l in range(self.layers):
            self.phase_ada(l)
            if self.upto == "ada":
                break
            A = self.A
            A.off = self.persist_end
            hT = A.alloc("hT", [128, 8, T], BF)
            self.hT = hT
            src = self.xz if l == 0 else self.scr["X2"]
            if not getattr(self, "skip_inproj", False):
                self.phase_ln1(l, src, hT)
            if "dbg_hT" in self.debug:
                self.S.barrier()
                hb = A.alloc("hdbg", [128, 1024], F32)
                self.cp(DVE, hb[:, :].rearrange("p (c t) -> p c t", c=8), hT[:, :, 0:128], [], ["hdbg"])
                self.dbg("dbg_hT", hb[:, :], [128, 1024], ["hdbg"])
            if self.upto == "ln1":
                break
            if not getattr(self, "skip_inproj", False):
                self.phase_inproj(l, hT)
            self.S.barrier()
            if self.upto == "inproj":
                break
            A.off = self.persist_end
            self.scan_setup(l)
            for mx in getattr(self, "mixers", "RMG"):
                self.phase_scan(l, mx)
            if self.upto == "scan":
                break
            self.S.barrier()
            A.off = self.persist_end
            if not getattr(self, "skip_merge", False):
                self.phase_merge(l)
            if self.upto == "merge":
                break
            self.phase_mlp(l)
        self.S.barrier()
        self.S.emit()


def prep_inputs(inputs, b):
    f = lambda a: np.ascontiguousarray(np.asarray(a, np.float32))
    m = {}
    m["xz"] = f(np.concatenate([inputs["ctx"][b], inputs["x"][b]], 0))
    cc = np.stack([np.asarray(inputs["c"][b]), np.asarray(inputs["c_ctx"])], -1)
    m["cc"] = f(cc.reshape(8, 128, 2).transpose(1, 0, 2))
    m["w_ada"] = f(inputs["w_ada"])
    m["b_ada"] = f(np.asarray(inputs["b_ada"]).reshape(DEPTH, 48, 128).transpose(0, 2, 1))
    m["w_in"] = f(inputs["w_in"])
    m["conv_w"] = f(np.asarray(inputs["conv_w"]).reshape(DEPTH, 5, 24, 128).transpose(0, 3, 2, 1))
    m["smallp"] = f(np.concatenate([np.asarray(inputs[k]).reshape(DEPTH, -1) for k in
                                    ("ret_decay", "mlstm_i_bias", "mlstm_f_bias", "gdn_a_log", "gdn_dt_bias")], 1)[:, :56])
    m["smallp"] = f(np.pad(m["smallp"], ((0, 0), (0, 8))))
    for k in ("mlstm_norm_w", "gdn_norm_w", "w_branch", "w_out", "ln1_g", "ln1_b", "ln2_g", "ln2_b", "w_mlp1", "w_mlp2"):
        m[k] = f(inputs[k])
    m["consts"] = f(CONSTS)
    rq, rk = _rope_tables()
    m["ropeq"], m["ropek"] = f(rq), f(rk)
    m["gmask"] = f(_gmasks())
    return m


def kernel(**inputs):
    nc = bass.Bass("TRN2", target_bir_lowering=False)
    Builder(nc).build()
    in_maps = [prep_inputs(inputs, b) for b in range(8)]
    res = run_bass_kernel_spmd(nc, in_maps, core_ids=list(range(8)))
    return np.stack([np.asarray(r["out"], np.float32) for r in res.results], 0)
```

```python
import contextlib
import math
import numpy as np
import ml_dtypes
import concourse.bass as bass
import concourse.mybir as mybir
from concourse.bass_utils import run_bass_kernel_spmd

F32 = mybir.dt.float32
BF = mybir.dt.bfloat16
AF = mybir.ActivationFunctionType
ALU = mybir.AluOpType
AX = mybir.AxisListType

PE, ACT, DVE, POOL, SP = "pe", "act", "dve", "pool", "sp"
COMPUTE = (PE, ACT, DVE, POOL)
EPOCH = 12000
NDMASEM = 12

NCTX = 256
NLAT = 4096
T = NCTX + NLAT
NT = T // 128
D = 1024
DEPTH = 2
IN_DIM = 14384
DFF = 4096
LN_EPS = 1e-5
RMS_EPS = 1e-6
DN_ALPHA = (2 * DEPTH) ** 0.25
NEG = -30000.0


class Sched:
    def __init__(self, nc):
        self.nc = nc
        self.q = {e: [] for e in (PE, ACT, DVE, POOL, SP)}
        self.cnt = {e: 0 for e in COMPUTE}
        self.dcnt = {POOL: 0, SP: 0}
        self.lastw = {}
        self.readers = {}
        self.waited = {}
        self.waited_dma = {e: set() for e in self.q}

    def _deps(self, reads, writes):
        deps = []
        for r in reads:
            t = self.lastw.get(r)
            if t is not None:
                deps.append(t)
        for w in writes:
            t = self.lastw.get(w)
            if t is not None:
                deps.append(t)
            deps.extend(self.readers.get(w, ()))
        return deps

    def _emit_waits(self, eng, deps):
        best = {}
        dmas = []
        for t in deps:
            if t[0] == "dma":
                if t not in self.waited_dma[eng]:
                    self.waited_dma[eng].add(t)
                    dmas.append(t)
            else:
                p, n = t
                if p == eng and (eng == PE or n <= self.cnt[eng] - 3):
                    continue
                if n > best.get(p, 0):
                    best[p] = n
        for p, n in best.items():
            if self.waited.get((eng, p), 0) >= n:
                continue
            self.waited[(eng, p)] = n
            self.q[eng].append(("wait", p, n))
        for t in dmas:
            self.q[eng].append(("waitdma", t[1], t[2]))

    def _record(self, tok, reads, writes):
        for r in reads:
            lst = self.readers.setdefault(r, [])
            if tok[0] == "dma":
                lst[:] = [x for x in lst if not (x[0] == "dma" and x[1] == tok[1] and x[2] <= tok[2] - NDMASEM)]
            else:
                lst[:] = [x for x in lst if x[0] != tok[0]]
            lst.append(tok)
        for w in writes:
            self.lastw[w] = tok
            self.readers[w] = []

    def op(self, eng, fn, reads=(), writes=()):
        self._emit_waits(eng, self._deps(reads, writes))
        self.cnt[eng] += 1
        tok = (eng, self.cnt[eng])
        self.q[eng].append(("op", fn, self.cnt[eng]))
        self._record(tok, reads, writes)
        return tok

    def dma(self, eng, out, in_, reads=(), writes=()):
        deps = self._deps(reads, writes)
        k = self.dcnt[eng]
        self.dcnt[eng] += 1
        if k >= NDMASEM:
            deps.append(("dma", eng, k - NDMASEM))
        self._emit_waits(eng, deps)
        tok = ("dma", eng, k)
        self.q[eng].append(("dma", out, in_, k))
        self._record(tok, reads, writes)
        return tok

    def barrier(self):
        for e in self.q:
            deps = [(p, self.cnt[p]) for p in COMPUTE if self.cnt[p] > 0 and p != e]
            for q_ in self.dcnt:
                lo = max(0, self.dcnt[q_] - NDMASEM)
                deps += [("dma", q_, k) for k in range(lo, self.dcnt[q_])]
            self._emit_waits(e, deps)
        self.lastw = {}
        self.readers = {}

    def emit(self):
        nc = self.nc
        with contextlib.ExitStack() as st:
            sems = {}
            for e in COMPUTE:
                n = max(1, (self.cnt[e] + EPOCH - 1) // EPOCH)
                sems[e] = [st.enter_context(nc.semaphore(f"c_{e}_{i}")) for i in range(n)]
            dsems = {e: [st.enter_context(nc.semaphore(f"d_{e}_{i}")) for i in range(NDMASEM)] for e in self.dcnt}
            block = st.enter_context(nc.Block())

            def run(name):
                def body(eng):
                    for item in self.q[name]:
                        k = item[0]
                        if k == "op":
                            _, fn, n = item
                            fn(eng).then_inc(sems[name][(n - 1) // EPOCH], 1)
                        elif k == "wait":
                            _, p, n = item
                            ep = (n - 1) // EPOCH
                            eng.wait_ge(sems[p][ep], n - ep * EPOCH)
                        elif k == "waitdma":
                            _, q_, kk = item
                            eng.wait_ge(dsems[q_][kk % NDMASEM], 16 * (kk // NDMASEM + 1))
                        else:
                            _, out, in_, kk = item
                            eng.dma_start(out=out, in_=in_).then_inc(dsems[name][kk % NDMASEM], 16)
                return body

            block.tensor(run(PE))
            block.scalar(run(ACT))
            block.vector(run(DVE))
            block.gpsimd(run(POOL))
            block.sync(run(SP))


class Arena:
    def __init__(self, nc, limit):
        self.nc, self.off, self.limit, self.n = nc, 16640, 16640 + limit, 0

    def alloc(self, name, shape, dtype):
        nb = int(np.prod(shape[1:])) * (2 if dtype == BF else 4)
        nb = (nb + 31) // 32 * 32
        assert self.off + nb <= self.limit, (name, self.off, nb, self.limit)
        self.n += 1
        t = self.nc.alloc_sbuf_tensor_at(f"{name}_{self.n}", list(shape), dtype, offset=self.off)
        self.off += nb
        return t


CST = {}


def _build_consts():
    cols = []

    def add(name, arr):
        arr = np.asarray(arr, np.float32)
        if arr.ndim == 1:
            arr = arr[:, None]
        CST[name] = (sum(a.shape[1] for a in cols), arr.shape[1])
        cols.append(arr)

    p = np.arange(128)
    t_, i_ = p[:, None], p[None, :]
    add("ident", np.eye(128))
    add("ones", np.ones((128, 128)))
    add("triF", t_ <= i_)
    add("triB", t_ >= i_)
    add("sF", t_ > i_)
    add("sB", t_ < i_)
    add("negF", np.where(i_ < t_, NEG, 0.0))
    add("negB", np.where(i_ > t_, NEG, 0.0))
    add("m01F", i_ >= t_)
    add("m01B", i_ <= t_)
    add("s01F", i_ > t_)
    add("s01B", i_ < t_)
    add("diffF", np.maximum(i_ - t_, 0))
    add("diffB", np.maximum(t_ - i_, 0))
    add("posv", np.stack([p + 1, 127 - p, 128 - p, p, np.full(128, 128)], 1))
    return np.concatenate(cols, 1)


def _gmasks():
    p = np.arange(128)
    j, i = p[:, None], p[None, :]
    ms = []
    for l in range(7):
        B = 2 ** (l + 1)
        ms.append((((j // B) == (i // B)) & ((j % B) < B // 2) & ((i % B) >= B // 2)).astype(np.float32))
    return np.stack(ms + [m.T for m in ms], 1)


CONSTS = _build_consts()
NCST = CONSTS.shape[1]


def _rope_tables():
    n_freq = 64
    freqs = (10000.0 ** (-np.arange(n_freq, dtype=np.float32) / n_freq)).astype(np.float32)
    row = np.repeat(np.arange(64, dtype=np.float32), 64)
    col = np.tile(np.arange(64, dtype=np.float32), 64)
    lat = np.stack([row[:, None] * freqs, col[:, None] * freqs], 1).astype(np.float32)
    ang = np.concatenate([np.zeros((NCTX, 2, n_freq), np.float32), lat], 0)
    cos, sin = np.cos(ang), np.sin(ang)
    tq = np.stack([np.stack([cos, cos], 1), np.stack([sin, sin], 1)], 1).reshape(T, 512)
    return tq.astype(np.float32), (tq / 16.0).astype(np.float32)


def _groups():
    g = []
    c = 0
    for name, w in (("RQ", 1024), ("RK", 1024), ("RV", 1024), ("RG", 1024), ("MQ", 512), ("MK", 512),
                    ("MV", 1024), ("MO", 1024), ("MIF", 16), ("GQKV", 3072), ("GG", 1024), ("GBA", 32),
                    ("MG", 3072)):
        g.append((name, c, w))
        c += w
    assert c == IN_DIM
    return g


GROUPS = _groups()


class Builder:
    def __init__(self, nc, debug=(), layers=DEPTH, upto="all", ext_in=()):
        self.nc = nc
        self.debug = set(debug)
        self.ext_in = set(ext_in)
        self.S = Sched(nc)
        self.layers = layers
        self.upto = upto
        self.A = Arena(nc, 206 * 1024)
        self.uid = 0
        self._dram()
        self._psum()

    def _din(self, name, shape, dt=F32):
        return self.nc.dram_tensor(name, list(shape), dt, kind="ExternalInput").ap()

    def _dscr(self, name, shape, dt):
        kind = "ExternalOutput" if name in self.debug else ("ExternalInput" if name in self.ext_in else "Internal")
        return self.nc.dram_tensor(name, list(shape), dt, kind=kind).ap()

    def _dram(self):
        i = self._din
        self.xz = i("xz", [T, D])
        self.cc = i("cc", [128, 8, 2])
        self.w_ada = i("w_ada", [DEPTH, D, 6 * D])
        self.b_ada = i("b_ada", [DEPTH, 128, 48])
        self.w_in = i("w_in", [DEPTH, D, IN_DIM])
        self.conv_w = i("conv_w", [DEPTH, 128, 24, 5])
        self.smallp = i("smallp", [DEPTH, 64])
        self.mnorm_w = i("mlstm_norm_w", [DEPTH, D])
        self.gnorm_w = i("gdn_norm_w", [DEPTH, 128])
        self.w_branch = i("w_branch", [DEPTH, 3, D, D])
        self.w_out = i("w_out", [DEPTH, D, D])
        self.ln1_g = i("ln1_g", [DEPTH, D]); self.ln1_b = i("ln1_b", [DEPTH, D])
        self.ln2_g = i("ln2_g", [DEPTH, D]); self.ln2_b = i("ln2_b", [DEPTH, D])
        self.w_mlp1 = i("w_mlp1", [DEPTH, D, DFF]); self.w_mlp2 = i("w_mlp2", [DEPTH, DFF, D])
        self.cst_d = i("consts", [128, NCST])
        self.ropeq_d = i("ropeq", [T, 512]); self.ropek_d = i("ropek", [T, 512])
        self.gmask_d = i("gmask", [128, 14, 128])
        self.out = self.nc.dram_tensor("out", [NLAT, D], F32, kind="ExternalOutput").ap()
        s = self._dscr
        self.scr = {}
        for name, w, dt in (("RQ", 1024, BF), ("RK", 1024, BF), ("RV", 1024, BF), ("RG", 1024, BF),
                            ("MQ", 512, BF), ("MK", 512, BF), ("MV", 1024, BF), ("MO", 1024, BF),
                            ("MIF", 16, F32), ("GQ", 1024, BF), ("GK", 1024, BF), ("GV", 1024, BF),
                            ("GG", 1024, BF), ("GBA", 32, F32), ("MG", 3072, BF),
                            ("OB_R", 1024, F32), ("OB_M", 1024, F32), ("OB_G", 1024, F32),
                            ("Y_R", 1024, BF), ("Y_M", 1024, BF), ("Y_G", 1024, BF),
                            ("X1", 1024, F32), ("X2", 1024, F32)):
            self.scr[name] = s(name, [T, w], dt)

    def _psum(self):
        self.ps = [self.nc.alloc_psum_tensor(f"ps{i}", [128, 512], F32) for i in range(8)]
        self._ring = 0

    def ring(self):
        self._ring = (self._ring + 1) % 4
        return 4 + self._ring

    def cst(self, name, rows=128):
        o, w = CST[name]
        return self.cstt[0:rows, o:o + w]

    def hk(self, t):
        return [("hT", t, c) for c in range(8)]

    def key(self, base):
        self.uid += 1
        return (base, self.uid)

    def mm(self, out, lhsT, rhs, start, stop, reads, writes):
        self.S.op(PE, lambda e: e.matmul(out, lhsT, rhs, start=start, stop=stop), reads, writes)

    def tr(self, out, in_, ident, reads, writes):
        self.S.op(PE, lambda e: e.transpose(out, in_, ident), reads, writes)

    def act(self, out, in_, func, reads, writes, bias=0.0, scale=1.0, eng=ACT):
        self.S.op(ACT, lambda e: e.activation(out, in_, func, bias=bias, scale=scale), reads, writes)

    def tt(self, eng, out, a, b, op, reads, writes):
        self.S.op(eng, lambda e: e.tensor_tensor(out, a, b, op), reads, writes)

    def ts(self, eng, out, a, s1, s2, op0, op1, reads, writes):
        if s2 is None:
            self.S.op(eng, lambda e: e.tensor_scalar(out, a, s1, None, op0), reads, writes)
        else:
            self.S.op(eng, lambda e: e.tensor_scalar(out, a, s1, s2, op0, op1), reads, writes)

    def stt(self, out, a, s, b, op0, op1, reads, writes):
        self.S.op(DVE, lambda e: e.scalar_tensor_tensor(out, a, s, b, op0, op1), reads, writes)

    def cp(self, eng, out, in_, reads, writes):
        if eng == ACT:
            self.S.op(ACT, lambda e: e.copy(out, in_), reads, writes)
        else:
            self.S.op(eng, lambda e: e.tensor_copy(out, in_), reads, writes)

    def dma(self, out, in_, reads, writes, q=SP):
        return self.S.dma(q, out, in_, reads, writes)

    def dbg(self, name, ap, shape, reads):
        if name in self.debug:
            d = self.nc.dram_tensor(name, list(shape), F32, kind="ExternalOutput").ap()
            self.dma(d, ap, reads, [("dbgout", name)])

    def setup(self):
        A = self.A
        self.cstt = A.alloc("cst", [128, NCST], F32)
        self.dma(self.cstt[:, :], self.cst_d, [], ["cst"])
        self.identb = A.alloc("identb", [128, 128], BF)
        self.cp(DVE, self.identb[:, :], self.cst("ident"), ["cst"], ["identb"])
        self.onesb = A.alloc("onesb", [128, 128], BF)
        self.cp(DVE, self.onesb[:, :], self.cst("ones"), ["cst"], ["onesb"])
        self.cct = A.alloc("cct", [128, 8, 2], F32)
        self.dma(self.cct[:, :, :], self.cc, [], ["cct"])
        self.scs = A.alloc("scs", [128, 8, 2], F32)
        self.act(self.scs[:, :, :], self.cct[:, :, :], AF.Silu, ["cct"], ["scs"])
        self.modc = A.alloc("modc", [128, 48, 2], F32)
        self.ops = A.alloc("ops", [128, 2, 8, 2], F32)
        self.gbc = A.alloc("gbc", [128, 4, 1024], F32)
        self.smp = A.alloc("smp", [128, 64], F32)
        self.persist_end = A.off

    def phase_ada(self, l):
        A, S = self.A, self.S
        A.off = self.persist_end
        badat = A.alloc("badat", [128, 48], F32)
        self.dma(badat[:, :], self.b_ada[l], [], ["badat"])
        self.dma(self.smp[:, :], self.smallp[l:l + 1, :].partition_broadcast(128), [], ["smp"])
        wsl = [A.alloc("wada", [128, 8, 768], F32) for _ in range(2)]
        wv = self.w_ada[l].rearrange("(kc p) c -> p kc c", p=128)
        psA = self.ps[0]
        psAk = ("ps", 0)
        for s in range(8):
            wt = wsl[s % 2]
            wk = ("wada", s % 2)
            self.dma(wt[:, :, :], wv[:, :, s * 768:(s + 1) * 768], [], [wk])
            for jj in range(6):
                j = s * 6 + jj
                for kc in range(8):
                    self.mm(psA[:, 2 * j:2 * j + 2], wt[:, kc, jj * 128:(jj + 1) * 128], self.scs[:, kc, :],
                            kc == 0, kc == 7, [wk, "scs"], [psAk])
        self.tt(DVE, self.modc[:, :, :], psA[:, 0:96].rearrange("p (j v) -> p j v", v=2),
                badat[:, :].unsqueeze(2).broadcast_to([128, 48, 2]), ALU.add, [psAk, "badat"], ["modc"])
        for sub in range(2):
            j0 = 8 + 24 * sub
            self.ts(DVE, self.ops[:, sub, :, :], self.modc[:, j0:j0 + 8, :], 1.0, None, ALU.add, None, ["modc"], ["ops"])
        tmp = [A.alloc("gtmp", [128, 128], F32) for _ in range(2)]
        n = 0
        for sub in range(2):
            for v in range(2):
                gi = sub * 2 + v
                pb = 1 + (gi % 2) * 2
                pst = self.ps[pb: pb + 2]
                for c in range(8):
                    tk = ("gtmp", n % 2)
                    col = self.modc[:, 16 + 24 * sub + c, v:v + 1]
                    self.ts(DVE, tmp[n % 2][:, :], self.cst("ones"), col, None, ALU.mult, None, ["cst", "modc"], [tk])
                    self.mm(pst[c // 4][:, (c % 4) * 128:(c % 4 + 1) * 128], tmp[n % 2][:, :], self.cst("ident"),
                            True, True, [tk, "cst"], [("ps", pb + c // 4)])
                    n += 1
                for hh in range(2):
                    self.cp(ACT, self.gbc[:, gi, hh * 512:(hh + 1) * 512], pst[hh][:, :], [("ps", pb + hh)], [("gbc", gi, hh)])
        S.barrier()
        self.dbg("dbg_modc", self.modc[:, :, :].rearrange("p j v -> p (j v)"), [128, 96], [])
        self.dbg("dbg_gbc", self.gbc[:, :, :].rearrange("p g d -> p (g d)"), [128, 4096], [])

    def ln_stats(self, xt, xk, st, mv, rs, uk):
        S = self.S
        for hh in range(2):
            S.op(DVE, lambda e, hh=hh: e.bn_stats(st[:, hh, :], xt[:, hh * 512:(hh + 1) * 512]), [xk], [("st", uk)])
        S.op(DVE, lambda e: e.bn_aggr(mv[:, :], st[:, :, :].rearrange("p a b -> p (a b)")), [("st", uk)], [("mv", uk)])
        self.act(rs[:, 2:3], mv[:, 1:2], AF.Ln, [("mv", uk)], [("rs2", uk)], bias=LN_EPS)
        self.act(rs[:, 0:1], rs[:, 2:3], AF.Exp, [("rs2", uk)], [("rs0", uk)], scale=-0.5)
        self.ts(DVE, rs[:, 1:2], mv[:, 0:1], rs[:, 0:1], -1.0, ALU.mult, ALU.mult, [("mv", uk), ("rs0", uk)], [("rs1", uk)])

    def phase_ln1(self, l, src, hT):
        A, S = self.A, self.S
        xt = [A.alloc("xt", [128, 1024], F32) for _ in range(2)]
        xn = [A.alloc("xn", [128, 1024], F32) for _ in range(2)]
        st = [A.alloc("st", [128, 2, 6], F32) for _ in range(2)]
        mv = [A.alloc("mv", [128, 2], F32) for _ in range(2)]
        rs = [A.alloc("rs", [128, 4], F32) for _ in range(2)]
        import os
        STEPS = int(os.environ.get("LN1_STEPS", "9"))
        evm = os.environ.get("EVM", "split")
        EV = (lambda c: True) if evm == "dve" else ((lambda c: False) if evm == "act" else (lambda c: c // 4 == 0))
        for t in range(int(os.environ.get("LN1_NT", NT))):
            b = t % 2
            v = 1 if t < 2 else 0
            self.dma(xt[b][:, :], src[t * 128:(t + 1) * 128, :], [], [("xt", b)])
            if STEPS < 2:
                continue
            self.ln_stats(xt[b], ("xt", b), st[b], mv[b], rs[b], b)
            if STEPS < 4:
                continue
            self.act(xn[b][:, :], xt[b][:, :], AF.Identity, [("xt", b), ("rs0", b), ("rs1", b)], [("xn", b)],
                     bias=rs[b][:, 1:2], scale=rs[b][:, 0:1])
            if STEPS < 5:
                continue
            for c in range(8):
                pt = self.ps[(t % 2) * 2 + c // 4]
                pk = ("ps", (t % 2) * 2 + c // 4)
                self.tr(pt[:, (c % 4) * 128:(c % 4 + 1) * 128], xn[b][:, c * 128:(c + 1) * 128], self.cst("ident"),
                        [("xn", b), "cst"], [pk])
            if STEPS < 6:
                continue
            for c in range(8):
                pt = self.ps[(t % 2) * 2 + c // 4]
                pk = ("ps", (t % 2) * 2 + c // 4)
                o = hT[:, c, t * 128:(t + 1) * 128]
                i_ = pt[:, (c % 4) * 128:(c % 4 + 1) * 128]
                sc_ = self.ops[:, 0, c, v:v + 1]
                sh_ = self.modc[:, c, v:v + 1]
                lock = ["evlock"] if os.environ.get("EVLOCK") else []
                if EV(c):
                    self.ts(DVE, o, i_, sc_, sh_, ALU.mult, ALU.add, [pk, "modc", "ops"], [("hT", t, c)] + lock)
                else:
                    self.act(o, i_, AF.Identity, [pk, "modc", "ops"], [("hT", t, c)] + lock, bias=sh_, scale=sc_)

    def phase_inproj(self, l, hT):
        A, S = self.A, self.S
        wg = [A.alloc("wg", [128, 8, 512], BF) for _ in range(2)]
        stg = [A.alloc("stg", [128, 512], BF) for _ in range(4)]
        stf = [A.alloc("stf", [128, 32], F32) for _ in range(2)]
        rp = [A.alloc("rp", [128, 512], F32) for _ in range(2)]
        t12 = [A.alloc("t12", [128, 2, 256], F32) for _ in range(2)]
        xc = A.alloc("xc", [128, 4, T + 8], BF)
        cwt = A.alloc("cwt", [128, 24, 5], F32)
        dg = A.alloc("dg", [128, 20, 128], BF)
        sl = [A.alloc("sl", [128, 512], F32) for _ in range(2)]
        sq = [A.alloc("sq", [128, 512], F32) for _ in range(2)]
        ssn = [A.alloc("ssn", [128, 12], F32) for _ in range(2)]
        self.dma(cwt[:, :, :], self.conv_w[l], [], ["cwt"])
        S.op(POOL, lambda e: e.memset(xc[:, :, :], 0.0), [], [("xc", i) for i in range(4)])
        wv = self.w_in[l].rearrange("(kc p) c -> p kc c", p=128)
        gi = 0
        si = 0
        for name, c0, width in GROUPS:
            if getattr(self, "only_groups", None) and name not in self.only_groups:
                continue
            nsub = max(1, width // 512)
            w_ = min(width, 512)
            for sub in range(nsub):
                cs = c0 + sub * 512
                wb = wg[gi % 2]
                wk = ("wg", gi % 2)
                gi += 1
                self.dma(wb[:, :, 0:w_], wv[:, :, cs:cs + w_], [], [wk], q=POOL)
                if name == "GQKV":
                    self._gdn_group(l, sub, wb, wk, hT, xc, cwt, dg, sl, sq, ssn, stg)
                    continue
                for t in range(NT):
                    pt = self.ps[4 + t % 4]
                    pk = ("ps", 4 + t % 4)
                    for kc in range(8):
                        self.mm(pt[:, 0:w_], hT[:, kc, t * 128:(t + 1) * 128], wb[:, kc, 0:w_], kc == 0, kc == 7,
                                self.hk(t) + [wk], [pk])
                    rows = slice(t * 128, (t + 1) * 128)
                    if name in ("MIF", "GBA"):
                        sb = stf[si % 2]; sk = ("stf", si % 2); si += 1
                        self.cp(DVE, sb[:, 0:w_], pt[:, 0:w_], [pk], [sk])
                        self.dma(self.scr[name][rows, :], sb[:, 0:w_], [sk], [(name, t)])
                        continue
                    sb = stg[si % 4]; sk = ("stg", si % 4); si += 1
                    dst = self.scr[name][rows, sub * 512:(sub + 1) * 512]
                    if name in ("RQ", "RK"):
                        b = t % 2
                        tab = self.ropeq_d if name == "RQ" else self.ropek_d
                        self.dma(rp[b][:, :], tab[rows, :], [], [("rp", b)])
                        psv = pt[:, :].rearrange("p (g ab f) -> p g ab f", g=4, ab=2)
                        cosv = rp[b][:, 0:256].rearrange("p (g f) -> p g f", g=4)
                        sinv = rp[b][:, 256:512].rearrange("p (g f) -> p g f", g=4)
                        sbv = sb[:, :].rearrange("p (g ab f) -> p g ab f", g=4, ab=2)
                        t1 = t12[b][:, 0, :].rearrange("p (g f) -> p g f", g=4)
                        t2 = t12[b][:, 1, :].rearrange("p (g f) -> p g f", g=4)
                        tk = ("t12", b)
                        a_, b_ = psv[:, :, 0, :], psv[:, :, 1, :]
                        self.tt(DVE, t1, a_, cosv, ALU.mult, [pk, ("rp", b)], [(tk, 0)])
                        self.tt(DVE, t2, b_, sinv, ALU.mult, [pk, ("rp", b)], [(tk, 1)])
                        self.tt(POOL, sbv[:, :, 0, :], t1, t2, ALU.subtract, [(tk, 0), (tk, 1)], [sk])
                        self.tt(DVE, t1, b_, cosv, ALU.mult, [pk, ("rp", b)], [(tk, 0)])
                        self.tt(DVE, t2, a_, sinv, ALU.mult, [pk, ("rp", b)], [(tk, 1)])
                        self.tt(POOL, sbv[:, :, 1, :], t1, t2, ALU.add, [(tk, 0), (tk, 1)], [sk])
                        self.dma(dst, sb[:, :], [sk], [(name, t, sub)])
                        continue
                    if name in ("RG", "GG"):
                        self.act(sb[:, :], pt[:, :], AF.Silu, [pk], [sk])
                    elif name in ("MO", "MG"):
                        self.act(sb[:, :], pt[:, :], AF.Sigmoid, [pk], [sk])
                    elif name == "MQ":
                        self.act(sb[:, :], pt[:, :], AF.Copy, [pk], [sk], scale=128 ** -0.5)
                    elif t % 2 == 0:
                        self.cp(ACT, sb[:, :], pt[:, :], [pk], [sk])
                    else:
                        self.cp(DVE, sb[:, :], pt[:, :], [pk], [sk])
                    self.dma(dst, sb[:, :], [sk], [(name, t, sub)])

    def _gdn_group(self, l, sub, wb, wk, hT, xc, cwt, dg, sl, sq, ssn, stg):
        S = self.S
        for ct in range(4):
            for tap in range(5):
                self.ts(POOL, dg[:, ct * 5 + tap, :], self.cst("ident"), cwt[:, sub * 4 + ct, tap:tap + 1], None,
                        ALU.mult, None, ["cst", "cwt"], [("dg", ct)])
        blocks = [(0, 256)] + [(256 + 512 * i, 512) for i in range(8)]
        n = 0
        for (t0, tw) in blocks:
            col0 = 2 + t0 if t0 < NCTX else 6 + t0
            for ct in range(4):
                pt = self.ps[n % 4]
                pk = ("ps", n % 4)
                for kc in range(8):
                    self.mm(pt[:, 0:tw], wb[:, kc, ct * 128:(ct + 1) * 128], hT[:, kc, t0:t0 + tw], kc == 0, kc == 7,
                            [wk] + sum([self.hk(t0 // 128 + i) for i in range(tw // 128)], []), [pk])
                self.cp(ACT if n % 2 == 0 else DVE, xc[:, ct, col0:col0 + tw], pt[:, 0:tw], [pk], [("xc", ct)])
                n += 1
        which = "GQ" if sub < 2 else ("GK" if sub < 4 else "GV")
        for t in range(NT):
            base = (2 + t * 128 if t < 2 else 6 + t * 128) - 2
            pt = self.ps[4 + t % 4]
            pk = ("ps", 4 + t % 4)
            for ct in range(4):
                for tap in range(5):
                    self.mm(pt[:, ct * 128:(ct + 1) * 128], xc[:, ct, base + tap:base + tap + 128], dg[:, ct * 5 + tap, :],
                            tap == 0, tap == 4, [("xc", ct), ("dg", ct)], [pk])
            b = t % 2
            sb = stg[t % 4]; sk = ("stg", t % 4)
            rows = slice(t * 128, (t + 1) * 128)
            dst = self.scr[which][rows, (sub % 2) * 512:(sub % 2 + 1) * 512]
            if which == "GV":
                self.act(sb[:, :], pt[:, :], AF.Silu, [pk], [sk])
            else:
                self.act(sl[b][:, :], pt[:, :], AF.Silu, [pk], [("sl", b)])
                self.tt(POOL, sq[b][:, :], sl[b][:, :], sl[b][:, :], ALU.mult, [("sl", b)], [("sq", b)])
                S.op(DVE, lambda e, b=b: e.tensor_reduce(ssn[b][:, 0:4], sq[b][:, :].rearrange("p (h f) -> p h f", h=4), AX.X, ALU.add),
                     [("sq", b)], [("ss", b)])
                self.act(ssn[b][:, 4:8], ssn[b][:, 0:4], AF.Ln, [("ss", b)], [("ssl", b)], bias=RMS_EPS)
                qs = math.log(128 ** -0.5) if which == "GQ" else 0.0
                self.act(ssn[b][:, 8:12], ssn[b][:, 4:8], AF.Exp, [("ssl", b)], [("ssr", b)], scale=-0.5, bias=qs)
                self.tt(DVE, sb[:, :].rearrange("p (h f) -> p h f", h=4), sl[b][:, :].rearrange("p (h f) -> p h f", h=4),
                        ssn[b][:, 8:12].unsqueeze(2).broadcast_to([128, 4, 128]), ALU.mult, [("sl", b), ("ssr", b)], [sk])
            self.dma(dst, sb[:, :], [sk], [(which, t, sub)])

    MIX = {"R": dict(H=4, nkt=2, dv=256, dvp=256, q="RQ", k="RK", v="RV", gate="RG", ob="OB_R", y="Y_R"),
           "M": dict(H=4, nkt=1, dv=256, dvp=257, q="MQ", k="MK", v="MV", gate="MO", ob="OB_M", y="Y_M"),
           "G": dict(H=8, nkt=1, dv=128, dvp=128, q="GQ", k="GK", v="GV", gate="GG", ob="OB_G", y="Y_G")}

    def scan_setup(self, l):
        A = self.A
        sp_ = self.smp
        self.prm = A.alloc("prm", [128, 64], F32)
        prm = self.prm
        self.act(prm[:, 0:8], sp_[:, 0:8], AF.Exp, ["smp"], ["prm_r0"])
        self.ts(DVE, prm[:, 0:8], prm[:, 0:8], -1.0, None, ALU.mult, None, ["prm_r0"], ["prm_r"])
        self.act(prm[:, 8:24], sp_[:, 24:40], AF.Exp, ["smp"], ["prm_g0"])
        self.ts(DVE, prm[:, 8:24], prm[:, 8:24], -1.0, None, ALU.mult, None, ["prm_g0"], ["prm_g"])
        self.retDT = A.alloc("retDT", [128, 8, 128], F32)
        self.retc = A.alloc("retc", [128, 8, 3], F32)
        o, _ = CST["posv"]
        for d_ in range(2):
            sfx = "F" if d_ == 0 else "B"
            for h in range(4):
                i = d_ * 4 + h
                lgc = prm[:, i:i + 1]
                self.act(self.retDT[:, i, :], self.cst("diff" + sfx), AF.Exp, ["cst", "prm_r"], [("retDT0", i)], scale=lgc)
                self.tt(DVE, self.retDT[:, i, :], self.retDT[:, i, :], self.cst("m01" + sfx), ALU.mult, [("retDT0", i), "cst"], [("retDT", i)])
                cols = (o + 0, o + 1) if d_ == 0 else (o + 2, o + 3)
                for j, cc in enumerate(cols + (o + 4,)):
                    self.act(self.retc[:, i, j:j + 1], self.cstt[:, cc:cc + 1], AF.Exp, ["cst", "prm_r"], [("retc", i, j)], scale=lgc)

    def phase_scan(self, l, mx):
        A, S = self.A, self.S
        cfg = self.MIX[mx]
        H, nkt, dv, dvp = cfg["H"], cfg["nkt"], cfg["dv"], cfg["dvp"]
        dk = 128 * nkt
        QW = H * dk
        base = A.off
        qt = [A.alloc("qt", [128, QW], BF) for _ in range(2)]
        kt = [A.alloc("kt", [128, QW], BF) for _ in range(2)]
        vt = [A.alloc("vt", [128, H, dvp], BF) for _ in range(2)]
        ks2 = [A.alloc("ks", [128, QW], BF) for _ in range(2)]
        QT2 = [A.alloc("QT", [128, H * nkt, 128], BF) for _ in range(2)]
        KT2 = [A.alloc("KT", [128, H * nkt, 128], BF) for _ in range(2)]
        Sf = A.alloc("Sf", [128, H * nkt, dvp], F32)
        Sb = A.alloc("Sb", [128, H * nkt, dvp], BF)
        pTa = A.alloc("pTa", [128, H, 128], BF)
        acc = [A.alloc("acc", [128, 1024], F32) for _ in range(2)]
        obt = [A.alloc("obt", [128, 1024], F32) for _ in range(2)]
        gt = [A.alloc("gt", [128, 1024], BF) for _ in range(2)]
        yt = [A.alloc("yt", [128, 1024], BF) for _ in range(2)]
        t1 = A.alloc("pp1", [128, 1024], F32)
        sc = [A.alloc("scl", [128, 96], F32) for _ in range(2)]
        sraw = [A.alloc("sraw", [128, 32], F32) for _ in range(2)]
        pst = A.alloc("pst", [128, 8, 6], F32)
        pmv = A.alloc("pmv", [128, 8, 2], F32)
        prs = A.alloc("prs", [128, 16], F32)
        if mx != "R":
            Lm = A.alloc("Lm", [128, H, 128], F32)
            dtb2 = [A.alloc("dtb", [128, H, 128], F32) for _ in range(2)]
        if mx == "M":
            dn = A.alloc("dn", [128, 12], F32)
            nwb = A.alloc("nwb", [128, 1024], F32)
            self.dma(nwb[:, :], self.mnorm_w[l:l + 1, :].partition_broadcast(128), [], ["nwb"])
            for b in range(2):
                S.op(POOL, lambda e, b=b: e.memset(vt[b][:, :, 256:257], 1.0), [], [("vt", b)])
        if mx == "G":
            dts = A.alloc("dts", [128, 8, 128], F32)
            gm = A.alloc("gm", [128, 14, 128], F32)
            self.dma(gm[:, :, :], self.gmask_d, [], ["gm"])
            M0 = A.alloc("M0", [128, 8, 128], F32)
            MTt = A.alloc("MTt", [128, 8, 128], F32)
            MTm = A.alloc("MTm", [128, 6, 8, 128], F32)
            Um = [A.alloc("Um", [128, 8, 128], F32) for _ in range(2)]
            Vm = A.alloc("Vm", [128, 8, 128], F32)
            Pm = A.alloc("Pm", [128, 8, 128], F32)
            self.gdn_Wb2 = [A.alloc("Wb", [128, 8, 128], BF) for _ in range(2)]
            self.ginv = (gm, M0, MTt, MTm, Um, Vm, Pm)
            r0 = A.alloc("r0", [128, 8, 128], BF)
            vn = A.alloc("vn", [128, 8, 128], BF)
            nwb = A.alloc("nwb", [128, 128], F32)
            self.dma(nwb[:, :], self.gnorm_w[l:l + 1, :].partition_broadcast(128), [], ["nwb"])
        ps = self.ps
        PK = lambda i: ("ps", i)
        identb = self.identb
        for d_ in (1, 0):
            sfx = "F" if d_ == 0 else "B"
            order = [0, 1] + list(range(2, NT)) if d_ == 0 else [1, 0] + list(range(NT - 1, 1, -1))
            S.op(POOL, lambda e: e.memset(Sf[:, :, :], 0.0), [], [("Sf", i) for i in range(H)])
            S.op(POOL, lambda e: e.memset(Sb[:, :, :], 0.0), [], ["Sb"])
            def body(n, c):
                b = n % 2
                ks, QT, KT = ks2[b], QT2[b], KT2[b]
                QTk, KTk, ksk = ("QT", b), ("KT", b), ("ks", b)
                if mx != "R":
                    dtb = dtb2[b]
                rows = slice(c * 128, (c + 1) * 128)
                self.dma(qt[b][:, :], self.scr[cfg["q"]][rows, :], [(cfg["q"], c)], [("qt", b)])
                self.dma(kt[b][:, :], self.scr[cfg["k"]][rows, :], [(cfg["k"], c)], [("kt", b)])
                self.dma(vt[b][:, :, 0:dv], self.scr[cfg["v"]][rows, :].rearrange("p (h e) -> p h e", h=H), [(cfg["v"], c)], [("vt", b)])
                if mx == "M":
                    self.dma(sraw[b][:, 0:16], self.scr["MIF"][rows, :], [], [("sraw", b)])
                if mx == "G":
                    self.dma(sraw[b][:, 0:32], self.scr["GBA"][rows, :], [], [("sraw", b)])
                if d_ == 0:
                    self.dma(obt[b][:, :], self.scr[cfg["ob"]][rows, :], [(cfg["ob"], c)], [("obt", b)])
                    self.dma(gt[b][:, :], self.scr[cfg["gate"]][rows, :], [], [("gt", b)])
                for src_, dst_, bank, nm in ((qt[b], QT, 0, "QT"), (kt[b], KT, 1, "KT")):
                    pv = ps[bank][:, :].bitcast(BF)
                    for j in range(H * nkt):
                        self.tr(pv[:, j * 128:(j + 1) * 128], src_[:, j * 128:(j + 1) * 128], identb[:, :],
                                [(nm.lower(), b), "identb"], [PK(bank)])
                    self.cp(ACT if bank == 0 else DVE, dst_[:, :, :].rearrange("p a t -> p (a t)"), pv[:, 0:H * nkt * 128],
                            [PK(bank)], [(nm, b)])
                yield "A"
                s_ = sc[b]
                sk = ("scl", b)
                HB = 4
                nbk = H // HB
                if mx == "R":
                    a_bc = lambda hs_, w: self.retc[:, d_ * 4 + hs_.start:d_ * 4 + hs_.stop, 0:1].broadcast_to([128, hs_.stop - hs_.start, w])
                    a_reads = [("retc", d_ * 4 + h, 0) for h in range(4)]
                    cd_col = lambda h: self.retc[:, d_ * 4 + h, 2:3]
                    cd_reads = [("retc", d_ * 4 + h, 2) for h in range(4)]
                    s_bc = self.retc[:, d_ * 4:d_ * 4 + 4, 1:2].broadcast_to([128, 4, 256])
                    s_reads = [("retc", d_ * 4 + h, 1) for h in range(4)]
                    DTall = lambda hs_: self.retDT[:, d_ * 4 + hs_.start:d_ * 4 + hs_.stop, :]
                    dt_reads = [("retDT", d_ * 4 + h) for h in range(4)]
                else:
                    nh = H
                    raw = sraw[b]
                    if mx == "M":
                        self.tt(DVE, s_[:, 0:4], raw[:, d_ * 4:d_ * 4 + 4], self.smp[:, 8 + d_ * 4:12 + d_ * 4], ALU.add, [("sraw", b), "smp"], [(sk, "ig")])
                        self.tt(DVE, s_[:, 4:8], raw[:, 8 + d_ * 4:12 + d_ * 4], self.smp[:, 16 + d_ * 4:20 + d_ * 4], ALU.add, [("sraw", b), "smp"], [(sk, "x")])
                        self.act(s_[:, 8:12], s_[:, 4:8], AF.Exp, [(sk, "x")], [(sk, "e")], scale=-1.0)
                        self.act(s_[:, 12:16], s_[:, 8:12], AF.Ln, [(sk, "e")], [(sk, "l")], bias=1.0)
                        self.ts(DVE, s_[:, 16:20], s_[:, 12:16], -1.0, None, ALU.mult, None, [(sk, "l")], [(sk, "g")])
                        self.act(s_[:, 20:24], s_[:, 0:4], AF.Exp, [(sk, "ig")], [(sk, "eig")])
                        gall = s_[:, 16:20]
                    else:
                        self.act(s_[:, 0:8], raw[:, d_ * 8:d_ * 8 + 8], AF.Sigmoid, [("sraw", b)], [(sk, "beta")])
                        self.ts(DVE, s_[:, 88:96], s_[:, 0:8], -1.0, None, ALU.mult, None, [(sk, "beta")], [(sk, "nbeta")])
                        self.tt(DVE, s_[:, 8:16], raw[:, 16 + d_ * 8:24 + d_ * 8], self.smp[:, 40 + d_ * 8:48 + d_ * 8], ALU.add, [("sraw", b), "smp"], [(sk, "x")])
                        self.act(s_[:, 16:24], s_[:, 8:16], AF.Exp, [(sk, "x")], [(sk, "e")])
                        self.act(s_[:, 24:32], s_[:, 16:24], AF.Ln, [(sk, "e")], [(sk, "l")], bias=1.0)
                        self.tt(DVE, s_[:, 32:40], s_[:, 24:32], self.prm[:, 8 + d_ * 8:16 + d_ * 8], ALU.mult, [(sk, "l"), "prm_g"], [(sk, "g")])
                        gall = s_[:, 32:40]
                    self.mm(ps[2][:, 0:nh], self.cst("tri" + sfx), gall, True, True, ["cst", (sk, "g")], [PK(2)])
                    self.mm(ps[2][:, nh:2 * nh], self.cst("ones"), gall, True, True, ["cst", (sk, "g")], [PK(2)])
                    self.cp(ACT, s_[:, 40:40 + 2 * nh], ps[2][:, 0:2 * nh], [PK(2)], [(sk, "B")])
                    Bc, Ba = s_[:, 40:40 + nh], s_[:, 40 + nh:40 + 2 * nh]
                    self.act(s_[:, 56:56 + nh], Bc, AF.Exp, [(sk, "B")], [(sk, "a")])
                    self.act(s_[:, 64:64 + nh], Ba, AF.Exp, [(sk, "B")], [(sk, "cd")])
                    self.tt(DVE, s_[:, 72:72 + nh], Ba, Bc, ALU.subtract, [(sk, "B")], [(sk, "s0")])
                    if mx == "M":
                        self.tt(DVE, s_[:, 72:72 + nh], s_[:, 72:72 + nh], s_[:, 0:4], ALU.add, [(sk, "s0"), (sk, "ig")], [(sk, "s1")])
                    else:
                        self.ts(DVE, s_[:, 80:88], s_[:, 56:64], -1.0, None, ALU.mult, None, [(sk, "a")], [(sk, "na")])
                    self.act(s_[:, 72:72 + nh], s_[:, 72:72 + nh], AF.Exp, [(sk, "s0"), (sk, "s1")], [(sk, "s")])
                    a_bc = lambda hs_, w: s_[:, 56 + hs_.start:56 + hs_.stop].unsqueeze(2).broadcast_to([128, hs_.stop - hs_.start, w])
                    a_reads = [(sk, "a")]
                    cd_col = lambda h: s_[:, 64 + h:65 + h]
                    cd_reads = [(sk, "cd")]
                    s_bc = s_[:, 72:72 + nh].unsqueeze(2).broadcast_to([128, nh, 128])
                    s_reads = [(sk, "s")]
                    yield "A"
                    g0 = 16 if mx == "M" else 32
                    for bk in range(nbk):
                        hs = slice(bk * HB, (bk + 1) * HB)
                        self.tt(DVE, Lm[:, hs, :], self.cst("s" + sfx).unsqueeze(1).broadcast_to([128, HB, 128]),
                                s_[:, g0 + hs.start:g0 + hs.stop].unsqueeze(2).broadcast_to([128, HB, 128]), ALU.mult, ["cst", (sk, "g")], [("Lm", bk)])
                        for q in range(HB):
                            h = bk * HB + q
                            self.mm(ps[2 + bk][:, q * 128:(q + 1) * 128], Lm[:, h, :], self.cst("tri" + sfx), True, False, [("Lm", bk), "cst"], [PK(2 + bk)])
                            self.mm(ps[2 + bk][:, q * 128:(q + 1) * 128], self.cst("ident"), self.cst("neg" + sfx), False, True, ["cst"], [PK(2 + bk)])
                        self.act(dtb[:, hs, :].rearrange("p a t -> p (a t)"), ps[2 + bk][:, :], AF.Exp, [PK(2 + bk)], [("dtb", b, bk)])
                        if mx == "M":
                            self.tt(POOL, dtb[:, hs, :], dtb[:, hs, :], s_[:, 20:24].unsqueeze(2).broadcast_to([128, 4, 128]), ALU.mult,
                                    [("dtb", b, bk), (sk, "eig")], [("dtb", b, bk)])
                        else:
                            self.tt(POOL, dts[:, hs, :], dtb[:, hs, :], self.cst("s01" + sfx).unsqueeze(1).broadcast_to([128, HB, 128]), ALU.mult,
                                    [("dtb", b, bk), "cst"], [("dts", h) for h in range(hs.start, hs.stop)])
                        yield "A"
                    DTall = lambda hs_: dtb[:, hs_, :]
                    dt_reads = [("dtb", b, bk) for bk in range(nbk)]
                self.tt(POOL, ks[:, :].rearrange("p (h e) -> p h e", h=H), kt[b][:, :].rearrange("p (h e) -> p h e", h=H), s_bc,
                        ALU.mult, [("kt", b)] + s_reads, [ksk])
                yield "A"
                if mx == "G":
                    Wt = self.gdn_Wb2[b]
                    yield from self._gdn_inverse(KT, KTk, s_, sk, dts, d_, Wt, b)
                yield "A_DONE"
                A_ = acc[b]
                ak = ("acc", b)
                ring = self.ring
                for bk in range(nbk):
                    hs = slice(bk * HB, (bk + 1) * HB)
                    bn = ring()
                    for q in range(HB):
                        h = bk * HB + q
                        for kk in range(nkt):
                            self.mm(ps[bn][:, q * 128:(q + 1) * 128], KT[:, h * nkt + kk, :], QT[:, h * nkt + kk, :], kk == 0, kk == nkt - 1, [KTk, QTk], [PK(bn)])
                    self.tt(DVE, pTa[:, hs, :].rearrange("p a t -> p (a t)"), ps[bn][:, :], DTall(hs).rearrange("p a t -> p (a t)"), ALU.mult,
                            [PK(bn)] + dt_reads, [("pTa", bk)])
                    yield "B"
                vsrc = lambda h: vt[b][:, h, 0:dv]
                vreads = [("vt", b)]
                if mx == "G":
                    for bk in range(nbk):
                        hs = slice(bk * HB, (bk + 1) * HB)
                        bn = ring()
                        for q in range(HB):
                            h = bk * HB + q
                            self.mm(ps[bn][:, q * 128:(q + 1) * 128], KT[:, h, :], Sb[:, h, :], True, True, [KTk, "Sb"], [PK(bn)])
                        for q in range(HB):
                            h = bk * HB + q
                            self.stt(r0[:, h, :], ps[bn][:, q * 128:(q + 1) * 128], s_[:, 80 + h:81 + h], vt[b][:, h, :], ALU.mult, ALU.add,
                                     [PK(bn), (sk, "na"), ("vt", b)], [("r0", bk)])
                        bn2 = ring()
                        for q in range(HB):
                            h = bk * HB + q
                            self.mm(ps[bn2][:, q * 128:(q + 1) * 128], Wt[:, h, :], r0[:, h, :], True, True, [("Wb", b, bk), ("r0", bk)], [PK(bn2)])
                        self.tt(DVE, vn[:, hs, :], ps[bn2][:, :].rearrange("p (a t) -> p a t", a=HB),
                                s_[:, hs.start:hs.stop].unsqueeze(2).broadcast_to([128, HB, 128]), ALU.mult, [PK(bn2), (sk, "beta")], [("vn", bk)])
                        yield "B"
                    vsrc = lambda h: vn[:, h, :]
                    vreads = [("vn", bk) for bk in range(nbk)]
                hpb = 512 // dv
                for g_ in range(H // hpb):
                    hs = slice(g_ * hpb, (g_ + 1) * hpb)
                    by, bz = ring(), ring()
                    for q in range(hpb):
                        h = g_ * hpb + q
                        self.mm(ps[by][:, q * dv:(q + 1) * dv], pTa[:, h, :], vsrc(h), True, True, [("pTa", h // HB)] + vreads, [PK(by)])
                    for q in range(hpb):
                        h = g_ * hpb + q
                        for kk in range(nkt):
                            self.mm(ps[bz][:, q * dv:(q + 1) * dv], QT[:, h * nkt + kk, :], Sb[:, h * nkt + kk, 0:dv], kk == 0, kk == nkt - 1, [QTk, "Sb"], [PK(bz)])
                    o3 = A_[:, hs.start * dv:hs.stop * dv].rearrange("p (a e) -> p a e", a=hpb)
                    self.tt(DVE, o3, ps[bz][:, :].rearrange("p (a e) -> p a e", a=hpb), a_bc(hs, dv), ALU.mult, [PK(bz)] + a_reads, [(ak, g_)])
                    self.tt(DVE, o3, o3, ps[by][:, :].rearrange("p (a e) -> p a e", a=hpb), ALU.add, [(ak, g_), PK(by)], [(ak, g_)])
                    yield "B"
                nacc = H // hpb
                if mx == "M":
                    bd = ring()
                    for h in range(4):
                        self.mm(ps[bd][:, h:h + 1], pTa[:, h, :], self.onesb[:, 0:1], True, True, [("pTa", 0), "onesb"], [PK(bd)])
                    for h in range(4):
                        self.mm(ps[bd][:, 4 + h:5 + h], QT[:, h, :], Sb[:, h, 256:257], True, True, [QTk, "Sb"], [PK(bd)])
                    self.tt(DVE, dn[:, 0:4], ps[bd][:, 4:8], s_[:, 56:60], ALU.mult, [PK(bd), (sk, "a")], ["dn0"])
                    self.tt(DVE, dn[:, 0:4], dn[:, 0:4], ps[bd][:, 0:4], ALU.add, ["dn0", PK(bd)], ["dn0"])
                    self.act(dn[:, 4:8], dn[:, 0:4], AF.Abs, ["dn0"], ["dn1"])
                    self.ts(DVE, dn[:, 4:8], dn[:, 4:8], 1.0, None, ALU.max, None, ["dn1"], ["dn2"])
                    S.op(DVE, lambda e: e.reciprocal(dn[:, 8:12], dn[:, 4:8]), ["dn2"], ["dn3"])
                    A3_ = A_[:, :].rearrange("p (h e) -> p h e", h=4)
                    self.tt(POOL, A3_, A3_, dn[:, 8:12].unsqueeze(2).broadcast_to([128, 4, 256]), ALU.mult, [(ak, 0), (ak, 1), "dn3"], [(ak, 0), (ak, 1)])
                if mx == "G":
                    for bk in range(nbk):
                        hs = slice(bk * HB, (bk + 1) * HB)
                        bu = ring()
                        for q in range(HB):
                            h = bk * HB + q
                            self.mm(ps[bu][:, q * 128:(q + 1) * 128], ks[:, h * 128:(h + 1) * 128], vn[:, h, :], True, True, [ksk, ("vn", bk)], [PK(bu)])
                        self.tt(POOL, Sf[:, hs, :], Sf[:, hs, :], s_[:, 64 + hs.start:64 + hs.stop].unsqueeze(2).broadcast_to([128, HB, 128]), ALU.mult,
                                [("Sf", bk), (sk, "cd")], [("Sf", bk)])
                        self.tt(DVE, Sf[:, hs, :], Sf[:, hs, :], ps[bu][:, :].rearrange("p (a t) -> p a t", a=HB), ALU.add, [("Sf", bk), PK(bu)], [("Sf", bk)])
                    self.cp(ACT, Sb[:, :, :], Sf[:, :, :], [("Sf", bk) for bk in range(nbk)], ["Sb"])
                else:
                    for h in range(H):
                        bu = ring()
                        for kk in range(nkt):
                            i = h * nkt + kk
                            self.mm(ps[bu][:, kk * 256:kk * 256 + dvp], ks[:, i * 128:(i + 1) * 128], vt[b][:, h, :], True, True, [ksk, ("vt", b)], [PK(bu)])
                        if nkt == 2:
                            self.stt(Sf[:, h * 2:h * 2 + 2, :].rearrange("p a e -> p (a e)"), Sf[:, h * 2:h * 2 + 2, :].rearrange("p a e -> p (a e)"),
                                     cd_col(h), ps[bu][:, :], ALU.mult, ALU.add, [("Sf", h), PK(bu)] + cd_reads, [("Sf", h)])
                        else:
                            self.stt(Sf[:, h, :], Sf[:, h, :], cd_col(h), ps[bu][:, 0:dvp], ALU.mult, ALU.add, [("Sf", h), PK(bu)] + cd_reads, [("Sf", h)])
                    self.cp(ACT, Sb[:, :, :], Sf[:, :, :], [("Sf", h) for h in range(H)], ["Sb"])
                yield "B"
                akeys = [(ak, g_) for g_ in range(nacc)]
                if d_ == 1:
                    self.dma(self.scr[cfg["ob"]][rows, :], A_[:, :], akeys, [(cfg["ob"], c)])
                    return
                self.tt(POOL, A_[:, :], A_[:, :], obt[b][:, :], ALU.add, akeys + [("obt", b)], akeys)
                A3 = A_[:, :].rearrange("p (h e) -> p h e", h=H)
                Y = yt[b]
                if "dbg_acc" in self.debug:
                    if not hasattr(self, "dbg_acc_d"):
                        self.dbg_acc_d = self.nc.dram_tensor("dbg_acc", [T, 1024], F32, kind="ExternalOutput").ap()
                    self.dma(self.dbg_acc_d[rows, :], A_[:, :], akeys, [("dbgacc", c)])
                if mx == "G":
                    self.tt(POOL, t1[:, :], A_[:, :], A_[:, :], ALU.mult, akeys, ["pp1"])
                    S.op(DVE, lambda e: e.tensor_reduce(prs[:, 0:8], t1[:, :].rearrange("p (h e) -> p h e", h=8), AX.X, ALU.add), ["pp1"], ["prs0"])
                    self.act(prs[:, 8:16], prs[:, 0:8], AF.Ln, ["prs0"], ["prs1"], bias=RMS_EPS, scale=1.0 / 128)
                    self.act(prs[:, 0:8], prs[:, 8:16], AF.Exp, ["prs1"], ["prs2"], scale=-0.5)
                    self.tt(DVE, t1[:, :].rearrange("p (h e) -> p h e", h=8), A3, prs[:, 0:8].unsqueeze(2).broadcast_to([128, 8, 128]), ALU.mult, akeys + ["prs2"], ["pp1"])
                    self.tt(POOL, t1[:, :].rearrange("p (h e) -> p h e", h=8), t1[:, :].rearrange("p (h e) -> p h e", h=8),
                            nwb[:, :].unsqueeze(1).broadcast_to([128, 8, 128]), ALU.mult, ["pp1", "nwb"], ["pp1"])
                    self.tt(DVE, Y[:, :], t1[:, :], gt[b][:, :], ALU.mult, ["pp1", ("gt", b)], [("yt", b)])
                else:
                    for h in range(4):
                        S.op(DVE, lambda e, h=h, A_=A_: e.bn_stats(pst[:, h, :], A_[:, h * 256:(h + 1) * 256]), akeys, [("pst", h)])
                        S.op(DVE, lambda e, h=h: e.bn_aggr(pmv[:, h, :], pst[:, h, :]), [("pst", h)], ["pmv"])
                    self.act(prs[:, 0:4], pmv[:, 0:4, 1:2].rearrange("p h o -> p (h o)"), AF.Ln, ["pmv"], ["prs0"], bias=LN_EPS)
                    self.act(prs[:, 4:8], prs[:, 0:4], AF.Exp, ["prs0"], ["prs1"], scale=-0.5)
                    t3 = t1[:, :].rearrange("p (h e) -> p h e", h=4)
                    self.tt(DVE, t3, A3, pmv[:, 0:4, 0:1].broadcast_to([128, 4, 256]), ALU.subtract, akeys + ["pmv"], ["pp1"])
                    self.tt(POOL, t3, t3, prs[:, 4:8].unsqueeze(2).broadcast_to([128, 4, 256]), ALU.mult, ["pp1", "prs1"], ["pp1"])
                    if "dbg_t1" in self.debug:
                        if not hasattr(self, "dbg_t1_d"):
                            self.dbg_t1_d = self.nc.dram_tensor("dbg_t1", [T, 1024], F32, kind="ExternalOutput").ap()
                            self.dbg_pmv_d = self.nc.dram_tensor("dbg_pmv", [T, 16], F32, kind="ExternalOutput").ap()
                            self.dbg_prs_d = self.nc.dram_tensor("dbg_prs", [T, 16], F32, kind="ExternalOutput").ap()
                        self.dma(self.dbg_t1_d[rows, :], t1[:, :], ["pp1"], [("dbgt1", c)])
                        self.dma(self.dbg_pmv_d[rows, 0:8], pmv[:, 0:4, :].rearrange("p a b -> p (a b)"), ["pmv"], [("dbgpmv", c)])
                        self.dma(self.dbg_prs_d[rows, 0:8], prs[:, 0:8], ["prs1", "prs0"], [("dbgprs", c)])
                    if mx == "M":
                        self.tt(POOL, t1[:, :], t1[:, :], nwb[:, :], ALU.mult, ["pp1", "nwb"], ["pp1"])
                    self.tt(DVE, Y[:, :], t1[:, :], gt[b][:, :], ALU.mult, ["pp1", ("gt", b)], [("yt", b)])
                self.dma(self.scr[cfg["y"]][rows, :], Y[:, :], [("yt", b)], [(cfg["y"], c)])

            gens = [body(n, c) for n, c in enumerate(order)]
            while next(gens[0]) != "A_DONE":
                pass
            for n in range(len(gens)):
                gA = gens[n + 1] if n + 1 < len(gens) else None
                gB = gens[n]
                doneA, doneB = gA is None, False
                while not (doneA and doneB):
                    if not doneA:
                        if next(gA) == "A_DONE":
                            doneA = True
                    if not doneB:
                        try:
                            next(gB)
                        except StopIteration:
                            doneB = True
            S.barrier()
        A.off = base

    def _gdn_inverse(self, KT, KTk, s_, sk, dts, d_, Wb, pb):
        S, ps = self.S, self.ps
        PK = lambda i: ("ps", i)
        gm, M0, MTt, MTm, Um, Vm, Pm = self.ginv
        fo = 0 if d_ == 0 else 7
        to = 7 if d_ == 0 else 0
        ident = self.cst("ident")
        for hh in range(2):
            bank = hh
            for q in range(4):
                h = hh * 4 + q
                self.mm(ps[bank][:, q * 128:(q + 1) * 128], KT[:, h, :], KT[:, h, :], True, True, [KTk], [PK(bank)])
            for q in range(4):
                h = hh * 4 + q
                self.stt(M0[:, h, :], ps[bank][:, q * 128:(q + 1) * 128], s_[:, 88 + h:89 + h], dts[:, h, :], ALU.mult, ALU.mult,
                         [PK(bank), (sk, "nbeta"), ("dts", h)], [("M0", hh)])
            bank = 2 + hh
            for q in range(4):
                h = hh * 4 + q
                self.tr(ps[bank][:, q * 128:(q + 1) * 128], M0[:, h, :], ident, [("M0", hh), "cst"], [PK(bank)])
            self.cp(ACT, MTt[:, hh * 4:(hh + 1) * 4, :].rearrange("p a t -> p (a t)"), ps[bank][:, :], [PK(bank)], [("MTt", hh)])
            yield "A"
        for lev in range(1, 7):
            self.tt(POOL, MTm[:, lev - 1, :, :], MTt[:, :, :], gm[:, to + lev:to + lev + 1, :].broadcast_to([128, 8, 128]), ALU.mult,
                    [("MTt", 0), ("MTt", 1), "gm"], [("MTm", lev)])
        U = Um[0]
        self.tt(DVE, U[:, :, :], M0[:, :, :], gm[:, fo:fo + 1, :].broadcast_to([128, 8, 128]), ALU.mult, [("M0", 0), ("M0", 1), "gm"], [("Um", 0, 0), ("Um", 0, 1)])
        self.tt(DVE, U[:, :, :], U[:, :, :], ident.unsqueeze(1).broadcast_to([128, 8, 128]), ALU.add, [("Um", 0, 0), ("Um", 0, 1), "cst"], [("Um", 0, 0), ("Um", 0, 1)])
        cur = 0
        for lev in range(1, 7):
            nxt = 1 - cur
            Uc, Un = Um[cur], Um[nxt]
            for hh in range(2):
                hs = slice(hh * 4, (hh + 1) * 4)
                uk = ("Um", cur, hh)
                for q in range(4):
                    h = hh * 4 + q
                    self.tr(ps[hh][:, q * 128:(q + 1) * 128], Uc[:, h, :], ident, [uk, "cst"], [PK(hh)])
                for q in range(4):
                    h = hh * 4 + q
                    self.mm(ps[2 + hh][:, q * 128:(q + 1) * 128], MTm[:, lev - 1, h, :], Uc[:, h, :], True, True, [("MTm", lev), uk], [PK(2 + hh)])
                self.cp(ACT, Vm[:, hs, :].rearrange("p a t -> p (a t)"), ps[hh][:, :], [PK(hh)], [("Vm", hh)])
                self.cp(DVE, Pm[:, hs, :].rearrange("p a t -> p (a t)"), ps[2 + hh][:, :], [PK(2 + hh)], [("Pm", hh)])
            yield "A"
            for hh in range(2):
                hs = slice(hh * 4, (hh + 1) * 4)
                uk = ("Um", cur, hh)
                for q in range(4):
                    h = hh * 4 + q
                    self.mm(ps[hh][:, q * 128:(q + 1) * 128], Vm[:, h, :], Pm[:, h, :], True, True, [("Vm", hh), ("Pm", hh)], [PK(hh)])
                self.tt(DVE, Un[:, hs, :].rearrange("p a t -> p (a t)"), Uc[:, hs, :].rearrange("p a t -> p (a t)"), ps[hh][:, :], ALU.add,
                        [uk, PK(hh)], [("Um", nxt, hh)])
            yield "A"
            cur = nxt
        for hh in range(2):
            hs = slice(hh * 4, (hh + 1) * 4)
            self.cp(ACT if hh == 0 else DVE, Wb[:, hs, :], Um[cur][:, hs, :], [("Um", cur, hh)], [("Wb", pb, hh)])

    def postnorm(self, pbanks, xt, xk, gi, li, ub, st, mv, rs, slot, dst, dstkey):
        uk = ("ub", slot)
        for hh in range(2):
            cs = slice(hh * 512, (hh + 1) * 512)
            self.tt(DVE, ub[:, cs], self.ps[pbanks[hh]][:, :], self.gbc[:, gi, cs], ALU.mult, [("ps", pbanks[hh]), ("gbc", gi, hh)], [(uk, hh)])
            self.stt(ub[:, cs], xt[:, cs], DN_ALPHA, ub[:, cs], ALU.mult, ALU.add, [xk, (uk, hh)], [(uk, hh)])
        S = self.S
        for hh in range(2):
            S.op(DVE, lambda e, hh=hh: e.bn_stats(st[:, hh, :], ub[:, hh * 512:(hh + 1) * 512]), [(uk, hh)], [("pst", slot)])
        S.op(DVE, lambda e: e.bn_aggr(mv[:, :], st[:, :, :].rearrange("p a b -> p (a b)")), [("pst", slot)], [("pmv", slot)])
        self.act(rs[:, 2:3], mv[:, 1:2], AF.Ln, [("pmv", slot)], [("prs2", slot)], bias=LN_EPS)
        self.act(rs[:, 0:1], rs[:, 2:3], AF.Exp, [("prs2", slot)], [("prs0", slot)], scale=-0.5)
        self.ts(DVE, rs[:, 1:2], mv[:, 0:1], rs[:, 0:1], -1.0, ALU.mult, ALU.mult, [("pmv", slot), ("prs0", slot)], [("prs1", slot)])
        self.act(ub[:, :], ub[:, :], AF.Identity, [(uk, 0), (uk, 1), ("prs0", slot), ("prs1", slot)], [(uk, 0), (uk, 1)],
                 bias=rs[:, 1:2], scale=rs[:, 0:1])
        self.tt(POOL, ub[:, :], ub[:, :], self.lnl[:, 0, :], ALU.mult, [(uk, 0), (uk, 1), ("lnl", 0)], [(uk, 0), (uk, 1)])
        self.tt(DVE, ub[:, :], ub[:, :], self.lnl[:, 1, :], ALU.add, [(uk, 0), (uk, 1), ("lnl", 1)], [(uk, 0), (uk, 1)])
        return self.dma(dst, ub[:, :], [(uk, 0), (uk, 1)], [dstkey])

    def phase_merge(self, l):
        A, S, ps = self.A, self.S, self.ps
        base = A.off
        last = (l == DEPTH - 1)
        self.lnl = A.alloc("lnl", [128, 2, 1024], F32)
        for j, src_ in enumerate((self.ln1_g, self.ln1_b)):
            self.dma(self.lnl[:, j, :], src_[l:l + 1, :].partition_broadcast(128), [], [("lnl", j)])
        wbr = A.alloc("wbr", [128, 3, 8, 1024], BF)
        wo = A.alloc("wo", [128, 8, 1024], BF)
        for br in range(3):
            for hh in range(2):
                self.dma(wbr[:, br, :, hh * 512:(hh + 1) * 512],
                         self.w_branch[l, br].rearrange("(kc p) c -> p kc c", p=128)[:, :, hh * 512:(hh + 1) * 512], [], [("wbr", br)], q=POOL)
        for hh in range(2):
            self.dma(wo[:, :, hh * 512:(hh + 1) * 512], self.w_out[l].rearrange("(kc p) c -> p kc c", p=128)[:, :, hh * 512:(hh + 1) * 512], [], ["wo"], q=POOL)
        yin = [[A.alloc("yin", [128, 1024], BF) for _ in range(3)] for _ in range(2)]
        mg = [A.alloc("mg", [128, 3072], BF) for _ in range(2)]
        xt = [A.alloc("xt", [128, 1024], F32) for _ in range(2)]
        yT = [A.alloc("yT", [128, 8, 128], BF) for _ in range(3)]
        mrg = A.alloc("mrg", [128, 1024], F32)
        mt2 = A.alloc("mt2", [128, 512], F32)
        mrb = A.alloc("mrb", [128, 1024], BF)
        mT = A.alloc("mT", [128, 8, 128], BF)
        ub = [A.alloc("ub", [128, 1024], F32) for _ in range(2)]
        st = [A.alloc("st", [128, 2, 6], F32) for _ in range(2)]
        mv = [A.alloc("mv", [128, 2], F32) for _ in range(2)]
        rs = [A.alloc("rs", [128, 4], F32) for _ in range(2)]
        src = self.xz if l == 0 else self.scr["X2"]
        names = ("Y_R", "Y_M", "Y_G")
        n = 0
        for t in range(2 if last else 0, NT):
            b = n % 2
            n += 1
            rows = slice(t * 128, (t + 1) * 128)
            v = 1 if t < 2 else 0
            for br in range(3):
                self.dma(yin[b][br][:, :], self.scr[names[br]][rows, :], [], [("yin", b, br)])
            self.dma(mg[b][:, :], self.scr["MG"][rows, :], [], [("mg", b)])
            self.dma(xt[b][:, :], src[rows, :], [], [("xt", b)])
            for br in range(3):
                bank = br % 2
                pv = ps[bank][:, :].bitcast(BF)
                for c in range(8):
                    self.tr(pv[:, c * 128:(c + 1) * 128], yin[b][br][:, c * 128:(c + 1) * 128], self.identb[:, :],
                            [("yin", b, br), "identb"], [("ps", bank)])
                self.cp(ACT if br != 1 else DVE, yT[br][:, :, :].rearrange("p a t -> p (a t)"), pv[:, :], [("ps", bank)], [("yT", br)])
            for hh in range(2):
                cs = slice(hh * 512, (hh + 1) * 512)
                for br in range(3):
                    for kc in range(8):
                        self.mm(ps[2 + br][:, :], yT[br][:, kc, :], wbr[:, br, kc, cs], kc == 0, kc == 7, [("yT", br), ("wbr", br)], [("ps", 2 + br)])
                self.tt(DVE, mrg[:, cs], ps[2][:, :], mg[b][:, hh * 512:(hh + 1) * 512], ALU.mult, [("ps", 2), ("mg", b)], [("mrg", hh)])
                self.tt(DVE, mt2[:, :], ps[3][:, :], mg[b][:, 1024 + hh * 512:1024 + (hh + 1) * 512], ALU.mult, [("ps", 3), ("mg", b)], ["mt2"])
                self.tt(POOL, mrg[:, cs], mrg[:, cs], mt2[:, :], ALU.add, [("mrg", hh), "mt2"], [("mrg", hh)])
                self.tt(DVE, mt2[:, :], ps[4][:, :], mg[b][:, 2048 + hh * 512:2048 + (hh + 1) * 512], ALU.mult, [("ps", 4), ("mg", b)], ["mt2"])
                self.tt(POOL, mrb[:, cs], mrg[:, cs], mt2[:, :], ALU.add, [("mrg", hh), "mt2"], [("mrb", hh)])
            pv = ps[5][:, :].bitcast(BF)
            for c in range(8):
                self.tr(pv[:, c * 128:(c + 1) * 128], mrb[:, c * 128:(c + 1) * 128], self.identb[:, :], [("mrb", c // 4), "identb"], [("ps", 5)])
            self.cp(ACT, mT[:, :, :].rearrange("p a t -> p (a t)"), pv[:, :], [("ps", 5)], ["mT"])
            for hh in range(2):
                for kc in range(8):
                    self.mm(ps[6 + hh][:, :], mT[:, kc, :], wo[:, kc, hh * 512:(hh + 1) * 512], kc == 0, kc == 7, ["mT", "wo"], [("ps", 6 + hh)])
            self.postnorm((6, 7), xt[b], ("xt", b), 0 + v, 0, ub[b], st[b], mv[b], rs[b], b, self.scr["X1"][rows, :], ("X1", t))
        S.barrier()
        A.off = base

    def phase_mlp(self, l):
        A, S, ps = self.A, self.S, self.ps
        base = A.off
        last = (l == DEPTH - 1)
        self.lnl = A.alloc("lnl", [128, 2, 1024], F32)
        for j, src_ in enumerate((self.ln2_g, self.ln2_b)):
            self.dma(self.lnl[:, j, :], src_[l:l + 1, :].partition_broadcast(128), [], [("lnl", j)])
        w1 = A.alloc("w1", [128, 8, DFF], BF)
        w2 = A.alloc("w2", [128, 32, D], BF)
        w1v = self.w_mlp1[l].rearrange("(kc p) c -> p kc c", p=128)
        w2v = self.w_mlp2[l].rearrange("(fc p) c -> p fc c", p=128)
        for i in range(8):
            self.dma(w1[:, :, i * 512:(i + 1) * 512], w1v[:, :, i * 512:(i + 1) * 512], [], [("w1", i)], q=POOL)
        for i in range(8):
            self.dma(w2[:, i * 4:(i + 1) * 4, :], w2v[:, i * 4:(i + 1) * 4, :], [], [("w2", i)], q=POOL)
        xt = [A.alloc("xt", [128, 1024], F32) for _ in range(3)]
        xn = [A.alloc("xn", [128, 1024], F32) for _ in range(1)] * 2
        h2T = A.alloc("h2T", [128, 8, 256], BF)
        hid = A.alloc("hid", [128, 32, 256], BF)
        rl = [A.alloc("rl", [128, 256], F32) for _ in range(2)]
        ub = [A.alloc("ub", [128, 1024], F32) for _ in range(1)] * 2
        st = [A.alloc("st", [128, 2, 6], F32) for _ in range(4)]
        mv = [A.alloc("mv", [128, 2], F32) for _ in range(4)]
        rs = [A.alloc("rs", [128, 4], F32) for _ in range(4)]
        tiles = list(range(2 if last else 0, NT))
        blocks = [tiles[i:i + 2] for i in range(0, len(tiles), 2)]
        xi = 0
        un = 0
        outs = []
        for blk in blocks:
            nb = len(blk)
            xts = []
            for j, t in enumerate(blk):
                b5 = xi % 3
                xi += 1
                b = 0
                v = 1 if t < 2 else 0
                rows = slice(t * 128, (t + 1) * 128)
                X = xt[b5]
                xk = ("xt", b5)
                xts.append((X, xk))
                self.dma(X[:, :], self.scr["X1"][rows, :], [], [xk])
                self.ln_stats(X, xk, st[b], mv[b], rs[b], ("m", b))
                self.act(xn[b][:, :], X[:, :], AF.Identity, [xk, ("rs0", ("m", b)), ("rs1", ("m", b))], [("xn", b)],
                         bias=rs[b][:, 1:2], scale=rs[b][:, 0:1])
                for c in range(8):
                    self.tr(ps[c // 4][:, (c % 4) * 128:(c % 4 + 1) * 128], xn[b][:, c * 128:(c + 1) * 128], self.cst("ident"),
                            [("xn", b), "cst"], [("ps", c // 4)])
                for c in range(8):
                    o = h2T[:, c, j * 128:(j + 1) * 128]
                    i_ = ps[c // 4][:, (c % 4) * 128:(c % 4 + 1) * 128]
                    sc_ = self.ops[:, 1, c, v:v + 1]
                    sh_ = self.modc[:, 24 + c, v:v + 1]
                    if c // 4 == 0:
                        self.ts(DVE, o, i_, sc_, sh_, ALU.mult, ALU.add, [("ps", 0), "modc", "ops"], [("h2T", j, c)])
                    else:
                        self.act(o, i_, AF.Identity, [("ps", 1), "modc", "ops"], [("h2T", j, c)], bias=sh_, scale=sc_)
            hkeys = [("h2T", j, c) for j in range(nb) for c in range(8)]
            N = nb * 128
            for f in range(32):
                bank = 2 + f % 4
                for kc in range(8):
                    self.mm(ps[bank][:, 0:N], w1[:, kc, f * 128:(f + 1) * 128], h2T[:, kc, 0:N], kc == 0, kc == 7,
                            hkeys + [("w1", f // 4)], [("ps", bank)])
                r_ = rl[f % 2]
                self.act(r_[:, 0:N], ps[bank][:, 0:N], AF.Relu, [("ps", bank)], [("rl", f % 2)])
                self.tt(POOL if f % 2 == 0 else DVE, hid[:, f, 0:N], r_[:, 0:N], r_[:, 0:N], ALU.mult, [("rl", f % 2)], [("hid", f)])
            for j, t in enumerate(blk):
                v = 1 if t < 2 else 0
                rows = slice(t * 128, (t + 1) * 128)
                for f in range(32):
                    for hh in range(2):
                        self.mm(ps[6 + hh][:, :], hid[:, f, j * 128:(j + 1) * 128], w2[:, f, hh * 512:(hh + 1) * 512], f == 0, f == 31,
                                [("hid", f), ("w2", f // 4)], [("ps", 6 + hh)])
                if last:
                    dst = self.out[t * 128 - NCTX:(t + 1) * 128 - NCTX, :]
                else:
                    dst = self.scr["X2"][rows, :]
                u = 0
                un += 1
                X, xk = xts[j]
                tok = self.postnorm((6, 7), X, xk, 2 + v, 2, ub[u], st[2 + u], mv[2 + u], rs[2 + u], ("p", u), dst, ("X2", t))
                outs.append(tok)
        S.barrier()
        A.off = base
        return outs

    def build(self):
        self.setup()
        for l in range(self.layers):
            self.phase_ada(l)
            if self.upto == "ada":
                break
            A = self.A
            A.off = self.persist_end
            hT = A.alloc("hT", [128, 8, T], BF)
            self.hT = hT
            src = self.xz if l == 0 else self.scr["X2"]
            if not getattr(self, "skip_inproj", False):
                self.phase_ln1(l, src, hT)
            if "dbg_hT" in self.debug:
                self.S.barrier()
                hb = A.alloc("hdbg", [128, 1024], F32)
                self.cp(DVE, hb[:, :].rearrange("p (c t) -> p c t", c=8), hT[:, :, 0:128], [], ["hdbg"])
                self.dbg("dbg_hT", hb[:, :], [128, 1024], ["hdbg"])
            if self.upto == "ln1":
                break
            if not getattr(self, "skip_inproj", False):
                self.phase_inproj(l, hT)
            self.S.barrier()
            if self.upto == "inproj":
                break
            A.off = self.persist_end
            self.scan_setup(l)
            for mx in getattr(self, "mixers", "RMG"):
                self.phase_scan(l, mx)
            if self.upto == "scan":
                break
            self.S.barrier()
            A.off = self.persist_end
            if not getattr(self, "skip_merge", False):
                self.phase_merge(l)
            if self.upto == "merge":
                break
            self.phase_mlp(l)
        self.S.barrier()
        self.S.emit()


def prep_inputs(inputs, b):
    f = lambda a: np.ascontiguousarray(np.asarray(a, np.float32))
    m = {}
    m["xz"] = f(np.concatenate([inputs["ctx"][b], inputs["x"][b]], 0))
    cc = np.stack([np.asarray(inputs["c"][b]), np.asarray(inputs["c_ctx"])], -1)
    m["cc"] = f(cc.reshape(8, 128, 2).transpose(1, 0, 2))
    m["w_ada"] = f(inputs["w_ada"])
    m["b_ada"] = f(np.asarray(inputs["b_ada"]).reshape(DEPTH, 48, 128).transpose(0, 2, 1))
    m["w_in"] = f(inputs["w_in"])
    m["conv_w"] = f(np.asarray(inputs["conv_w"]).reshape(DEPTH, 5, 24, 128).transpose(0, 3, 2, 1))
    m["smallp"] = f(np.concatenate([np.asarray(inputs[k]).reshape(DEPTH, -1) for k in
                                    ("ret_decay", "mlstm_i_bias", "mlstm_f_bias", "gdn_a_log", "gdn_dt_bias")], 1)[:, :56])
    m["smallp"] = f(np.pad(m["smallp"], ((0, 0), (0, 8))))
    for k in ("mlstm_norm_w", "gdn_norm_w", "w_branch", "w_out", "ln1_g", "ln1_b", "ln2_g", "ln2_b", "w_mlp1", "w_mlp2"):
        m[k] = f(inputs[k])
    m["consts"] = f(CONSTS)
    rq, rk = _rope_tables()
    m["ropeq"], m["ropek"] = f(rq), f(rk)
    m["gmask"] = f(_gmasks())
    return m


def kernel(**inputs):
    nc = bass.Bass("TRN2", target_bir_lowering=False)
    Builder(nc).build()
    in_maps = [prep_inputs(inputs, b) for b in range(8)]
    res = run_bass_kernel_spmd(nc, in_maps, core_ids=list(range(8)))
    return np.stack([np.asarray(r["out"], np.float32) for r in res.results], 0)
```

```python
import contextlib
import os
import math
import numpy as np
import ml_dtypes
import concourse.bass as bass
import concourse.mybir as mybir
from concourse.bass_utils import run_bass_kernel_spmd

F32 = mybir.dt.float32
BF = mybir.dt.bfloat16
AF = mybir.ActivationFunctionType
ALU = mybir.AluOpType
AX = mybir.AxisListType

PE, ACT, DVE, POOL, SP = "pe", "act", "dve", "pool", "sp"
COMPUTE = (PE, ACT, DVE, POOL)
EPOCH = 12000
NDMASEM = 12
STOREQ = {"pool": "pool", "sp": "sp"}[os.environ.get("STOREQ", "pool")]

NCTX = 256
NLAT = 4096
T = NCTX + NLAT
NT = T // 128
D = 1024
DEPTH = 2
IN_DIM = 14384
DFF = 4096
LN_EPS = 1e-5
RMS_EPS = 1e-6
DN_ALPHA = (2 * DEPTH) ** 0.25
NEG = -30000.0


class Sched:
    def __init__(self, nc):
        self.nc = nc
        self.q = {e: [] for e in (PE, ACT, DVE, POOL, SP)}
        self.cnt = {e: 0 for e in COMPUTE}
        self.dcnt = {POOL: 0, SP: 0}
        self.lastw = {}
        self.readers = {}
        self.waited = {}
        self.waited_dma = {e: set() for e in self.q}

    def _deps(self, reads, writes):
        deps = []
        for r in reads:
            t = self.lastw.get(r)
            if t is not None:
                deps.append(t)
        for w in writes:
            t = self.lastw.get(w)
            if t is not None:
                deps.append(t)
            deps.extend(self.readers.get(w, ()))
        return deps

    def _emit_waits(self, eng, deps):
        best = {}
        dmas = []
        for t in deps:
            if t[0] == "dma":
                if t not in self.waited_dma[eng]:
                    self.waited_dma[eng].add(t)
                    dmas.append(t)
            else:
                p, n = t
                if p == eng and (eng == PE or n <= self.cnt[eng] - 3):
                    continue
                if n > best.get(p, 0):
                    best[p] = n
        for p, n in best.items():
            if self.waited.get((eng, p), 0) >= n:
                continue
            self.waited[(eng, p)] = n
            self.q[eng].append(("wait", p, n))
        for t in dmas:
            self.q[eng].append(("waitdma", t[1], t[2]))

    def _record(self, tok, reads, writes):
        for r in reads:
            lst = self.readers.setdefault(r, [])
            if tok[0] == "dma":
                lst[:] = [x for x in lst if not (x[0] == "dma" and x[1] == tok[1] and x[2] <= tok[2] - NDMASEM)]
            else:
                lst[:] = [x for x in lst if x[0] != tok[0]]
            lst.append(tok)
        for w in writes:
            self.lastw[w] = tok
            self.readers[w] = []

    def op(self, eng, fn, reads=(), writes=()):
        self._emit_waits(eng, self._deps(reads, writes))
        self.cnt[eng] += 1
        tok = (eng, self.cnt[eng])
        self.q[eng].append(("op", fn, self.cnt[eng]))
        self._record(tok, reads, writes)
        return tok

    def dma(self, eng, out, in_, reads=(), writes=()):
        deps = self._deps(reads, writes)
        k = self.dcnt[eng]
        self.dcnt[eng] += 1
        if k >= NDMASEM:
            deps.append(("dma", eng, k - NDMASEM))
        self._emit_waits(eng, deps)
        tok = ("dma", eng, k)
        self.q[eng].append(("dma", out, in_, k))
        self._record(tok, reads, writes)
        return tok

    def barrier(self):
        for e in self.q:
            deps = [(p, self.cnt[p]) for p in COMPUTE if self.cnt[p] > 0 and p != e]
            for q_ in self.dcnt:
                lo = max(0, self.dcnt[q_] - NDMASEM)
                deps += [("dma", q_, k) for k in range(lo, self.dcnt[q_])]
            self._emit_waits(e, deps)
        self.lastw = {}
        self.readers = {}

    def emit(self):
        nc = self.nc
        with contextlib.ExitStack() as st:
            sems = {}
            for e in COMPUTE:
                n = max(1, (self.cnt[e] + EPOCH - 1) // EPOCH)
                sems[e] = [st.enter_context(nc.semaphore(f"c_{e}_{i}")) for i in range(n)]
            dsems = {e: [st.enter_context(nc.semaphore(f"d_{e}_{i}")) for i in range(NDMASEM)] for e in self.dcnt}
            block = st.enter_context(nc.Block())

            def run(name):
                def body(eng):
                    for item in self.q[name]:
                        k = item[0]
                        if k == "op":
                            _, fn, n = item
                            fn(eng).then_inc(sems[name][(n - 1) // EPOCH], 1)
                        elif k == "wait":
                            _, p, n = item
                            ep = (n - 1) // EPOCH
                            eng.wait_ge(sems[p][ep], n - ep * EPOCH)
                        elif k == "waitdma":
                            _, q_, kk = item
                            eng.wait_ge(dsems[q_][kk % NDMASEM], 16 * (kk // NDMASEM + 1))
                        else:
                            _, out, in_, kk = item
                            eng.dma_start(out=out, in_=in_).then_inc(dsems[name][kk % NDMASEM], 16)
                return body

            block.tensor(run(PE))
            block.scalar(run(ACT))
            block.vector(run(DVE))
            block.gpsimd(run(POOL))
            block.sync(run(SP))


class Arena:
    def __init__(self, nc, limit):
        self.nc, self.off, self.limit, self.n = nc, 16640, 16640 + limit, 0

    def alloc(self, name, shape, dtype):
        nb = int(np.prod(shape[1:])) * (2 if dtype == BF else 4)
        nb = (nb + 31) // 32 * 32
        assert self.off + nb <= self.limit, (name, self.off, nb, self.limit)
        self.n += 1
        t = self.nc.alloc_sbuf_tensor_at(f"{name}_{self.n}", list(shape), dtype, offset=self.off)
        self.off += nb
        return t


CST = {}


def _build_consts():
    cols = []

    def add(name, arr):
        arr = np.asarray(arr, np.float32)
        if arr.ndim == 1:
            arr = arr[:, None]
        CST[name] = (sum(a.shape[1] for a in cols), arr.shape[1])
        cols.append(arr)

    p = np.arange(128)
    t_, i_ = p[:, None], p[None, :]
    add("ident", np.eye(128))
    add("ones", np.ones((128, 128)))
    add("triF", t_ <= i_)
    add("triB", t_ >= i_)
    add("sF", t_ > i_)
    add("sB", t_ < i_)
    add("negF", np.where(i_ < t_, NEG, 0.0))
    add("negB", np.where(i_ > t_, NEG, 0.0))
    add("m01F", i_ >= t_)
    add("m01B", i_ <= t_)
    add("s01F", i_ > t_)
    add("s01B", i_ < t_)
    add("diffF", np.maximum(i_ - t_, 0))
    add("diffB", np.maximum(t_ - i_, 0))
    add("posv", np.stack([p + 1, 127 - p, 128 - p, p, np.full(128, 128)], 1))
    return np.concatenate(cols, 1)


def _gmasks():
    p = np.arange(128)
    j, i = p[:, None], p[None, :]
    ms = []
    for l in range(7):
        B = 2 ** (l + 1)
        ms.append((((j // B) == (i // B)) & ((j % B) < B // 2) & ((i % B) >= B // 2)).astype(np.float32))
    return np.stack(ms + [m.T for m in ms], 1)


CONSTS = _build_consts()
NCST = CONSTS.shape[1]


def _rope_tables():
    n_freq = 64
    freqs = (10000.0 ** (-np.arange(n_freq, dtype=np.float32) / n_freq)).astype(np.float32)
    row = np.repeat(np.arange(64, dtype=np.float32), 64)
    col = np.tile(np.arange(64, dtype=np.float32), 64)
    lat = np.stack([row[:, None] * freqs, col[:, None] * freqs], 1).astype(np.float32)
    ang = np.concatenate([np.zeros((NCTX, 2, n_freq), np.float32), lat], 0)
    cos, sin = np.cos(ang), np.sin(ang)
    tq = np.stack([np.stack([cos, cos], 1), np.stack([sin, sin], 1)], 1).reshape(T, 512)
    return tq.astype(np.float32), (tq / 16.0).astype(np.float32)


def _groups():
    g = []
    c = 0
    for name, w in (("RQ", 1024), ("RK", 1024), ("RV", 1024), ("RG", 1024), ("MQ", 512), ("MK", 512),
                    ("MV", 1024), ("MO", 1024), ("MIF", 16), ("GQKV", 3072), ("GG", 1024), ("GBA", 32),
                    ("MG", 3072)):
        g.append((name, c, w))
        c += w
    assert c == IN_DIM
    return g


GROUPS = _groups()


class Builder:
    def __init__(self, nc, debug=(), layers=DEPTH, upto="all", ext_in=()):
        self.nc = nc
        self.debug = set(debug)
        self.ext_in = set(ext_in)
        self.S = Sched(nc)
        self.layers = layers
        self.upto = upto
        self.A = Arena(nc, 206 * 1024)
        self.uid = 0
        self._dram()
        self._psum()

    def _din(self, name, shape, dt=F32):
        return self.nc.dram_tensor(name, list(shape), dt, kind="ExternalInput").ap()

    def _dscr(self, name, shape, dt):
        kind = "ExternalOutput" if name in self.debug else ("ExternalInput" if name in self.ext_in else "Internal")
        return self.nc.dram_tensor(name, list(shape), dt, kind=kind).ap()

    def _dram(self):
        i = self._din
        self.xz = i("xz", [T, D])
        self.cc = i("cc", [128, 8, 2])
        self.w_ada = i("w_ada", [DEPTH, D, 6 * D])
        self.b_ada = i("b_ada", [DEPTH, 128, 48])
        self.w_in = i("w_in", [DEPTH, D, IN_DIM])
        self.conv_w = i("conv_w", [DEPTH, 128, 24, 5])
        self.smallp = i("smallp", [DEPTH, 64])
        self.mnorm_w = i("mlstm_norm_w", [DEPTH, D])
        self.gnorm_w = i("gdn_norm_w", [DEPTH, 128])
        self.w_branch = i("w_branch", [DEPTH, 3, D, D])
        self.w_out = i("w_out", [DEPTH, D, D])
        self.ln1_g = i("ln1_g", [DEPTH, D]); self.ln1_b = i("ln1_b", [DEPTH, D])
        self.ln2_g = i("ln2_g", [DEPTH, D]); self.ln2_b = i("ln2_b", [DEPTH, D])
        self.w_mlp1 = i("w_mlp1", [DEPTH, D, DFF]); self.w_mlp2 = i("w_mlp2", [DEPTH, DFF, D])
        self.cst_d = i("consts", [128, NCST])
        self.ropeq_d = i("ropeq", [T, 512]); self.ropek_d = i("ropek", [T, 512])
        self.gmask_d = i("gmask", [128, 14, 128])
        self.out = self.nc.dram_tensor("out", [NLAT, D], F32, kind="ExternalOutput").ap()
        s = self._dscr
        self.scr = {}
        for name, w, dt in (("RQ", 1024, BF), ("RK", 1024, BF), ("RV", 1024, BF), ("RG", 1024, BF),
                            ("MQ", 512, BF), ("MK", 512, BF), ("MV", 1024, BF), ("MO", 1024, BF),
                            ("MIF", 16, F32), ("GQ", 1024, BF), ("GK", 1024, BF), ("GV", 1024, BF),
                            ("GG", 1024, BF), ("GBA", 32, F32), ("MG", 3072, BF),
                            ("OB_R", 1024, F32), ("OB_M", 1024, F32), ("OB_G", 1024, F32),
                            ("Y_R", 1024, BF), ("Y_M", 1024, BF), ("Y_G", 1024, BF),
                            ("X1", 1024, F32), ("X2", 1024, F32)):
            self.scr[name] = s(name, [T, w], dt)

    def _psum(self):
        self.ps = [self.nc.alloc_psum_tensor(f"ps{i}", [128, 512], F32) for i in range(8)]
        self._ring = 0

    def ring(self):
        self._ring = (self._ring + 1) % 4
        return 4 + self._ring

    def cst(self, name, rows=128):
        o, w = CST[name]
        return self.cstt[0:rows, o:o + w]

    def hk(self, t):
        return [("hT", t, c) for c in range(8)]

    def key(self, base):
        self.uid += 1
        return (base, self.uid)

    def mm(self, out, lhsT, rhs, start, stop, reads, writes):
        self.S.op(PE, lambda e: e.matmul(out, lhsT, rhs, start=start, stop=stop), reads, writes)

    def tr(self, out, in_, ident, reads, writes):
        self.S.op(PE, lambda e: e.transpose(out, in_, ident), reads, writes)

    def act(self, out, in_, func, reads, writes, bias=0.0, scale=1.0, eng=ACT):
        self.S.op(ACT, lambda e: e.activation(out, in_, func, bias=bias, scale=scale), reads, writes)

    def tt(self, eng, out, a, b, op, reads, writes):
        self.S.op(eng, lambda e: e.tensor_tensor(out, a, b, op), reads, writes)

    def ts(self, eng, out, a, s1, s2, op0, op1, reads, writes):
        if s2 is None:
            self.S.op(eng, lambda e: e.tensor_scalar(out, a, s1, None, op0), reads, writes)
        else:
            self.S.op(eng, lambda e: e.tensor_scalar(out, a, s1, s2, op0, op1), reads, writes)

    def stt(self, out, a, s, b, op0, op1, reads, writes):
        self.S.op(DVE, lambda e: e.scalar_tensor_tensor(out, a, s, b, op0, op1), reads, writes)

    def cp(self, eng, out, in_, reads, writes):
        if eng == ACT:
            self.S.op(ACT, lambda e: e.copy(out, in_), reads, writes)
        else:
            self.S.op(eng, lambda e: e.tensor_copy(out, in_), reads, writes)

    def dma(self, out, in_, reads, writes, q=SP):
        return self.S.dma(q, out, in_, reads, writes)

    def store(self, out, in_, reads, writes):
        return self.S.dma(STOREQ, out, in_, reads, writes)

    def dbg(self, name, ap, shape, reads):
        if name in self.debug:
            d = self.nc.dram_tensor(name, list(shape), F32, kind="ExternalOutput").ap()
            self.dma(d, ap, reads, [("dbgout", name)])

    def setup(self):
        A = self.A
        self.cstt = A.alloc("cst", [128, NCST], F32)
        self.dma(self.cstt[:, :], self.cst_d, [], ["cst"])
        self.identb = A.alloc("identb", [128, 128], BF)
        self.cp(DVE, self.identb[:, :], self.cst("ident"), ["cst"], ["identb"])
        self.onesb = A.alloc("onesb", [128, 128], BF)
        self.cp(DVE, self.onesb[:, :], self.cst("ones"), ["cst"], ["onesb"])
        self.cct = A.alloc("cct", [128, 8, 2], F32)
        self.dma(self.cct[:, :, :], self.cc, [], ["cct"])
        self.scs = A.alloc("scs", [128, 8, 2], F32)
        self.act(self.scs[:, :, :], self.cct[:, :, :], AF.Silu, ["cct"], ["scs"])
        self.modc = A.alloc("modc", [128, 48, 2], F32)
        self.ops = A.alloc("ops", [128, 2, 8, 2], F32)
        self.gbc = A.alloc("gbc", [128, 4, 1024], F32)
        self.smp = A.alloc("smp", [128, 64], F32)
        self.persist_end = A.off

    def phase_ada(self, l):
        A, S = self.A, self.S
        A.off = self.persist_end
        badat = A.alloc("badat", [128, 48], F32)
        self.dma(badat[:, :], self.b_ada[l], [], ["badat"])
        self.dma(self.smp[:, :], self.smallp[l:l + 1, :].partition_broadcast(128), [], ["smp"])
        wsl = [A.alloc("wada", [128, 8, 768], F32) for _ in range(2)]
        wv = self.w_ada[l].rearrange("(kc p) c -> p kc c", p=128)
        psA = self.ps[0]
        psAk = ("ps", 0)
        for s in range(8):
            wt = wsl[s % 2]
            wk = ("wada", s % 2)
            self.dma(wt[:, :, :], wv[:, :, s * 768:(s + 1) * 768], [], [wk])
            for jj in range(6):
                j = s * 6 + jj
                for kc in range(8):
                    self.mm(psA[:, 2 * j:2 * j + 2], wt[:, kc, jj * 128:(jj + 1) * 128], self.scs[:, kc, :],
                            kc == 0, kc == 7, [wk, "scs"], [psAk])
        self.tt(DVE, self.modc[:, :, :], psA[:, 0:96].rearrange("p (j v) -> p j v", v=2),
                badat[:, :].unsqueeze(2).broadcast_to([128, 48, 2]), ALU.add, [psAk, "badat"], ["modc"])
        for sub in range(2):
            j0 = 8 + 24 * sub
            self.ts(DVE, self.ops[:, sub, :, :], self.modc[:, j0:j0 + 8, :], 1.0, None, ALU.add, None, ["modc"], ["ops"])
        tmp = [A.alloc("gtmp", [128, 128], F32) for _ in range(2)]
        n = 0
        for sub in range(2):
            for v in range(2):
                gi = sub * 2 + v
                pb = 1 + (gi % 2) * 2
                pst = self.ps[pb: pb + 2]
                for c in range(8):
                    tk = ("gtmp", n % 2)
                    col = self.modc[:, 16 + 24 * sub + c, v:v + 1]
                    self.ts(DVE, tmp[n % 2][:, :], self.cst("ones"), col, None, ALU.mult, None, ["cst", "modc"], [tk])
                    self.mm(pst[c // 4][:, (c % 4) * 128:(c % 4 + 1) * 128], tmp[n % 2][:, :], self.cst("ident"),
                            True, True, [tk, "cst"], [("ps", pb + c // 4)])
                    n += 1
                for hh in range(2):
                    self.cp(ACT, self.gbc[:, gi, hh * 512:(hh + 1) * 512], pst[hh][:, :], [("ps", pb + hh)], [("gbc", gi, hh)])
        S.barrier()
        self.dbg("dbg_modc", self.modc[:, :, :].rearrange("p j v -> p (j v)"), [128, 96], [])
        self.dbg("dbg_gbc", self.gbc[:, :, :].rearrange("p g d -> p (g d)"), [128, 4096], [])

    def ln_stats(self, xt, xk, st, mv, rs, uk):
        S = self.S
        for hh in range(2):
            S.op(DVE, lambda e, hh=hh: e.bn_stats(st[:, hh, :], xt[:, hh * 512:(hh + 1) * 512]), [xk], [("st", uk)])
        S.op(DVE, lambda e: e.bn_aggr(mv[:, :], st[:, :, :].rearrange("p a b -> p (a b)")), [("st", uk)], [("mv", uk)])
        self.act(rs[:, 2:3], mv[:, 1:2], AF.Ln, [("mv", uk)], [("rs2", uk)], bias=LN_EPS)
        self.act(rs[:, 0:1], rs[:, 2:3], AF.Exp, [("rs2", uk)], [("rs0", uk)], scale=-0.5)
        self.ts(DVE, rs[:, 1:2], mv[:, 0:1], rs[:, 0:1], -1.0, ALU.mult, ALU.mult, [("mv", uk), ("rs0", uk)], [("rs1", uk)])

    def phase_ln1(self, l, src, hT):
        A, S = self.A, self.S
        xt = [A.alloc("xt", [128, 1024], F32) for _ in range(2)]
        xn = [A.alloc("xn", [128, 1024], F32) for _ in range(2)]
        st = [A.alloc("st", [128, 2, 6], F32) for _ in range(2)]
        mv = [A.alloc("mv", [128, 2], F32) for _ in range(2)]
        rs = [A.alloc("rs", [128, 4], F32) for _ in range(2)]
        import os
        STEPS = int(os.environ.get("LN1_STEPS", "9"))
        evm = os.environ.get("EVM", "split")
        EV = (lambda c: True) if evm == "dve" else ((lambda c: False) if evm == "act" else (lambda c: c // 4 == 0))
        for t in range(int(os.environ.get("LN1_NT", NT))):
            b = t % 2
            v = 1 if t < 2 else 0
            self.dma(xt[b][:, :], src[t * 128:(t + 1) * 128, :], [], [("xt", b)])
            if STEPS < 2:
                continue
            self.ln_stats(xt[b], ("xt", b), st[b], mv[b], rs[b], b)
            if STEPS < 4:
                continue
            self.act(xn[b][:, :], xt[b][:, :], AF.Identity, [("xt", b), ("rs0", b), ("rs1", b)], [("xn", b)],
                     bias=rs[b][:, 1:2], scale=rs[b][:, 0:1])
            if STEPS < 5:
                continue
            for c in range(8):
                pt = self.ps[(t % 2) * 2 + c // 4]
                pk = ("ps", (t % 2) * 2 + c // 4)
                self.tr(pt[:, (c % 4) * 128:(c % 4 + 1) * 128], xn[b][:, c * 128:(c + 1) * 128], self.cst("ident"),
                        [("xn", b), "cst"], [pk])
            if STEPS < 6:
                continue
            for c in range(8):
                pt = self.ps[(t % 2) * 2 + c // 4]
                pk = ("ps", (t % 2) * 2 + c // 4)
                o = hT[:, c, t * 128:(t + 1) * 128]
                i_ = pt[:, (c % 4) * 128:(c % 4 + 1) * 128]
                sc_ = self.ops[:, 0, c, v:v + 1]
                sh_ = self.modc[:, c, v:v + 1]
                lock = ["evlock"] if os.environ.get("EVLOCK") else []
                if EV(c):
                    self.ts(DVE, o, i_, sc_, sh_, ALU.mult, ALU.add, [pk, "modc", "ops"], [("hT", t, c)] + lock)
                else:
                    self.act(o, i_, AF.Identity, [pk, "modc", "ops"], [("hT", t, c)] + lock, bias=sh_, scale=sc_)

    def phase_inproj(self, l, hT):
        A, S = self.A, self.S
        wg = [A.alloc("wg", [128, 8, 512], BF) for _ in range(2)]
        stg = [A.alloc("stg", [128, 512], BF) for _ in range(4)]
        stf = [A.alloc("stf", [128, 32], F32) for _ in range(2)]
        rp = [A.alloc("rp", [128, 512], F32) for _ in range(6)]
        t12 = [A.alloc("t12", [128, 4, 256], F32) for _ in range(2)]
        xc = A.alloc("xc", [128, 4, T + 8], BF)
        cwt = A.alloc("cwt", [128, 24, 5], F32)
        dg = A.alloc("dg", [128, 20, 128], BF)
        sl = [A.alloc("sl", [128, 512], F32) for _ in range(2)]
        sq = [A.alloc("sq", [128, 512], F32) for _ in range(2)]
        ssn = [A.alloc("ssn", [128, 12], F32) for _ in range(2)]
        self.dma(cwt[:, :, :], self.conv_w[l], [], ["cwt"])
        S.op(POOL, lambda e: e.memset(xc[:, :, :], 0.0), [], [("xc", i) for i in range(4)])
        wv = self.w_in[l].rearrange("(kc p) c -> p kc c", p=128)
        gi = 0
        si = 0
        for name, c0, width in GROUPS:
            if getattr(self, "only_groups", None) and name not in self.only_groups:
                continue
            nsub = max(1, width // 512)
            w_ = min(width, 512)
            for sub in range(nsub):
                cs = c0 + sub * 512
                wb = wg[gi % 2]
                wk = ("wg", gi % 2)
                gi += 1
                self.dma(wb[:, :, 0:w_], wv[:, :, cs:cs + w_], [], [wk], q=POOL)
                if name == "GQKV":
                    self._gdn_group(l, sub, wb, wk, hT, xc, cwt, dg, sl, sq, ssn, stg)
                    continue
                for t in range(NT):
                    pt = self.ps[4 + t % 4]
                    pk = ("ps", 4 + t % 4)
                    for kc in range(8):
                        self.mm(pt[:, 0:w_], hT[:, kc, t * 128:(t + 1) * 128], wb[:, kc, 0:w_], kc == 0, kc == 7,
                                self.hk(t) + [wk], [pk])
                    rows = slice(t * 128, (t + 1) * 128)
                    if name in ("MIF", "GBA"):
                        sb = stf[si % 2]; sk = ("stf", si % 2); si += 1
                        self.cp(DVE, sb[:, 0:w_], pt[:, 0:w_], [pk], [sk])
                        self.store(self.scr[name][rows, :], sb[:, 0:w_], [sk], [(name, t)])
                        continue
                    sb = stg[si % 4]; sk = ("stg", si % 4); si += 1
                    dst = self.scr[name][rows, sub * 512:(sub + 1) * 512]
                    if name in ("RQ", "RK"):
                        b = t % 2
                        rb = t % 6
                        tab = self.ropeq_d if name == "RQ" else self.ropek_d
                        self.dma(rp[rb][:, :], tab[rows, :], [], [("rp", rb)])
                        psv = pt[:, :].rearrange("p (g ab f) -> p g ab f", g=4, ab=2)
                        cosv = rp[rb][:, 0:256].rearrange("p (g f) -> p g f", g=4)
                        sinv = rp[rb][:, 256:512].rearrange("p (g f) -> p g f", g=4)
                        sbv = sb[:, :].rearrange("p (g ab f) -> p g ab f", g=4, ab=2)
                        tk = ("t12", b)
                        t1, t2, t3, t4 = [t12[b][:, i, :].rearrange("p (g f) -> p g f", g=4) for i in range(4)]
                        a_, b_ = psv[:, :, 0, :], psv[:, :, 1, :]
                        self.tt(DVE, t1, a_, cosv, ALU.mult, [pk, ("rp", rb)], [(tk, 0)])
                        self.tt(DVE, t2, b_, sinv, ALU.mult, [pk, ("rp", rb)], [(tk, 1)])
                        self.tt(DVE, t3, b_, cosv, ALU.mult, [pk, ("rp", rb)], [(tk, 2)])
                        self.tt(DVE, t4, a_, sinv, ALU.mult, [pk, ("rp", rb)], [(tk, 3)])
                        self.tt(POOL, sbv[:, :, 0, :], t1, t2, ALU.subtract, [(tk, 0), (tk, 1)], [sk])
                        self.tt(POOL, sbv[:, :, 1, :], t3, t4, ALU.add, [(tk, 2), (tk, 3)], [sk])
                        self.store(dst, sb[:, :], [sk], [(name, t, sub)])
                        continue
                    if name in ("RG", "GG"):
                        self.act(sb[:, :], pt[:, :], AF.Silu, [pk], [sk])
                    elif name in ("MO", "MG"):
                        self.act(sb[:, :], pt[:, :], AF.Sigmoid, [pk], [sk])
                    elif name == "MQ":
                        self.act(sb[:, :], pt[:, :], AF.Copy, [pk], [sk], scale=128 ** -0.5)
                    elif t % 2 == 0:
                        self.cp(ACT, sb[:, :], pt[:, :], [pk], [sk])
                    else:
                        self.cp(DVE, sb[:, :], pt[:, :], [pk], [sk])
                    self.store(dst, sb[:, :], [sk], [(name, t, sub)])

    def _gdn_group(self, l, sub, wb, wk, hT, xc, cwt, dg, sl, sq, ssn, stg):
        S = self.S
        for ct in range(4):
            for tap in range(5):
                self.ts(POOL, dg[:, ct * 5 + tap, :], self.cst("ident"), cwt[:, sub * 4 + ct, tap:tap + 1], None,
                        ALU.mult, None, ["cst", "cwt"], [("dg", ct)])
        blocks = [(0, 256)] + [(256 + 512 * i, 512) for i in range(8)]
        n = 0
        for (t0, tw) in blocks:
            col0 = 2 + t0 if t0 < NCTX else 6 + t0
            for ct in range(4):
                pt = self.ps[n % 4]
                pk = ("ps", n % 4)
                for kc in range(8):
                    self.mm(pt[:, 0:tw], wb[:, kc, ct * 128:(ct + 1) * 128], hT[:, kc, t0:t0 + tw], kc == 0, kc == 7,
                            [wk] + sum([self.hk(t0 // 128 + i) for i in range(tw // 128)], []), [pk])
                self.cp(ACT if n % 2 == 0 else DVE, xc[:, ct, col0:col0 + tw], pt[:, 0:tw], [pk], [("xc", ct)])
                n += 1
        which = "GQ" if sub < 2 else ("GK" if sub < 4 else "GV")
        for t in range(NT):
            base = (2 + t * 128 if t < 2 else 6 + t * 128) - 2
            pt = self.ps[4 + t % 4]
            pk = ("ps", 4 + t % 4)
            for ct in range(4):
                for tap in range(5):
                    self.mm(pt[:, ct * 128:(ct + 1) * 128], xc[:, ct, base + tap:base + tap + 128], dg[:, ct * 5 + tap, :],
                            tap == 0, tap == 4, [("xc", ct), ("dg", ct)], [pk])
            b = t % 2
            sb = stg[t % 4]; sk = ("stg", t % 4)
            rows = slice(t * 128, (t + 1) * 128)
            dst = self.scr[which][rows, (sub % 2) * 512:(sub % 2 + 1) * 512]
            if which == "GV":
                self.act(sb[:, :], pt[:, :], AF.Silu, [pk], [sk])
            else:
                self.act(sl[b][:, :], pt[:, :], AF.Silu, [pk], [("sl", b)])
                self.tt(POOL, sq[b][:, :], sl[b][:, :], sl[b][:, :], ALU.mult, [("sl", b)], [("sq", b)])
                S.op(DVE, lambda e, b=b: e.tensor_reduce(ssn[b][:, 0:4], sq[b][:, :].rearrange("p (h f) -> p h f", h=4), AX.X, ALU.add),
                     [("sq", b)], [("ss", b)])
                self.act(ssn[b][:, 4:8], ssn[b][:, 0:4], AF.Ln, [("ss", b)], [("ssl", b)], bias=RMS_EPS)
                qs = math.log(128 ** -0.5) if which == "GQ" else 0.0
                self.act(ssn[b][:, 8:12], ssn[b][:, 4:8], AF.Exp, [("ssl", b)], [("ssr", b)], scale=-0.5, bias=qs)
                self.tt(DVE, sb[:, :].rearrange("p (h f) -> p h f", h=4), sl[b][:, :].rearrange("p (h f) -> p h f", h=4),
                        ssn[b][:, 8:12].unsqueeze(2).broadcast_to([128, 4, 128]), ALU.mult, [("sl", b), ("ssr", b)], [sk])
            self.store(dst, sb[:, :], [sk], [(which, t, sub)])

    MIX = {"R": dict(H=4, nkt=2, dv=256, dvp=256, q="RQ", k="RK", v="RV", gate="RG", ob="OB_R", y="Y_R"),
           "M": dict(H=4, nkt=1, dv=256, dvp=257, q="MQ", k="MK", v="MV", gate="MO", ob="OB_M", y="Y_M"),
           "G": dict(H=8, nkt=1, dv=128, dvp=128, q="GQ", k="GK", v="GV", gate="GG", ob="OB_G", y="Y_G")}

    def scan_setup(self, l):
        A = self.A
        sp_ = self.smp
        self.prm = A.alloc("prm", [128, 64], F32)
        prm = self.prm
        self.act(prm[:, 0:8], sp_[:, 0:8], AF.Exp, ["smp"], ["prm_r0"])
        self.ts(DVE, prm[:, 0:8], prm[:, 0:8], -1.0, None, ALU.mult, None, ["prm_r0"], ["prm_r"])
        self.act(prm[:, 8:24], sp_[:, 24:40], AF.Exp, ["smp"], ["prm_g0"])
        self.ts(DVE, prm[:, 8:24], prm[:, 8:24], -1.0, None, ALU.mult, None, ["prm_g0"], ["prm_g"])
        self.retDT = A.alloc("retDT", [128, 8, 128], F32)
        self.retc = A.alloc("retc", [128, 8, 3], F32)
        o, _ = CST["posv"]
        for d_ in range(2):
            sfx = "F" if d_ == 0 else "B"
            for h in range(4):
                i = d_ * 4 + h
                lgc = prm[:, i:i + 1]
                self.act(self.retDT[:, i, :], self.cst("diff" + sfx), AF.Exp, ["cst", "prm_r"], [("retDT0", i)], scale=lgc)
                self.tt(DVE, self.retDT[:, i, :], self.retDT[:, i, :], self.cst("m01" + sfx), ALU.mult, [("retDT0", i), "cst"], [("retDT", i)])
                cols = (o + 0, o + 1) if d_ == 0 else (o + 2, o + 3)
                for j, cc in enumerate(cols + (o + 4,)):
                    self.act(self.retc[:, i, j:j + 1], self.cstt[:, cc:cc + 1], AF.Exp, ["cst", "prm_r"], [("retc", i, j)], scale=lgc)

    def phase_scan(self, l, mx):
        A, S = self.A, self.S
        cfg = self.MIX[mx]
        H, nkt, dv, dvp = cfg["H"], cfg["nkt"], cfg["dv"], cfg["dvp"]
        dk = 128 * nkt
        QW = H * dk
        base = A.off
        qt = [A.alloc("qt", [128, QW], BF) for _ in range(2)]
        kt = [A.alloc("kt", [128, QW], BF) for _ in range(2)]
        vt = [A.alloc("vt", [128, H, dvp], BF) for _ in range(2)]
        ks2 = [A.alloc("ks", [128, QW], BF) for _ in range(2)]
        QT2 = [A.alloc("QT", [128, H * nkt, 128], BF) for _ in range(2)]
        KT2 = [A.alloc("KT", [128, H * nkt, 128], BF) for _ in range(2)]
        Sf = A.alloc("Sf", [128, H * nkt, dvp], F32)
        Sb = A.alloc("Sb", [128, H * nkt, dvp], BF)
        pTa = A.alloc("pTa", [128, H, 128], BF)
        acc = [A.alloc("acc", [128, 1024], F32) for _ in range(2)]
        obt = [A.alloc("obt", [128, 1024], F32) for _ in range(2)]
        gt = [A.alloc("gt", [128, 1024], BF) for _ in range(2)]
        yt = [A.alloc("yt", [128, 1024], BF) for _ in range(2)]
        t1 = A.alloc("pp1", [128, 1024], F32)
        sc = [A.alloc("scl", [128, 96], F32) for _ in range(2)]
        sraw = [A.alloc("sraw", [128, 32], F32) for _ in range(2)]
        pst = A.alloc("pst", [128, 8, 6], F32)
        pmv = A.alloc("pmv", [128, 8, 2], F32)
        prs = A.alloc("prs", [128, 16], F32)
        if mx != "R":
            Lm = A.alloc("Lm", [128, H, 128], F32)
            dtb2 = [A.alloc("dtb", [128, H, 128], F32) for _ in range(2)]
        if mx == "M":
            dn = A.alloc("dn", [128, 12], F32)
            nwb = A.alloc("nwb", [128, 1024], F32)
            self.dma(nwb[:, :], self.mnorm_w[l:l + 1, :].partition_broadcast(128), [], ["nwb"])
            for b in range(2):
                S.op(POOL, lambda e, b=b: e.memset(vt[b][:, :, 256:257], 1.0), [], [("vt", b)])
        if mx == "G":
            dts = A.alloc("dts", [128, 8, 128], F32)
            gm = A.alloc("gm", [128, 14, 128], F32)
            self.dma(gm[:, :, :], self.gmask_d, [], ["gm"])
            M0 = A.alloc("M0", [128, 8, 128], F32)
            MTt = A.alloc("MTt", [128, 8, 128], F32)
            MTm = A.alloc("MTm", [128, 6, 8, 128], F32)
            Um = [A.alloc("Um", [128, 8, 128], F32) for _ in range(2)]
            Vm = A.alloc("Vm", [128, 8, 128], F32)
            Pm = A.alloc("Pm", [128, 8, 128], F32)
            self.gdn_Wb2 = [A.alloc("Wb", [128, 8, 128], BF) for _ in range(2)]
            self.ginv = (gm, M0, MTt, MTm, Um, Vm, Pm)
            r0 = A.alloc("r0", [128, 8, 128], BF)
            vn = A.alloc("vn", [128, 8, 128], BF)
            nwb = A.alloc("nwb", [128, 128], F32)
            self.dma(nwb[:, :], self.gnorm_w[l:l + 1, :].partition_broadcast(128), [], ["nwb"])
        ps = self.ps
        PK = lambda i: ("ps", i)
        identb = self.identb
        for d_ in (1, 0):
            sfx = "F" if d_ == 0 else "B"
            order = [0, 1] + list(range(2, NT)) if d_ == 0 else [1, 0] + list(range(NT - 1, 1, -1))
            S.op(POOL, lambda e: e.memset(Sf[:, :, :], 0.0), [], [("Sf", i) for i in range(H)])
            S.op(POOL, lambda e: e.memset(Sb[:, :, :], 0.0), [], ["Sb"])
            def body(n, c):
                b = n % 2
                ks, QT, KT = ks2[b], QT2[b], KT2[b]
                QTk, KTk, ksk = ("QT", b), ("KT", b), ("ks", b)
                if mx != "R":
                    dtb = dtb2[b]
                rows = slice(c * 128, (c + 1) * 128)
                self.dma(qt[b][:, :], self.scr[cfg["q"]][rows, :], [(cfg["q"], c)], [("qt", b)])
                self.dma(kt[b][:, :], self.scr[cfg["k"]][rows, :], [(cfg["k"], c)], [("kt", b)])
                self.dma(vt[b][:, :, 0:dv], self.scr[cfg["v"]][rows, :].rearrange("p (h e) -> p h e", h=H), [(cfg["v"], c)], [("vt", b)])
                if mx == "M":
                    self.dma(sraw[b][:, 0:16], self.scr["MIF"][rows, :], [], [("sraw", b)])
                if mx == "G":
                    self.dma(sraw[b][:, 0:32], self.scr["GBA"][rows, :], [], [("sraw", b)])
                if d_ == 0:
                    self.dma(obt[b][:, :], self.scr[cfg["ob"]][rows, :], [(cfg["ob"], c)], [("obt", b)])
                    self.dma(gt[b][:, :], self.scr[cfg["gate"]][rows, :], [], [("gt", b)])
                for src_, dst_, bank, nm in ((qt[b], QT, 0, "QT"), (kt[b], KT, 1, "KT")):
                    pv = ps[bank][:, :].bitcast(BF)
                    for j in range(H * nkt):
                        self.tr(pv[:, j * 128:(j + 1) * 128], src_[:, j * 128:(j + 1) * 128], identb[:, :],
                                [(nm.lower(), b), "identb"], [PK(bank)])
                    self.cp(ACT if bank == 0 else DVE, dst_[:, :, :].rearrange("p a t -> p (a t)"), pv[:, 0:H * nkt * 128],
                            [PK(bank)], [(nm, b)])
                yield "A"
                s_ = sc[b]
                sk = ("scl", b)
                HB = 4
                nbk = H // HB
                if mx == "R":
                    a_bc = lambda hs_, w: self.retc[:, d_ * 4 + hs_.start:d_ * 4 + hs_.stop, 0:1].broadcast_to([128, hs_.stop - hs_.start, w])
                    a_reads = [("retc", d_ * 4 + h, 0) for h in range(4)]
                    cd_col = lambda h: self.retc[:, d_ * 4 + h, 2:3]
                    cd_reads = [("retc", d_ * 4 + h, 2) for h in range(4)]
                    s_bc = self.retc[:, d_ * 4:d_ * 4 + 4, 1:2].broadcast_to([128, 4, 256])
                    s_reads = [("retc", d_ * 4 + h, 1) for h in range(4)]
                    DTall = lambda hs_: self.retDT[:, d_ * 4 + hs_.start:d_ * 4 + hs_.stop, :]
                    dt_reads = [("retDT", d_ * 4 + h) for h in range(4)]
                else:
                    nh = H
                    raw = sraw[b]
                    if mx == "M":
                        self.tt(DVE, s_[:, 0:4], raw[:, d_ * 4:d_ * 4 + 4], self.smp[:, 8 + d_ * 4:12 + d_ * 4], ALU.add, [("sraw", b), "smp"], [(sk, "ig")])
                        self.tt(DVE, s_[:, 4:8], raw[:, 8 + d_ * 4:12 + d_ * 4], self.smp[:, 16 + d_ * 4:20 + d_ * 4], ALU.add, [("sraw", b), "smp"], [(sk, "x")])
                        self.act(s_[:, 8:12], s_[:, 4:8], AF.Exp, [(sk, "x")], [(sk, "e")], scale=-1.0)
                        self.act(s_[:, 12:16], s_[:, 8:12], AF.Ln, [(sk, "e")], [(sk, "l")], bias=1.0)
                        self.ts(DVE, s_[:, 16:20], s_[:, 12:16], -1.0, None, ALU.mult, None, [(sk, "l")], [(sk, "g")])
                        self.act(s_[:, 20:24], s_[:, 0:4], AF.Exp, [(sk, "ig")], [(sk, "eig")])
                        gall = s_[:, 16:20]
                    else:
                        self.act(s_[:, 0:8], raw[:, d_ * 8:d_ * 8 + 8], AF.Sigmoid, [("sraw", b)], [(sk, "beta")])
                        self.ts(DVE, s_[:, 88:96], s_[:, 0:8], -1.0, None, ALU.mult, None, [(sk, "beta")], [(sk, "nbeta")])
                        self.tt(DVE, s_[:, 8:16], raw[:, 16 + d_ * 8:24 + d_ * 8], self.smp[:, 40 + d_ * 8:48 + d_ * 8], ALU.add, [("sraw", b), "smp"], [(sk, "x")])
                        self.act(s_[:, 16:24], s_[:, 8:16], AF.Exp, [(sk, "x")], [(sk, "e")])
                        self.act(s_[:, 24:32], s_[:, 16:24], AF.Ln, [(sk, "e")], [(sk, "l")], bias=1.0)
                        self.tt(DVE, s_[:, 32:40], s_[:, 24:32], self.prm[:, 8 + d_ * 8:16 + d_ * 8], ALU.mult, [(sk, "l"), "prm_g"], [(sk, "g")])
                        gall = s_[:, 32:40]
                    self.mm(ps[2][:, 0:nh], self.cst("tri" + sfx), gall, True, True, ["cst", (sk, "g")], [PK(2)])
                    self.mm(ps[2][:, nh:2 * nh], self.cst("ones"), gall, True, True, ["cst", (sk, "g")], [PK(2)])
                    self.cp(ACT, s_[:, 40:40 + 2 * nh], ps[2][:, 0:2 * nh], [PK(2)], [(sk, "B")])
                    Bc, Ba = s_[:, 40:40 + nh], s_[:, 40 + nh:40 + 2 * nh]
                    self.act(s_[:, 56:56 + nh], Bc, AF.Exp, [(sk, "B")], [(sk, "a")])
                    self.act(s_[:, 64:64 + nh], Ba, AF.Exp, [(sk, "B")], [(sk, "cd")])
                    self.tt(DVE, s_[:, 72:72 + nh], Ba, Bc, ALU.subtract, [(sk, "B")], [(sk, "s0")])
                    if mx == "M":
                        self.tt(DVE, s_[:, 72:72 + nh], s_[:, 72:72 + nh], s_[:, 0:4], ALU.add, [(sk, "s0"), (sk, "ig")], [(sk, "s1")])
                    else:
                        self.ts(DVE, s_[:, 80:88], s_[:, 56:64], -1.0, None, ALU.mult, None, [(sk, "a")], [(sk, "na")])
                    self.act(s_[:, 72:72 + nh], s_[:, 72:72 + nh], AF.Exp, [(sk, "s0"), (sk, "s1")], [(sk, "s")])
                    a_bc = lambda hs_, w: s_[:, 56 + hs_.start:56 + hs_.stop].unsqueeze(2).broadcast_to([128, hs_.stop - hs_.start, w])
                    a_reads = [(sk, "a")]
                    cd_col = lambda h: s_[:, 64 + h:65 + h]
                    cd_reads = [(sk, "cd")]
                    s_bc = s_[:, 72:72 + nh].unsqueeze(2).broadcast_to([128, nh, 128])
                    s_reads = [(sk, "s")]
                    yield "A"
                    g0 = 16 if mx == "M" else 32
                    for bk in range(nbk):
                        hs = slice(bk * HB, (bk + 1) * HB)
                        self.tt(DVE, Lm[:, hs, :], self.cst("s" + sfx).unsqueeze(1).broadcast_to([128, HB, 128]),
                                s_[:, g0 + hs.start:g0 + hs.stop].unsqueeze(2).broadcast_to([128, HB, 128]), ALU.mult, ["cst", (sk, "g")], [("Lm", bk)])
                        for q in range(HB):
                            h = bk * HB + q
                            self.mm(ps[2 + bk][:, q * 128:(q + 1) * 128], Lm[:, h, :], self.cst("tri" + sfx), True, False, [("Lm", bk), "cst"], [PK(2 + bk)])
                            self.mm(ps[2 + bk][:, q * 128:(q + 1) * 128], self.cst("ident"), self.cst("neg" + sfx), False, True, ["cst"], [PK(2 + bk)])
                        self.act(dtb[:, hs, :].rearrange("p a t -> p (a t)"), ps[2 + bk][:, :], AF.Exp, [PK(2 + bk)], [("dtb", b, bk)])
                        if mx == "M":
                            self.tt(POOL, dtb[:, hs, :], dtb[:, hs, :], s_[:, 20:24].unsqueeze(2).broadcast_to([128, 4, 128]), ALU.mult,
                                    [("dtb", b, bk), (sk, "eig")], [("dtb", b, bk)])
                        else:
                            self.tt(POOL, dts[:, hs, :], dtb[:, hs, :], self.cst("s01" + sfx).unsqueeze(1).broadcast_to([128, HB, 128]), ALU.mult,
                                    [("dtb", b, bk), "cst"], [("dts", h) for h in range(hs.start, hs.stop)])
                        yield "A"
                    DTall = lambda hs_: dtb[:, hs_, :]
                    dt_reads = [("dtb", b, bk) for bk in range(nbk)]
                self.tt(POOL, ks[:, :].rearrange("p (h e) -> p h e", h=H), kt[b][:, :].rearrange("p (h e) -> p h e", h=H), s_bc,
                        ALU.mult, [("kt", b)] + s_reads, [ksk])
                yield "A"
                if mx == "G":
                    Wt = self.gdn_Wb2[b]
                    yield from self._gdn_inverse(KT, KTk, s_, sk, dts, d_, Wt, b)
                yield "A_DONE"
                A_ = acc[b]
                ak = ("acc", b)
                ring = self.ring
                for bk in range(nbk):
                    hs = slice(bk * HB, (bk + 1) * HB)
                    bn = ring()
                    for q in range(HB):
                        h = bk * HB + q
                        for kk in range(nkt):
                            self.mm(ps[bn][:, q * 128:(q + 1) * 128], KT[:, h * nkt + kk, :], QT[:, h * nkt + kk, :], kk == 0, kk == nkt - 1, [KTk, QTk], [PK(bn)])
                    self.tt(DVE, pTa[:, hs, :].rearrange("p a t -> p (a t)"), ps[bn][:, :], DTall(hs).rearrange("p a t -> p (a t)"), ALU.mult,
                            [PK(bn)] + dt_reads, [("pTa", bk)])
                    yield "B"
                vsrc = lambda h: vt[b][:, h, 0:dv]
                vreads = [("vt", b)]
                if mx == "G":
                    for bk in range(nbk):
                        hs = slice(bk * HB, (bk + 1) * HB)
                        bn = ring()
                        for q in range(HB):
                            h = bk * HB + q
                            self.mm(ps[bn][:, q * 128:(q + 1) * 128], KT[:, h, :], Sb[:, h, :], True, True, [KTk, "Sb"], [PK(bn)])
                        for q in range(HB):
                            h = bk * HB + q
                            self.stt(r0[:, h, :], ps[bn][:, q * 128:(q + 1) * 128], s_[:, 80 + h:81 + h], vt[b][:, h, :], ALU.mult, ALU.add,
                                     [PK(bn), (sk, "na"), ("vt", b)], [("r0", bk)])
                        bn2 = ring()
                        for q in range(HB):
                            h = bk * HB + q
                            self.mm(ps[bn2][:, q * 128:(q + 1) * 128], Wt[:, h, :], r0[:, h, :], True, True, [("Wb", b, bk), ("r0", bk)], [PK(bn2)])
                        self.tt(DVE, vn[:, hs, :], ps[bn2][:, :].rearrange("p (a t) -> p a t", a=HB),
                                s_[:, hs.start:hs.stop].unsqueeze(2).broadcast_to([128, HB, 128]), ALU.mult, [PK(bn2), (sk, "beta")], [("vn", bk)])
                        yield "B"
                    vsrc = lambda h: vn[:, h, :]
                    vreads = [("vn", bk) for bk in range(nbk)]
                hpb = 512 // dv
                for g_ in range(H // hpb):
                    hs = slice(g_ * hpb, (g_ + 1) * hpb)
                    by, bz = ring(), ring()
                    for q in range(hpb):
                        h = g_ * hpb + q
                        self.mm(ps[by][:, q * dv:(q + 1) * dv], pTa[:, h, :], vsrc(h), True, True, [("pTa", h // HB)] + vreads, [PK(by)])
                    for q in range(hpb):
                        h = g_ * hpb + q
                        for kk in range(nkt):
                            self.mm(ps[bz][:, q * dv:(q + 1) * dv], QT[:, h * nkt + kk, :], Sb[:, h * nkt + kk, 0:dv], kk == 0, kk == nkt - 1, [QTk, "Sb"], [PK(bz)])
                    o3 = A_[:, hs.start * dv:hs.stop * dv].rearrange("p (a e) -> p a e", a=hpb)
                    self.tt(DVE, o3, ps[bz][:, :].rearrange("p (a e) -> p a e", a=hpb), a_bc(hs, dv), ALU.mult, [PK(bz)] + a_reads, [(ak, g_)])
                    self.tt(DVE, o3, o3, ps[by][:, :].rearrange("p (a e) -> p a e", a=hpb), ALU.add, [(ak, g_), PK(by)], [(ak, g_)])
                    yield "B"
                nacc = H // hpb
                if mx == "M":
                    bd = ring()
                    for h in range(4):
                        self.mm(ps[bd][:, h:h + 1], pTa[:, h, :], self.onesb[:, 0:1], True, True, [("pTa", 0), "onesb"], [PK(bd)])
                    for h in range(4):
                        self.mm(ps[bd][:, 4 + h:5 + h], QT[:, h, :], Sb[:, h, 256:257], True, True, [QTk, "Sb"], [PK(bd)])
                    self.tt(DVE, dn[:, 0:4], ps[bd][:, 4:8], s_[:, 56:60], ALU.mult, [PK(bd), (sk, "a")], ["dn0"])
                    self.tt(DVE, dn[:, 0:4], dn[:, 0:4], ps[bd][:, 0:4], ALU.add, ["dn0", PK(bd)], ["dn0"])
                    self.act(dn[:, 4:8], dn[:, 0:4], AF.Abs, ["dn0"], ["dn1"])
                    self.ts(DVE, dn[:, 4:8], dn[:, 4:8], 1.0, None, ALU.max, None, ["dn1"], ["dn2"])
                    S.op(DVE, lambda e: e.reciprocal(dn[:, 8:12], dn[:, 4:8]), ["dn2"], ["dn3"])
                    A3_ = A_[:, :].rearrange("p (h e) -> p h e", h=4)
                    self.tt(POOL, A3_, A3_, dn[:, 8:12].unsqueeze(2).broadcast_to([128, 4, 256]), ALU.mult, [(ak, 0), (ak, 1), "dn3"], [(ak, 0), (ak, 1)])
                if mx == "G":
                    for bk in range(nbk):
                        hs = slice(bk * HB, (bk + 1) * HB)
                        bu = ring()
                        for q in range(HB):
                            h = bk * HB + q
                            self.mm(ps[bu][:, q * 128:(q + 1) * 128], ks[:, h * 128:(h + 1) * 128], vn[:, h, :], True, True, [ksk, ("vn", bk)], [PK(bu)])
                        self.tt(POOL, Sf[:, hs, :], Sf[:, hs, :], s_[:, 64 + hs.start:64 + hs.stop].unsqueeze(2).broadcast_to([128, HB, 128]), ALU.mult,
                                [("Sf", bk), (sk, "cd")], [("Sf", bk)])
                        self.tt(DVE, Sf[:, hs, :], Sf[:, hs, :], ps[bu][:, :].rearrange("p (a t) -> p a t", a=HB), ALU.add, [("Sf", bk), PK(bu)], [("Sf", bk)])
                    self.cp(ACT, Sb[:, :, :], Sf[:, :, :], [("Sf", bk) for bk in range(nbk)], ["Sb"])
                else:
                    for h in range(H):
                        bu = ring()
                        for kk in range(nkt):
                            i = h * nkt + kk
                            self.mm(ps[bu][:, kk * 256:kk * 256 + dvp], ks[:, i * 128:(i + 1) * 128], vt[b][:, h, :], True, True, [ksk, ("vt", b)], [PK(bu)])
                        if nkt == 2:
                            self.stt(Sf[:, h * 2:h * 2 + 2, :].rearrange("p a e -> p (a e)"), Sf[:, h * 2:h * 2 + 2, :].rearrange("p a e -> p (a e)"),
                                     cd_col(h), ps[bu][:, :], ALU.mult, ALU.add, [("Sf", h), PK(bu)] + cd_reads, [("Sf", h)])
                        else:
                            self.stt(Sf[:, h, :], Sf[:, h, :], cd_col(h), ps[bu][:, 0:dvp], ALU.mult, ALU.add, [("Sf", h), PK(bu)] + cd_reads, [("Sf", h)])
                    self.cp(ACT, Sb[:, :, :], Sf[:, :, :], [("Sf", h) for h in range(H)], ["Sb"])
                yield "B"
                akeys = [(ak, g_) for g_ in range(nacc)]
                if d_ == 1:
                    self.store(self.scr[cfg["ob"]][rows, :], A_[:, :], akeys, [(cfg["ob"], c)])
                    return
                self.tt(POOL, A_[:, :], A_[:, :], obt[b][:, :], ALU.add, akeys + [("obt", b)], akeys)
                A3 = A_[:, :].rearrange("p (h e) -> p h e", h=H)
                Y = yt[b]
                if "dbg_acc" in self.debug:
                    if not hasattr(self, "dbg_acc_d"):
                        self.dbg_acc_d = self.nc.dram_tensor("dbg_acc", [T, 1024], F32, kind="ExternalOutput").ap()
                    self.dma(self.dbg_acc_d[rows, :], A_[:, :], akeys, [("dbgacc", c)])
                if mx == "G":
                    self.tt(POOL, t1[:, :], A_[:, :], A_[:, :], ALU.mult, akeys, ["pp1"])
                    S.op(DVE, lambda e: e.tensor_reduce(prs[:, 0:8], t1[:, :].rearrange("p (h e) -> p h e", h=8), AX.X, ALU.add), ["pp1"], ["prs0"])
                    self.act(prs[:, 8:16], prs[:, 0:8], AF.Ln, ["prs0"], ["prs1"], bias=RMS_EPS, scale=1.0 / 128)
                    self.act(prs[:, 0:8], prs[:, 8:16], AF.Exp, ["prs1"], ["prs2"], scale=-0.5)
                    self.tt(DVE, t1[:, :].rearrange("p (h e) -> p h e", h=8), A3, prs[:, 0:8].unsqueeze(2).broadcast_to([128, 8, 128]), ALU.mult, akeys + ["prs2"], ["pp1"])
                    self.tt(POOL, t1[:, :].rearrange("p (h e) -> p h e", h=8), t1[:, :].rearrange("p (h e) -> p h e", h=8),
                            nwb[:, :].unsqueeze(1).broadcast_to([128, 8, 128]), ALU.mult, ["pp1", "nwb"], ["pp1"])
                    self.tt(DVE, Y[:, :], t1[:, :], gt[b][:, :], ALU.mult, ["pp1", ("gt", b)], [("yt", b)])
                else:
                    for h in range(4):
                        S.op(DVE, lambda e, h=h, A_=A_: e.bn_stats(pst[:, h, :], A_[:, h * 256:(h + 1) * 256]), akeys, [("pst", h)])
                        S.op(DVE, lambda e, h=h: e.bn_aggr(pmv[:, h, :], pst[:, h, :]), [("pst", h)], ["pmv"])
                    self.act(prs[:, 0:4], pmv[:, 0:4, 1:2].rearrange("p h o -> p (h o)"), AF.Ln, ["pmv"], ["prs0"], bias=LN_EPS)
                    self.act(prs[:, 4:8], prs[:, 0:4], AF.Exp, ["prs0"], ["prs1"], scale=-0.5)
                    t3 = t1[:, :].rearrange("p (h e) -> p h e", h=4)
                    self.tt(DVE, t3, A3, pmv[:, 0:4, 0:1].broadcast_to([128, 4, 256]), ALU.subtract, akeys + ["pmv"], ["pp1"])
                    self.tt(POOL, t3, t3, prs[:, 4:8].unsqueeze(2).broadcast_to([128, 4, 256]), ALU.mult, ["pp1", "prs1"], ["pp1"])
                    if "dbg_t1" in self.debug:
                        if not hasattr(self, "dbg_t1_d"):
                            self.dbg_t1_d = self.nc.dram_tensor("dbg_t1", [T, 1024], F32, kind="ExternalOutput").ap()
                            self.dbg_pmv_d = self.nc.dram_tensor("dbg_pmv", [T, 16], F32, kind="ExternalOutput").ap()
                            self.dbg_prs_d = self.nc.dram_tensor("dbg_prs", [T, 16], F32, kind="ExternalOutput").ap()
                        self.dma(self.dbg_t1_d[rows, :], t1[:, :], ["pp1"], [("dbgt1", c)])
                        self.dma(self.dbg_pmv_d[rows, 0:8], pmv[:, 0:4, :].rearrange("p a b -> p (a b)"), ["pmv"], [("dbgpmv", c)])
                        self.dma(self.dbg_prs_d[rows, 0:8], prs[:, 0:8], ["prs1", "prs0"], [("dbgprs", c)])
                    if mx == "M":
                        self.tt(POOL, t1[:, :], t1[:, :], nwb[:, :], ALU.mult, ["pp1", "nwb"], ["pp1"])
                    self.tt(DVE, Y[:, :], t1[:, :], gt[b][:, :], ALU.mult, ["pp1", ("gt", b)], [("yt", b)])
                self.store(self.scr[cfg["y"]][rows, :], Y[:, :], [("yt", b)], [(cfg["y"], c)])

            gens = [body(n, c) for n, c in enumerate(order)]
            while next(gens[0]) != "A_DONE":
                pass
            for n in range(len(gens)):
                gA = gens[n + 1] if n + 1 < len(gens) else None
                gB = gens[n]
                doneA, doneB = gA is None, False
                while not (doneA and doneB):
                    if not doneA:
                        if next(gA) == "A_DONE":
                            doneA = True
                    if not doneB:
                        try:
                            next(gB)
                        except StopIteration:
                            doneB = True
            S.barrier()
        A.off = base

    def _gdn_inverse(self, KT, KTk, s_, sk, dts, d_, Wb, pb):
        S, ps = self.S, self.ps
        PK = lambda i: ("ps", i)
        gm, M0, MTt, MTm, Um, Vm, Pm = self.ginv
        fo = 0 if d_ == 0 else 7
        to = 7 if d_ == 0 else 0
        ident = self.cst("ident")
        for hh in range(2):
            bank = hh
            for q in range(4):
                h = hh * 4 + q
                self.mm(ps[bank][:, q * 128:(q + 1) * 128], KT[:, h, :], KT[:, h, :], True, True, [KTk], [PK(bank)])
            for q in range(4):
                h = hh * 4 + q
                self.stt(M0[:, h, :], ps[bank][:, q * 128:(q + 1) * 128], s_[:, 88 + h:89 + h], dts[:, h, :], ALU.mult, ALU.mult,
                         [PK(bank), (sk, "nbeta"), ("dts", h)], [("M0", hh)])
            bank = 2 + hh
            for q in range(4):
                h = hh * 4 + q
                self.tr(ps[bank][:, q * 128:(q + 1) * 128], M0[:, h, :], ident, [("M0", hh), "cst"], [PK(bank)])
            self.cp(ACT, MTt[:, hh * 4:(hh + 1) * 4, :].rearrange("p a t -> p (a t)"), ps[bank][:, :], [PK(bank)], [("MTt", hh)])
            yield "A"
        for lev in range(1, 7):
            self.tt(POOL, MTm[:, lev - 1, :, :], MTt[:, :, :], gm[:, to + lev:to + lev + 1, :].broadcast_to([128, 8, 128]), ALU.mult,
                    [("MTt", 0), ("MTt", 1), "gm"], [("MTm", lev)])
        U = Um[0]
        self.tt(DVE, U[:, :, :], M0[:, :, :], gm[:, fo:fo + 1, :].broadcast_to([128, 8, 128]), ALU.mult, [("M0", 0), ("M0", 1), "gm"], [("Um", 0, 0), ("Um", 0, 1)])
        self.tt(DVE, U[:, :, :], U[:, :, :], ident.unsqueeze(1).broadcast_to([128, 8, 128]), ALU.add, [("Um", 0, 0), ("Um", 0, 1), "cst"], [("Um", 0, 0), ("Um", 0, 1)])
        cur = 0
        for lev in range(1, 7):
            nxt = 1 - cur
            Uc, Un = Um[cur], Um[nxt]
            for hh in range(2):
                hs = slice(hh * 4, (hh + 1) * 4)
                uk = ("Um", cur, hh)
                for q in range(4):
                    h = hh * 4 + q
                    self.tr(ps[hh][:, q * 128:(q + 1) * 128], Uc[:, h, :], ident, [uk, "cst"], [PK(hh)])
                for q in range(4):
                    h = hh * 4 + q
                    self.mm(ps[2 + hh][:, q * 128:(q + 1) * 128], MTm[:, lev - 1, h, :], Uc[:, h, :], True, True, [("MTm", lev), uk], [PK(2 + hh)])
                self.cp(ACT, Vm[:, hs, :].rearrange("p a t -> p (a t)"), ps[hh][:, :], [PK(hh)], [("Vm", hh)])
                self.cp(DVE, Pm[:, hs, :].rearrange("p a t -> p (a t)"), ps[2 + hh][:, :], [PK(2 + hh)], [("Pm", hh)])
            yield "A"
            for hh in range(2):
                hs = slice(hh * 4, (hh + 1) * 4)
                uk = ("Um", cur, hh)
                for q in range(4):
                    h = hh * 4 + q
                    self.mm(ps[hh][:, q * 128:(q + 1) * 128], Vm[:, h, :], Pm[:, h, :], True, True, [("Vm", hh), ("Pm", hh)], [PK(hh)])
                self.tt(DVE, Un[:, hs, :].rearrange("p a t -> p (a t)"), Uc[:, hs, :].rearrange("p a t -> p (a t)"), ps[hh][:, :], ALU.add,
                        [uk, PK(hh)], [("Um", nxt, hh)])
            yield "A"
            cur = nxt
        for hh in range(2):
            hs = slice(hh * 4, (hh + 1) * 4)
            self.cp(ACT if hh == 0 else DVE, Wb[:, hs, :], Um[cur][:, hs, :], [("Um", cur, hh)], [("Wb", pb, hh)])

    def postnorm(self, pbanks, xt, xk, gi, li, ub, st, mv, rs, slot, dst, dstkey):
        uk = ("ub", slot)
        for hh in range(2):
            cs = slice(hh * 512, (hh + 1) * 512)
            self.tt(DVE, ub[:, cs], self.ps[pbanks[hh]][:, :], self.gbc[:, gi, cs], ALU.mult, [("ps", pbanks[hh]), ("gbc", gi, hh)], [(uk, hh)])
            self.stt(ub[:, cs], xt[:, cs], DN_ALPHA, ub[:, cs], ALU.mult, ALU.add, [xk, (uk, hh)], [(uk, hh)])
        S = self.S
        for hh in range(2):
            S.op(DVE, lambda e, hh=hh: e.bn_stats(st[:, hh, :], ub[:, hh * 512:(hh + 1) * 512]), [(uk, hh)], [("pst", slot)])
        S.op(DVE, lambda e: e.bn_aggr(mv[:, :], st[:, :, :].rearrange("p a b -> p (a b)")), [("pst", slot)], [("pmv", slot)])
        self.act(rs[:, 2:3], mv[:, 1:2], AF.Ln, [("pmv", slot)], [("prs2", slot)], bias=LN_EPS)
        self.act(rs[:, 0:1], rs[:, 2:3], AF.Exp, [("prs2", slot)], [("prs0", slot)], scale=-0.5)
        self.ts(DVE, rs[:, 1:2], mv[:, 0:1], rs[:, 0:1], -1.0, ALU.mult, ALU.mult, [("pmv", slot), ("prs0", slot)], [("prs1", slot)])
        self.act(ub[:, :], ub[:, :], AF.Identity, [(uk, 0), (uk, 1), ("prs0", slot), ("prs1", slot)], [(uk, 0), (uk, 1)],
                 bias=rs[:, 1:2], scale=rs[:, 0:1])
        self.tt(POOL, ub[:, :], ub[:, :], self.lnl[:, 0, :], ALU.mult, [(uk, 0), (uk, 1), ("lnl", 0)], [(uk, 0), (uk, 1)])
        self.tt(POOL, ub[:, :], ub[:, :], self.lnl[:, 1, :], ALU.add, [(uk, 0), (uk, 1), ("lnl", 1)], [(uk, 0), (uk, 1)])
        return self.store(dst, ub[:, :], [(uk, 0), (uk, 1)], [dstkey])

    def phase_merge(self, l):
        A, S, ps = self.A, self.S, self.ps
        base = A.off
        last = (l == DEPTH - 1)
        self.lnl = A.alloc("lnl", [128, 2, 1024], F32)
        for j, src_ in enumerate((self.ln1_g, self.ln1_b)):
            self.dma(self.lnl[:, j, :], src_[l:l + 1, :].partition_broadcast(128), [], [("lnl", j)])
        wbr = A.alloc("wbr", [128, 3, 8, 1024], BF)
        wo = A.alloc("wo", [128, 8, 1024], BF)
        for br in range(3):
            for hh in range(2):
                self.dma(wbr[:, br, :, hh * 512:(hh + 1) * 512],
                         self.w_branch[l, br].rearrange("(kc p) c -> p kc c", p=128)[:, :, hh * 512:(hh + 1) * 512], [], [("wbr", br)], q=POOL)
        for hh in range(2):
            self.dma(wo[:, :, hh * 512:(hh + 1) * 512], self.w_out[l].rearrange("(kc p) c -> p kc c", p=128)[:, :, hh * 512:(hh + 1) * 512], [], ["wo"], q=POOL)
        yin = [[A.alloc("yin", [128, 1024], BF) for _ in range(3)] for _ in range(2)]
        mg = [A.alloc("mg", [128, 3072], BF) for _ in range(2)]
        xt = [A.alloc("xt", [128, 1024], F32) for _ in range(2)]
        yT = [A.alloc("yT", [128, 8, 128], BF) for _ in range(3)]
        mrg = A.alloc("mrg", [128, 1024], F32)
        mt2 = A.alloc("mt2", [128, 512], F32)
        mrb = A.alloc("mrb", [128, 1024], BF)
        mT = A.alloc("mT", [128, 8, 128], BF)
        ub = [A.alloc("ub", [128, 1024], F32) for _ in range(2)]
        st = [A.alloc("st", [128, 2, 6], F32) for _ in range(2)]
        mv = [A.alloc("mv", [128, 2], F32) for _ in range(2)]
        rs = [A.alloc("rs", [128, 4], F32) for _ in range(2)]
        src = self.xz if l == 0 else self.scr["X2"]
        names = ("Y_R", "Y_M", "Y_G")
        n = 0
        for t in range(2 if last else 0, NT):
            b = n % 2
            n += 1
            rows = slice(t * 128, (t + 1) * 128)
            v = 1 if t < 2 else 0
            for br in range(3):
                self.dma(yin[b][br][:, :], self.scr[names[br]][rows, :], [], [("yin", b, br)])
            self.dma(mg[b][:, :], self.scr["MG"][rows, :], [], [("mg", b)])
            self.dma(xt[b][:, :], src[rows, :], [], [("xt", b)])
            for br in range(3):
                bank = br % 2
                pv = ps[bank][:, :].bitcast(BF)
                for c in range(8):
                    self.tr(pv[:, c * 128:(c + 1) * 128], yin[b][br][:, c * 128:(c + 1) * 128], self.identb[:, :],
                            [("yin", b, br), "identb"], [("ps", bank)])
                self.cp(ACT if br != 1 else DVE, yT[br][:, :, :].rearrange("p a t -> p (a t)"), pv[:, :], [("ps", bank)], [("yT", br)])
            for hh in range(2):
                cs = slice(hh * 512, (hh + 1) * 512)
                for br in range(3):
                    for kc in range(8):
                        self.mm(ps[2 + br][:, :], yT[br][:, kc, :], wbr[:, br, kc, cs], kc == 0, kc == 7, [("yT", br), ("wbr", br)], [("ps", 2 + br)])
                self.tt(DVE, mrg[:, cs], ps[2][:, :], mg[b][:, hh * 512:(hh + 1) * 512], ALU.mult, [("ps", 2), ("mg", b)], [("mrg", hh)])
                self.tt(DVE, mt2[:, :], ps[3][:, :], mg[b][:, 1024 + hh * 512:1024 + (hh + 1) * 512], ALU.mult, [("ps", 3), ("mg", b)], ["mt2"])
                self.tt(DVE, mrg[:, cs], mrg[:, cs], mt2[:, :], ALU.add, [("mrg", hh), "mt2"], [("mrg", hh)])
                self.tt(DVE, mt2[:, :], ps[4][:, :], mg[b][:, 2048 + hh * 512:2048 + (hh + 1) * 512], ALU.mult, [("ps", 4), ("mg", b)], ["mt2"])
                self.tt(DVE, mrb[:, cs], mrg[:, cs], mt2[:, :], ALU.add, [("mrg", hh), "mt2"], [("mrb", hh)])
            pv = ps[5][:, :].bitcast(BF)
            for c in range(8):
                self.tr(pv[:, c * 128:(c + 1) * 128], mrb[:, c * 128:(c + 1) * 128], self.identb[:, :], [("mrb", c // 4), "identb"], [("ps", 5)])
            self.cp(ACT, mT[:, :, :].rearrange("p a t -> p (a t)"), pv[:, :], [("ps", 5)], ["mT"])
            for hh in range(2):
                for kc in range(8):
                    self.mm(ps[6 + hh][:, :], mT[:, kc, :], wo[:, kc, hh * 512:(hh + 1) * 512], kc == 0, kc == 7, ["mT", "wo"], [("ps", 6 + hh)])
            self.postnorm((6, 7), xt[b], ("xt", b), 0 + v, 0, ub[b], st[b], mv[b], rs[b], b, self.scr["X1"][rows, :], ("X1", t))
        S.barrier()
        A.off = base

    def phase_mlp(self, l):
        A, S, ps = self.A, self.S, self.ps
        base = A.off
        last = (l == DEPTH - 1)
        self.lnl = A.alloc("lnl", [128, 2, 1024], F32)
        for j, src_ in enumerate((self.ln2_g, self.ln2_b)):
            self.dma(self.lnl[:, j, :], src_[l:l + 1, :].partition_broadcast(128), [], [("lnl", j)])
        w1 = A.alloc("w1", [128, 8, DFF], BF)
        w2 = A.alloc("w2", [128, 32, D], BF)
        w1v = self.w_mlp1[l].rearrange("(kc p) c -> p kc c", p=128)
        w2v = self.w_mlp2[l].rearrange("(fc p) c -> p fc c", p=128)
        for i in range(8):
            self.dma(w1[:, :, i * 512:(i + 1) * 512], w1v[:, :, i * 512:(i + 1) * 512], [], [("w1", i)], q=POOL)
        for i in range(8):
            self.dma(w2[:, i * 4:(i + 1) * 4, :], w2v[:, i * 4:(i + 1) * 4, :], [], [("w2", i)], q=POOL)
        xt = [A.alloc("xt", [128, 1024], F32) for _ in range(3)]
        xn = [A.alloc("xn", [128, 1024], F32) for _ in range(1)] * 2
        h2T = A.alloc("h2T", [128, 8, 256], BF)
        hid = A.alloc("hid", [128, 32, 256], BF)
        rl = [A.alloc("rl", [128, 256], F32) for _ in range(2)]
        ub = [A.alloc("ub", [128, 1024], F32) for _ in range(1)] * 2
        st = [A.alloc("st", [128, 2, 6], F32) for _ in range(4)]
        mv = [A.alloc("mv", [128, 2], F32) for _ in range(4)]
        rs = [A.alloc("rs", [128, 4], F32) for _ in range(4)]
        tiles = list(range(2 if last else 0, NT))
        blocks = [tiles[i:i + 2] for i in range(0, len(tiles), 2)]
        xi = 0
        un = 0
        outs = []
        for blk in blocks:
            nb = len(blk)
            xts = []
            for j, t in enumerate(blk):
                b5 = xi % 3
                xi += 1
                b = 0
                v = 1 if t < 2 else 0
                rows = slice(t * 128, (t + 1) * 128)
                X = xt[b5]
                xk = ("xt", b5)
                xts.append((X, xk))
                self.dma(X[:, :], self.scr["X1"][rows, :], [], [xk])
                self.ln_stats(X, xk, st[b], mv[b], rs[b], ("m", b))
                self.act(xn[b][:, :], X[:, :], AF.Identity, [xk, ("rs0", ("m", b)), ("rs1", ("m", b))], [("xn", b)],
                         bias=rs[b][:, 1:2], scale=rs[b][:, 0:1])
                for c in range(8):
                    self.tr(ps[c // 4][:, (c % 4) * 128:(c % 4 + 1) * 128], xn[b][:, c * 128:(c + 1) * 128], self.cst("ident"),
                            [("xn", b), "cst"], [("ps", c // 4)])
                for c in range(8):
                    o = h2T[:, c, j * 128:(j + 1) * 128]
                    i_ = ps[c // 4][:, (c % 4) * 128:(c % 4 + 1) * 128]
                    sc_ = self.ops[:, 1, c, v:v + 1]
                    sh_ = self.modc[:, 24 + c, v:v + 1]
                    if c // 4 == 0:
                        self.ts(DVE, o, i_, sc_, sh_, ALU.mult, ALU.add, [("ps", 0), "modc", "ops"], [("h2T", j, c)])
                    else:
                        self.act(o, i_, AF.Identity, [("ps", 1), "modc", "ops"], [("h2T", j, c)], bias=sh_, scale=sc_)
            hkeys = [("h2T", j, c) for j in range(nb) for c in range(8)]
            N = nb * 128
            for f in range(32):
                bank = 2 + f % 4
                for kc in range(8):
                    self.mm(ps[bank][:, 0:N], w1[:, kc, f * 128:(f + 1) * 128], h2T[:, kc, 0:N], kc == 0, kc == 7,
                            hkeys + [("w1", f // 4)], [("ps", bank)])
                r_ = rl[f % 2]
                self.act(r_[:, 0:N], ps[bank][:, 0:N], AF.Relu, [("ps", bank)], [("rl", f % 2)])
                self.tt(POOL if f % 2 == 0 else DVE, hid[:, f, 0:N], r_[:, 0:N], r_[:, 0:N], ALU.mult, [("rl", f % 2)], [("hid", f)])
            for j, t in enumerate(blk):
                v = 1 if t < 2 else 0
                rows = slice(t * 128, (t + 1) * 128)
                for f in range(32):
                    for hh in range(2):
                        self.mm(ps[6 + hh][:, :], hid[:, f, j * 128:(j + 1) * 128], w2[:, f, hh * 512:(hh + 1) * 512], f == 0, f == 31,
                                [("hid", f), ("w2", f // 4)], [("ps", 6 + hh)])
                if last:
                    dst = self.out[t * 128 - NCTX:(t + 1) * 128 - NCTX, :]
                else:
                    dst = self.scr["X2"][rows, :]
                u = 0
                un += 1
                X, xk = xts[j]
                tok = self.postnorm((6, 7), X, xk, 2 + v, 2, ub[u], st[2 + u], mv[2 + u], rs[2 + u], ("p", u), dst, ("X2", t))
                outs.append(tok)
        S.barrier()
        A.off = base
        return outs

    def build(self):
        self.setup()
        for l in range(self.layers):
            self.phase_ada(l)
            if self.upto == "ada":
                break
            A = self.A
            A.off = self.persist_end
            hT = A.alloc("hT", [128, 8, T], BF)
            self.hT = hT
            src = self.xz if l == 0 else self.scr["X2"]
            if not getattr(self, "skip_inproj", False):
                self.phase_ln1(l, src, hT)
            if "dbg_hT" in self.debug:
                self.S.barrier()
                hb = A.alloc("hdbg", [128, 1024], F32)
                self.cp(DVE, hb[:, :].rearrange("p (c t) -> p c t", c=8), hT[:, :, 0:128], [], ["hdbg"])
                self.dbg("dbg_hT", hb[:, :], [128, 1024], ["hdbg"])
            if self.upto == "ln1":
                break
            if not getattr(self, "skip_inproj", False):
                self.phase_inproj(l, hT)
            self.S.barrier()
            if self.upto == "inproj":
                break
            A.off = self.persist_end
            self.scan_setup(l)
            for mx in getattr(self, "mixers", "RMG"):
                self.phase_scan(l, mx)
            if self.upto == "scan":
                break
            self.S.barrier()
            A.off = self.persist_end
            if not getattr(self, "skip_merge", False):
                self.phase_merge(l)
            if self.upto == "merge":
                break
            self.phase_mlp(l)
        self.S.barrier()
        self.S.emit()


def prep_inputs(inputs, b):
    f = lambda a: np.ascontiguousarray(np.asarray(a, np.float32))
    m = {}
    m["xz"] = f(np.concatenate([inputs["ctx"][b], inputs["x"][b]], 0))
    cc = np.stack([np.asarray(inputs["c"][b]), np.asarray(inputs["c_ctx"])], -1)
    m["cc"] = f(cc.reshape(8, 128, 2).transpose(1, 0, 2))
    m["w_ada"] = f(inputs["w_ada"])
    m["b_ada"] = f(np.asarray(inputs["b_ada"]).reshape(DEPTH, 48, 128).transpose(0, 2, 1))
    m["w_in"] = f(inputs["w_in"])
    m["conv_w"] = f(np.asarray(inputs["conv_w"]).reshape(DEPTH, 5, 24, 128).transpose(0, 3, 2, 1))
    m["smallp"] = f(np.concatenate([np.asarray(inputs[k]).reshape(DEPTH, -1) for k in
                                    ("ret_decay", "mlstm_i_bias", "mlstm_f_bias", "gdn_a_log", "gdn_dt_bias")], 1)[:, :56])
    m["smallp"] = f(np.pad(m["smallp"], ((0, 0), (0, 8))))
    for k in ("mlstm_norm_w", "gdn_norm_w", "w_branch", "w_out", "ln1_g", "ln1_b", "ln2_g", "ln2_b", "w_mlp1", "w_mlp2"):
        m[k] = f(inputs[k])
    m["consts"] = f(CONSTS)
    rq, rk = _rope_tables()
    m["ropeq"], m["ropek"] = f(rq), f(rk)
    m["gmask"] = f(_gmasks())
    return m


def kernel(**inputs):
    nc = bass.Bass("TRN2", target_bir_lowering=False)
    Builder(nc).build()
    in_maps = [prep_inputs(inputs, b) for b in range(8)]
    res = run_bass_kernel_spmd(nc, in_maps, core_ids=list(range(8)))
    return np.stack([np.asarray(r["out"], np.float32) for r in res.results], 0)
```

```python
import contextlib
import os
import math
import numpy as np
import ml_dtypes
import concourse.bass as bass
import concourse.mybir as mybir
from concourse.bass_utils import run_bass_kernel_spmd

F32 = mybir.dt.float32
BF = mybir.dt.bfloat16
AF = mybir.ActivationFunctionType
ALU = mybir.AluOpType
AX = mybir.AxisListType

PE, ACT, DVE, POOL, SP = "pe", "act", "dve", "pool", "sp"
COMPUTE = (PE, ACT, DVE, POOL)
EPOCH = 12000
NDMASEM = 12
STOREQ = {"pool": "pool", "sp": "sp"}[os.environ.get("STOREQ", "pool")]

NCTX = 256
NLAT = 4096
T = NCTX + NLAT
NT = T // 128
D = 1024
DEPTH = 2
IN_DIM = 14384
DFF = 4096
LN_EPS = 1e-5
RMS_EPS = 1e-6
DN_ALPHA = (2 * DEPTH) ** 0.25
NEG = -30000.0


class Sched:
    def __init__(self, nc):
        self.nc = nc
        self.q = {e: [] for e in (PE, ACT, DVE, POOL, SP)}
        self.cnt = {e: 0 for e in COMPUTE}
        self.dcnt = {POOL: 0, SP: 0}
        self.lastw = {}
        self.readers = {}
        self.waited = {}
        self.waited_dma = {e: set() for e in self.q}

    def _deps(self, reads, writes):
        deps = []
        for r in reads:
            t = self.lastw.get(r)
            if t is not None:
                deps.append(t)
        for w in writes:
            t = self.lastw.get(w)
            if t is not None:
                deps.append(t)
            deps.extend(self.readers.get(w, ()))
        return deps

    def _emit_waits(self, eng, deps):
        best = {}
        dmas = []
        for t in deps:
            if t[0] == "dma":
                if t not in self.waited_dma[eng]:
                    self.waited_dma[eng].add(t)
                    dmas.append(t)
            else:
                p, n = t
                if p == eng and (eng == PE or n <= self.cnt[eng] - 3):
                    continue
                if n > best.get(p, 0):
                    best[p] = n
        for p, n in best.items():
            if self.waited.get((eng, p), 0) >= n:
                continue
            self.waited[(eng, p)] = n
            self.q[eng].append(("wait", p, n))
        for t in dmas:
            self.q[eng].append(("waitdma", t[1], t[2]))

    def _record(self, tok, reads, writes):
        for r in reads:
            lst = self.readers.setdefault(r, [])
            if tok[0] == "dma":
                lst[:] = [x for x in lst if not (x[0] == "dma" and x[1] == tok[1] and x[2] <= tok[2] - NDMASEM)]
            else:
                lst[:] = [x for x in lst if x[0] != tok[0]]
            lst.append(tok)
        for w in writes:
            self.lastw[w] = tok
            self.readers[w] = []

    def op(self, eng, fn, reads=(), writes=()):
        self._emit_waits(eng, self._deps(reads, writes))
        self.cnt[eng] += 1
        tok = (eng, self.cnt[eng])
        self.q[eng].append(("op", fn, self.cnt[eng]))
        self._record(tok, reads, writes)
        return tok

    def dma(self, eng, out, in_, reads=(), writes=()):
        deps = self._deps(reads, writes)
        k = self.dcnt[eng]
        self.dcnt[eng] += 1
        if k >= NDMASEM:
            deps.append(("dma", eng, k - NDMASEM))
        self._emit_waits(eng, deps)
        tok = ("dma", eng, k)
        self.q[eng].append(("dma", out, in_, k))
        self._record(tok, reads, writes)
        return tok

    def barrier(self):
        for e in self.q:
            deps = [(p, self.cnt[p]) for p in COMPUTE if self.cnt[p] > 0 and p != e]
            for q_ in self.dcnt:
                lo = max(0, self.dcnt[q_] - NDMASEM)
                deps += [("dma", q_, k) for k in range(lo, self.dcnt[q_])]
            self._emit_waits(e, deps)
        self.lastw = {}
        self.readers = {}

    def emit(self):
        nc = self.nc
        with contextlib.ExitStack() as st:
            sems = {}
            for e in COMPUTE:
                n = max(1, (self.cnt[e] + EPOCH - 1) // EPOCH)
                sems[e] = [st.enter_context(nc.semaphore(f"c_{e}_{i}")) for i in range(n)]
            dsems = {e: [st.enter_context(nc.semaphore(f"d_{e}_{i}")) for i in range(NDMASEM)] for e in self.dcnt}
            block = st.enter_context(nc.Block())

            def run(name):
                def body(eng):
                    for item in self.q[name]:
                        k = item[0]
                        if k == "op":
                            _, fn, n = item
                            fn(eng).then_inc(sems[name][(n - 1) // EPOCH], 1)
                        elif k == "wait":
                            _, p, n = item
                            ep = (n - 1) // EPOCH
                            eng.wait_ge(sems[p][ep], n - ep * EPOCH)
                        elif k == "waitdma":
                            _, q_, kk = item
                            eng.wait_ge(dsems[q_][kk % NDMASEM], 16 * (kk // NDMASEM + 1))
                        else:
                            _, out, in_, kk = item
                            eng.dma_start(out=out, in_=in_).then_inc(dsems[name][kk % NDMASEM], 16)
                return body

            block.tensor(run(PE))
            block.scalar(run(ACT))
            block.vector(run(DVE))
            block.gpsimd(run(POOL))
            block.sync(run(SP))


class Arena:
    def __init__(self, nc, limit):
        self.nc, self.off, self.limit, self.n = nc, 16640, 16640 + limit, 0

    def alloc(self, name, shape, dtype):
        nb = int(np.prod(shape[1:])) * (2 if dtype == BF else 4)
        nb = (nb + 31) // 32 * 32
        assert self.off + nb <= self.limit, (name, self.off, nb, self.limit)
        self.n += 1
        t = self.nc.alloc_sbuf_tensor_at(f"{name}_{self.n}", list(shape), dtype, offset=self.off)
        self.off += nb
        return t


CST = {}


def _build_consts():
    cols = []

    def add(name, arr):
        arr = np.asarray(arr, np.float32)
        if arr.ndim == 1:
            arr = arr[:, None]
        CST[name] = (sum(a.shape[1] for a in cols), arr.shape[1])
        cols.append(arr)

    p = np.arange(128)
    t_, i_ = p[:, None], p[None, :]
    add("ident", np.eye(128))
    add("ones", np.ones((128, 128)))
    add("triF", t_ <= i_)
    add("triB", t_ >= i_)
    add("sF", t_ > i_)
    add("sB", t_ < i_)
    add("negF", np.where(i_ < t_, NEG, 0.0))
    add("negB", np.where(i_ > t_, NEG, 0.0))
    add("m01F", i_ >= t_)
    add("m01B", i_ <= t_)
    add("s01F", i_ > t_)
    add("s01B", i_ < t_)
    add("diffF", np.maximum(i_ - t_, 0))
    add("diffB", np.maximum(t_ - i_, 0))
    add("posv", np.stack([p + 1, 127 - p, 128 - p, p, np.full(128, 128)], 1))
    return np.concatenate(cols, 1)


def _gmasks():
    p = np.arange(128)
    j, i = p[:, None], p[None, :]
    ms = []
    for l in range(7):
        B = 2 ** (l + 1)
        ms.append((((j // B) == (i // B)) & ((j % B) < B // 2) & ((i % B) >= B // 2)).astype(np.float32))
    return np.stack(ms + [m.T for m in ms], 1)


CONSTS = _build_consts()
NCST = CONSTS.shape[1]


def _rope_tables():
    n_freq = 64
    freqs = (10000.0 ** (-np.arange(n_freq, dtype=np.float32) / n_freq)).astype(np.float32)
    row = np.repeat(np.arange(64, dtype=np.float32), 64)
    col = np.tile(np.arange(64, dtype=np.float32), 64)
    lat = np.stack([row[:, None] * freqs, col[:, None] * freqs], 1).astype(np.float32)
    ang = np.concatenate([np.zeros((NCTX, 2, n_freq), np.float32), lat], 0)
    cos, sin = np.cos(ang), np.sin(ang)
    tq = np.stack([np.stack([cos, cos], 1), np.stack([sin, sin], 1)], 1).reshape(T, 512)
    return tq.astype(np.float32), (tq / 16.0).astype(np.float32)


def _groups():
    g = []
    c = 0
    for name, w in (("RQ", 1024), ("RK", 1024), ("RV", 1024), ("RG", 1024), ("MQ", 512), ("MK", 512),
                    ("MV", 1024), ("MO", 1024), ("MIF", 16), ("GQKV", 3072), ("GG", 1024), ("GBA", 32),
                    ("MG", 3072)):
        g.append((name, c, w))
        c += w
    assert c == IN_DIM
    return g


GROUPS = _groups()


class Builder:
    def __init__(self, nc, debug=(), layers=DEPTH, upto="all", ext_in=()):
        self.nc = nc
        self.debug = set(debug)
        self.ext_in = set(ext_in)
        self.S = Sched(nc)
        self.layers = layers
        self.upto = upto
        self.A = Arena(nc, 206 * 1024)
        self.uid = 0
        self._dram()
        self._psum()

    def _din(self, name, shape, dt=F32):
        return self.nc.dram_tensor(name, list(shape), dt, kind="ExternalInput").ap()

    def _dscr(self, name, shape, dt):
        kind = "ExternalOutput" if name in self.debug else ("ExternalInput" if name in self.ext_in else "Internal")
        return self.nc.dram_tensor(name, list(shape), dt, kind=kind).ap()

    def _dram(self):
        i = self._din
        self.xz = i("xz", [T, D])
        self.cc = i("cc", [128, 8, 2])
        self.w_ada = i("w_ada", [DEPTH, D, 6 * D])
        self.b_ada = i("b_ada", [DEPTH, 128, 48])
        self.w_in = i("w_in", [DEPTH, D, IN_DIM])
        self.conv_w = i("conv_w", [DEPTH, 128, 24, 5])
        self.smallp = i("smallp", [DEPTH, 64])
        self.mnorm_w = i("mlstm_norm_w", [DEPTH, D])
        self.gnorm_w = i("gdn_norm_w", [DEPTH, 128])
        self.w_branch = i("w_branch", [DEPTH, 3, D, D])
        self.w_out = i("w_out", [DEPTH, D, D])
        self.ln1_g = i("ln1_g", [DEPTH, D]); self.ln1_b = i("ln1_b", [DEPTH, D])
        self.ln2_g = i("ln2_g", [DEPTH, D]); self.ln2_b = i("ln2_b", [DEPTH, D])
        self.w_mlp1 = i("w_mlp1", [DEPTH, D, DFF]); self.w_mlp2 = i("w_mlp2", [DEPTH, DFF, D])
        self.cst_d = i("consts", [128, NCST])
        self.ropeq_d = i("ropeq", [T, 512]); self.ropek_d = i("ropek", [T, 512])
        self.gmask_d = i("gmask", [128, 14, 128])
        self.out = self.nc.dram_tensor("out", [NLAT, D], F32, kind="ExternalOutput").ap()
        s = self._dscr
        self.scr = {}
        for name, w, dt in (("RQ", 1024, BF), ("RK", 1024, BF), ("RV", 1024, BF), ("RG", 1024, BF),
                            ("MQ", 512, BF), ("MK", 512, BF), ("MV", 1024, BF), ("MO", 1024, BF),
                            ("MIF", 16, F32), ("GQ", 1024, BF), ("GK", 1024, BF), ("GV", 1024, BF),
                            ("GG", 1024, BF), ("GBA", 32, F32), ("MG", 3072, BF),
                            ("OB_R", 1024, F32), ("OB_M", 1024, F32), ("OB_G", 1024, F32),
                            ("Y_R", 1024, BF), ("Y_M", 1024, BF), ("Y_G", 1024, BF),
                            ("X1", 1024, F32), ("X2", 1024, F32)):
            self.scr[name] = s(name, [T, w], dt)

    def _psum(self):
        self.ps = [self.nc.alloc_psum_tensor(f"ps{i}", [128, 512], F32) for i in range(8)]
        self._ring = 0

    def ring(self):
        self._ring = (self._ring + 1) % 4
        return 4 + self._ring

    def cst(self, name, rows=128):
        o, w = CST[name]
        return self.cstt[0:rows, o:o + w]

    def hk(self, t):
        return [("hT", t, c) for c in range(8)]

    def key(self, base):
        self.uid += 1
        return (base, self.uid)

    def mm(self, out, lhsT, rhs, start, stop, reads, writes):
        self.S.op(PE, lambda e: e.matmul(out, lhsT, rhs, start=start, stop=stop), reads, writes)

    def tr(self, out, in_, ident, reads, writes):
        self.S.op(PE, lambda e: e.transpose(out, in_, ident), reads, writes)

    def act(self, out, in_, func, reads, writes, bias=0.0, scale=1.0, eng=ACT):
        self.S.op(ACT, lambda e: e.activation(out, in_, func, bias=bias, scale=scale), reads, writes)

    def tt(self, eng, out, a, b, op, reads, writes):
        self.S.op(eng, lambda e: e.tensor_tensor(out, a, b, op), reads, writes)

    def ts(self, eng, out, a, s1, s2, op0, op1, reads, writes):
        if s2 is None:
            self.S.op(eng, lambda e: e.tensor_scalar(out, a, s1, None, op0), reads, writes)
        else:
            self.S.op(eng, lambda e: e.tensor_scalar(out, a, s1, s2, op0, op1), reads, writes)

    def stt(self, out, a, s, b, op0, op1, reads, writes):
        self.S.op(DVE, lambda e: e.scalar_tensor_tensor(out, a, s, b, op0, op1), reads, writes)

    def cp(self, eng, out, in_, reads, writes):
        if eng == ACT:
            self.S.op(ACT, lambda e: e.copy(out, in_), reads, writes)
        else:
            self.S.op(eng, lambda e: e.tensor_copy(out, in_), reads, writes)

    def dma(self, out, in_, reads, writes, q=SP):
        return self.S.dma(q, out, in_, reads, writes)

    def store(self, out, in_, reads, writes):
        return self.S.dma(STOREQ, out, in_, reads, writes)

    def dbg(self, name, ap, shape, reads):
        if name in self.debug:
            d = self.nc.dram_tensor(name, list(shape), F32, kind="ExternalOutput").ap()
            self.dma(d, ap, reads, [("dbgout", name)])

    def setup(self):
        A = self.A
        self.cstt = A.alloc("cst", [128, NCST], F32)
        self.dma(self.cstt[:, :], self.cst_d, [], ["cst"])
        self.identb = A.alloc("identb", [128, 128], BF)
        self.cp(DVE, self.identb[:, :], self.cst("ident"), ["cst"], ["identb"])
        self.onesb = A.alloc("onesb", [128, 128], BF)
        self.cp(DVE, self.onesb[:, :], self.cst("ones"), ["cst"], ["onesb"])
        self.cct = A.alloc("cct", [128, 8, 2], F32)
        self.dma(self.cct[:, :, :], self.cc, [], ["cct"])
        self.scs = A.alloc("scs", [128, 8, 2], F32)
        self.act(self.scs[:, :, :], self.cct[:, :, :], AF.Silu, ["cct"], ["scs"])
        self.modc = A.alloc("modc", [128, 48, 2], F32)
        self.ops = A.alloc("ops", [128, 2, 8, 2], F32)
        self.gbc = A.alloc("gbc", [128, 4, 1024], F32)
        self.smp = A.alloc("smp", [128, 64], F32)
        self.persist_end = A.off

    def phase_ada(self, l):
        A, S = self.A, self.S
        A.off = self.persist_end
        badat = A.alloc("badat", [128, 48], F32)
        self.dma(badat[:, :], self.b_ada[l], [], ["badat"])
        self.dma(self.smp[:, :], self.smallp[l:l + 1, :].partition_broadcast(128), [], ["smp"])
        wsl = [A.alloc("wada", [128, 8, 768], F32) for _ in range(2)]
        wv = self.w_ada[l].rearrange("(kc p) c -> p kc c", p=128)
        psA = self.ps[0]
        psAk = ("ps", 0)
        for s in range(8):
            wt = wsl[s % 2]
            wk = ("wada", s % 2)
            self.dma(wt[:, :, :], wv[:, :, s * 768:(s + 1) * 768], [], [wk])
            for jj in range(6):
                j = s * 6 + jj
                for kc in range(8):
                    self.mm(psA[:, 2 * j:2 * j + 2], wt[:, kc, jj * 128:(jj + 1) * 128], self.scs[:, kc, :],
                            kc == 0, kc == 7, [wk, "scs"], [psAk])
        self.tt(DVE, self.modc[:, :, :], psA[:, 0:96].rearrange("p (j v) -> p j v", v=2),
                badat[:, :].unsqueeze(2).broadcast_to([128, 48, 2]), ALU.add, [psAk, "badat"], ["modc"])
        for sub in range(2):
            j0 = 8 + 24 * sub
            self.ts(DVE, self.ops[:, sub, :, :], self.modc[:, j0:j0 + 8, :], 1.0, None, ALU.add, None, ["modc"], ["ops"])
        tmp = [A.alloc("gtmp", [128, 128], F32) for _ in range(2)]
        n = 0
        for sub in range(2):
            for v in range(2):
                gi = sub * 2 + v
                pb = 1 + (gi % 2) * 2
                pst = self.ps[pb: pb + 2]
                for c in range(8):
                    tk = ("gtmp", n % 2)
                    col = self.modc[:, 16 + 24 * sub + c, v:v + 1]
                    self.ts(DVE, tmp[n % 2][:, :], self.cst("ones"), col, None, ALU.mult, None, ["cst", "modc"], [tk])
                    self.mm(pst[c // 4][:, (c % 4) * 128:(c % 4 + 1) * 128], tmp[n % 2][:, :], self.cst("ident"),
                            True, True, [tk, "cst"], [("ps", pb + c // 4)])
                    n += 1
                for hh in range(2):
                    self.cp(ACT, self.gbc[:, gi, hh * 512:(hh + 1) * 512], pst[hh][:, :], [("ps", pb + hh)], [("gbc", gi, hh)])
        S.barrier()
        self.dbg("dbg_modc", self.modc[:, :, :].rearrange("p j v -> p (j v)"), [128, 96], [])
        self.dbg("dbg_gbc", self.gbc[:, :, :].rearrange("p g d -> p (g d)"), [128, 4096], [])

    def ln_stats(self, xt, xk, st, mv, rs, uk):
        S = self.S
        for hh in range(2):
            S.op(DVE, lambda e, hh=hh: e.bn_stats(st[:, hh, :], xt[:, hh * 512:(hh + 1) * 512]), [xk], [("st", uk)])
        S.op(DVE, lambda e: e.bn_aggr(mv[:, :], st[:, :, :].rearrange("p a b -> p (a b)")), [("st", uk)], [("mv", uk)])
        self.act(rs[:, 2:3], mv[:, 1:2], AF.Ln, [("mv", uk)], [("rs2", uk)], bias=LN_EPS)
        self.act(rs[:, 0:1], rs[:, 2:3], AF.Exp, [("rs2", uk)], [("rs0", uk)], scale=-0.5)
        self.ts(DVE, rs[:, 1:2], mv[:, 0:1], rs[:, 0:1], -1.0, ALU.mult, ALU.mult, [("mv", uk), ("rs0", uk)], [("rs1", uk)])

    def phase_ln1(self, l, src, hT):
        A, S = self.A, self.S
        xt = [A.alloc("xt", [128, 1024], F32) for _ in range(2)]
        xn = [A.alloc("xn", [128, 1024], F32) for _ in range(2)]
        st = [A.alloc("st", [128, 2, 6], F32) for _ in range(2)]
        mv = [A.alloc("mv", [128, 2], F32) for _ in range(2)]
        rs = [A.alloc("rs", [128, 4], F32) for _ in range(2)]
        import os
        STEPS = int(os.environ.get("LN1_STEPS", "9"))
        evm = os.environ.get("EVM", "split")
        EV = (lambda c: True) if evm == "dve" else ((lambda c: False) if evm == "act" else (lambda c: c // 4 == 0))
        for t in range(int(os.environ.get("LN1_NT", NT))):
            b = t % 2
            v = 1 if t < 2 else 0
            self.dma(xt[b][:, :], src[t * 128:(t + 1) * 128, :], [], [("xt", b)])
            if STEPS < 2:
                continue
            self.ln_stats(xt[b], ("xt", b), st[b], mv[b], rs[b], b)
            if STEPS < 4:
                continue
            self.act(xn[b][:, :], xt[b][:, :], AF.Identity, [("xt", b), ("rs0", b), ("rs1", b)], [("xn", b)],
                     bias=rs[b][:, 1:2], scale=rs[b][:, 0:1])
            if STEPS < 5:
                continue
            for c in range(8):
                pt = self.ps[(t % 2) * 2 + c // 4]
                pk = ("ps", (t % 2) * 2 + c // 4)
                self.tr(pt[:, (c % 4) * 128:(c % 4 + 1) * 128], xn[b][:, c * 128:(c + 1) * 128], self.cst("ident"),
                        [("xn", b), "cst"], [pk])
            if STEPS < 6:
                continue
            for c in range(8):
                pt = self.ps[(t % 2) * 2 + c // 4]
                pk = ("ps", (t % 2) * 2 + c // 4)
                o = hT[:, c, t * 128:(t + 1) * 128]
                i_ = pt[:, (c % 4) * 128:(c % 4 + 1) * 128]
                sc_ = self.ops[:, 0, c, v:v + 1]
                sh_ = self.modc[:, c, v:v + 1]
                lock = ["evlock"] if os.environ.get("EVLOCK") else []
                if EV(c):
                    self.ts(DVE, o, i_, sc_, sh_, ALU.mult, ALU.add, [pk, "modc", "ops"], [("hT", t, c)] + lock)
                else:
                    self.act(o, i_, AF.Identity, [pk, "modc", "ops"], [("hT", t, c)] + lock, bias=sh_, scale=sc_)

    def phase_inproj(self, l, hT):
        A, S = self.A, self.S
        wg = [A.alloc("wg", [128, 8, 512], BF) for _ in range(2)]
        stg = [A.alloc("stg", [128, 512], BF) for _ in range(4)]
        stf = [A.alloc("stf", [128, 32], F32) for _ in range(2)]
        rp = [A.alloc("rp", [128, 512], F32) for _ in range(6)]
        t12 = [A.alloc("t12", [128, 4, 256], F32) for _ in range(2)]
        xc = A.alloc("xc", [128, 4, T + 8], BF)
        cwt = A.alloc("cwt", [128, 24, 5], F32)
        dg = A.alloc("dg", [128, 20, 128], BF)
        sl = [A.alloc("sl", [128, 512], F32) for _ in range(2)]
        sq = [A.alloc("sq", [128, 512], F32) for _ in range(2)]
        ssn = [A.alloc("ssn", [128, 12], F32) for _ in range(2)]
        self.dma(cwt[:, :, :], self.conv_w[l], [], ["cwt"])
        S.op(POOL, lambda e: e.memset(xc[:, :, :], 0.0), [], [("xc", i) for i in range(4)])
        wv = self.w_in[l].rearrange("(kc p) c -> p kc c", p=128)
        gi = 0
        si = 0
        subs = []
        for name, c0, width in GROUPS:
            if getattr(self, "only_groups", None) and name not in self.only_groups:
                continue
            for sub in range(max(1, width // 512)):
                subs.append((name, sub, c0 + sub * 512, min(width, 512)))

        def wload(i):
            name_, sub_, cs_, w__ = subs[i]
            self.dma(wg[i % 2][:, :, 0:w__], wv[:, :, cs_:cs_ + w__], [], [("wg", i % 2)], q=POOL)

        wload(0)
        for gi, (name, sub, cs, w_) in enumerate(subs):
            if True:
                wb = wg[gi % 2]
                wk = ("wg", gi % 2)
                if gi + 1 < len(subs) and gi >= 1:
                    pass
                if gi + 1 < len(subs):
                    wload(gi + 1)
                if name == "GQKV":
                    self._gdn_group(l, sub, wb, wk, hT, xc, cwt, dg, sl, sq, ssn, stg)
                    continue
                for t in range(NT):
                    pt = self.ps[4 + t % 4]
                    pk = ("ps", 4 + t % 4)
                    for kc in range(8):
                        self.mm(pt[:, 0:w_], hT[:, kc, t * 128:(t + 1) * 128], wb[:, kc, 0:w_], kc == 0, kc == 7,
                                self.hk(t) + [wk], [pk])
                    rows = slice(t * 128, (t + 1) * 128)
                    if name in ("MIF", "GBA"):
                        sb = stf[si % 2]; sk = ("stf", si % 2); si += 1
                        self.cp(DVE, sb[:, 0:w_], pt[:, 0:w_], [pk], [sk])
                        self.store(self.scr[name][rows, :], sb[:, 0:w_], [sk], [(name, t)])
                        continue
                    sb = stg[si % 4]; sk = ("stg", si % 4); si += 1
                    dst = self.scr[name][rows, sub * 512:(sub + 1) * 512]
                    if name in ("RQ", "RK"):
                        b = t % 2
                        rb = t % 6
                        tab = self.ropeq_d if name == "RQ" else self.ropek_d
                        self.dma(rp[rb][:, :], tab[rows, :], [], [("rp", rb)])
                        psv = pt[:, :].rearrange("p (g ab f) -> p g ab f", g=4, ab=2)
                        cosv = rp[rb][:, 0:256].rearrange("p (g f) -> p g f", g=4)
                        sinv = rp[rb][:, 256:512].rearrange("p (g f) -> p g f", g=4)
                        sbv = sb[:, :].rearrange("p (g ab f) -> p g ab f", g=4, ab=2)
                        tk = ("t12", b)
                        t1, t2, t3, t4 = [t12[b][:, i, :].rearrange("p (g f) -> p g f", g=4) for i in range(4)]
                        a_, b_ = psv[:, :, 0, :], psv[:, :, 1, :]
                        self.tt(DVE, t1, a_, cosv, ALU.mult, [pk, ("rp", rb)], [(tk, 0)])
                        self.tt(DVE, t2, b_, sinv, ALU.mult, [pk, ("rp", rb)], [(tk, 1)])
                        self.tt(DVE, t3, b_, cosv, ALU.mult, [pk, ("rp", rb)], [(tk, 2)])
                        self.tt(DVE, t4, a_, sinv, ALU.mult, [pk, ("rp", rb)], [(tk, 3)])
                        self.tt(POOL, sbv[:, :, 0, :], t1, t2, ALU.subtract, [(tk, 0), (tk, 1)], [sk])
                        self.tt(POOL, sbv[:, :, 1, :], t3, t4, ALU.add, [(tk, 2), (tk, 3)], [sk])
                        self.store(dst, sb[:, :], [sk], [(name, t, sub)])
                        continue
                    if name in ("RG", "GG"):
                        self.act(sb[:, :], pt[:, :], AF.Silu, [pk], [sk])
                    elif name in ("MO", "MG"):
                        self.act(sb[:, :], pt[:, :], AF.Sigmoid, [pk], [sk])
                    elif name == "MQ":
                        self.act(sb[:, :], pt[:, :], AF.Copy, [pk], [sk], scale=128 ** -0.5)
                    elif t % 2 == 0:
                        self.cp(ACT, sb[:, :], pt[:, :], [pk], [sk])
                    else:
                        self.cp(DVE, sb[:, :], pt[:, :], [pk], [sk])
                    self.store(dst, sb[:, :], [sk], [(name, t, sub)])

    def _gdn_group(self, l, sub, wb, wk, hT, xc, cwt, dg, sl, sq, ssn, stg):
        S = self.S
        for ct in range(4):
            for tap in range(5):
                self.ts(POOL, dg[:, ct * 5 + tap, :], self.cst("ident"), cwt[:, sub * 4 + ct, tap:tap + 1], None,
                        ALU.mult, None, ["cst", "cwt"], [("dg", ct)])
        blocks = [(0, 256)] + [(256 + 512 * i, 512) for i in range(8)]
        n = 0
        for (t0, tw) in blocks:
            col0 = 2 + t0 if t0 < NCTX else 6 + t0
            for ct in range(4):
                pt = self.ps[n % 4]
                pk = ("ps", n % 4)
                for kc in range(8):
                    self.mm(pt[:, 0:tw], wb[:, kc, ct * 128:(ct + 1) * 128], hT[:, kc, t0:t0 + tw], kc == 0, kc == 7,
                            [wk] + sum([self.hk(t0 // 128 + i) for i in range(tw // 128)], []), [pk])
                self.cp(ACT if n % 2 == 0 else DVE, xc[:, ct, col0:col0 + tw], pt[:, 0:tw], [pk], [("xc", ct)])
                n += 1
        which = "GQ" if sub < 2 else ("GK" if sub < 4 else "GV")
        for t in range(NT):
            base = (2 + t * 128 if t < 2 else 6 + t * 128) - 2
            pt = self.ps[4 + t % 4]
            pk = ("ps", 4 + t % 4)
            for ct in range(4):
                for tap in range(5):
                    self.mm(pt[:, ct * 128:(ct + 1) * 128], xc[:, ct, base + tap:base + tap + 128], dg[:, ct * 5 + tap, :],
                            tap == 0, tap == 4, [("xc", ct), ("dg", ct)], [pk])
            b = t % 2
            sb = stg[t % 4]; sk = ("stg", t % 4)
            rows = slice(t * 128, (t + 1) * 128)
            dst = self.scr[which][rows, (sub % 2) * 512:(sub % 2 + 1) * 512]
            if which == "GV":
                self.act(sb[:, :], pt[:, :], AF.Silu, [pk], [sk])
            else:
                self.act(sl[b][:, :], pt[:, :], AF.Silu, [pk], [("sl", b)])
                self.tt(POOL, sq[b][:, :], sl[b][:, :], sl[b][:, :], ALU.mult, [("sl", b)], [("sq", b)])
                S.op(DVE, lambda e, b=b: e.tensor_reduce(ssn[b][:, 0:4], sq[b][:, :].rearrange("p (h f) -> p h f", h=4), AX.X, ALU.add),
                     [("sq", b)], [("ss", b)])
                self.act(ssn[b][:, 4:8], ssn[b][:, 0:4], AF.Ln, [("ss", b)], [("ssl", b)], bias=RMS_EPS)
                qs = math.log(128 ** -0.5) if which == "GQ" else 0.0
                self.act(ssn[b][:, 8:12], ssn[b][:, 4:8], AF.Exp, [("ssl", b)], [("ssr", b)], scale=-0.5, bias=qs)
                self.tt(DVE, sb[:, :].rearrange("p (h f) -> p h f", h=4), sl[b][:, :].rearrange("p (h f) -> p h f", h=4),
                        ssn[b][:, 8:12].unsqueeze(2).broadcast_to([128, 4, 128]), ALU.mult, [("sl", b), ("ssr", b)], [sk])
            self.store(dst, sb[:, :], [sk], [(which, t, sub)])

    MIX = {"R": dict(H=4, nkt=2, dv=256, dvp=256, q="RQ", k="RK", v="RV", gate="RG", ob="OB_R", y="Y_R"),
           "M": dict(H=4, nkt=1, dv=256, dvp=257, q="MQ", k="MK", v="MV", gate="MO", ob="OB_M", y="Y_M"),
           "G": dict(H=8, nkt=1, dv=128, dvp=128, q="GQ", k="GK", v="GV", gate="GG", ob="OB_G", y="Y_G")}

    def scan_setup(self, l):
        A = self.A
        sp_ = self.smp
        self.prm = A.alloc("prm", [128, 64], F32)
        prm = self.prm
        self.act(prm[:, 0:8], sp_[:, 0:8], AF.Exp, ["smp"], ["prm_r0"])
        self.ts(DVE, prm[:, 0:8], prm[:, 0:8], -1.0, None, ALU.mult, None, ["prm_r0"], ["prm_r"])
        self.act(prm[:, 8:24], sp_[:, 24:40], AF.Exp, ["smp"], ["prm_g0"])
        self.ts(DVE, prm[:, 8:24], prm[:, 8:24], -1.0, None, ALU.mult, None, ["prm_g0"], ["prm_g"])
        self.retDT = A.alloc("retDT", [128, 8, 128], F32)
        self.retc = A.alloc("retc", [128, 8, 3], F32)
        o, _ = CST["posv"]
        for d_ in range(2):
            sfx = "F" if d_ == 0 else "B"
            for h in range(4):
                i = d_ * 4 + h
                lgc = prm[:, i:i + 1]
                self.act(self.retDT[:, i, :], self.cst("diff" + sfx), AF.Exp, ["cst", "prm_r"], [("retDT0", i)], scale=lgc)
                self.tt(DVE, self.retDT[:, i, :], self.retDT[:, i, :], self.cst("m01" + sfx), ALU.mult, [("retDT0", i), "cst"], [("retDT", i)])
                cols = (o + 0, o + 1) if d_ == 0 else (o + 2, o + 3)
                for j, cc in enumerate(cols + (o + 4,)):
                    self.act(self.retc[:, i, j:j + 1], self.cstt[:, cc:cc + 1], AF.Exp, ["cst", "prm_r"], [("retc", i, j)], scale=lgc)

    def phase_scan(self, l, mx):
        A, S = self.A, self.S
        cfg = self.MIX[mx]
        H, nkt, dv, dvp = cfg["H"], cfg["nkt"], cfg["dv"], cfg["dvp"]
        dk = 128 * nkt
        QW = H * dk
        base = A.off
        qt = [A.alloc("qt", [128, QW], BF) for _ in range(2)]
        kt = [A.alloc("kt", [128, QW], BF) for _ in range(2)]
        vt = [A.alloc("vt", [128, H, dvp], BF) for _ in range(2)]
        ks2 = [A.alloc("ks", [128, QW], BF) for _ in range(2)]
        QT2 = [A.alloc("QT", [128, H * nkt, 128], BF) for _ in range(2)]
        KT2 = [A.alloc("KT", [128, H * nkt, 128], BF) for _ in range(2)]
        Sf = A.alloc("Sf", [128, H * nkt, dvp], F32)
        Sb = A.alloc("Sb", [128, H * nkt, dvp], BF)
        pTa = A.alloc("pTa", [128, H, 128], BF)
        acc = [A.alloc("acc", [128, 1024], F32) for _ in range(2)]
        obt = [A.alloc("obt", [128, 1024], F32) for _ in range(2)]
        gt = [A.alloc("gt", [128, 1024], BF) for _ in range(2)]
        yt = [A.alloc("yt", [128, 1024], BF) for _ in range(2)]
        t1 = A.alloc("pp1", [128, 1024], F32)
        sc = [A.alloc("scl", [128, 96], F32) for _ in range(2)]
        sraw = [A.alloc("sraw", [128, 32], F32) for _ in range(2)]
        pst = A.alloc("pst", [128, 8, 6], F32)
        pmv = A.alloc("pmv", [128, 8, 2], F32)
        prs = A.alloc("prs", [128, 16], F32)
        if mx != "R":
            Lm = A.alloc("Lm", [128, H, 128], F32)
            dtb2 = [A.alloc("dtb", [128, H, 128], F32) for _ in range(2)]
        if mx == "M":
            dn = A.alloc("dn", [128, 12], F32)
            nwb = A.alloc("nwb", [128, 1024], F32)
            self.dma(nwb[:, :], self.mnorm_w[l:l + 1, :].partition_broadcast(128), [], ["nwb"])
            for b in range(2):
                S.op(POOL, lambda e, b=b: e.memset(vt[b][:, :, 256:257], 1.0), [], [("vt", b)])
        if mx == "G":
            dts = A.alloc("dts", [128, 8, 128], F32)
            gm = A.alloc("gm", [128, 14, 128], F32)
            self.dma(gm[:, :, :], self.gmask_d, [], ["gm"])
            M0 = A.alloc("M0", [128, 8, 128], F32)
            MTt = A.alloc("MTt", [128, 8, 128], F32)
            MTm = A.alloc("MTm", [128, 6, 8, 128], F32)
            Um = [A.alloc("Um", [128, 8, 128], F32) for _ in range(2)]
            Vm = A.alloc("Vm", [128, 8, 128], F32)
            Pm = A.alloc("Pm", [128, 8, 128], F32)
            self.gdn_Wb2 = [A.alloc("Wb", [128, 8, 128], BF) for _ in range(2)]
            self.ginv = (gm, M0, MTt, MTm, Um, Vm, Pm)
            r0 = A.alloc("r0", [128, 8, 128], BF)
            vn = A.alloc("vn", [128, 8, 128], BF)
            nwb = A.alloc("nwb", [128, 128], F32)
            self.dma(nwb[:, :], self.gnorm_w[l:l + 1, :].partition_broadcast(128), [], ["nwb"])
        ps = self.ps
        PK = lambda i: ("ps", i)
        identb = self.identb
        for d_ in (1, 0):
            sfx = "F" if d_ == 0 else "B"
            order = [0, 1] + list(range(2, NT)) if d_ == 0 else [1, 0] + list(range(NT - 1, 1, -1))
            S.op(POOL, lambda e: e.memset(Sf[:, :, :], 0.0), [], [("Sf", i) for i in range(H)])
            S.op(POOL, lambda e: e.memset(Sb[:, :, :], 0.0), [], ["Sb"])
            def body(n, c):
                b = n % 2
                ks, QT, KT = ks2[b], QT2[b], KT2[b]
                QTk, KTk, ksk = ("QT", b), ("KT", b), ("ks", b)
                if mx != "R":
                    dtb = dtb2[b]
                rows = slice(c * 128, (c + 1) * 128)
                self.dma(qt[b][:, :], self.scr[cfg["q"]][rows, :], [(cfg["q"], c)], [("qt", b)])
                self.dma(kt[b][:, :], self.scr[cfg["k"]][rows, :], [(cfg["k"], c)], [("kt", b)])
                self.dma(vt[b][:, :, 0:dv], self.scr[cfg["v"]][rows, :].rearrange("p (h e) -> p h e", h=H), [(cfg["v"], c)], [("vt", b)])
                if mx == "M":
                    self.dma(sraw[b][:, 0:16], self.scr["MIF"][rows, :], [], [("sraw", b)])
                if mx == "G":
                    self.dma(sraw[b][:, 0:32], self.scr["GBA"][rows, :], [], [("sraw", b)])
                if d_ == 0:
                    self.dma(obt[b][:, :], self.scr[cfg["ob"]][rows, :], [(cfg["ob"], c)], [("obt", b)])
                    self.dma(gt[b][:, :], self.scr[cfg["gate"]][rows, :], [], [("gt", b)])
                for src_, dst_, bank, nm in ((qt[b], QT, 0, "QT"), (kt[b], KT, 1, "KT")):
                    pv = ps[bank][:, :].bitcast(BF)
                    for j in range(H * nkt):
                        self.tr(pv[:, j * 128:(j + 1) * 128], src_[:, j * 128:(j + 1) * 128], identb[:, :],
                                [(nm.lower(), b), "identb"], [PK(bank)])
                    self.cp(ACT if bank == 0 else DVE, dst_[:, :, :].rearrange("p a t -> p (a t)"), pv[:, 0:H * nkt * 128],
                            [PK(bank)], [(nm, b)])
                yield "A"
                s_ = sc[b]
                sk = ("scl", b)
                HB = 4
                nbk = H // HB
                if mx == "R":
                    a_bc = lambda hs_, w: self.retc[:, d_ * 4 + hs_.start:d_ * 4 + hs_.stop, 0:1].broadcast_to([128, hs_.stop - hs_.start, w])
                    a_reads = [("retc", d_ * 4 + h, 0) for h in range(4)]
                    cd_col = lambda h: self.retc[:, d_ * 4 + h, 2:3]
                    cd_reads = [("retc", d_ * 4 + h, 2) for h in range(4)]
                    s_bc = self.retc[:, d_ * 4:d_ * 4 + 4, 1:2].broadcast_to([128, 4, 256])
                    s_reads = [("retc", d_ * 4 + h, 1) for h in range(4)]
                    DTall = lambda hs_: self.retDT[:, d_ * 4 + hs_.start:d_ * 4 + hs_.stop, :]
                    dt_reads = [("retDT", d_ * 4 + h) for h in range(4)]
                else:
                    nh = H
                    raw = sraw[b]
                    if mx == "M":
                        self.tt(DVE, s_[:, 0:4], raw[:, d_ * 4:d_ * 4 + 4], self.smp[:, 8 + d_ * 4:12 + d_ * 4], ALU.add, [("sraw", b), "smp"], [(sk, "ig")])
                        self.tt(DVE, s_[:, 4:8], raw[:, 8 + d_ * 4:12 + d_ * 4], self.smp[:, 16 + d_ * 4:20 + d_ * 4], ALU.add, [("sraw", b), "smp"], [(sk, "x")])
                        self.act(s_[:, 8:12], s_[:, 4:8], AF.Exp, [(sk, "x")], [(sk, "e")], scale=-1.0)
                        self.act(s_[:, 12:16], s_[:, 8:12], AF.Ln, [(sk, "e")], [(sk, "l")], bias=1.0)
                        self.ts(DVE, s_[:, 16:20], s_[:, 12:16], -1.0, None, ALU.mult, None, [(sk, "l")], [(sk, "g")])
                        self.act(s_[:, 20:24], s_[:, 0:4], AF.Exp, [(sk, "ig")], [(sk, "eig")])
                        gall = s_[:, 16:20]
                    else:
                        self.act(s_[:, 0:8], raw[:, d_ * 8:d_ * 8 + 8], AF.Sigmoid, [("sraw", b)], [(sk, "beta")])
                        self.ts(DVE, s_[:, 88:96], s_[:, 0:8], -1.0, None, ALU.mult, None, [(sk, "beta")], [(sk, "nbeta")])
                        self.tt(DVE, s_[:, 8:16], raw[:, 16 + d_ * 8:24 + d_ * 8], self.smp[:, 40 + d_ * 8:48 + d_ * 8], ALU.add, [("sraw", b), "smp"], [(sk, "x")])
                        self.act(s_[:, 16:24], s_[:, 8:16], AF.Exp, [(sk, "x")], [(sk, "e")])
                        self.act(s_[:, 24:32], s_[:, 16:24], AF.Ln, [(sk, "e")], [(sk, "l")], bias=1.0)
                        self.tt(DVE, s_[:, 32:40], s_[:, 24:32], self.prm[:, 8 + d_ * 8:16 + d_ * 8], ALU.mult, [(sk, "l"), "prm_g"], [(sk, "g")])
                        gall = s_[:, 32:40]
                    self.mm(ps[2][:, 0:nh], self.cst("tri" + sfx), gall, True, True, ["cst", (sk, "g")], [PK(2)])
                    self.mm(ps[2][:, nh:2 * nh], self.cst("ones"), gall, True, True, ["cst", (sk, "g")], [PK(2)])
                    self.cp(ACT, s_[:, 40:40 + 2 * nh], ps[2][:, 0:2 * nh], [PK(2)], [(sk, "B")])
                    Bc, Ba = s_[:, 40:40 + nh], s_[:, 40 + nh:40 + 2 * nh]
                    self.act(s_[:, 56:56 + nh], Bc, AF.Exp, [(sk, "B")], [(sk, "a")])
                    self.act(s_[:, 64:64 + nh], Ba, AF.Exp, [(sk, "B")], [(sk, "cd")])
                    self.tt(DVE, s_[:, 72:72 + nh], Ba, Bc, ALU.subtract, [(sk, "B")], [(sk, "s0")])
                    if mx == "M":
                        self.tt(DVE, s_[:, 72:72 + nh], s_[:, 72:72 + nh], s_[:, 0:4], ALU.add, [(sk, "s0"), (sk, "ig")], [(sk, "s1")])
                    else:
                        self.ts(DVE, s_[:, 80:88], s_[:, 56:64], -1.0, None, ALU.mult, None, [(sk, "a")], [(sk, "na")])
                    self.act(s_[:, 72:72 + nh], s_[:, 72:72 + nh], AF.Exp, [(sk, "s0"), (sk, "s1")], [(sk, "s")])
                    a_bc = lambda hs_, w: s_[:, 56 + hs_.start:56 + hs_.stop].unsqueeze(2).broadcast_to([128, hs_.stop - hs_.start, w])
                    a_reads = [(sk, "a")]
                    cd_col = lambda h: s_[:, 64 + h:65 + h]
                    cd_reads = [(sk, "cd")]
                    s_bc = s_[:, 72:72 + nh].unsqueeze(2).broadcast_to([128, nh, 128])
                    s_reads = [(sk, "s")]
                    yield "A"
                    g0 = 16 if mx == "M" else 32
                    for bk in range(nbk):
                        hs = slice(bk * HB, (bk + 1) * HB)
                        self.tt(DVE, Lm[:, hs, :], self.cst("s" + sfx).unsqueeze(1).broadcast_to([128, HB, 128]),
                                s_[:, g0 + hs.start:g0 + hs.stop].unsqueeze(2).broadcast_to([128, HB, 128]), ALU.mult, ["cst", (sk, "g")], [("Lm", bk)])
                        for q in range(HB):
                            h = bk * HB + q
                            self.mm(ps[2 + bk][:, q * 128:(q + 1) * 128], Lm[:, h, :], self.cst("tri" + sfx), True, False, [("Lm", bk), "cst"], [PK(2 + bk)])
                            self.mm(ps[2 + bk][:, q * 128:(q + 1) * 128], self.cst("ident"), self.cst("neg" + sfx), False, True, ["cst"], [PK(2 + bk)])
                        self.act(dtb[:, hs, :].rearrange("p a t -> p (a t)"), ps[2 + bk][:, :], AF.Exp, [PK(2 + bk)], [("dtb", b, bk)])
                        if mx == "M":
                            self.tt(POOL, dtb[:, hs, :], dtb[:, hs, :], s_[:, 20:24].unsqueeze(2).broadcast_to([128, 4, 128]), ALU.mult,
                                    [("dtb", b, bk), (sk, "eig")], [("dtb", b, bk)])
                        else:
                            self.tt(POOL, dts[:, hs, :], dtb[:, hs, :], self.cst("s01" + sfx).unsqueeze(1).broadcast_to([128, HB, 128]), ALU.mult,
                                    [("dtb", b, bk), "cst"], [("dts", h) for h in range(hs.start, hs.stop)])
                        yield "A"
                    DTall = lambda hs_: dtb[:, hs_, :]
                    dt_reads = [("dtb", b, bk) for bk in range(nbk)]
                self.tt(POOL, ks[:, :].rearrange("p (h e) -> p h e", h=H), kt[b][:, :].rearrange("p (h e) -> p h e", h=H), s_bc,
                        ALU.mult, [("kt", b)] + s_reads, [ksk])
                yield "A"
                if mx == "G":
                    Wt = self.gdn_Wb2[b]
                    yield from self._gdn_inverse(KT, KTk, s_, sk, dts, d_, Wt, b)
                yield "A_DONE"
                A_ = acc[b]
                ak = ("acc", b)
                ring = self.ring
                for bk in range(nbk):
                    hs = slice(bk * HB, (bk + 1) * HB)
                    bn = ring()
                    for q in range(HB):
                        h = bk * HB + q
                        for kk in range(nkt):
                            self.mm(ps[bn][:, q * 128:(q + 1) * 128], KT[:, h * nkt + kk, :], QT[:, h * nkt + kk, :], kk == 0, kk == nkt - 1, [KTk, QTk], [PK(bn)])
                    self.tt(DVE, pTa[:, hs, :].rearrange("p a t -> p (a t)"), ps[bn][:, :], DTall(hs).rearrange("p a t -> p (a t)"), ALU.mult,
                            [PK(bn)] + dt_reads, [("pTa", bk)])
                    yield "B"
                vsrc = lambda h: vt[b][:, h, 0:dv]
                vreads = [("vt", b)]
                if mx == "G":
                    for bk in range(nbk):
                        hs = slice(bk * HB, (bk + 1) * HB)
                        bn = ring()
                        for q in range(HB):
                            h = bk * HB + q
                            self.mm(ps[bn][:, q * 128:(q + 1) * 128], KT[:, h, :], Sb[:, h, :], True, True, [KTk, "Sb"], [PK(bn)])
                        for q in range(HB):
                            h = bk * HB + q
                            self.stt(r0[:, h, :], ps[bn][:, q * 128:(q + 1) * 128], s_[:, 80 + h:81 + h], vt[b][:, h, :], ALU.mult, ALU.add,
                                     [PK(bn), (sk, "na"), ("vt", b)], [("r0", bk)])
                        bn2 = ring()
                        for q in range(HB):
                            h = bk * HB + q
                            self.mm(ps[bn2][:, q * 128:(q + 1) * 128], Wt[:, h, :], r0[:, h, :], True, True, [("Wb", b, bk), ("r0", bk)], [PK(bn2)])
                        self.tt(DVE, vn[:, hs, :], ps[bn2][:, :].rearrange("p (a t) -> p a t", a=HB),
                                s_[:, hs.start:hs.stop].unsqueeze(2).broadcast_to([128, HB, 128]), ALU.mult, [PK(bn2), (sk, "beta")], [("vn", bk)])
                        yield "B"
                    vsrc = lambda h: vn[:, h, :]
                    vreads = [("vn", bk) for bk in range(nbk)]
                hpb = 512 // dv
                for g_ in range(H // hpb):
                    hs = slice(g_ * hpb, (g_ + 1) * hpb)
                    by, bz = ring(), ring()
                    for q in range(hpb):
                        h = g_ * hpb + q
                        self.mm(ps[by][:, q * dv:(q + 1) * dv], pTa[:, h, :], vsrc(h), True, True, [("pTa", h // HB)] + vreads, [PK(by)])
                    for q in range(hpb):
                        h = g_ * hpb + q
                        for kk in range(nkt):
                            self.mm(ps[bz][:, q * dv:(q + 1) * dv], QT[:, h * nkt + kk, :], Sb[:, h * nkt + kk, 0:dv], kk == 0, kk == nkt - 1, [QTk, "Sb"], [PK(bz)])
                    o3 = A_[:, hs.start * dv:hs.stop * dv].rearrange("p (a e) -> p a e", a=hpb)
                    self.tt(DVE, o3, ps[bz][:, :].rearrange("p (a e) -> p a e", a=hpb), a_bc(hs, dv), ALU.mult, [PK(bz)] + a_reads, [(ak, g_)])
                    self.tt(DVE, o3, o3, ps[by][:, :].rearrange("p (a e) -> p a e", a=hpb), ALU.add, [(ak, g_), PK(by)], [(ak, g_)])
                    yield "B"
                nacc = H // hpb
                if mx == "M":
                    bd = ring()
                    for h in range(4):
                        self.mm(ps[bd][:, h:h + 1], pTa[:, h, :], self.onesb[:, 0:1], True, True, [("pTa", 0), "onesb"], [PK(bd)])
                    for h in range(4):
                        self.mm(ps[bd][:, 4 + h:5 + h], QT[:, h, :], Sb[:, h, 256:257], True, True, [QTk, "Sb"], [PK(bd)])
                    self.tt(DVE, dn[:, 0:4], ps[bd][:, 4:8], s_[:, 56:60], ALU.mult, [PK(bd), (sk, "a")], ["dn0"])
                    self.tt(DVE, dn[:, 0:4], dn[:, 0:4], ps[bd][:, 0:4], ALU.add, ["dn0", PK(bd)], ["dn0"])
                    self.act(dn[:, 4:8], dn[:, 0:4], AF.Abs, ["dn0"], ["dn1"])
                    self.ts(DVE, dn[:, 4:8], dn[:, 4:8], 1.0, None, ALU.max, None, ["dn1"], ["dn2"])
                    S.op(DVE, lambda e: e.reciprocal(dn[:, 8:12], dn[:, 4:8]), ["dn2"], ["dn3"])
                    A3_ = A_[:, :].rearrange("p (h e) -> p h e", h=4)
                    self.tt(POOL, A3_, A3_, dn[:, 8:12].unsqueeze(2).broadcast_to([128, 4, 256]), ALU.mult, [(ak, 0), (ak, 1), "dn3"], [(ak, 0), (ak, 1)])
                if mx == "G":
                    for bk in range(nbk):
                        hs = slice(bk * HB, (bk + 1) * HB)
                        bu = ring()
                        for q in range(HB):
                            h = bk * HB + q
                            self.mm(ps[bu][:, q * 128:(q + 1) * 128], ks[:, h * 128:(h + 1) * 128], vn[:, h, :], True, True, [ksk, ("vn", bk)], [PK(bu)])
                        self.tt(POOL, Sf[:, hs, :], Sf[:, hs, :], s_[:, 64 + hs.start:64 + hs.stop].unsqueeze(2).broadcast_to([128, HB, 128]), ALU.mult,
                                [("Sf", bk), (sk, "cd")], [("Sf", bk)])
                        self.tt(DVE, Sf[:, hs, :], Sf[:, hs, :], ps[bu][:, :].rearrange("p (a t) -> p a t", a=HB), ALU.add, [("Sf", bk), PK(bu)], [("Sf", bk)])
                    self.cp(ACT, Sb[:, :, :], Sf[:, :, :], [("Sf", bk) for bk in range(nbk)], ["Sb"])
                else:
                    for h in range(H):
                        bu = ring()
                        for kk in range(nkt):
                            i = h * nkt + kk
                            self.mm(ps[bu][:, kk * 256:kk * 256 + dvp], ks[:, i * 128:(i + 1) * 128], vt[b][:, h, :], True, True, [ksk, ("vt", b)], [PK(bu)])
                        if nkt == 2:
                            self.stt(Sf[:, h * 2:h * 2 + 2, :].rearrange("p a e -> p (a e)"), Sf[:, h * 2:h * 2 + 2, :].rearrange("p a e -> p (a e)"),
                                     cd_col(h), ps[bu][:, :], ALU.mult, ALU.add, [("Sf", h), PK(bu)] + cd_reads, [("Sf", h)])
                        else:
                            self.stt(Sf[:, h, :], Sf[:, h, :], cd_col(h), ps[bu][:, 0:dvp], ALU.mult, ALU.add, [("Sf", h), PK(bu)] + cd_reads, [("Sf", h)])
                    self.cp(ACT, Sb[:, :, :], Sf[:, :, :], [("Sf", h) for h in range(H)], ["Sb"])
                yield "B"
                akeys = [(ak, g_) for g_ in range(nacc)]
                if d_ == 1:
                    self.store(self.scr[cfg["ob"]][rows, :], A_[:, :], akeys, [(cfg["ob"], c)])
                    return
                self.tt(POOL, A_[:, :], A_[:, :], obt[b][:, :], ALU.add, akeys + [("obt", b)], akeys)
                A3 = A_[:, :].rearrange("p (h e) -> p h e", h=H)
                Y = yt[b]
                if "dbg_acc" in self.debug:
                    if not hasattr(self, "dbg_acc_d"):
                        self.dbg_acc_d = self.nc.dram_tensor("dbg_acc", [T, 1024], F32, kind="ExternalOutput").ap()
                    self.dma(self.dbg_acc_d[rows, :], A_[:, :], akeys, [("dbgacc", c)])
                if mx == "G":
                    self.tt(POOL, t1[:, :], A_[:, :], A_[:, :], ALU.mult, akeys, ["pp1"])
                    S.op(DVE, lambda e: e.tensor_reduce(prs[:, 0:8], t1[:, :].rearrange("p (h e) -> p h e", h=8), AX.X, ALU.add), ["pp1"], ["prs0"])
                    self.act(prs[:, 8:16], prs[:, 0:8], AF.Ln, ["prs0"], ["prs1"], bias=RMS_EPS, scale=1.0 / 128)
                    self.act(prs[:, 0:8], prs[:, 8:16], AF.Exp, ["prs1"], ["prs2"], scale=-0.5)
                    self.tt(DVE, t1[:, :].rearrange("p (h e) -> p h e", h=8), A3, prs[:, 0:8].unsqueeze(2).broadcast_to([128, 8, 128]), ALU.mult, akeys + ["prs2"], ["pp1"])
                    self.tt(POOL, t1[:, :].rearrange("p (h e) -> p h e", h=8), t1[:, :].rearrange("p (h e) -> p h e", h=8),
                            nwb[:, :].unsqueeze(1).broadcast_to([128, 8, 128]), ALU.mult, ["pp1", "nwb"], ["pp1"])
                    self.tt(DVE, Y[:, :], t1[:, :], gt[b][:, :], ALU.mult, ["pp1", ("gt", b)], [("yt", b)])
                else:
                    for h in range(4):
                        S.op(DVE, lambda e, h=h, A_=A_: e.bn_stats(pst[:, h, :], A_[:, h * 256:(h + 1) * 256]), akeys, [("pst", h)])
                        S.op(DVE, lambda e, h=h: e.bn_aggr(pmv[:, h, :], pst[:, h, :]), [("pst", h)], ["pmv"])
                    self.act(prs[:, 0:4], pmv[:, 0:4, 1:2].rearrange("p h o -> p (h o)"), AF.Ln, ["pmv"], ["prs0"], bias=LN_EPS)
                    self.act(prs[:, 4:8], prs[:, 0:4], AF.Exp, ["prs0"], ["prs1"], scale=-0.5)
                    t3 = t1[:, :].rearrange("p (h e) -> p h e", h=4)
                    self.tt(DVE, t3, A3, pmv[:, 0:4, 0:1].broadcast_to([128, 4, 256]), ALU.subtract, akeys + ["pmv"], ["pp1"])
                    self.tt(POOL, t3, t3, prs[:, 4:8].unsqueeze(2).broadcast_to([128, 4, 256]), ALU.mult, ["pp1", "prs1"], ["pp1"])
                    if "dbg_t1" in self.debug:
                        if not hasattr(self, "dbg_t1_d"):
                            self.dbg_t1_d = self.nc.dram_tensor("dbg_t1", [T, 1024], F32, kind="ExternalOutput").ap()
                            self.dbg_pmv_d = self.nc.dram_tensor("dbg_pmv", [T, 16], F32, kind="ExternalOutput").ap()
                            self.dbg_prs_d = self.nc.dram_tensor("dbg_prs", [T, 16], F32, kind="ExternalOutput").ap()
                        self.dma(self.dbg_t1_d[rows, :], t1[:, :], ["pp1"], [("dbgt1", c)])
                        self.dma(self.dbg_pmv_d[rows, 0:8], pmv[:, 0:4, :].rearrange("p a b -> p (a b)"), ["pmv"], [("dbgpmv", c)])
                        self.dma(self.dbg_prs_d[rows, 0:8], prs[:, 0:8], ["prs1", "prs0"], [("dbgprs", c)])
                    if mx == "M":
                        self.tt(POOL, t1[:, :], t1[:, :], nwb[:, :], ALU.mult, ["pp1", "nwb"], ["pp1"])
                    self.tt(DVE, Y[:, :], t1[:, :], gt[b][:, :], ALU.mult, ["pp1", ("gt", b)], [("yt", b)])
                self.store(self.scr[cfg["y"]][rows, :], Y[:, :], [("yt", b)], [(cfg["y"], c)])

            gens = [body(n, c) for n, c in enumerate(order)]
            while next(gens[0]) != "A_DONE":
                pass
            for n in range(len(gens)):
                gA = gens[n + 1] if n + 1 < len(gens) else None
                gB = gens[n]
                doneA, doneB = gA is None, False
                while not (doneA and doneB):
                    if not doneA:
                        if next(gA) == "A_DONE":
                            doneA = True
                    if not doneB:
                        try:
                            next(gB)
                        except StopIteration:
                            doneB = True
            S.barrier()
        A.off = base

    def _gdn_inverse(self, KT, KTk, s_, sk, dts, d_, Wb, pb):
        S, ps = self.S, self.ps
        PK = lambda i: ("ps", i)
        gm, M0, MTt, MTm, Um, Vm, Pm = self.ginv
        fo = 0 if d_ == 0 else 7
        to = 7 if d_ == 0 else 0
        ident = self.cst("ident")
        for hh in range(2):
            bank = hh
            for q in range(4):
                h = hh * 4 + q
                self.mm(ps[bank][:, q * 128:(q + 1) * 128], KT[:, h, :], KT[:, h, :], True, True, [KTk], [PK(bank)])
            for q in range(4):
                h = hh * 4 + q
                self.stt(M0[:, h, :], ps[bank][:, q * 128:(q + 1) * 128], s_[:, 88 + h:89 + h], dts[:, h, :], ALU.mult, ALU.mult,
                         [PK(bank), (sk, "nbeta"), ("dts", h)], [("M0", hh)])
            bank = 2 + hh
            for q in range(4):
                h = hh * 4 + q
                self.tr(ps[bank][:, q * 128:(q + 1) * 128], M0[:, h, :], ident, [("M0", hh), "cst"], [PK(bank)])
            self.cp(ACT, MTt[:, hh * 4:(hh + 1) * 4, :].rearrange("p a t -> p (a t)"), ps[bank][:, :], [PK(bank)], [("MTt", hh)])
            yield "A"
        for lev in range(1, 7):
            self.tt(POOL, MTm[:, lev - 1, :, :], MTt[:, :, :], gm[:, to + lev:to + lev + 1, :].broadcast_to([128, 8, 128]), ALU.mult,
                    [("MTt", 0), ("MTt", 1), "gm"], [("MTm", lev)])
        U = Um[0]
        self.tt(DVE, U[:, :, :], M0[:, :, :], gm[:, fo:fo + 1, :].broadcast_to([128, 8, 128]), ALU.mult, [("M0", 0), ("M0", 1), "gm"], [("Um", 0, 0), ("Um", 0, 1)])
        self.tt(DVE, U[:, :, :], U[:, :, :], ident.unsqueeze(1).broadcast_to([128, 8, 128]), ALU.add, [("Um", 0, 0), ("Um", 0, 1), "cst"], [("Um", 0, 0), ("Um", 0, 1)])
        cur = 0
        for lev in range(1, 7):
            nxt = 1 - cur
            Uc, Un = Um[cur], Um[nxt]
            for hh in range(2):
                hs = slice(hh * 4, (hh + 1) * 4)
                uk = ("Um", cur, hh)
                for q in range(4):
                    h = hh * 4 + q
                    self.tr(ps[hh][:, q * 128:(q + 1) * 128], Uc[:, h, :], ident, [uk, "cst"], [PK(hh)])
                for q in range(4):
                    h = hh * 4 + q
                    self.mm(ps[2 + hh][:, q * 128:(q + 1) * 128], MTm[:, lev - 1, h, :], Uc[:, h, :], True, True, [("MTm", lev), uk], [PK(2 + hh)])
                self.cp(ACT, Vm[:, hs, :].rearrange("p a t -> p (a t)"), ps[hh][:, :], [PK(hh)], [("Vm", hh)])
                self.cp(DVE, Pm[:, hs, :].rearrange("p a t -> p (a t)"), ps[2 + hh][:, :], [PK(2 + hh)], [("Pm", hh)])
            yield "A"
            for hh in range(2):
                hs = slice(hh * 4, (hh + 1) * 4)
                uk = ("Um", cur, hh)
                for q in range(4):
                    h = hh * 4 + q
                    self.mm(ps[hh][:, q * 128:(q + 1) * 128], Vm[:, h, :], Pm[:, h, :], True, True, [("Vm", hh), ("Pm", hh)], [PK(hh)])
                self.tt(DVE, Un[:, hs, :].rearrange("p a t -> p (a t)"), Uc[:, hs, :].rearrange("p a t -> p (a t)"), ps[hh][:, :], ALU.add,
                        [uk, PK(hh)], [("Um", nxt, hh)])
            yield "A"
            cur = nxt
        for hh in range(2):
            hs = slice(hh * 4, (hh + 1) * 4)
            self.cp(ACT if hh == 0 else DVE, Wb[:, hs, :], Um[cur][:, hs, :], [("Um", cur, hh)], [("Wb", pb, hh)])

    def postnorm(self, pbanks, xt, xk, gi, li, ub, st, mv, rs, slot, dst, dstkey):
        uk = ("ub", slot)
        for hh in range(2):
            cs = slice(hh * 512, (hh + 1) * 512)
            self.tt(DVE, ub[:, cs], self.ps[pbanks[hh]][:, :], self.gbc[:, gi, cs], ALU.mult, [("ps", pbanks[hh]), ("gbc", gi, hh)], [(uk, hh)])
            self.stt(ub[:, cs], xt[:, cs], DN_ALPHA, ub[:, cs], ALU.mult, ALU.add, [xk, (uk, hh)], [(uk, hh)])
        S = self.S
        for hh in range(2):
            S.op(DVE, lambda e, hh=hh: e.bn_stats(st[:, hh, :], ub[:, hh * 512:(hh + 1) * 512]), [(uk, hh)], [("pst", slot)])
        S.op(DVE, lambda e: e.bn_aggr(mv[:, :], st[:, :, :].rearrange("p a b -> p (a b)")), [("pst", slot)], [("pmv", slot)])
        self.act(rs[:, 2:3], mv[:, 1:2], AF.Ln, [("pmv", slot)], [("prs2", slot)], bias=LN_EPS)
        self.act(rs[:, 0:1], rs[:, 2:3], AF.Exp, [("prs2", slot)], [("prs0", slot)], scale=-0.5)
        self.ts(DVE, rs[:, 1:2], mv[:, 0:1], rs[:, 0:1], -1.0, ALU.mult, ALU.mult, [("pmv", slot), ("prs0", slot)], [("prs1", slot)])
        self.act(ub[:, :], ub[:, :], AF.Identity, [(uk, 0), (uk, 1), ("prs0", slot), ("prs1", slot)], [(uk, 0), (uk, 1)],
                 bias=rs[:, 1:2], scale=rs[:, 0:1])
        self.tt(POOL, ub[:, :], ub[:, :], self.lnl[:, 0, :], ALU.mult, [(uk, 0), (uk, 1), ("lnl", 0)], [(uk, 0), (uk, 1)])
        self.tt(POOL, ub[:, :], ub[:, :], self.lnl[:, 1, :], ALU.add, [(uk, 0), (uk, 1), ("lnl", 1)], [(uk, 0), (uk, 1)])
        return self.store(dst, ub[:, :], [(uk, 0), (uk, 1)], [dstkey])

    def phase_merge(self, l):
        A, S, ps = self.A, self.S, self.ps
        base = A.off
        last = (l == DEPTH - 1)
        self.lnl = A.alloc("lnl", [128, 2, 1024], F32)
        for j, src_ in enumerate((self.ln1_g, self.ln1_b)):
            self.dma(self.lnl[:, j, :], src_[l:l + 1, :].partition_broadcast(128), [], [("lnl", j)])
        wbr = A.alloc("wbr", [128, 3, 8, 1024], BF)
        wo = A.alloc("wo", [128, 8, 1024], BF)
        for br in range(3):
            for hh in range(2):
                self.dma(wbr[:, br, :, hh * 512:(hh + 1) * 512],
                         self.w_branch[l, br].rearrange("(kc p) c -> p kc c", p=128)[:, :, hh * 512:(hh + 1) * 512], [], [("wbr", br)], q=POOL)
        for hh in range(2):
            self.dma(wo[:, :, hh * 512:(hh + 1) * 512], self.w_out[l].rearrange("(kc p) c -> p kc c", p=128)[:, :, hh * 512:(hh + 1) * 512], [], ["wo"], q=POOL)
        yin = [[A.alloc("yin", [128, 1024], BF) for _ in range(3)] for _ in range(2)]
        mg = [A.alloc("mg", [128, 3072], BF) for _ in range(2)]
        xt = [A.alloc("xt", [128, 1024], F32) for _ in range(2)]
        yT = [A.alloc("yT", [128, 8, 128], BF) for _ in range(3)]
        mrg = A.alloc("mrg", [128, 1024], F32)
        mt2 = A.alloc("mt2", [128, 512], F32)
        mrb = A.alloc("mrb", [128, 1024], BF)
        mT = A.alloc("mT", [128, 8, 128], BF)
        ub = [A.alloc("ub", [128, 1024], F32) for _ in range(2)]
        st = [A.alloc("st", [128, 2, 6], F32) for _ in range(2)]
        mv = [A.alloc("mv", [128, 2], F32) for _ in range(2)]
        rs = [A.alloc("rs", [128, 4], F32) for _ in range(2)]
        src = self.xz if l == 0 else self.scr["X2"]
        names = ("Y_R", "Y_M", "Y_G")
        n = 0
        for t in range(2 if last else 0, NT):
            b = n % 2
            n += 1
            rows = slice(t * 128, (t + 1) * 128)
            v = 1 if t < 2 else 0
            for br in range(3):
                self.dma(yin[b][br][:, :], self.scr[names[br]][rows, :], [], [("yin", b, br)])
            self.dma(mg[b][:, :], self.scr["MG"][rows, :], [], [("mg", b)])
            self.dma(xt[b][:, :], src[rows, :], [], [("xt", b)])
            for br in range(3):
                bank = br % 2
                pv = ps[bank][:, :].bitcast(BF)
                for c in range(8):
                    self.tr(pv[:, c * 128:(c + 1) * 128], yin[b][br][:, c * 128:(c + 1) * 128], self.identb[:, :],
                            [("yin", b, br), "identb"], [("ps", bank)])
                self.cp(ACT if br != 1 else DVE, yT[br][:, :, :].rearrange("p a t -> p (a t)"), pv[:, :], [("ps", bank)], [("yT", br)])
            for hh in range(2):
                cs = slice(hh * 512, (hh + 1) * 512)
                for br in range(3):
                    for kc in range(8):
                        self.mm(ps[2 + br][:, :], yT[br][:, kc, :], wbr[:, br, kc, cs], kc == 0, kc == 7, [("yT", br), ("wbr", br)], [("ps", 2 + br)])
                self.tt(DVE, mrg[:, cs], ps[2][:, :], mg[b][:, hh * 512:(hh + 1) * 512], ALU.mult, [("ps", 2), ("mg", b)], [("mrg", hh)])
                self.tt(DVE, mt2[:, :], ps[3][:, :], mg[b][:, 1024 + hh * 512:1024 + (hh + 1) * 512], ALU.mult, [("ps", 3), ("mg", b)], ["mt2"])
                self.tt(DVE, mrg[:, cs], mrg[:, cs], mt2[:, :], ALU.add, [("mrg", hh), "mt2"], [("mrg", hh)])
                self.tt(DVE, mt2[:, :], ps[4][:, :], mg[b][:, 2048 + hh * 512:2048 + (hh + 1) * 512], ALU.mult, [("ps", 4), ("mg", b)], ["mt2"])
                self.tt(DVE, mrb[:, cs], mrg[:, cs], mt2[:, :], ALU.add, [("mrg", hh), "mt2"], [("mrb", hh)])
            pv = ps[5][:, :].bitcast(BF)
            for c in range(8):
                self.tr(pv[:, c * 128:(c + 1) * 128], mrb[:, c * 128:(c + 1) * 128], self.identb[:, :], [("mrb", c // 4), "identb"], [("ps", 5)])
            self.cp(ACT, mT[:, :, :].rearrange("p a t -> p (a t)"), pv[:, :], [("ps", 5)], ["mT"])
            for hh in range(2):
                for kc in range(8):
                    self.mm(ps[6 + hh][:, :], mT[:, kc, :], wo[:, kc, hh * 512:(hh + 1) * 512], kc == 0, kc == 7, ["mT", "wo"], [("ps", 6 + hh)])
            self.postnorm((6, 7), xt[b], ("xt", b), 0 + v, 0, ub[b], st[b], mv[b], rs[b], b, self.scr["X1"][rows, :], ("X1", t))
        S.barrier()
        A.off = base

    def phase_mlp(self, l):
        A, S, ps = self.A, self.S, self.ps
        base = A.off
        last = (l == DEPTH - 1)
        self.lnl = A.alloc("lnl", [128, 2, 1024], F32)
        for j, src_ in enumerate((self.ln2_g, self.ln2_b)):
            self.dma(self.lnl[:, j, :], src_[l:l + 1, :].partition_broadcast(128), [], [("lnl", j)])
        w1 = A.alloc("w1", [128, 8, DFF], BF)
        w2 = A.alloc("w2", [128, 32, D], BF)
        w1v = self.w_mlp1[l].rearrange("(kc p) c -> p kc c", p=128)
        w2v = self.w_mlp2[l].rearrange("(fc p) c -> p fc c", p=128)
        for i in range(8):
            self.dma(w1[:, :, i * 512:(i + 1) * 512], w1v[:, :, i * 512:(i + 1) * 512], [], [("w1", i)], q=POOL)
        for i in range(8):
            self.dma(w2[:, i * 4:(i + 1) * 4, :], w2v[:, i * 4:(i + 1) * 4, :], [], [("w2", i)], q=POOL)
        xt = [A.alloc("xt", [128, 1024], F32) for _ in range(3)]
        xn = [A.alloc("xn", [128, 1024], F32) for _ in range(1)] * 2
        h2T = A.alloc("h2T", [128, 8, 256], BF)
        hid = A.alloc("hid", [128, 32, 256], BF)
        rl = [A.alloc("rl", [128, 256], F32) for _ in range(2)]
        ub = [A.alloc("ub", [128, 1024], F32) for _ in range(1)] * 2
        st = [A.alloc("st", [128, 2, 6], F32) for _ in range(4)]
        mv = [A.alloc("mv", [128, 2], F32) for _ in range(4)]
        rs = [A.alloc("rs", [128, 4], F32) for _ in range(4)]
        tiles = list(range(2 if last else 0, NT))
        blocks = [tiles[i:i + 2] for i in range(0, len(tiles), 2)]
        xi = 0
        un = 0
        outs = []
        for blk in blocks:
            nb = len(blk)
            xts = []
            for j, t in enumerate(blk):
                b5 = xi % 3
                xi += 1
                b = 0
                v = 1 if t < 2 else 0
                rows = slice(t * 128, (t + 1) * 128)
                X = xt[b5]
                xk = ("xt", b5)
                xts.append((X, xk))
                self.dma(X[:, :], self.scr["X1"][rows, :], [], [xk])
                self.ln_stats(X, xk, st[b], mv[b], rs[b], ("m", b))
                self.act(xn[b][:, :], X[:, :], AF.Identity, [xk, ("rs0", ("m", b)), ("rs1", ("m", b))], [("xn", b)],
                         bias=rs[b][:, 1:2], scale=rs[b][:, 0:1])
                for c in range(8):
                    self.tr(ps[c // 4][:, (c % 4) * 128:(c % 4 + 1) * 128], xn[b][:, c * 128:(c + 1) * 128], self.cst("ident"),
                            [("xn", b), "cst"], [("ps", c // 4)])
                for c in range(8):
                    o = h2T[:, c, j * 128:(j + 1) * 128]
                    i_ = ps[c // 4][:, (c % 4) * 128:(c % 4 + 1) * 128]
                    sc_ = self.ops[:, 1, c, v:v + 1]
                    sh_ = self.modc[:, 24 + c, v:v + 1]
                    if c // 4 == 0:
                        self.ts(DVE, o, i_, sc_, sh_, ALU.mult, ALU.add, [("ps", 0), "modc", "ops"], [("h2T", j, c)])
                    else:
                        self.act(o, i_, AF.Identity, [("ps", 1), "modc", "ops"], [("h2T", j, c)], bias=sh_, scale=sc_)
            hkeys = [("h2T", j, c) for j in range(nb) for c in range(8)]
            N = nb * 128
            for f in range(32):
                bank = 2 + f % 4
                for kc in range(8):
                    self.mm(ps[bank][:, 0:N], w1[:, kc, f * 128:(f + 1) * 128], h2T[:, kc, 0:N], kc == 0, kc == 7,
                            hkeys + [("w1", f // 4)], [("ps", bank)])
                r_ = rl[f % 2]
                self.act(r_[:, 0:N], ps[bank][:, 0:N], AF.Relu, [("ps", bank)], [("rl", f % 2)])
                self.tt(POOL if f % 2 == 0 else DVE, hid[:, f, 0:N], r_[:, 0:N], r_[:, 0:N], ALU.mult, [("rl", f % 2)], [("hid", f)])
            for j, t in enumerate(blk):
                v = 1 if t < 2 else 0
                rows = slice(t * 128, (t + 1) * 128)
                for f in range(32):
                    for hh in range(2):
                        self.mm(ps[6 + hh][:, :], hid[:, f, j * 128:(j + 1) * 128], w2[:, f, hh * 512:(hh + 1) * 512], f == 0, f == 31,
                                [("hid", f), ("w2", f // 4)], [("ps", 6 + hh)])
                if last:
                    dst = self.out[t * 128 - NCTX:(t + 1) * 128 - NCTX, :]
                else:
                    dst = self.scr["X2"][rows, :]
                u = 0
                un += 1
                X, xk = xts[j]
                tok = self.postnorm((6, 7), X, xk, 2 + v, 2, ub[u], st[2 + u], mv[2 + u], rs[2 + u], ("p", u), dst, ("X2", t))
                outs.append(tok)
        S.barrier()
        A.off = base
        return outs

    def build(self):
        self.setup()
        for l in range(self.layers):
            self.phase_ada(l)
            if self.upto == "ada":
                break
            A = self.A
            A.off = self.persist_end
            hT = A.alloc("hT", [128, 8, T], BF)
            self.hT = hT
            src = self.xz if l == 0 else self.scr["X2"]
            if not getattr(self, "skip_inproj", False):
                self.phase_ln1(l, src, hT)
            if "dbg_hT" in self.debug:
                self.S.barrier()
                hb = A.alloc("hdbg", [128, 1024], F32)
                self.cp(DVE, hb[:, :].rearrange("p (c t) -> p c t", c=8), hT[:, :, 0:128], [], ["hdbg"])
                self.dbg("dbg_hT", hb[:, :], [128, 1024], ["hdbg"])
            if self.upto == "ln1":
                break
            if not getattr(self, "skip_inproj", False):
                self.phase_inproj(l, hT)
            self.S.barrier()
            if self.upto == "inproj":
                break
            A.off = self.persist_end
            self.scan_setup(l)
            for mx in getattr(self, "mixers", "RMG"):
                self.phase_scan(l, mx)
            if self.upto == "scan":
                break
            self.S.barrier()
            A.off = self.persist_end
            if not getattr(self, "skip_merge", False):
                self.phase_merge(l)
            if self.upto == "merge":
                break
            self.phase_mlp(l)
        self.S.barrier()
        self.S.emit()


def prep_inputs(inputs, b):
    f = lambda a: np.ascontiguousarray(np.asarray(a, np.float32))
    m = {}
    m["xz"] = f(np.concatenate([inputs["ctx"][b], inputs["x"][b]], 0))
    cc = np.stack([np.asarray(inputs["c"][b]), np.asarray(inputs["c_ctx"])], -1)
    m["cc"] = f(cc.reshape(8, 128, 2).transpose(1, 0, 2))
    m["w_ada"] = f(inputs["w_ada"])
    m["b_ada"] = f(np.asarray(inputs["b_ada"]).reshape(DEPTH, 48, 128).transpose(0, 2, 1))
    m["w_in"] = f(inputs["w_in"])
    m["conv_w"] = f(np.asarray(inputs["conv_w"]).reshape(DEPTH, 5, 24, 128).transpose(0, 3, 2, 1))
    m["smallp"] = f(np.concatenate([np.asarray(inputs[k]).reshape(DEPTH, -1) for k in
                                    ("ret_decay", "mlstm_i_bias", "mlstm_f_bias", "gdn_a_log", "gdn_dt_bias")], 1)[:, :56])
    m["smallp"] = f(np.pad(m["smallp"], ((0, 0), (0, 8))))
    for k in ("mlstm_norm_w", "gdn_norm_w", "w_branch", "w_out", "ln1_g", "ln1_b", "ln2_g", "ln2_b", "w_mlp1", "w_mlp2"):
        m[k] = f(inputs[k])
    m["consts"] = f(CONSTS)
    rq, rk = _rope_tables()
    m["ropeq"], m["ropek"] = f(rq), f(rk)
    m["gmask"] = f(_gmasks())
    return m


def kernel(**inputs):
    nc = bass.Bass("TRN2", target_bir_lowering=False)
    Builder(nc).build()
    in_maps = [prep_inputs(inputs, b) for b in range(8)]
    res = run_bass_kernel_spmd(nc, in_maps, core_ids=list(range(8)))
    return np.stack([np.asarray(r["out"], np.float32) for r in res.results], 0)
```
